# Optimizing a Trainium2 kernel written in Bass

```python
import math
import jax
import jax.numpy as jnp
from jax import lax
import numpy as np

D_MODEL = 1024
BATCH = 8
SEQ = 2048
DEPTH = 4

N_EVEN = (DEPTH + 1) // 2
N_ODD = DEPTH // 2

NORM_EPS = 1e-6

RG_W = D_MODEL
RG_BLOCKS = 8
RG_BLOCK_W = RG_W // RG_BLOCKS
RG_C = 8.0
CONV_WIDTH = 4

MLA_HEADS = 8
MLA_NOPE = 128
MLA_ROPE = 64
MLA_V = 128
MLA_QK = MLA_NOPE + MLA_ROPE
Q_LORA = 512
KV_LORA = 256
ROPE_THETA = 10000.0
Q_BLOCK = 128

HY_SIZES = (RG_W, RG_W, Q_LORA, KV_LORA, MLA_ROPE)
HY_IN = RG_W + RG_W + Q_LORA + KV_LORA + MLA_ROPE
HY_SPLITS = [RG_W, 2 * RG_W, 2 * RG_W + Q_LORA, 2 * RG_W + Q_LORA + KV_LORA]
HY_MIX = RG_W + MLA_HEADS * MLA_V

GDN_K_HEADS = 8
GDN_V_HEADS = 16
GDN_HEAD_DIM = 128
GDN_QK_W = GDN_K_HEADS * GDN_HEAD_DIM
GDN_V_W = GDN_V_HEADS * GDN_HEAD_DIM
GDN_CONV_W = 2 * GDN_QK_W + GDN_V_W
GDN_IN = GDN_CONV_W + GDN_V_W + 2 * GDN_V_HEADS
GDN_SPLITS = [GDN_CONV_W, GDN_CONV_W + GDN_V_W, GDN_CONV_W + GDN_V_W + GDN_V_HEADS]
GDN_CHUNK = 64

FFN_HIDDEN = -(-8 * D_MODEL // (3 * 256)) * 256

kernel_name = "hybrid_rglru_mla_gdn_sandwich"


def rms_norm(x, w):
    xf = x.astype(jnp.float32)
    y = xf * lax.rsqrt(jnp.mean(xf * xf, axis=-1, keepdims=True) + NORM_EPS)
    return (y * w.astype(jnp.float32)).astype(x.dtype)


def l2_norm(x):
    xf = x.astype(jnp.float32)
    return xf * lax.rsqrt(jnp.sum(xf * xf, axis=-1, keepdims=True) + NORM_EPS)


def causal_depthwise_conv(x, w):
    width, chans = w.shape
    return lax.conv_general_dilated(
        x, w[:, None, :].astype(x.dtype), window_strides=(1,),
        padding=((width - 1, 0),), dimension_numbers=("NWC", "WIO", "NWC"),
        feature_group_count=chans)


def rope_tables(positions):
    inv_freq = 1.0 / (ROPE_THETA ** (jnp.arange(0, MLA_ROPE, 2, dtype=jnp.float32) / MLA_ROPE))
    ang = positions.astype(jnp.float32)[..., None] * inv_freq
    return jnp.cos(ang), jnp.sin(ang)


def apply_rope(x, cos, sin):
    half = x.shape[-1] // 2
    c = cos[:, :, None, :].astype(x.dtype)
    s = sin[:, :, None, :].astype(x.dtype)
    x1, x2 = x[..., :half], x[..., half:]
    return jnp.concatenate([x1 * c - x2 * s, x2 * c + x1 * s], axis=-1)


def causal_block_attention(q, k, v, scale):
    T = q.shape[1]
    outs = []
    for start in range(0, T, Q_BLOCK):
        end = min(start + Q_BLOCK, T)
        qb, kb, vb = q[:, start:end], k[:, :end], v[:, :end]
        s = jnp.einsum('bqhd,bkhd->bhqk', qb, kb).astype(jnp.float32) * scale
        mask = (start + jnp.arange(end - start))[:, None] >= jnp.arange(end)[None, :]
        p = jax.nn.softmax(jnp.where(mask, s, -jnp.inf), axis=-1).astype(v.dtype)
        outs.append(jnp.einsum('bhqk,bkhd->bqhd', p, vb))
    return jnp.concatenate(outs, axis=1)


def _lru_combine(e1, e2):
    a1, b1 = e1
    a2, b2 = e2
    return a1 * a2, a2 * b1 + b2


def rglru_branch(x_r, gate_r, conv_w, conv_b, gate_a_w, gate_a_b, gate_x_w, gate_x_b, lam):
    B, T, _ = x_r.shape
    xc = causal_depthwise_conv(x_r, conv_w) + conv_b.astype(x_r.dtype)
    xb = xc.reshape(B, T, RG_BLOCKS, RG_BLOCK_W)
    r = jax.nn.sigmoid((jnp.einsum('btni,nij->btnj', xb, gate_a_w).reshape(B, T, RG_W)
                        + gate_a_b).astype(jnp.float32))
    i = jax.nn.sigmoid((jnp.einsum('btni,nij->btnj', xb, gate_x_w).reshape(B, T, RG_W)
                        + gate_x_b).astype(jnp.float32))
    log_a = -RG_C * r * jax.nn.softplus(-lam.astype(jnp.float32))
    a = jnp.exp(log_a)
    b = jnp.sqrt(-jnp.expm1(2.0 * log_a)) * (i * xc.astype(jnp.float32))
    _, h = lax.associative_scan(_lru_combine, (a, b), axis=1)
    return h.astype(x_r.dtype) * jax.nn.gelu(gate_r)


def mla_branch(c_q, c_kv, k_pe, cos, sin, q_norm, w_uq, kv_norm, w_ukv):
    B, T, _ = c_q.shape
    q = (rms_norm(c_q, q_norm) @ w_uq).reshape(B, T, MLA_HEADS, MLA_QK)
    q = jnp.concatenate([q[..., :MLA_NOPE], apply_rope(q[..., MLA_NOPE:], cos, sin)], axis=-1)
    kv = (rms_norm(c_kv, kv_norm) @ w_ukv).reshape(B, T, MLA_HEADS, MLA_NOPE + MLA_V)
    k_nope, v = kv[..., :MLA_NOPE], kv[..., MLA_NOPE:]
    k_rope = apply_rope(k_pe[:, :, None, :], cos, sin)
    k = jnp.concatenate([k_nope, jnp.broadcast_to(k_rope, (B, T, MLA_HEADS, MLA_ROPE))], axis=-1)
    o = causal_block_attention(q, k, v, MLA_QK ** -0.5)
    return o.reshape(B, T, MLA_HEADS * MLA_V)


def hybrid_rglru_mla(h, cos, sin, w_in, conv_w, conv_b, gate_a_w, gate_a_b, gate_x_w,
                     gate_x_b, lam, q_norm, w_uq, kv_norm, w_ukv, w_out):
    proj = h @ w_in
    x_r, gate_r, c_q, c_kv, k_pe = jnp.split(proj, HY_SPLITS, axis=-1)
    y_a = rglru_branch(x_r, gate_r, conv_w, conv_b, gate_a_w, gate_a_b, gate_x_w, gate_x_b, lam)
    y_b = mla_branch(c_q, c_kv, k_pe, cos, sin, q_norm, w_uq, kv_norm, w_ukv)
    return jnp.concatenate([y_a, y_b], axis=-1) @ w_out


def chunk_gated_delta_rule(q, k, v, g, beta):
    B, T, H, DK = q.shape
    DV = v.shape[-1]
    N = T // GDN_CHUNK
    f32 = jnp.float32

    def chunks(t):
        return t.astype(f32).reshape(B, N, GDN_CHUNK, H, -1).transpose(0, 1, 3, 2, 4)

    qc, kc, vc = chunks(q), chunks(k), chunks(v)
    gch = chunks(g[..., None])[..., 0]
    bch = chunks(beta[..., None])[..., 0]
    gc = jnp.cumsum(gch, axis=-1)
    idx = jnp.arange(GDN_CHUNK)
    causal = idx[:, None] >= idx[None, :]
    strict = idx[:, None] > idx[None, :]
    decay = jnp.exp(jnp.where(causal, gc[..., :, None] - gc[..., None, :], -jnp.inf))
    k_beta = kc * bch[..., None]
    a_strict = jnp.where(strict, jnp.einsum('bnhid,bnhjd->bnhij', k_beta, kc) * decay, 0.0)
    lhs = a_strict + jnp.eye(GDN_CHUNK, dtype=f32)
    u = lax.linalg.triangular_solve(lhs, vc * bch[..., None], left_side=True,
                                    lower=True, unit_diagonal=True)
    w = lax.linalg.triangular_solve(lhs, k_beta * jnp.exp(gc)[..., None], left_side=True,
                                    lower=True, unit_diagonal=True)
    qk = jnp.einsum('bnhid,bnhjd->bnhij', qc, kc) * decay
    q_dec = qc * jnp.exp(gc)[..., None]
    k_dec = kc * jnp.exp(gc[..., -1:] - gc)[..., None]
    chunk_decay = jnp.exp(gc[..., -1])

    def step(S, xs):
        u_n, w_n, qk_n, q_n, k_n, d_n = xs
        v_new = u_n - jnp.einsum('bhck,bhkv->bhcv', w_n, S)
        o_n = jnp.einsum('bhck,bhkv->bhcv', q_n, S) + jnp.einsum('bhij,bhjv->bhiv', qk_n, v_new)
        S = S * d_n[..., None, None] + jnp.einsum('bhck,bhcv->bhkv', k_n, v_new)
        return S, o_n

    xs = tuple(jnp.moveaxis(t, 1, 0) for t in (u, w, qk, q_dec, k_dec, chunk_decay))
    S0 = jnp.zeros((B, H, DK, DV), f32)
    _, o = lax.scan(step, S0, xs)
    return o.transpose(1, 0, 3, 2, 4).reshape(B, T, H, DV).astype(v.dtype)


def gated_delta_net(h, w_in, conv_w, a_log, dt_bias, norm_w, w_out):
    B, T, _ = h.shape
    proj = h @ w_in
    qkv, z, b, a = jnp.split(proj, GDN_SPLITS, axis=-1)
    qkv = jax.nn.silu(causal_depthwise_conv(qkv, conv_w))
    q, k, v = jnp.split(qkv, [GDN_QK_W, 2 * GDN_QK_W], axis=-1)
    rep = GDN_V_HEADS // GDN_K_HEADS
    q = l2_norm(q.reshape(B, T, GDN_K_HEADS, GDN_HEAD_DIM)) * GDN_HEAD_DIM ** -0.5
    k = l2_norm(k.reshape(B, T, GDN_K_HEADS, GDN_HEAD_DIM))
    q = jnp.repeat(q, rep, axis=2)
    k = jnp.repeat(k, rep, axis=2)
    v = v.reshape(B, T, GDN_V_HEADS, GDN_HEAD_DIM)
    beta = jax.nn.sigmoid(b.astype(jnp.float32))
    g = -jnp.exp(a_log.astype(jnp.float32)) * jax.nn.softplus(
        a.astype(jnp.float32) + dt_bias.astype(jnp.float32))
    o = chunk_gated_delta_rule(q, k, v, g, beta)
    o = rms_norm(o, norm_w) * jax.nn.silu(z.reshape(B, T, GDN_V_HEADS, GDN_HEAD_DIM))
    return o.reshape(B, T, GDN_V_W) @ w_out


def swiglu(h, w_gate, w_up, w_down):
    return (jax.nn.silu(h @ w_gate) * (h @ w_up)) @ w_down


def setup_inputs(seed: int = 0) -> dict:
    key = jax.random.key(seed)
    ks = list(jax.random.split(key, 32))
    f32 = jnp.float32

    def normal(k, shape, fan_in):
        return jax.random.normal(k, shape, f32) * fan_in ** -0.5

    def gain(k, shape):
        return 1.0 + 0.02 * jax.random.normal(k, shape, f32)

    def bias(k, shape):
        return 0.01 * jax.random.normal(k, shape, f32)

    x = jax.random.normal(ks[0], (BATCH, SEQ, D_MODEL), f32)
    offset = jax.random.randint(ks[1], (BATCH, 1), 0, 4096, dtype=jnp.int32)
    positions = offset + jnp.arange(SEQ, dtype=jnp.int32)[None, :]

    a_c = jax.random.uniform(ks[12], (N_EVEN, RG_W), f32, 0.9, 0.999)
    s = a_c ** (1.0 / RG_C)
    rg_lambda = jnp.log(s) - jnp.log1p(-s)

    a_log = jnp.log(jax.random.uniform(ks[20], (N_ODD, GDN_V_HEADS), f32, 1.0, 16.0))
    dt = jnp.exp(jax.random.uniform(ks[21], (N_ODD, GDN_V_HEADS), f32,
                                    math.log(1e-3), math.log(1e-1)))
    dt_bias = dt + jnp.log(-jnp.expm1(-dt))

    return {
        "x": x,
        "positions": positions,
        "norm_mix_pre": gain(ks[2], (DEPTH, D_MODEL)),
        "norm_mix_post": gain(ks[3], (DEPTH, D_MODEL)),
        "norm_ffn_pre": gain(ks[4], (DEPTH, D_MODEL)),
        "norm_ffn_post": gain(ks[5], (DEPTH, D_MODEL)),
        "hy_w_in": normal(ks[6], (N_EVEN, D_MODEL, HY_IN), D_MODEL),
        "rg_conv_w": normal(ks[7], (N_EVEN, CONV_WIDTH, RG_W), CONV_WIDTH),
        "rg_conv_b": bias(ks[8], (N_EVEN, RG_W)),
        "rg_gate_a_w": normal(ks[9], (N_EVEN, RG_BLOCKS, RG_BLOCK_W, RG_BLOCK_W), RG_BLOCK_W),
        "rg_gate_a_b": bias(ks[10], (N_EVEN, RG_W)),
        "rg_gate_x_w": normal(ks[11], (N_EVEN, RG_BLOCKS, RG_BLOCK_W, RG_BLOCK_W), RG_BLOCK_W),
        "rg_gate_x_b": bias(ks[13], (N_EVEN, RG_W)),
        "rg_lambda": rg_lambda,
        "mla_q_norm": gain(ks[14], (N_EVEN, Q_LORA)),
        "mla_w_uq": normal(ks[15], (N_EVEN, Q_LORA, MLA_HEADS * MLA_QK), Q_LORA),
        "mla_kv_norm": gain(ks[16], (N_EVEN, KV_LORA)),
        "mla_w_ukv": normal(ks[17], (N_EVEN, KV_LORA, MLA_HEADS * (MLA_NOPE + MLA_V)), KV_LORA),
        "hy_w_out": normal(ks[18], (N_EVEN, HY_MIX, D_MODEL), HY_MIX),
        "gdn_w_in": normal(ks[19], (N_ODD, D_MODEL, GDN_IN), D_MODEL),
        "gdn_conv_w": normal(ks[22], (N_ODD, CONV_WIDTH, GDN_CONV_W), CONV_WIDTH),
        "gdn_a_log": a_log,
        "gdn_dt_bias": dt_bias,
        "gdn_norm": gain(ks[23], (N_ODD, GDN_HEAD_DIM)),
        "gdn_w_out": normal(ks[24], (N_ODD, GDN_V_W, D_MODEL), GDN_V_W),
        "ffn_w_gate": normal(ks[25], (DEPTH, D_MODEL, FFN_HIDDEN), D_MODEL),
        "ffn_w_up": normal(ks[26], (DEPTH, D_MODEL, FFN_HIDDEN), D_MODEL),
        "ffn_w_down": normal(ks[27], (DEPTH, FFN_HIDDEN, D_MODEL), FFN_HIDDEN),
    }


def reference(x, positions, norm_mix_pre, norm_mix_post, norm_ffn_pre, norm_ffn_post,
              hy_w_in, rg_conv_w, rg_conv_b, rg_gate_a_w, rg_gate_a_b, rg_gate_x_w,
              rg_gate_x_b, rg_lambda, mla_q_norm, mla_w_uq, mla_kv_norm, mla_w_ukv, hy_w_out,
              gdn_w_in, gdn_conv_w, gdn_a_log, gdn_dt_bias, gdn_norm, gdn_w_out,
              ffn_w_gate, ffn_w_up, ffn_w_down):
    cos, sin = rope_tables(positions)
    for layer in range(DEPTH):
        i = layer // 2
        h = rms_norm(x, norm_mix_pre[layer])
        if layer % 2 == 0:
            m = hybrid_rglru_mla(h, cos, sin, hy_w_in[i], rg_conv_w[i], rg_conv_b[i],
                                 rg_gate_a_w[i], rg_gate_a_b[i], rg_gate_x_w[i], rg_gate_x_b[i],
                                 rg_lambda[i], mla_q_norm[i], mla_w_uq[i], mla_kv_norm[i],
                                 mla_w_ukv[i], hy_w_out[i])
        else:
            m = gated_delta_net(h, gdn_w_in[i], gdn_conv_w[i], gdn_a_log[i], gdn_dt_bias[i],
                                gdn_norm[i], gdn_w_out[i])
        x = x + rms_norm(m, norm_mix_post[layer])
        h = rms_norm(x, norm_ffn_pre[layer])
        x = x + rms_norm(swiglu(h, ffn_w_gate[layer], ffn_w_up[layer], ffn_w_down[layer]),
                         norm_ffn_post[layer])
    return x
```

```python
import contextlib
import numpy as np
import concourse.bass as bass
import concourse.mybir as mybir
from concourse.bass_utils import run_bass_kernel_spmd

F32 = mybir.dt.float32
BF16 = mybir.dt.bfloat16
I32 = mybir.dt.int32
AF = mybir.ActivationFunctionType
ALU = mybir.AluOpType
AX = mybir.AxisListType

ENGS = ("pe", "dve", "act", "pool", "sp")
EPOCH = 24000


class Tok:
    __slots__ = ("lw", "rd", "name")

    def __init__(self, name=""):
        self.lw = None
        self.rd = {}
        self.name = name


class Op:
    __slots__ = ("eng", "fn", "waits", "inc", "known", "dma")

    def __init__(self, eng, fn):
        self.eng = eng
        self.fn = fn
        self.waits = []
        self.inc = False
        self.known = None
        self.dma = None


class Prog:
    def __init__(self, nc, same_engine_sync=True):
        self.nc = nc
        self.ops = {e: [] for e in ENGS}
        self.known = {e: {} for e in ENGS}
        self.dma_cnt = {}
        self.dma_ops = {}
        self.same_engine_sync = same_engine_sync
        self.nwaits = 0

    def _lookup(self, s, p):
        if isinstance(s, tuple):
            return self.dma_ops[s][p - 1]
        return self.ops[s][p]

    def add(self, eng, fn, reads=(), writes=(), dma_key=None):
        ops = self.ops[eng]
        idx = len(ops)
        op = Op(eng, fn)
        deps = {}
        for t in reads:
            if t.lw is not None:
                s, p = t.lw
                if deps.get(s, -1) < p:
                    deps[s] = p
        for t in writes:
            if t.lw is not None:
                s, p = t.lw
                if deps.get(s, -1) < p:
                    deps[s] = p
            for s, p in t.rd.items():
                if deps.get(s, -1) < p:
                    deps[s] = p
        known = self.known[eng]
        for s, p in deps.items():
            if s == eng and dma_key is None and (eng == "pe" or not self.same_engine_sync):
                continue
            if known.get(s, -1) >= p:
                continue
            op.waits.append((s, p))
            known[s] = p
            src = self._lookup(s, p)
            for s2, p2 in src.known.items():
                if known.get(s2, -1) < p2:
                    known[s2] = p2
            src.inc = True
        self.nwaits += len(op.waits)
        if dma_key is None:
            mypos = (eng, idx)
        else:
            k = ("dma", dma_key)
            n = self.dma_cnt.get(k, 0) + 1
            self.dma_cnt[k] = n
            self.dma_ops.setdefault(k, []).append(op)
            mypos = (k, n)
            op.dma = k
        op.known = dict(known)
        for t in reads:
            if t.rd.get(mypos[0], -1) < mypos[1]:
                t.rd[mypos[0]] = mypos[1]
        for t in writes:
            t.lw = mypos
            t.rd = {}
        ops.append(op)
        return op

    def emit(self, stack):
        nc = self.nc
        pref = {}
        nsem = {}
        for e in ENGS:
            c = 0
            arr = []
            for op in self.ops[e]:
                if op.inc and op.dma is None:
                    c += 1
                arr.append(c)
            pref[e] = arr
            nsem[e] = (c + EPOCH - 1) // EPOCH
        sems = {e: [stack.enter_context(nc.semaphore(f"s_{e}_{i}")) for i in range(nsem[e])]
                for e in ENGS}
        dsems = {k: stack.enter_context(nc.semaphore("d_%d" % i))
                 for i, k in enumerate(self.dma_cnt)}
        self.n_sems = sum(nsem.values()) + len(dsems)
        block = stack.enter_context(nc.Block())
        hw = {"pe": block.tensor, "dve": block.vector, "act": block.scalar,
              "pool": block.gpsimd, "sp": block.sync}

        def run(e):
            def body(engine):
                for op in self.ops[e]:
                    for s, p in op.waits:
                        if isinstance(s, tuple):
                            engine.wait_ge(dsems[s], 16 * p)
                        else:
                            c = pref[s][p]
                            engine.wait_ge(sems[s][(c - 1) // EPOCH], (c - 1) % EPOCH + 1)
                    if op.fn is None:
                        continue
                    ins = op.fn(engine)
                    if op.dma is not None:
                        ins.then_inc(dsems[op.dma], 16)
                    elif op.inc:
                        c = pref[e][self.ops[e].index(op)] if False else None
                        ins.then_inc(sems[e][(op_c[id(op)] - 1) // EPOCH], 1)
            return body

        op_c = {}
        for e in ENGS:
            for i, op in enumerate(self.ops[e]):
                if op.inc and op.dma is None:
                    op_c[id(op)] = pref[e][i]
        for e in ENGS:
            hw[e](run(e))


T = 2048
D = 1024
KT = 8
TB = 512
NB = T // TB
DEPTH = 4
FF = 2816
FT = 22
SLOT_EL = 2048
EPS = 1e-6
NEG = -30000.0
TWO_PI = 6.283185307179586


def vec_layout():
    off = {}
    c = 0

    def put(name, n):
        nonlocal c
        off[name] = c
        c += n
    for l in range(4):
        for nm in ("nmp", "nmo", "nfp", "nfo"):
            put(f"{nm}{l}", 8)
    for i in range(2):
        for j in range(4):
            put(f"cw{i}_{j}", 8)
        for nm in ("cb", "gab", "gxb", "lam"):
            put(f"{nm}{i}", 8)
        put(f"qn{i}", 4)
        put(f"kvn{i}", 2)
    for i in range(2):
        for j in range(4):
            put(f"gcw{i}_{j}", 32)
        put(f"gn{i}", 1)
        put(f"alog{i}", 16)
        put(f"dtb{i}", 16)
    put("invf", 1)
    put("sgn", 1)
    return off, c


def cst_layout():
    off = {}
    c = 0
    for nm in ("ident", "amask", "U2", "SL2", "NMS", "NMC", "BD", "ONA", "ONB"):
        off[nm] = c
        c += 128
    return off, c


def slot_layout():
    off = {}
    c = 0

    def put(name, n):
        nonlocal c
        off[name] = c
        c += n
    for i in range(2):
        put(f"e_win{i}", 12)
        put(f"e_gate{i}", 1)
        put(f"e_uq{i}", 4)
        put(f"e_ukv{i}", 2)
        put(f"e_wout{i}", 8)
    for i in range(2):
        put(f"g_win{i}", 24)
        put(f"g_wba{i}", 1)
        put(f"g_wout{i}", 8)
    for l in range(4):
        put(f"f_gu{l}", 22)
        put(f"f_down{l}", 16)
    return off, c


def _tile_w(W, k0, nk, cols):
    sub = W[k0:k0 + nk * 128][:, cols]
    return sub.reshape(nk, 128, -1).transpose(1, 0, 2).reshape(128, -1)


def _pad(a):
    out = np.zeros((128, SLOT_EL), np.float32)
    out[:, :a.shape[1]] = a
    return out


def pack_weights(inp):
    soff, ns = slot_layout()
    W = np.zeros((ns, 128, SLOT_EL), np.float32)
    ar = np.arange
    for i in range(2):
        win = inp["hy_w_in"][i]
        s0 = soff[f"e_win{i}"]
        for n in range(8):
            cols = np.concatenate([ar(n * 128, n * 128 + 128), ar(1024 + n * 128, 1024 + n * 128 + 128)])
            W[s0 + n] = _pad(_tile_w(win, 0, 8, cols))
        W[s0 + 8] = _pad(_tile_w(win, 0, 8, ar(2048, 2304)))
        W[s0 + 9] = _pad(_tile_w(win, 0, 8, ar(2304, 2560)))
        W[s0 + 10] = _pad(_tile_w(win, 0, 8, ar(2560, 2816)))
        cols = np.concatenate([ar(2816, 2880), ar(2848, 2880), ar(2816, 2848)])
        W[s0 + 11] = _pad(_tile_w(win, 0, 8, cols))
        ga, gx = inp["rg_gate_a_w"][i], inp["rg_gate_x_w"][i]
        g = np.concatenate([ga, gx], axis=2)
        W[soff[f"e_gate{i}"]] = _pad(g.transpose(1, 0, 2).reshape(128, -1))
        uq = inp["mla_w_uq"][i]
        for s in range(4):
            parts = []
            for hh in range(2):
                h = 2 * s + hh
                b = h * 192
                cols = np.concatenate([ar(b, b + 192), ar(b + 160, b + 192), ar(b + 128, b + 160)])
                parts.append(_tile_w(uq, 0, 4, cols))
            W[soff[f"e_uq{i}"] + s] = _pad(np.concatenate(parts, axis=1))
        ukv = inp["mla_w_ukv"][i]
        for s in range(2):
            parts = [_tile_w(ukv, 0, 2, ar((4 * s + hh) * 256, (4 * s + hh) * 256 + 256)) for hh in range(4)]
            W[soff[f"e_ukv{i}"] + s] = _pad(np.concatenate(parts, axis=1))
        wo = inp["hy_w_out"][i]
        for m in range(8):
            W[soff[f"e_wout{i}"] + m] = _pad(_tile_w(wo, 0, 16, ar(m * 128, m * 128 + 128)))
    for i in range(2):
        win = inp["gdn_w_in"][i]
        s0 = soff[f"g_win{i}"]
        for j in range(8):
            q = ar(j * 128, j * 128 + 128)
            k = ar(1024 + j * 128, 1024 + j * 128 + 128)
            v = ar(2048 + 2 * j * 128, 2048 + 2 * j * 128 + 256)
            z = ar(4096 + 2 * j * 128, 4096 + 2 * j * 128 + 256)
            W[s0 + 3 * j] = _pad(_tile_w(win, 0, 8, np.concatenate([q, k])))
            W[s0 + 3 * j + 1] = _pad(_tile_w(win, 0, 8, v))
            W[s0 + 3 * j + 2] = _pad(_tile_w(win, 0, 8, z))
        W[soff[f"g_wba{i}"]] = _pad(_tile_w(win, 0, 8, ar(6144, 6176)))
        wo = inp["gdn_w_out"][i]
        for m in range(8):
            W[soff[f"g_wout{i}"] + m] = _pad(_tile_w(wo, 0, 16, ar(m * 128, m * 128 + 128)))
    for l in range(4):
        wg, wu, wd = inp["ffn_w_gate"][l], inp["ffn_w_up"][l], inp["ffn_w_down"][l]
        for m in range(FT):
            c = ar(m * 128, m * 128 + 128)
            W[soff[f"f_gu{l}"] + m] = _pad(np.concatenate([_tile_w(wg, 0, 8, c), _tile_w(wu, 0, 8, c)], axis=1)
                                           .reshape(128, 2, 8, 128).transpose(0, 2, 1, 3).reshape(128, -1))
        for m in range(8):
            for hf in range(2):
                W[soff[f"f_down{l}"] + 2 * m + hf] = _pad(_tile_w(wd, hf * 11 * 128, 11, ar(m * 128, m * 128 + 128)))
    return W


def pack_vecs(inp):
    voff, nv = vec_layout()
    V = np.zeros((128, nv), np.float32)

    def v8(name, v):
        a = np.asarray(v, np.float32).reshape(-1, 128).T
        V[:, voff[name]:voff[name] + a.shape[1]] = a
    for l in range(4):
        v8(f"nmp{l}", inp["norm_mix_pre"][l])
        v8(f"nmo{l}", inp["norm_mix_post"][l])
        v8(f"nfp{l}", inp["norm_ffn_pre"][l])
        v8(f"nfo{l}", inp["norm_ffn_post"][l])
    for i in range(2):
        for j in range(4):
            v8(f"cw{i}_{j}", inp["rg_conv_w"][i][j])
        v8(f"cb{i}", inp["rg_conv_b"][i])
        v8(f"gab{i}", inp["rg_gate_a_b"][i])
        v8(f"gxb{i}", inp["rg_gate_x_b"][i])
        v8(f"lam{i}", inp["rg_lambda"][i])
        v8(f"qn{i}", inp["mla_q_norm"][i])
        v8(f"kvn{i}", inp["mla_kv_norm"][i])
    for i in range(2):
        for j in range(4):
            v8(f"gcw{i}_{j}", inp["gdn_conv_w"][i][j])
        v8(f"gn{i}", inp["gdn_norm"][i])
        V[:, voff[f"alog{i}"]:voff[f"alog{i}"] + 16] = np.asarray(inp["gdn_a_log"][i], np.float32)[None, :]
        V[:, voff[f"dtb{i}"]:voff[f"dtb{i}"] + 16] = np.asarray(inp["gdn_dt_bias"][i], np.float32)[None, :]
    inv_freq = (1.0 / (np.float32(10000.0) ** (np.arange(0, 64, 2, dtype=np.float32) / np.float32(64)))).astype(np.float32)
    V[:64, voff["invf"]] = np.concatenate([inv_freq, inv_freq])
    V[:32, voff["sgn"]] = -1.0
    V[32:64, voff["sgn"]] = 1.0
    return V


def make_consts():
    coff, ncc = cst_layout()
    C = np.zeros((128, ncc), np.float32)
    idx = np.arange(128)
    C[:, coff["ident"]:coff["ident"] + 128] = np.eye(128, dtype=np.float32)
    C[:, coff["amask"]:coff["amask"] + 128] = np.where(idx[:, None] <= idx[None, :], 0.0, NEG)
    same = (idx[:, None] // 64) == (idx[None, :] // 64)
    C[:, coff["U2"]:coff["U2"] + 128] = (same & (idx[:, None] <= idx[None, :])).astype(np.float32)
    C[:, coff["SL2"]:coff["SL2"] + 128] = (same & (idx[:, None] > idx[None, :])).astype(np.float32)
    C[:, coff["NMS"]:coff["NMS"] + 128] = np.where(same & (idx[:, None] > idx[None, :]), 0.0, NEG)
    C[:, coff["NMC"]:coff["NMC"] + 128] = np.where(same & (idx[:, None] <= idx[None, :]), 0.0, NEG)
    C[:, coff["BD"]:coff["BD"] + 128] = same.astype(np.float32)
    C[0:64, coff["ONA"]:coff["ONA"] + 128] = 1.0
    C[64:128, coff["ONB"]:coff["ONB"] + 128] = 1.0
    return C


class Ctx:
    pass


def build_program(layers=(0, 1, 2, 3), do_ffn=True, do_mix=True, cast_engs=("dve", "act", "dve"), dbg=False, stop=99, sub=99):
    nc = bass.Bass("TRN2", target_bir_lowering=False, dynamic_dma_scratch_size=1024)
    voff, nv = vec_layout()
    coff, ncc = cst_layout()
    soff, nslots = slot_layout()
    d_x = nc.dram_tensor("xT", [128, KT, T], F32, kind="ExternalInput").ap()
    d_pos = nc.dram_tensor("pos", [64, T], I32, kind="ExternalInput").ap()
    d_vec = nc.dram_tensor("vecs", [128, nv], F32, kind="ExternalInput").ap()
    d_cst = nc.dram_tensor("cst", [128, ncc], F32, kind="ExternalInput").ap()
    d_w = nc.dram_tensor("wts", [nslots, 128, SLOT_EL], F32, kind="ExternalInput").ap()
    d_y = nc.dram_tensor("yT", [128, KT, T], F32, kind="ExternalOutput").ap()
    with contextlib.ExitStack() as st:
        P = Prog(nc)
        g = Ctx()
        g.P, g.nc, g.voff, g.coff, g.soff = P, nc, voff, coff, soff
        g.dbg = dbg
        g.stop = stop
        g.sub = sub
        g.d_dbg = nc.dram_tensor("dbg", [8, 128, 512], F32, kind="ExternalOutput").ap() if dbg else None

        def sb(name, shape, dt):
            return st.enter_context(nc.sbuf_tensor(name, shape, dt))

        g.sb = sb
        xT = sb("xT_sb", [128, KT, T], F32)
        xtok = [[Tok() for _ in range(NB)] for _ in range(KT)]
        vec = sb("vec_sb", [128, nv], F32)
        t_vec = Tok()
        cst = sb("cst_sb", [128, ncc], F32)
        t_cst = Tok()
        identb = sb("identb", [128, 128], BF16)
        amaskb = sb("amaskb", [128, 128], BF16)
        onesb = sb("onesb", [128, 128], BF16)
        onesf = sb("onesf", [128, 128], F32)
        t_cb = Tok()
        hT = sb("hT", [128, KT, TB], BF16)
        t_h = Tok()
        bigb = sb("bigb", [128, 22, TB], BF16)
        tb = [Tok() for _ in range(22)]
        bigf = sb("bigf", [128, 8, TB], F32)
        tf = [Tok() for _ in range(8)]
        rstd = sb("rstd", [128, TB], F32)
        t_rstd = Tok()
        lnt = sb("lnt", [128, TB], F32)
        t_lnt = Tok()
        NSTG, NSL = 2, 3
        stage = [sb(f"stage{i}", [128, SLOT_EL], F32) for i in range(NSTG)]
        t_stage = [Tok() for _ in range(NSTG)]
        slots = [sb(f"slot{i}", [128, SLOT_EL], BF16) for i in range(NSL)]
        t_slot = [Tok() for _ in range(NSL)]
        ARENA32 = 11072
        arena = sb("arena", [128, ARENA32], F32)

        def mk_asb():
            off = [0]

            def asb(name, shape, dt):
                n = int(np.prod(shape[1:]))
                n32 = (n * (4 if dt in (F32, I32) else 2) + 3) // 4
                v = arena[:, off[0]:off[0] + n32]
                off[0] += n32
                assert off[0] <= ARENA32, (name, off[0])
                if dt != F32:
                    v = v.bitcast(dt)
                if len(shape) == 3:
                    v = v.rearrange("p (a b) -> p a b", b=shape[2])
                if shape[0] < 128:
                    v = v[0:shape[0]]
                return v
            return asb
        g.mk_asb = mk_asb
        g.arena_toks = []
        scr = sb("scr", [128, 2], F32)

        def arena_barrier():
            P.add("dve", lambda e: e.memset(scr[:, :], 0.0), writes=list(g.arena_toks))
        g.arena_barrier = arena_barrier
        ps = [st.enter_context(nc.psum_tensor(f"ps{i}", [128, 512], F32)) for i in range(8)]
        t_ps = [Tok() for _ in range(8)]
        g.xT, g.xtok, g.vec, g.t_vec, g.cst, g.t_cst = xT, xtok, vec, t_vec, cst, t_cst
        g.identb, g.amaskb, g.onesb, g.onesf, g.t_cb = identb, amaskb, onesb, onesf, t_cb
        g.hT, g.t_h, g.bigb, g.tb, g.bigf, g.tf = hT, t_h, bigb, tb, bigf, tf
        g.rstd, g.t_rstd, g.ps, g.t_ps = rstd, t_rstd, ps, t_ps
        g.d_pos = d_pos

        def V(name, k=0, n=1, rows=128):
            c = voff[name] + k
            return vec[0:rows, c:c + n]

        def C(name, rows=128, cols=128, c0=0):
            c = coff[name] + c0
            return cst[0:rows, c:c + cols]

        g.V, g.C = V, C

        class PsRot:
            def __init__(self, ids):
                self.ids = list(ids)
                self.i = 0

            def get(self):
                b = self.ids[self.i % len(self.ids)]
                self.i += 1
                return b
        g.PsRot = PsRot

        wctr = [0]

        def load_slot(idx, nel=SLOT_EL):
            k = wctr[0]
            wctr[0] += 1
            sg, sl = k % NSTG, k % NSL
            P.add("sp", lambda e: e.dma_start(out=stage[sg][:, 0:nel], in_=d_w[idx, :, 0:nel]),
                  writes=[t_stage[sg]], dma_key=("stg", sg))
            ce = cast_engs[k % len(cast_engs)]
            if ce == "act":
                fn = lambda e: e.copy(out=slots[sl][:, 0:nel], in_=stage[sg][:, 0:nel])
            else:
                fn = lambda e: e.tensor_copy(out=slots[sl][:, 0:nel], in_=stage[sg][:, 0:nel])
            P.add(ce, fn, reads=[t_stage[sg]], writes=[t_slot[sl]])
            return slots[sl], t_slot[sl]
        g.load_slot = load_slot

        P.add("sp", lambda e: e.dma_start(out=vec[:, :], in_=d_vec[:, :]), writes=[t_vec], dma_key="vec")
        P.add("sp", lambda e: e.dma_start(out=cst[:, :], in_=d_cst[:, :]), writes=[t_cst], dma_key="cst")
        for kt in range(KT):
            P.add("sp", lambda e, kt=kt: e.dma_start(out=xT[:, kt, :], in_=d_x[:, kt, :]),
                  writes=xtok[kt], dma_key=("x", kt))

        def setup_consts(e):
            e.tensor_copy(out=identb[:, :], in_=C("ident"))
            e.tensor_copy(out=amaskb[:, :], in_=C("amask"))
            e.memset(onesb[:, :], 1.0)
            return e.memset(onesf[:, :], 1.0)
        P.add("dve", setup_consts, reads=[t_cst], writes=[t_cb])

        def rstd_from_sq(sq_aps, sq_toks, n, dsum, eps, bank):
            def mm(e):
                ins = None
                for i, a in enumerate(sq_aps):
                    ins = e.matmul(ps[bank][:, 0:n], lhsT=onesb[:, :], rhs=a,
                                   start=(i == 0), stop=(i == len(sq_aps) - 1))
                return ins
            P.add("pe", mm, reads=list(sq_toks) + [t_cb], writes=[t_ps[bank]])
            P.add("act", lambda e: e.activation(out=lnt[:, 0:n], in_=ps[bank][:, 0:n], func=AF.Ln,
                                                scale=1.0 / dsum, bias=g.eps_ap),
                  reads=[t_ps[bank], t_cb2], writes=[t_lnt])
            P.add("act", lambda e: e.activation(out=rstd[:, 0:n], in_=lnt[:, 0:n], func=AF.Exp, scale=-0.5),
                  reads=[t_lnt], writes=[t_rstd])
        g.rstd_from_sq = rstd_from_sq
        epst = sb("epst", [128, 2], F32)
        t_cb2 = Tok()
        g.eps_ap = epst[:, 0:1]
        g.one_ap = epst[:, 1:2]
        g.t_cb2 = t_cb2

        def setup_eps(e):
            e.memset(epst[:, 0:1], EPS)
            return e.memset(epst[:, 1:2], 1.0)
        P.add("dve", setup_eps, writes=[t_cb2])

        def prenorm(b, wname, bank):
            blk = slice(b * TB, (b + 1) * TB)
            P.add("act", lambda e: e.activation(out=bigb[:, 0:8, :], in_=xT[:, :, blk], func=AF.Square),
                  reads=[xtok[kt][b] for kt in range(KT)], writes=tb[0:8])
            rstd_from_sq([bigb[:, kt, :] for kt in range(KT)], tb[0:8], TB, float(D), EPS, bank)

            def f(e):
                ins = None
                for kt in range(KT):
                    ins = e.scalar_tensor_tensor(out=hT[:, kt, :], in0=xT[:, kt, blk], scalar=V(wname, kt),
                                                 in1=rstd[:, :], op0=ALU.mult, op1=ALU.mult)
                return ins
            P.add("dve", f, reads=[xtok[kt][b] for kt in range(KT)] + [t_rstd, t_vec], writes=[t_h])
        g.prenorm = prenorm

        def post_evac(bank, m, wname):
            P.add("act", lambda e: e.activation(out=bigf[:, m, :], in_=ps[bank][:, :], func=AF.Identity,
                                                scale=V(wname, m)),
                  reads=[t_ps[bank], t_vec], writes=[tf[m]])
            P.add("act", lambda e: e.activation(out=hT[:, m, :], in_=ps[bank][:, :], func=AF.Square),
                  reads=[t_ps[bank]], writes=[t_h])
        g.post_evac = post_evac

        def post_finish(b, bank):
            blk = slice(b * TB, (b + 1) * TB)
            rstd_from_sq([hT[:, kt, :] for kt in range(KT)], [t_h], TB, float(D), EPS, bank)
            for kt in range(KT):
                P.add("dve", lambda e, kt=kt: e.tensor_tensor(out=bigf[:, kt, :], in0=bigf[:, kt, :], in1=rstd[:, :],
                                                              op=ALU.mult),
                      reads=[tf[kt], t_rstd], writes=[tf[kt]])
                P.add("dve", lambda e, kt=kt: e.tensor_tensor(out=xT[:, kt, blk], in0=xT[:, kt, blk],
                                                              in1=bigf[:, kt, :], op=ALU.add),
                      reads=[tf[kt], xtok[kt][b]], writes=[xtok[kt][b]])
        g.post_finish = post_finish

        sgt = [sb(f"sgt{i}", [128, TB], F32) for i in range(2)]
        t_sgt = [Tok(), Tok()]

        def ffn_block(l, b):
            rot = PsRot([0, 1, 2, 3, 4, 5])
            prenorm(b, f"nfp{l}", 7)
            for m in range(FT):
                sl, tsl = load_slot(soff[f"f_gu{l}"] + m)
                bg, bu = rot.get(), rot.get()

                def mm(e, sl=sl, bg=bg, bu=bu):
                    ins = None
                    for kt in range(KT):
                        ins = e.matmul(ps[bg][:, :], lhsT=sl[:, kt * 256:kt * 256 + 128], rhs=hT[:, kt, :],
                                       start=(kt == 0), stop=(kt == KT - 1))
                    for kt in range(KT):
                        ins = e.matmul(ps[bu][:, :], lhsT=sl[:, kt * 256 + 128:kt * 256 + 256], rhs=hT[:, kt, :],
                                       start=(kt == 0), stop=(kt == KT - 1))
                    return ins
                P.add("pe", mm, reads=[tsl, t_h], writes=[t_ps[bg], t_ps[bu]])
                s = m % 2
                P.add("act", lambda e, s=s, bg=bg: e.activation(out=sgt[s][:, :], in_=ps[bg][:, :], func=AF.Silu),
                      reads=[t_ps[bg]], writes=[t_sgt[s]])
                P.add("dve", lambda e, s=s, bu=bu, m=m: e.tensor_tensor(out=bigb[:, m, :], in0=sgt[s][:, :],
                                                                        in1=ps[bu][:, :], op=ALU.mult),
                      reads=[t_sgt[s], t_ps[bu]], writes=[tb[m]])
            for dm in range(8):
                s0, ts0 = load_slot(soff[f"f_down{l}"] + 2 * dm, 11 * 128)
                s1, ts1 = load_slot(soff[f"f_down{l}"] + 2 * dm + 1, 11 * 128)
                bk = rot.get()

                def mm(e, s0=s0, s1=s1, bk=bk):
                    ins = None
                    for k in range(FT):
                        s_, kk = (s0, k) if k < 11 else (s1, k - 11)
                        ins = e.matmul(ps[bk][:, :], lhsT=s_[:, kk * 128:(kk + 1) * 128], rhs=bigb[:, k, :],
                                       start=(k == 0), stop=(k == FT - 1))
                    return ins
                P.add("pe", mm, reads=[ts0, ts1] + tb, writes=[t_ps[bk]])
                post_evac(bk, dm, f"nfo{l}")
            post_finish(b, 7)
        g.ffn_block = ffn_block

        even_state = make_even(g) if do_mix else None
        odd_state = make_odd(g) if do_mix else None
        for l in layers:
            i = l // 2
            if do_mix:
                if l % 2 == 0:
                    even_state.begin_layer(l)
                else:
                    odd_state.begin_layer(l)
            for b in range(1 if dbg else NB):
                if do_mix:
                    if l % 2 == 0:
                        even_state.block(l, b)
                    else:
                        odd_state.block(l, b)
                if do_ffn:
                    ffn_block(l, b)

        outs = []
        for kt in range(KT):
            P.add("sp", lambda e, kt=kt: e.dma_start(out=d_y[:, kt, :], in_=xT[:, kt, :]),
                  reads=xtok[kt], dma_key=("y", kt))
            outs.extend(xtok[kt])
        P.add("sp", None, writes=outs)
        P.emit(st)
        g.stats = {e: len(P.ops[e]) for e in ENGS}
        g.stats["waits"] = P.nwaits
        g.stats["sems"] = P.n_sems
    return nc, g


def reg_tok(g):
    t = Tok()
    g.arena_toks.append(t)
    return t


def make_even(g):
    P, sb, ps, t_ps, V, C = g.P, g.sb, g.ps, g.t_ps, g.V, g.C
    bigb, tb, bigf, tf, hT, t_h = g.bigb, g.tb, g.bigf, g.tf, g.hT, g.t_h
    rstd, t_rstd, t_vec, t_cb = g.rstd, g.t_rstd, g.t_vec, g.t_cb
    soff = g.soff
    S = Ctx()
    sb = g.mk_asb()
    Tok = lambda: reg_tok(g)
    ckvn = sb("ckvn", [128, 2, T], BF16)
    t_ckv = [Tok() for _ in range(NB)]
    kpe = sb("kpe", [128, T], BF16)
    t_kpe = [Tok() for _ in range(NB)]
    Kh = sb("Kh", [128, T], BF16)
    t_Kh = [Tok() for _ in range(NB)]
    Vh = sb("Vh", [128, 16, 128], BF16)
    t_Vh = [Tok() for _ in range(NB)]
    qn = sb("qn", [128, TB], BF16)
    t_qn = Tok()
    qr = sb("qr", [128, TB], BF16)
    t_qr = Tok()
    PT = [sb(f"PT{i}", [128, TB], BF16) for i in range(2)]
    t_PT = [Tok(), Tok()]
    Ct = sb("Ct", [64, TB], F32)
    St = sb("St", [64, TB], F32)
    t_rope = Tok()
    posi = sb("posi", [64, TB], I32)
    t_posi = Tok()
    ki = sb("ki", [64, TB], I32)
    t_ki = Tok()
    xrw = sb("xrw", [128, TB + 4], F32)
    t_xrw = Tok()
    xcb = sb("xcb", [128, TB], BF16)
    t_xcb = Tok()
    tail = sb("rgtail", [128, 8, 3], F32)
    t_tail = [Tok() for _ in range(8)]
    hst = sb("hst", [128, 8], F32)
    t_hst = [Tok() for _ in range(8)]
    nsp8 = sb("nsp8", [128, 8], F32)
    t_nsp = Tok()
    lt8 = sb("lt8", [128, 8], F32)
    t_lt8 = Tok()
    PI_LO = 3.1415925
    c1 = 6.28125
    c2 = float(np.float32(TWO_PI - c1).view(np.uint32) & np.uint32(0xFFFFF000)) if False else None
    c2 = float((np.array([TWO_PI - c1], np.float32).view(np.uint32) & np.uint32(0xFFFFF000)).view(np.float32)[0])
    c3 = float(np.float32(TWO_PI - c1 - c2))
    SCALE = float(192 ** -0.5)

    def begin_layer(l):
        i = l // 2
        g.arena_barrier()
        P.add("act", lambda e: e.activation(out=lt8[:, :], in_=V(f"lam{i}", 0, 8), func=AF.Exp, scale=-1.0),
              reads=[t_vec], writes=[t_lt8])
        P.add("act", lambda e: e.activation(out=lt8[:, :], in_=lt8[:, :], func=AF.Ln, bias=g.one_ap),
              reads=[t_lt8, g.t_cb2], writes=[t_lt8])
        P.add("dve", lambda e: e.tensor_scalar(out=nsp8[:, :], in0=lt8[:, :], scalar1=-8.0, scalar2=None, op0=ALU.mult),
              reads=[t_lt8], writes=[t_nsp])
        P.add("dve", lambda e: e.memset(tail[:, :, :], 0.0), writes=t_tail)
        P.add("dve", lambda e: e.memset(kpe[64:128, :], 0.0), writes=t_kpe)
        P.add("dve", lambda e: e.memset(qr[64:128, :], 0.0), writes=[t_qr])
        P.add("dve", lambda e: e.memset(hst[:, :], 0.0), writes=t_hst)

    def rope_tables(b):
        blk = slice(b * TB, (b + 1) * TB)
        f0, f1, f2 = bigf[0:64, 0, :], bigf[0:64, 1, :], bigf[0:64, 2, :]
        P.add("sp", lambda e: e.dma_start(out=posi[:, :], in_=g.d_pos[:, blk]), writes=[t_posi], dma_key="pos")
        P.add("dve", lambda e: e.tensor_copy(out=f0, in_=posi[:, :]), reads=[t_posi], writes=[tf[0]])
        P.add("dve", lambda e: e.tensor_scalar(out=f1, in0=f0, scalar1=V("invf", 0, 1, 64), scalar2=None, op0=ALU.mult),
              reads=[tf[0], t_vec], writes=[tf[1]])
        P.add("dve", lambda e: e.tensor_scalar(out=ki[:, :], in0=f1, scalar1=float(1.0 / TWO_PI), scalar2=None, op0=ALU.mult),
              reads=[tf[1]], writes=[t_ki])
        P.add("dve", lambda e: e.tensor_copy(out=f2, in_=ki[:, :]), reads=[t_ki], writes=[tf[2]])
        for cc in (c1, c2, c3):
            P.add("dve", lambda e, cc=cc: e.scalar_tensor_tensor(out=f1, in0=f2, scalar=-cc, in1=f1, op0=ALU.mult, op1=ALU.add),
                  reads=[tf[1], tf[2]], writes=[tf[1]])

        def wrap(y, ty):
            P.add("dve", lambda e: e.tensor_scalar(out=f0, in0=y, scalar1=float(np.pi), scalar2=None, op0=ALU.is_gt),
                  reads=[ty], writes=[tf[0]])
            P.add("dve", lambda e: e.scalar_tensor_tensor(out=y, in0=f0, scalar=-TWO_PI, in1=y, op0=ALU.mult, op1=ALU.add),
                  reads=[tf[0], ty], writes=[ty])
            P.add("dve", lambda e: e.tensor_scalar(out=f0, in0=y, scalar1=float(-np.pi), scalar2=None, op0=ALU.is_lt),
                  reads=[ty], writes=[tf[0]])
            P.add("dve", lambda e: e.scalar_tensor_tensor(out=y, in0=f0, scalar=TWO_PI, in1=y, op0=ALU.mult, op1=ALU.add),
                  reads=[tf[0], ty], writes=[ty])
            P.add("dve", lambda e: e.tensor_scalar(out=y, in0=y, scalar1=PI_LO, scalar2=-PI_LO, op0=ALU.min, op1=ALU.max),
                  reads=[ty], writes=[ty])
        P.add("dve", lambda e: e.tensor_scalar(out=f2, in0=f1, scalar1=float(np.pi / 2), scalar2=None, op0=ALU.add),
              reads=[tf[1]], writes=[tf[2]])
        wrap(f1, tf[1])
        wrap(f2, tf[2])
        P.add("act", lambda e: e.activation(out=St[:, :], in_=f1, func=AF.Sin, scale=V("sgn", 0, 1, 64)),
              reads=[tf[1], t_vec], writes=[t_rope])
        P.add("act", lambda e: e.activation(out=Ct[:, :], in_=f2, func=AF.Sin),
              reads=[tf[2]], writes=[t_rope])

    def rope_apply(bA, bB, out_ap, out_toks):
        t1, t2 = bigf[0:64, 0, :], bigf[0:64, 1, :]
        P.add("dve", lambda e: e.tensor_tensor(out=t1, in0=ps[bA][0:64, :], in1=Ct[:, :], op=ALU.mult),
              reads=[t_ps[bA], t_rope], writes=[tf[0]])
        P.add("dve", lambda e: e.tensor_tensor(out=t2, in0=ps[bB][0:64, :], in1=St[:, :], op=ALU.mult),
              reads=[t_ps[bB], t_rope], writes=[tf[1]])
        P.add("dve", lambda e: e.tensor_tensor(out=out_ap, in0=t1, in1=t2, op=ALU.add),
              reads=[tf[0], tf[1]], writes=out_toks)

    def block(l, b):
        i = l // 2
        blk = slice(b * TB, (b + 1) * TB)
        rot = g.PsRot([0, 1, 2, 3])
        rope_tables(b)
        if g.stop < 1:
            return
        g.prenorm(b, f"nmp{l}", 7)
        s0 = soff[f"e_win{i}"]
        if g.stop < 2:
            return
        for half in range(2):
            sl, tsl = g.load_slot(s0 + 8 + half)
            for mm_ in range(2):
                mt = 2 * half + mm_
                bk = rot.get()

                def mm(e, sl=sl, bk=bk, mm_=mm_):
                    ins = None
                    for kt in range(KT):
                        ins = e.matmul(ps[bk][:, :], lhsT=sl[:, kt * 256 + mm_ * 128:kt * 256 + mm_ * 128 + 128],
                                       rhs=hT[:, kt, :], start=(kt == 0), stop=(kt == KT - 1))
                    return ins
                P.add("pe", mm, reads=[tsl, t_h], writes=[t_ps[bk]])
                P.add("act", lambda e, bk=bk, mt=mt: e.activation(out=bigf[:, 4 + mt, :], in_=ps[bk][:, :], func=AF.Copy),
                      reads=[t_ps[bk]], writes=[tf[4 + mt]])
                P.add("act", lambda e, bk=bk, mt=mt: e.activation(out=bigb[:, 8 + mt, :], in_=ps[bk][:, :], func=AF.Square),
                      reads=[t_ps[bk]], writes=[tb[8 + mt]])
        g.rstd_from_sq([bigb[:, 8 + mt, :] for mt in range(4)], tb[8:12], TB, 512.0, EPS, 7)
        for mt in range(4):
            P.add("dve", lambda e, mt=mt: e.scalar_tensor_tensor(out=bigb[:, 16 + mt, :], in0=bigf[:, 4 + mt, :],
                                                                 scalar=V(f"qn{i}", mt), in1=rstd[:, :],
                                                                 op0=ALU.mult, op1=ALU.mult),
                  reads=[tf[4 + mt], t_rstd, t_vec], writes=[tb[16 + mt]])
        sl, tsl = g.load_slot(s0 + 10)
        for mt in range(2):
            bk = rot.get()

            def mm(e, sl=sl, bk=bk, mt=mt):
                ins = None
                for kt in range(KT):
                    ins = e.matmul(ps[bk][:, :], lhsT=sl[:, kt * 256 + mt * 128:kt * 256 + mt * 128 + 128],
                                   rhs=hT[:, kt, :], start=(kt == 0), stop=(kt == KT - 1))
                return ins
            P.add("pe", mm, reads=[tsl, t_h], writes=[t_ps[bk]])
            P.add("act", lambda e, bk=bk, mt=mt: e.activation(out=bigf[:, 2 + mt, :], in_=ps[bk][:, :], func=AF.Copy),
                  reads=[t_ps[bk]], writes=[tf[2 + mt]])
            P.add("act", lambda e, bk=bk, mt=mt: e.activation(out=bigb[:, 20 + mt, :], in_=ps[bk][:, :], func=AF.Square),
                  reads=[t_ps[bk]], writes=[tb[20 + mt]])
        g.rstd_from_sq([bigb[:, 20 + mt, :] for mt in range(2)], tb[20:22], TB, 256.0, EPS, 7)
        for mt in range(2):
            P.add("dve", lambda e, mt=mt: e.scalar_tensor_tensor(out=ckvn[:, mt, blk], in0=bigf[:, 2 + mt, :],
                                                                 scalar=V(f"kvn{i}", mt), in1=rstd[:, :],
                                                                 op0=ALU.mult, op1=ALU.mult),
                  reads=[tf[2 + mt], t_rstd, t_vec], writes=[t_ckv[b]])
        sl, tsl = g.load_slot(s0 + 11, 8 * 128)
        bA, bB = rot.get(), rot.get()

        def mmk(e, sl=sl, bA=bA, bB=bB):
            ins = None
            for kt in range(KT):
                ins = e.matmul(ps[bA][0:64, :], lhsT=sl[:, kt * 128:kt * 128 + 64], rhs=hT[:, kt, :],
                               start=(kt == 0), stop=(kt == KT - 1))
            for kt in range(KT):
                ins = e.matmul(ps[bB][0:64, :], lhsT=sl[:, kt * 128 + 64:kt * 128 + 128], rhs=hT[:, kt, :],
                               start=(kt == 0), stop=(kt == KT - 1))
            return ins
        P.add("pe", mmk, reads=[tsl, t_h], writes=[t_ps[bA], t_ps[bB]])
        rope_apply(bA, bB, kpe[0:64, blk], [t_kpe[b]])
        if g.stop < 3:
            return
        gs, tgs = g.load_slot(soff[f"e_gate{i}"])
        P.add("pool", lambda e: e.tensor_copy(out=S.gatew[:, :], in_=gs[:, :]), reads=[tgs], writes=[S.t_gatew])
        for n in range(8):
            sl, tsl = g.load_slot(s0 + n)
            bX, bG = rot.get(), rot.get()

            def mm(e, sl=sl, bX=bX, bG=bG):
                ins = None
                for kt in range(KT):
                    ins = e.matmul(ps[bX][:, :], lhsT=sl[:, kt * 256:kt * 256 + 128], rhs=hT[:, kt, :],
                                   start=(kt == 0), stop=(kt == KT - 1))
                for kt in range(KT):
                    ins = e.matmul(ps[bG][:, :], lhsT=sl[:, kt * 256 + 128:kt * 256 + 256], rhs=hT[:, kt, :],
                                   start=(kt == 0), stop=(kt == KT - 1))
                return ins
            P.add("pe", mm, reads=[tsl, t_h], writes=[t_ps[bX], t_ps[bG]])
            P.add("dve", lambda e, n=n: e.tensor_copy(out=xrw[:, 0:3], in_=tail[:, n, :]), reads=[t_tail[n]], writes=[t_xrw])
            P.add("act", lambda e, bX=bX: e.activation(out=xrw[:, 3:TB + 3], in_=ps[bX][:, :], func=AF.Copy),
                  reads=[t_ps[bX]], writes=[t_xrw])
            xc = bigf[:, 0, :]
            P.add("dve", lambda e, n=n: e.tensor_scalar(out=xc, in0=xrw[:, 3:TB + 3], scalar1=V(f"cw{i}_3", n),
                                                        scalar2=V(f"cb{i}", n), op0=ALU.mult, op1=ALU.add),
                  reads=[t_xrw, t_vec], writes=[tf[0]])
            for j in (2, 1, 0):
                P.add("dve", lambda e, n=n, j=j: e.scalar_tensor_tensor(out=xc, in0=xrw[:, j:TB + j], scalar=V(f"cw{i}_{j}", n),
                                                                        in1=xc, op0=ALU.mult, op1=ALU.add),
                      reads=[t_xrw, t_vec, tf[0]], writes=[tf[0]])
            P.add("dve", lambda e, n=n: e.tensor_copy(out=tail[:, n, :], in_=xrw[:, TB:TB + 3]), reads=[t_xrw], writes=[t_tail[n]])
            P.add("act", lambda e: e.activation(out=xcb[:, :], in_=xc, func=AF.Copy), reads=[tf[0]], writes=[t_xcb])
            bR, bI = rot.get(), rot.get()

            def mmg(e, n=n, bR=bR, bI=bI):
                e.matmul(ps[bR][:, :], lhsT=S.gatew[:, n * 256:n * 256 + 128], rhs=xcb[:, :], start=True, stop=True)
                return e.matmul(ps[bI][:, :], lhsT=S.gatew[:, n * 256 + 128:n * 256 + 256], rhs=xcb[:, :], start=True, stop=True)
            P.add("pe", mmg, reads=[S.t_gatew, t_xcb], writes=[t_ps[bR], t_ps[bI]])
            P.add("act", lambda e, n=n, bR=bR: e.activation(out=bigf[:, 1, :], in_=ps[bR][:, :], func=AF.Sigmoid, bias=V(f"gab{i}", n)),
                  reads=[t_ps[bR], t_vec], writes=[tf[1]])
            P.add("act", lambda e, n=n, bI=bI: e.activation(out=bigf[:, 2, :], in_=ps[bI][:, :], func=AF.Sigmoid, bias=V(f"gxb{i}", n)),
                  reads=[t_ps[bI], t_vec], writes=[tf[2]])
            P.add("act", lambda e, n=n: e.activation(out=bigf[:, 3, :], in_=bigf[:, 1, :], func=AF.Exp, scale=nsp8[:, n:n + 1]),
                  reads=[tf[1], t_nsp], writes=[tf[3]])
            P.add("dve", lambda e: e.tensor_tensor(out=bigf[:, 4, :], in0=bigf[:, 3, :], in1=bigf[:, 3, :], op=ALU.mult),
                  reads=[tf[3]], writes=[tf[4]])
            P.add("act", lambda e: e.activation(out=bigf[:, 4, :], in_=bigf[:, 4, :], func=AF.Sqrt, scale=-1.0, bias=g.one_ap),
                  reads=[tf[4], g.t_cb2], writes=[tf[4]])
            P.add("dve", lambda e: e.tensor_tensor(out=bigf[:, 5, :], in0=bigf[:, 2, :], in1=xc, op=ALU.mult),
                  reads=[tf[2], tf[0]], writes=[tf[5]])
            P.add("dve", lambda e: e.tensor_tensor(out=bigf[:, 5, :], in0=bigf[:, 5, :], in1=bigf[:, 4, :], op=ALU.mult),
                  reads=[tf[5], tf[4]], writes=[tf[5]])
            P.add("dve", lambda e, n=n: e.tensor_tensor_scan(out=bigf[:, 6, :], data0=bigf[:, 3, :], data1=bigf[:, 5, :],
                                                             initial=hst[:, n:n + 1], op0=ALU.mult, op1=ALU.add),
                  reads=[tf[3], tf[5], t_hst[n]], writes=[tf[6]])
            P.add("dve", lambda e, n=n: e.tensor_copy(out=hst[:, n:n + 1], in_=bigf[:, 6, TB - 1:TB]), reads=[tf[6]], writes=[t_hst[n]])
            P.add("act", lambda e, bG=bG: e.activation(out=bigf[:, 7, :], in_=ps[bG][:, :], func=AF.Gelu_apprx_tanh),
                  reads=[t_ps[bG]], writes=[tf[7]])
            P.add("dve", lambda e, n=n: e.tensor_tensor(out=bigb[:, n, :], in0=bigf[:, 6, :], in1=bigf[:, 7, :], op=ALU.mult),
                  reads=[tf[6], tf[7]], writes=[tb[n]])
        if g.stop < 4:
            return
        nkc = b + 1
        srot = g.PsRot([4, 5])
        for h in range(8):
            if h % 2 == 0:
                uq, tuq = g.load_slot(soff[f"e_uq{i}"] + h // 2)
            if h % 4 == 0:
                ukv_s, tukv_s = g.load_slot(soff[f"e_ukv{i}"] + h // 4)
                P.add("pool", lambda e, ukv_s=ukv_s: e.tensor_copy(out=S.ukvw[:, :], in_=ukv_s[:, :]), reads=[tukv_s], writes=[S.t_ukvw])
            ukv, tukv = S.ukvw, S.t_ukvw
            qb = (h % 2) * 4 * 256
            kb_ = (h % 4) * 2 * 256
            bQ, bA, bB = rot.get(), rot.get(), rot.get()

            def mmq(e, uq=uq, qb=qb, bQ=bQ, bA=bA, bB=bB):
                ins = None
                for kt in range(4):
                    ins = e.matmul(ps[bQ][:, :], lhsT=uq[:, qb + kt * 256:qb + kt * 256 + 128], rhs=bigb[:, 16 + kt, :],
                                   start=(kt == 0), stop=(kt == 3))
                for kt in range(4):
                    ins = e.matmul(ps[bA][0:64, :], lhsT=uq[:, qb + kt * 256 + 128:qb + kt * 256 + 192], rhs=bigb[:, 16 + kt, :],
                                   start=(kt == 0), stop=(kt == 3))
                for kt in range(4):
                    ins = e.matmul(ps[bB][0:64, :], lhsT=uq[:, qb + kt * 256 + 192:qb + kt * 256 + 256], rhs=bigb[:, 16 + kt, :],
                                   start=(kt == 0), stop=(kt == 3))
                return ins
            P.add("pe", mmq, reads=[tuq] + tb[16:20], writes=[t_ps[bQ], t_ps[bA], t_ps[bB]])
            P.add("act", lambda e, bQ=bQ: e.activation(out=qn[:, :], in_=ps[bQ][:, :], func=AF.Copy), reads=[t_ps[bQ]], writes=[t_qn])
            rope_apply(bA, bB, qr[0:64, :], [t_qr])
            for c in range(nkc):
                bk = rot.get()

                def mmK(e, c=c, bk=bk, kb_=kb_):
                    ins = None
                    for kt in range(2):
                        ins = e.matmul(ps[bk][:, :], lhsT=ukv[:, kb_ + kt * 256:kb_ + kt * 256 + 128],
                                       rhs=ckvn[:, kt, c * TB:(c + 1) * TB], start=(kt == 0), stop=(kt == 1))
                    return ins
                P.add("pe", mmK, reads=[tukv, t_ckv[c]], writes=[t_ps[bk]])
                P.add("act", lambda e, c=c, bk=bk: e.activation(out=Kh[:, c * TB:(c + 1) * TB], in_=ps[bk][:, :], func=AF.Copy),
                      reads=[t_ps[bk]], writes=[t_Kh[c]])
                bv = rot.get()

                def mmV(e, c=c, bv=bv, kb_=kb_):
                    ins = None
                    for jj in range(4):
                        for kt in range(2):
                            ins = e.matmul(ps[bv][:, jj * 128:(jj + 1) * 128],
                                           lhsT=ckvn[:, kt, c * TB + jj * 128:c * TB + (jj + 1) * 128],
                                           rhs=ukv[:, kb_ + kt * 256 + 128:kb_ + kt * 256 + 256],
                                           start=(kt == 0), stop=(kt == 1))
                    return ins
                P.add("pe", mmV, reads=[tukv, t_ckv[c]], writes=[t_ps[bv]])
                P.add("dve", lambda e, c=c, bv=bv: e.tensor_copy(out=Vh[:, 4 * c:4 * c + 4, :], in_=ps[bv][:, :]),
                      reads=[t_ps[bv]], writes=[t_Vh[c]])
            nj = 4 * nkc
            for j in range(nj):
                jj = j - 4 * b
                c0 = max(0, jj) * 128
                sbk = srot.get()
                pt = j % 2

                def mmS(e, j=j, jj=jj, c0=c0, sbk=sbk):
                    e.matmul(ps[sbk][:, c0:TB], lhsT=Kh[:, j * 128:(j + 1) * 128], rhs=qn[:, c0:TB], start=True, stop=False)
                    ins = e.matmul(ps[sbk][:, c0:TB], lhsT=kpe[:, j * 128:(j + 1) * 128], rhs=qr[:, c0:TB],
                                   start=False, stop=(jj < 0))
                    if jj >= 0:
                        ins = e.matmul(ps[sbk][:, c0:c0 + 128], lhsT=g.identb[:, :], rhs=g.amaskb[:, :], start=False, stop=True)
                    return ins
                P.add("pe", mmS, reads=[t_Kh[j // 4], t_kpe[j // 4], t_qn, t_qr, t_cb], writes=[t_ps[sbk]])
                P.add("act", lambda e, c0=c0, sbk=sbk, pt=pt: e.activation(out=PT[pt][:, c0:TB], in_=ps[sbk][:, c0:TB],
                                                                           func=AF.Exp, scale=SCALE),
                      reads=[t_ps[sbk]], writes=[t_PT[pt]])

                def mmO(e, j=j, c0=c0, pt=pt, nj=nj):
                    e.matmul(ps[6][:, c0:TB], lhsT=Vh[:, j, :], rhs=PT[pt][:, c0:TB], start=(j == 0), stop=(j == nj - 1))
                    return e.matmul(ps[7][:, c0:TB], lhsT=g.onesb[:, :], rhs=PT[pt][:, c0:TB], start=(j == 0), stop=(j == nj - 1))
                P.add("pe", mmO, reads=[t_Vh[j // 4], t_PT[pt], t_cb], writes=[t_ps[6], t_ps[7]])
            P.add("dve", lambda e: e.reciprocal(out=bigf[:, 2, :], in_=ps[7][:, :]), reads=[t_ps[7]], writes=[tf[2]])
            P.add("dve", lambda e, h=h: e.tensor_tensor(out=bigb[:, 8 + h, :], in0=ps[6][:, :], in1=bigf[:, 2, :], op=ALU.mult),
                  reads=[t_ps[6], tf[2]], writes=[tb[8 + h]])
        if g.stop < 5:
            return
        for m in range(8):
            sl, tsl = g.load_slot(soff[f"e_wout{i}"] + m)
            bk = rot.get()

            def mmo(e, sl=sl, bk=bk):
                ins = None
                for kt in range(16):
                    ins = e.matmul(ps[bk][:, :], lhsT=sl[:, kt * 128:(kt + 1) * 128], rhs=bigb[:, kt, :],
                                   start=(kt == 0), stop=(kt == 15))
                return ins
            P.add("pe", mmo, reads=[tsl] + tb[0:16], writes=[t_ps[bk]])
            g.post_evac(bk, m, f"nmo{l}")
        g.post_finish(b, 7)

    S.gatew = sb("gatew", [128, SLOT_EL], BF16)
    S.t_gatew = Tok()
    S.ukvw = sb("ukvw", [128, SLOT_EL], BF16)
    S.t_ukvw = Tok()
    S.begin_layer = begin_layer
    S.block = block
    return S


def make_odd(g):
    P, ps, t_ps, V, C = g.P, g.ps, g.t_ps, g.V, g.C
    bigb, tb, bigf, tf, hT, t_h = g.bigb, g.tb, g.bigf, g.tf, g.hT, g.t_h
    rstd, t_rstd, t_vec, t_cb, t_cst = g.rstd, g.t_rstd, g.t_vec, g.t_cb, g.t_cst
    onesf = g.onesf
    soff = g.soff
    S = Ctx()
    sb = g.mk_asb()
    Tok = lambda: reg_tok(g)
    S_all = sb("S_all", [128, 16, 128], F32)
    t_S = [Tok() for _ in range(16)]
    gtail = sb("gtail", [128, 32, 3], F32)
    t_gt = [Tok() for _ in range(32)]
    xw = sb("gxw", [128, TB + 4], F32)
    t_xw = Tok()
    beta = sb("beta", [128, 4, 16], F32)
    nbeta = sb("nbeta", [128, 4, 16], F32)
    gg = sb("gg", [128, 4, 16], F32)
    eg = sb("eg", [128, 4, 16], F32)
    egr = sb("egr", [128, 4, 16], F32)
    dch = sb("dch", [128, 8, 16], F32)
    t_gate = Tok()
    nexpA = sb("nexpA", [128, 16], F32)
    t_nexp = Tok()
    KKs = sb("KKs", [128, 4, 128], F32)
    QKs = sb("QKs", [128, 4, 128], F32)
    Qt = sb("Qt", [128, 4, 128], F32)
    Ktt = sb("Ktt", [128, 4, 128], F32)
    t_pair = [Tok() for _ in range(4)]
    CH = []
    for p in range(4):
        c = Ctx()
        c.X = [sb(f"X{p}_{k}", [128, 128], F32) for k in range(2)]
        c.Y = [sb(f"Y{p}_{k}", [128, 128], F32) for k in range(2)]
        c.R = [sb(f"R{p}_{k}", [128, 128], F32) for k in range(2)]
        c.QKT = sb(f"QKT{p}", [128, 128], F32)
        c.kdec = sb(f"kdec{p}", [128, 128], F32)
        c.kdecB = sb(f"kdecB{p}", [128, 128], F32)
        c.Kbe = sb(f"Kbe{p}", [128, 128], F32)
        c.Vb = sb(f"Vb{p}", [128, 128], F32)
        c.tX = [Tok(), Tok()]
        c.tY = [Tok(), Tok()]
        c.tR = [Tok(), Tok()]
        c.tQKT, c.tkdec, c.tKbe, c.tVb = Tok(), Tok(), Tok(), Tok()
        CH.append(c)
    ident = C("ident")
    QSCALE = float(128 ** -0.5)

    def begin_layer(l):
        i = l // 2
        g.arena_barrier()
        P.add("act", lambda e: e.activation(out=nexpA[:, :], in_=V(f"alog{i}", 0, 16), func=AF.Exp),
              reads=[t_vec], writes=[t_nexp])
        P.add("dve", lambda e: e.tensor_scalar(out=nexpA[:, :], in0=nexpA[:, :], scalar1=-1.0, scalar2=None, op0=ALU.mult),
              reads=[t_nexp], writes=[t_nexp])
        P.add("dve", lambda e: e.memset(S_all[:, :, :], 0.0), writes=t_S)
        P.add("dve", lambda e: e.memset(gtail[:, :, :], 0.0), writes=t_gt)

    def conv_silu(i, bank, tile, out_ap, out_tok):
        P.add("dve", lambda e: e.tensor_copy(out=xw[:, 0:3], in_=gtail[:, tile, :]), reads=[t_gt[tile]], writes=[t_xw])
        P.add("act", lambda e: e.activation(out=xw[:, 3:TB + 3], in_=ps[bank][:, :], func=AF.Copy),
              reads=[t_ps[bank]], writes=[t_xw])
        P.add("dve", lambda e: e.tensor_scalar(out=out_ap, in0=xw[:, 3:TB + 3], scalar1=V(f"gcw{i}_3", tile), scalar2=None,
                                               op0=ALU.mult), reads=[t_xw, t_vec], writes=[out_tok])
        for j in (2, 1, 0):
            P.add("dve", lambda e, j=j: e.scalar_tensor_tensor(out=out_ap, in0=xw[:, j:TB + j], scalar=V(f"gcw{i}_{j}", tile),
                                                               in1=out_ap, op0=ALU.mult, op1=ALU.add),
                  reads=[t_xw, t_vec, out_tok], writes=[out_tok])
        P.add("dve", lambda e: e.tensor_copy(out=gtail[:, tile, :], in_=xw[:, TB:TB + 3]), reads=[t_xw], writes=[t_gt[tile]])
        P.add("act", lambda e: e.activation(out=out_ap, in_=out_ap, func=AF.Silu), reads=[out_tok], writes=[out_tok])

    def l2norm(x_ap, x_tok, scale):
        P.add("act", lambda e: e.activation(out=bigb[:, 17, :], in_=x_ap, func=AF.Square), reads=[x_tok], writes=[tb[17]])
        g.rstd_from_sq([bigb[:, 17, :]], [tb[17]], TB, 1.0, EPS, 7)
        P.add("dve", lambda e: e.scalar_tensor_tensor(out=x_ap, in0=x_ap, scalar=scale, in1=rstd[:, :], op0=ALU.mult, op1=ALU.mult),
              reads=[x_tok, t_rstd], writes=[x_tok])

    def proj2(sl, tsl, b0, b1):
        def mm(e):
            ins = None
            for kt in range(KT):
                ins = e.matmul(ps[b0][:, :], lhsT=sl[:, kt * 256:kt * 256 + 128], rhs=hT[:, kt, :], start=(kt == 0), stop=(kt == KT - 1))
            for kt in range(KT):
                ins = e.matmul(ps[b1][:, :], lhsT=sl[:, kt * 256 + 128:kt * 256 + 256], rhs=hT[:, kt, :], start=(kt == 0), stop=(kt == KT - 1))
            return ins
        P.add("pe", mm, reads=[tsl, t_h], writes=[t_ps[b0], t_ps[b1]])

    def block(l, b):
        i = l // 2
        rot = g.PsRot([0, 1, 2, 3, 4, 5])
        g.prenorm(b, f"nmp{l}", 7)
        if g.stop < 1:
            return
        wba, twba = g.load_slot(soff[f"g_wba{i}"], 8 * 32)
        bk = rot.get()

        def mmba(e):
            ins = None
            for tt in range(4):
                for kt in range(KT):
                    ins = e.matmul(ps[bk][:, tt * 32:(tt + 1) * 32], lhsT=hT[:, kt, tt * 128:(tt + 1) * 128],
                                   rhs=wba[:, kt * 32:(kt + 1) * 32], start=(kt == 0), stop=(kt == KT - 1))
            return ins
        P.add("pe", mmba, reads=[twba, t_h], writes=[t_ps[bk]])
        for tt in range(4):
            P.add("act", lambda e, tt=tt: e.activation(out=beta[:, tt, :], in_=ps[bk][:, tt * 32:tt * 32 + 16], func=AF.Sigmoid),
                  reads=[t_ps[bk]], writes=[t_gate])
            P.add("dve", lambda e, tt=tt: e.tensor_tensor(out=gg[:, tt, :], in0=ps[bk][:, tt * 32 + 16:tt * 32 + 32],
                                                          in1=V(f"dtb{i}", 0, 16), op=ALU.add),
                  reads=[t_ps[bk], t_vec], writes=[t_gate])
        P.add("act", lambda e: e.activation(out=gg[:, :, :], in_=gg[:, :, :], func=AF.Exp), reads=[t_gate], writes=[t_gate])
        P.add("act", lambda e: e.activation(out=gg[:, :, :], in_=gg[:, :, :], func=AF.Ln, bias=g.one_ap),
              reads=[t_gate, g.t_cb2], writes=[t_gate])
        for tt in range(4):
            P.add("dve", lambda e, tt=tt: e.tensor_tensor(out=gg[:, tt, :], in0=gg[:, tt, :], in1=nexpA[:, :], op=ALU.mult),
                  reads=[t_gate, t_nexp], writes=[t_gate])
        P.add("dve", lambda e: e.tensor_scalar(out=nbeta[:, :, :], in0=beta[:, :, :], scalar1=-1.0, scalar2=None, op0=ALU.mult),
              reads=[t_gate], writes=[t_gate])
        for p in range(4):
            bc = rot.get()

            def mmc(e, p=p, bc=bc):
                e.matmul(ps[bc][:, 0:16], lhsT=C("U2"), rhs=gg[:, p, :], start=True, stop=True)
                e.matmul(ps[bc][:, 16:32], lhsT=C("SL2"), rhs=gg[:, p, :], start=True, stop=True)
                e.matmul(ps[bc][:, 32:48], lhsT=C("ONA"), rhs=gg[:, p, :], start=True, stop=True)
                return e.matmul(ps[bc][:, 48:64], lhsT=C("ONB"), rhs=gg[:, p, :], start=True, stop=True)
            P.add("pe", mmc, reads=[t_gate, t_cst, t_cb], writes=[t_ps[bc]])

            def ex(e, p=p, bc=bc):
                e.activation(out=eg[:, p, :], in_=ps[bc][:, 0:16], func=AF.Exp)
                e.activation(out=egr[:, p, :], in_=ps[bc][:, 16:32], func=AF.Exp)
                e.activation(out=dch[:, 2 * p, :], in_=ps[bc][:, 32:48], func=AF.Exp)
                return e.activation(out=dch[:, 2 * p + 1, :], in_=ps[bc][:, 48:64], func=AF.Exp)
            P.add("act", ex, reads=[t_ps[bc]], writes=[t_gate])
        if g.stop < 2:
            return
        qf, kf = bigf[:, 0, :], bigf[:, 1, :]
        vfs = [bigf[:, 2, :], bigf[:, 3, :]]
        zfs = [bigf[:, 4, :], bigf[:, 5, :]]
        for j in range(g.dbg[0] + 1 if g.dbg else 8):
            s0 = soff[f"g_win{i}"] + 3 * j
            sl, tsl = g.load_slot(s0)
            b0, b1 = rot.get(), rot.get()
            proj2(sl, tsl, b0, b1)
            conv_silu(i, b0, j, qf, tf[0])
            conv_silu(i, b1, 8 + j, kf, tf[1])
            l2norm(qf, tf[0], QSCALE)
            l2norm(kf, tf[1], 1.0)
            sl, tsl = g.load_slot(s0 + 1)
            b0, b1 = rot.get(), rot.get()
            proj2(sl, tsl, b0, b1)
            conv_silu(i, b0, 16 + 2 * j, vfs[0], tf[2])
            conv_silu(i, b1, 16 + 2 * j + 1, vfs[1], tf[3])
            sl, tsl = g.load_slot(s0 + 2)
            b0, b1 = rot.get(), rot.get()
            proj2(sl, tsl, b0, b1)
            P.add("act", lambda e, b0=b0: e.activation(out=zfs[0], in_=ps[b0][:, :], func=AF.Silu), reads=[t_ps[b0]], writes=[tf[4]])
            P.add("act", lambda e, b1=b1: e.activation(out=zfs[1], in_=ps[b1][:, :], func=AF.Silu), reads=[t_ps[b1]], writes=[tf[5]])
            for p in range(4):
                cols = slice(p * 128, (p + 1) * 128)
                bkk = rot.get()

                def mmr(e, cols=cols, bkk=bkk):
                    e.matmul(ps[bkk][:, 0:128], lhsT=kf[:, cols], rhs=kf[:, cols], start=True, stop=True)
                    e.matmul(ps[bkk][:, 128:256], lhsT=kf[:, cols], rhs=qf[:, cols], start=True, stop=True)
                    e.transpose(ps[bkk][:, 256:384], qf[:, cols], ident)
                    return e.transpose(ps[bkk][:, 384:512], kf[:, cols], ident)
                P.add("pe", mmr, reads=[tf[0], tf[1], t_cst], writes=[t_ps[bkk]])

                def ev1(e, p=p, bkk=bkk):
                    e.activation(out=KKs[:, p, :], in_=ps[bkk][:, 0:128], func=AF.Copy)
                    return e.activation(out=Qt[:, p, :], in_=ps[bkk][:, 256:384], func=AF.Copy)
                P.add("act", ev1, reads=[t_ps[bkk]], writes=[t_pair[p]])

                def ev2(e, p=p, bkk=bkk):
                    e.tensor_copy(out=QKs[:, p, :], in_=ps[bkk][:, 128:256])
                    return e.tensor_copy(out=Ktt[:, p, :], in_=ps[bkk][:, 384:512])
                P.add("dve", ev2, reads=[t_ps[bkk]], writes=[t_pair[p]])
            for hh in range(2 if g.stop >= 3 else 0):
                h = 2 * j + hh
                vf, tvf = vfs[hh], tf[2 + hh]
                bvt = rot.get()

                def mmvt(e, vf=vf, bvt=bvt):
                    ins = None
                    for p in range(4):
                        ins = e.transpose(ps[bvt][:, p * 128:(p + 1) * 128], vf[:, p * 128:(p + 1) * 128], ident)
                    return ins
                P.add("pe", mmvt, reads=[tvf, t_cst], writes=[t_ps[bvt]])
                for p in range(4):
                    c = CH[p]
                    P.add("dve", lambda e, c=c, p=p, h=h: e.tensor_scalar(out=c.X[1][:, :], in0=C("U2"), scalar1=gg[:, p, h:h + 1],
                                                                          scalar2=None, op0=ALU.mult),
                          reads=[t_cst, t_gate], writes=[c.tX[1]])
                    P.add("dve", lambda e, c=c, p=p, h=h: e.tensor_scalar(out=c.Y[1][:, :], in0=C("SL2"), scalar1=gg[:, p, h:h + 1],
                                                                          scalar2=None, op0=ALU.mult),
                          reads=[t_cst, t_gate], writes=[c.tY[1]])
                    P.add("dve", lambda e, c=c, p=p, h=h: e.tensor_scalar(out=c.Kbe[:, :], in0=Ktt[:, p, :], scalar1=beta[:, p, h:h + 1],
                                                                          scalar2=eg[:, p, h:h + 1], op0=ALU.mult, op1=ALU.mult),
                          reads=[t_pair[p], t_gate], writes=[c.tKbe])
                    P.add("dve", lambda e, c=c, p=p, h=h: e.tensor_scalar(out=c.kdec[:, :], in0=Ktt[:, p, :], scalar1=egr[:, p, h:h + 1],
                                                                          scalar2=C("ONA", 128, 1), op0=ALU.mult, op1=ALU.mult),
                          reads=[t_pair[p], t_gate, t_cst], writes=[c.tkdec])
                    P.add("dve", lambda e, c=c, p=p, h=h: e.tensor_scalar(out=c.kdecB[:, :], in0=Ktt[:, p, :], scalar1=egr[:, p, h:h + 1],
                                                                          scalar2=C("ONB", 128, 1), op0=ALU.mult, op1=ALU.mult),
                          reads=[t_pair[p], t_gate, t_cst], writes=[c.tkdec])
                    P.add("dve", lambda e, c=c, p=p, h=h, bvt=bvt: e.tensor_scalar(out=c.Vb[:, :], in0=ps[bvt][:, p * 128:(p + 1) * 128],
                                                                                   scalar1=beta[:, p, h:h + 1], scalar2=None, op0=ALU.mult),
                          reads=[t_ps[bvt], t_gate], writes=[c.tVb])
                if g.stop < 3.1:
                    continue
                bDs = []
                for p in range(4):
                    c = CH[p]
                    bD = rot.get()
                    bDs.append(bD)

                    def mmD(e, c=c, bD=bD):
                        e.matmul(ps[bD][:, 0:128], lhsT=c.X[1][:, :], rhs=C("SL2"), start=True, stop=False)
                        e.matmul(ps[bD][:, 0:128], lhsT=ident, rhs=C("NMS"), start=False, stop=True)
                        e.matmul(ps[bD][:, 128:256], lhsT=c.Y[1][:, :], rhs=C("U2"), start=True, stop=False)
                        return e.matmul(ps[bD][:, 128:256], lhsT=ident, rhs=C("NMC"), start=False, stop=True)
                    P.add("pe", mmD, reads=[c.tX[1], c.tY[1], t_cst], writes=[t_ps[bD]])
                    P.add("act", lambda e, c=c, bD=bD: e.activation(out=c.R[1][:, :], in_=ps[bD][:, 0:128], func=AF.Exp),
                          reads=[t_ps[bD]], writes=[c.tR[1]])
                    P.add("act", lambda e, c=c, bD=bD: e.activation(out=c.QKT[:, :], in_=ps[bD][:, 128:256], func=AF.Exp),
                          reads=[t_ps[bD]], writes=[c.tQKT])
                if g.stop < 3.2:
                    continue
                for p in range(4):
                    c = CH[p]
                    P.add("dve", lambda e, c=c, p=p, h=h: e.scalar_tensor_tensor(out=c.X[0][:, :], in0=KKs[:, p, :], scalar=nbeta[:, p, h:h + 1],
                                                                                 in1=c.R[1][:, :], op0=ALU.mult, op1=ALU.mult),
                          reads=[t_pair[p], t_gate, c.tR[1]], writes=[c.tX[0]])
                    P.add("dve", lambda e, c=c, p=p: e.tensor_tensor(out=c.QKT[:, :], in0=QKs[:, p, :], in1=c.QKT[:, :], op=ALU.mult),
                          reads=[t_pair[p], c.tQKT], writes=[c.tQKT])
                if g.stop < 3.3:
                    continue
                bTs = []
                for p in range(4):
                    c = CH[p]
                    bT = rot.get()
                    bTs.append(bT)
                    P.add("pe", lambda e, c=c, bT=bT: e.transpose(ps[bT][:, 0:128], c.X[0][:, :], ident),
                          reads=[c.tX[0], t_cst], writes=[t_ps[bT]])
                for p in range(4):
                    c = CH[p]
                    bT = bTs[p]
                    if g.stop < 3.31:
                        continue
                    P.add("act", lambda e, c=c, bT=bT: e.activation(out=c.Y[0][:, :], in_=ps[bT][:, 0:128], func=AF.Copy),
                          reads=[t_ps[bT]], writes=[c.tY[0]])
                    if g.stop < 3.32:
                        continue
                    P.add("dve", lambda e, c=c: e.tensor_tensor(out=c.R[0][:, :], in0=c.Y[0][:, :], in1=ident, op=ALU.add),
                          reads=[c.tY[0], t_cst], writes=[c.tR[0]])
                if g.stop < 4:
                    continue
                a, ra = 0, 0
                for k in range(1, 6):
                    if g.stop < 4.0 + 0.1 * k - 0.05:
                        break
                    b1s = []
                    for p in range(4):
                        c = CH[p]
                        b1_ = rot.get()
                        b1s.append(b1_)

                        def mmsq(e, c=c, b1_=b1_, a=a, k=k):
                            ins = e.matmul(ps[b1_][:, 0:128], lhsT=c.Y[a][:, :], rhs=c.X[a][:, :], start=True, stop=True)
                            if k < 5:
                                ins = e.matmul(ps[b1_][:, 128:256], lhsT=c.X[a][:, :], rhs=c.Y[a][:, :], start=True, stop=True)
                            return ins
                        P.add("pe", mmsq, reads=[c.tX[a], c.tY[a]], writes=[t_ps[b1_]])
                    if g.sub < 2:
                        break
                    for p in range(4):
                        c = CH[p]
                        b1_ = b1s[p]
                        P.add("act", lambda e, c=c, b1_=b1_, a=a: e.activation(out=c.X[1 - a][:, :], in_=ps[b1_][:, 0:128], func=AF.Copy),
                              reads=[t_ps[b1_]], writes=[c.tX[1 - a]])
                        if k < 5:
                            P.add("act", lambda e, c=c, b1_=b1_, a=a: e.activation(out=c.Y[1 - a][:, :], in_=ps[b1_][:, 128:256], func=AF.Copy),
                                  reads=[t_ps[b1_]], writes=[c.tY[1 - a]])
                    if g.sub < 3:
                        break
                    b2s = []
                    for p in range(4):
                        c = CH[p]
                        b2_ = rot.get()
                        b2s.append(b2_)
                        def mmR(e, c=c, b2_=b2_, a=a, ra=ra):
                            e.matmul(ps[b2_][:, 0:128], lhsT=c.X[1 - a][:, :], rhs=c.R[ra][:, :], start=True, stop=False)
                            return e.matmul(ps[b2_][:, 0:128], lhsT=ident, rhs=c.R[ra][:, :], start=False, stop=True)
                        P.add("pe", mmR, reads=[c.tX[1 - a], c.tR[ra], t_cst], writes=[t_ps[b2_]])
                    if g.sub < 4:
                        break
                    for p in range(4):
                        c = CH[p]
                        b2_ = b2s[p]
                        P.add("act", lambda e, c=c, b2_=b2_, ra=ra: e.activation(out=c.R[1 - ra][:, :], in_=ps[b2_][:, 0:128], func=AF.Copy),
                              reads=[t_ps[b2_]], writes=[c.tR[1 - ra]])
                    a, ra = 1 - a, 1 - ra
                if g.stop < 4.6:
                    continue
                bws = []
                for p in range(4):
                    c = CH[p]
                    bw = rot.get()
                    bws.append(bw)

                    def mmw(e, c=c, bw=bw, ra=ra):
                        e.matmul(ps[bw][:, 0:128], lhsT=c.Kbe[:, :], rhs=c.R[ra][:, :], start=True, stop=True)
                        return e.matmul(ps[bw][:, 128:256], lhsT=c.R[ra][:, :], rhs=c.Vb[:, :], start=True, stop=True)
                    P.add("pe", mmw, reads=[c.tKbe, c.tVb, c.tR[ra]], writes=[t_ps[bw]])
                    P.add("dve", lambda e, c=c, p=p, h=h: e.tensor_scalar(out=c.Y[1][:, :], in0=ident, scalar1=eg[:, p, h:h + 1], scalar2=None,
                                                                          op0=ALU.mult),
                          reads=[t_cst, t_gate, c.tY[1]], writes=[c.tY[1]])
                for p in range(4):
                    c = CH[p]
                    bw = bws[p]
                    P.add("act", lambda e, c=c, bw=bw: e.activation(out=c.X[0][:, :], in_=ps[bw][:, 0:128], func=AF.Identity, scale=-1.0),
                          reads=[t_ps[bw]], writes=[c.tX[0]])
                    P.add("act", lambda e, c=c, bw=bw: e.activation(out=c.X[1][:, :], in_=ps[bw][:, 128:256], func=AF.Copy),
                          reads=[t_ps[bw]], writes=[c.tX[1]])
                bqs = []
                for p in range(4):
                    c = CH[p]
                    bq = rot.get()
                    bqs.append(bq)
                    P.add("pe", lambda e, c=c, bq=bq, p=p: e.matmul(ps[bq][:, 0:128], lhsT=Qt[:, p, :], rhs=c.Y[1][:, :], start=True, stop=True),
                          reads=[t_pair[p], c.tY[1]], writes=[t_ps[bq]])
                for p in range(4):
                    c = CH[p]
                    bq = bqs[p]
                    P.add("act", lambda e, c=c, bq=bq: e.activation(out=c.Y[0][:, :], in_=ps[bq][:, 0:128], func=AF.Copy),
                          reads=[t_ps[bq]], writes=[c.tY[0]])
                if g.stop < 5:
                    continue
                Sh = S_all[:, h, :]
                for cc in range(8):
                    p, s_ = cc // 2, cc % 2
                    c = CH[p]
                    r0 = 64 * s_
                    rows = slice(r0, r0 + 64)
                    mrows = slice(0, 128)
                    kd = c.kdec if s_ == 0 else c.kdecB
                    bw_ = rot.get()
                    dI = c.Kbe if s_ == 0 else c.Y[1]
                    tdI = c.tKbe if s_ == 0 else c.tY[1]
                    P.add("dve", lambda e, dI=dI, cc=cc, h=h: e.tensor_scalar(out=dI[:, :], in0=ident, scalar1=dch[:, cc, h:h + 1], scalar2=None,
                                                                           op0=ALU.mult),
                          reads=[t_cst, t_gate, tdI], writes=[tdI])

                    def mmv(e, c=c, bw_=bw_, Sh=Sh):
                        e.matmul(ps[bw_][:, 0:128], lhsT=ident, rhs=c.X[1][:, :], start=True, stop=False)
                        return e.matmul(ps[bw_][:, 0:128], lhsT=c.X[0][:, :], rhs=Sh, start=False, stop=True)
                    P.add("pe", mmv, reads=[c.tX[0], c.tX[1], t_S[h], t_cst], writes=[t_ps[bw_]])
                    P.add("act", lambda e, c=c, bw_=bw_, rows=rows: e.activation(out=c.X[1][rows, :], in_=ps[bw_][rows, 0:128], func=AF.Copy),
                          reads=[t_ps[bw_]], writes=[c.tX[1]])
                    bsu = rot.get()

                    def mmo(e, c=c, cc=cc, rows=rows, bsu=bsu, Sh=Sh, kd=kd, dI=dI):
                        e.matmul(ps[6][:, cc * 64:(cc + 1) * 64], lhsT=Sh, rhs=c.Y[0][:, rows], start=True, stop=False)
                        e.matmul(ps[6][:, cc * 64:(cc + 1) * 64], lhsT=c.X[1][:, :], rhs=c.QKT[:, rows], start=False, stop=True)
                        e.matmul(ps[bsu][:, 0:128], lhsT=kd[:, :], rhs=c.X[1][:, :], start=True, stop=False)
                        return e.matmul(ps[bsu][:, 0:128], lhsT=dI[:, :], rhs=Sh, start=False, stop=True)
                    P.add("pe", mmo, reads=[t_S[h], c.tY[0], c.tX[1], c.tQKT, c.tkdec, tdI], writes=[t_ps[6], t_ps[bsu]])
                    P.add("act", lambda e, bsu=bsu, Sh=Sh: e.activation(out=Sh, in_=ps[bsu][:, 0:128], func=AF.Copy),
                          reads=[t_ps[bsu]], writes=[t_S[h]])
                P.add("act", lambda e: e.activation(out=bigb[:, 16, :], in_=ps[6][:, :], func=AF.Square), reads=[t_ps[6]], writes=[tb[16]])
                g.rstd_from_sq([bigb[:, 16, :]], [tb[16]], TB, 128.0, EPS, 7)
                P.add("act", lambda e: e.activation(out=bigf[:, 7, :], in_=ps[6][:, :], func=AF.Identity, scale=V(f"gn{i}")),
                      reads=[t_ps[6], t_vec], writes=[tf[7]])
                P.add("dve", lambda e: e.tensor_tensor(out=bigf[:, 6, :], in0=bigf[:, 7, :], in1=rstd[:, :], op=ALU.mult),
                      reads=[tf[7], t_rstd], writes=[tf[6]])
                P.add("dve", lambda e, h=h, hh=hh: e.tensor_tensor(out=bigb[:, h, :], in0=bigf[:, 6, :], in1=zfs[hh], op=ALU.mult),
                      reads=[tf[6], tf[4 + hh]], writes=[tb[h]])
                if g.dbg and (j, hh) == tuple(g.dbg):
                    for q_ in range(7):
                        P.add("sp", lambda e, q_=q_: e.dma_start(out=g.d_dbg[q_], in_=bigf[:, q_, :]), reads=[tf[q_]], dma_key=("dbg", q_))
                    P.add("act", lambda e: e.activation(out=bigf[:, 7, :], in_=ps[6][:, :], func=AF.Copy), reads=[t_ps[6]], writes=[tf[7]])
                    P.add("sp", lambda e: e.dma_start(out=g.d_dbg[7], in_=bigf[:, 7, :]), reads=[tf[7]], dma_key=("dbg", 7))
                    P.add("sp", None, writes=tf)
        if g.dbg or g.stop < 6:
            return
        for m in range(8):
            sl, tsl = g.load_slot(soff[f"g_wout{i}"] + m)
            bk = rot.get()

            def mmo2(e, sl=sl, bk=bk):
                ins = None
                for kt in range(16):
                    ins = e.matmul(ps[bk][:, :], lhsT=sl[:, kt * 128:(kt + 1) * 128], rhs=bigb[:, kt, :],
                                   start=(kt == 0), stop=(kt == 15))
                return ins
            P.add("pe", mmo2, reads=[tsl] + tb[0:16], writes=[t_ps[bk]])
            g.post_evac(bk, m, f"nmo{l}")
        g.post_finish(b, 7)

    S.begin_layer = begin_layer
    S.block = block
    return S


_CACHE = {}


def kernel(**inputs):
    inp = {k: np.asarray(v) for k, v in inputs.items()}
    W = pack_weights(inp)
    Vv = pack_vecs(inp)
    Cc = make_consts()
    if "nc" not in _CACHE:
        _CACHE["nc"] = build_program()[0]
    nc = _CACHE["nc"]
    in_maps = []
    for b in range(8):
        x = np.asarray(inp["x"][b], np.float32)
        xT = np.ascontiguousarray(x.T.reshape(8, 128, T).transpose(1, 0, 2))
        pos = np.ascontiguousarray(np.broadcast_to(np.asarray(inp["positions"][b], np.int32)[None, :], (64, T)))
        in_maps.append({"xT": xT, "pos": pos, "vecs": Vv, "cst": Cc, "wts": W})
    res = run_bass_kernel_spmd(nc, in_maps, core_ids=list(range(8)))
    out = np.stack([np.asarray(r["yT"]).transpose(1, 0, 2).reshape(D, T).T for r in res.results])
    return np.ascontiguousarray(out.astype(np.float32))
```

```python
import contextlib
import numpy as np
import concourse.bass as bass
import concourse.mybir as mybir
from concourse.bass_utils import run_bass_kernel_spmd

F32 = mybir.dt.float32
BF16 = mybir.dt.bfloat16
I32 = mybir.dt.int32
AF = mybir.ActivationFunctionType
ALU = mybir.AluOpType
AX = mybir.AxisListType

ENGS = ("pe", "dve", "act", "pool", "sp")
EPOCH = 24000


class Tok:
    __slots__ = ("lw", "rd", "name")

    def __init__(self, name=""):
        self.lw = None
        self.rd = {}
        self.name = name


class Op:
    __slots__ = ("eng", "fn", "waits", "inc", "known", "dma")

    def __init__(self, eng, fn):
        self.eng = eng
        self.fn = fn
        self.waits = []
        self.inc = False
        self.known = None
        self.dma = None


class Prog:
    def __init__(self, nc, same_engine_sync=True):
        self.nc = nc
        self.ops = {e: [] for e in ENGS}
        self.known = {e: {} for e in ENGS}
        self.dma_cnt = {}
        self.dma_ops = {}
        self.same_engine_sync = same_engine_sync
        self.nwaits = 0

    def _lookup(self, s, p):
        if isinstance(s, tuple):
            return self.dma_ops[s][p - 1]
        return self.ops[s][p]

    def add(self, eng, fn, reads=(), writes=(), dma_key=None):
        ops = self.ops[eng]
        idx = len(ops)
        op = Op(eng, fn)
        deps = {}
        for t in reads:
            if t.lw is not None:
                s, p = t.lw
                if deps.get(s, -1) < p:
                    deps[s] = p
        for t in writes:
            if t.lw is not None:
                s, p = t.lw
                if deps.get(s, -1) < p:
                    deps[s] = p
            for s, p in t.rd.items():
                if deps.get(s, -1) < p:
                    deps[s] = p
        known = self.known[eng]
        for s, p in deps.items():
            if s == eng and dma_key is None and (eng == "pe" or not self.same_engine_sync):
                continue
            if known.get(s, -1) >= p:
                continue
            op.waits.append((s, p))
            known[s] = p
            src = self._lookup(s, p)
            for s2, p2 in src.known.items():
                if known.get(s2, -1) < p2:
                    known[s2] = p2
            src.inc = True
        self.nwaits += len(op.waits)
        if dma_key is None:
            mypos = (eng, idx)
        else:
            k = ("dma", dma_key)
            n = self.dma_cnt.get(k, 0) + 1
            self.dma_cnt[k] = n
            self.dma_ops.setdefault(k, []).append(op)
            mypos = (k, n)
            op.dma = k
        op.known = dict(known)
        for t in reads:
            if t.rd.get(mypos[0], -1) < mypos[1]:
                t.rd[mypos[0]] = mypos[1]
        for t in writes:
            t.lw = mypos
            t.rd = {}
        ops.append(op)
        return op

    def emit(self, stack):
        nc = self.nc
        pref = {}
        nsem = {}
        for e in ENGS:
            c = 0
            arr = []
            for op in self.ops[e]:
                if op.inc and op.dma is None:
                    c += 1
                arr.append(c)
            pref[e] = arr
            nsem[e] = (c + EPOCH - 1) // EPOCH
        sems = {e: [stack.enter_context(nc.semaphore(f"s_{e}_{i}")) for i in range(nsem[e])]
                for e in ENGS}
        dsems = {k: stack.enter_context(nc.semaphore("d_%d" % i))
                 for i, k in enumerate(self.dma_cnt)}
        self.n_sems = sum(nsem.values()) + len(dsems)
        block = stack.enter_context(nc.Block())
        hw = {"pe": block.tensor, "dve": block.vector, "act": block.scalar,
              "pool": block.gpsimd, "sp": block.sync}

        def run(e):
            def body(engine):
                for op in self.ops[e]:
                    for s, p in op.waits:
                        if isinstance(s, tuple):
                            engine.wait_ge(dsems[s], 16 * p)
                        else:
                            c = pref[s][p]
                            engine.wait_ge(sems[s][(c - 1) // EPOCH], (c - 1) % EPOCH + 1)
                    if op.fn is None:
                        continue
                    ins = op.fn(engine)
                    if op.dma is not None:
                        ins.then_inc(dsems[op.dma], 16)
                    elif op.inc:
                        c = pref[e][self.ops[e].index(op)] if False else None
                        ins.then_inc(sems[e][(op_c[id(op)] - 1) // EPOCH], 1)
            return body

        op_c = {}
        for e in ENGS:
            for i, op in enumerate(self.ops[e]):
                if op.inc and op.dma is None:
                    op_c[id(op)] = pref[e][i]
        for e in ENGS:
            hw[e](run(e))


T = 2048
D = 1024
KT = 8
TB = 512
NB = T // TB
DEPTH = 4
FF = 2816
FT = 22
SLOT_EL = 2048
EPS = 1e-6
NEG = -30000.0
TWO_PI = 6.283185307179586


def vec_layout():
    off = {}
    c = 0

    def put(name, n):
        nonlocal c
        off[name] = c
        c += n
    for l in range(4):
        for nm in ("nmp", "nmo", "nfp", "nfo"):
            put(f"{nm}{l}", 8)
    for i in range(2):
        for j in range(4):
            put(f"cw{i}_{j}", 8)
        for nm in ("cb", "gab", "gxb", "lam"):
            put(f"{nm}{i}", 8)
        put(f"qn{i}", 4)
        put(f"kvn{i}", 2)
    for i in range(2):
        for j in range(4):
            put(f"gcw{i}_{j}", 32)
        put(f"gn{i}", 1)
        put(f"alog{i}", 16)
        put(f"dtb{i}", 16)
    put("invf", 1)
    put("sgn", 1)
    return off, c


def cst_layout():
    off = {}
    c = 0
    for nm in ("ident", "amask", "U2", "SL2", "NMS", "NMC", "BD", "ONA", "ONB"):
        off[nm] = c
        c += 128
    return off, c


def slot_layout():
    off = {}
    c = 0

    def put(name, n):
        nonlocal c
        off[name] = c
        c += n
    for i in range(2):
        put(f"e_win{i}", 12)
        put(f"e_gate{i}", 1)
        put(f"e_uq{i}", 4)
        put(f"e_ukv{i}", 2)
        put(f"e_wout{i}", 8)
    for i in range(2):
        put(f"g_win{i}", 24)
        put(f"g_wba{i}", 1)
        put(f"g_wout{i}", 8)
    for l in range(4):
        put(f"f_gu{l}", 22)
        put(f"f_down{l}", 16)
    return off, c


def _tile_w(W, k0, nk, cols):
    sub = W[k0:k0 + nk * 128][:, cols]
    return sub.reshape(nk, 128, -1).transpose(1, 0, 2).reshape(128, -1)


def _pad(a):
    out = np.zeros((128, SLOT_EL), np.float32)
    out[:, :a.shape[1]] = a
    return out


def pack_weights(inp):
    soff, ns = slot_layout()
    W = np.zeros((ns, 128, SLOT_EL), np.float32)
    ar = np.arange
    for i in range(2):
        win = inp["hy_w_in"][i]
        s0 = soff[f"e_win{i}"]
        for n in range(8):
            cols = np.concatenate([ar(n * 128, n * 128 + 128), ar(1024 + n * 128, 1024 + n * 128 + 128)])
            W[s0 + n] = _pad(_tile_w(win, 0, 8, cols))
        W[s0 + 8] = _pad(_tile_w(win, 0, 8, ar(2048, 2304)))
        W[s0 + 9] = _pad(_tile_w(win, 0, 8, ar(2304, 2560)))
        W[s0 + 10] = _pad(_tile_w(win, 0, 8, ar(2560, 2816)))
        cols = np.concatenate([ar(2816, 2880), ar(2848, 2880), ar(2816, 2848)])
        W[s0 + 11] = _pad(_tile_w(win, 0, 8, cols))
        ga, gx = inp["rg_gate_a_w"][i], inp["rg_gate_x_w"][i]
        g = np.concatenate([ga, gx], axis=2)
        W[soff[f"e_gate{i}"]] = _pad(g.transpose(1, 0, 2).reshape(128, -1))
        uq = inp["mla_w_uq"][i]
        for s in range(4):
            parts = []
            for hh in range(2):
                h = 2 * s + hh
                b = h * 192
                cols = np.concatenate([ar(b, b + 192), ar(b + 160, b + 192), ar(b + 128, b + 160)])
                parts.append(_tile_w(uq, 0, 4, cols))
            W[soff[f"e_uq{i}"] + s] = _pad(np.concatenate(parts, axis=1))
        ukv = inp["mla_w_ukv"][i]
        for s in range(2):
            parts = [_tile_w(ukv, 0, 2, ar((4 * s + hh) * 256, (4 * s + hh) * 256 + 256)) for hh in range(4)]
            W[soff[f"e_ukv{i}"] + s] = _pad(np.concatenate(parts, axis=1))
        wo = inp["hy_w_out"][i]
        for m in range(8):
            W[soff[f"e_wout{i}"] + m] = _pad(_tile_w(wo, 0, 16, ar(m * 128, m * 128 + 128)))
    for i in range(2):
        win = inp["gdn_w_in"][i]
        s0 = soff[f"g_win{i}"]
        for j in range(8):
            q = ar(j * 128, j * 128 + 128)
            k = ar(1024 + j * 128, 1024 + j * 128 + 128)
            v = ar(2048 + 2 * j * 128, 2048 + 2 * j * 128 + 256)
            z = ar(4096 + 2 * j * 128, 4096 + 2 * j * 128 + 256)
            W[s0 + 3 * j] = _pad(_tile_w(win, 0, 8, np.concatenate([q, k])))
            W[s0 + 3 * j + 1] = _pad(_tile_w(win, 0, 8, v))
            W[s0 + 3 * j + 2] = _pad(_tile_w(win, 0, 8, z))
        W[soff[f"g_wba{i}"]] = _pad(_tile_w(win, 0, 8, ar(6144, 6176)))
        wo = inp["gdn_w_out"][i]
        for m in range(8):
            W[soff[f"g_wout{i}"] + m] = _pad(_tile_w(wo, 0, 16, ar(m * 128, m * 128 + 128)))
    for l in range(4):
        wg, wu, wd = inp["ffn_w_gate"][l], inp["ffn_w_up"][l], inp["ffn_w_down"][l]
        for m in range(FT):
            c = ar(m * 128, m * 128 + 128)
            W[soff[f"f_gu{l}"] + m] = _pad(np.concatenate([_tile_w(wg, 0, 8, c), _tile_w(wu, 0, 8, c)], axis=1)
                                           .reshape(128, 2, 8, 128).transpose(0, 2, 1, 3).reshape(128, -1))
        for m in range(8):
            for hf in range(2):
                W[soff[f"f_down{l}"] + 2 * m + hf] = _pad(_tile_w(wd, hf * 11 * 128, 11, ar(m * 128, m * 128 + 128)))
    return W


def pack_vecs(inp):
    voff, nv = vec_layout()
    V = np.zeros((128, nv), np.float32)

    def v8(name, v):
        a = np.asarray(v, np.float32).reshape(-1, 128).T
        V[:, voff[name]:voff[name] + a.shape[1]] = a
    for l in range(4):
        v8(f"nmp{l}", inp["norm_mix_pre"][l])
        v8(f"nmo{l}", inp["norm_mix_post"][l])
        v8(f"nfp{l}", inp["norm_ffn_pre"][l])
        v8(f"nfo{l}", inp["norm_ffn_post"][l])
    for i in range(2):
        for j in range(4):
            v8(f"cw{i}_{j}", inp["rg_conv_w"][i][j])
        v8(f"cb{i}", inp["rg_conv_b"][i])
        v8(f"gab{i}", inp["rg_gate_a_b"][i])
        v8(f"gxb{i}", inp["rg_gate_x_b"][i])
        v8(f"lam{i}", inp["rg_lambda"][i])
        v8(f"qn{i}", inp["mla_q_norm"][i])
        v8(f"kvn{i}", inp["mla_kv_norm"][i])
    for i in range(2):
        for j in range(4):
            v8(f"gcw{i}_{j}", inp["gdn_conv_w"][i][j])
        v8(f"gn{i}", inp["gdn_norm"][i])
        V[:, voff[f"alog{i}"]:voff[f"alog{i}"] + 16] = np.asarray(inp["gdn_a_log"][i], np.float32)[None, :]
        V[:, voff[f"dtb{i}"]:voff[f"dtb{i}"] + 16] = np.asarray(inp["gdn_dt_bias"][i], np.float32)[None, :]
    inv_freq = (1.0 / (np.float32(10000.0) ** (np.arange(0, 64, 2, dtype=np.float32) / np.float32(64)))).astype(np.float32)
    V[:64, voff["invf"]] = np.concatenate([inv_freq, inv_freq])
    V[:32, voff["sgn"]] = -1.0
    V[32:64, voff["sgn"]] = 1.0
    return V


def make_consts():
    coff, ncc = cst_layout()
    C = np.zeros((128, ncc), np.float32)
    idx = np.arange(128)
    C[:, coff["ident"]:coff["ident"] + 128] = np.eye(128, dtype=np.float32)
    C[:, coff["amask"]:coff["amask"] + 128] = np.where(idx[:, None] <= idx[None, :], 0.0, NEG)
    same = (idx[:, None] // 64) == (idx[None, :] // 64)
    C[:, coff["U2"]:coff["U2"] + 128] = (same & (idx[:, None] <= idx[None, :])).astype(np.float32)
    C[:, coff["SL2"]:coff["SL2"] + 128] = (same & (idx[:, None] > idx[None, :])).astype(np.float32)
    C[:, coff["NMS"]:coff["NMS"] + 128] = np.where(same & (idx[:, None] > idx[None, :]), 0.0, NEG)
    C[:, coff["NMC"]:coff["NMC"] + 128] = np.where(same & (idx[:, None] <= idx[None, :]), 0.0, NEG)
    C[:, coff["BD"]:coff["BD"] + 128] = same.astype(np.float32)
    C[0:64, coff["ONA"]:coff["ONA"] + 128] = 1.0
    C[64:128, coff["ONB"]:coff["ONB"] + 128] = 1.0
    return C


class Ctx:
    pass


def build_program(layers=(0, 1, 2, 3), do_ffn=True, do_mix=True, cast_engs=("dve", "act", "dve"), dbg=False, stop=99, sub=99):
    nc = bass.Bass("TRN2", target_bir_lowering=False, dynamic_dma_scratch_size=1024)
    voff, nv = vec_layout()
    coff, ncc = cst_layout()
    soff, nslots = slot_layout()
    d_x = nc.dram_tensor("xT", [128, KT, T], F32, kind="ExternalInput").ap()
    d_pos = nc.dram_tensor("pos", [64, T], I32, kind="ExternalInput").ap()
    d_vec = nc.dram_tensor("vecs", [128, nv], F32, kind="ExternalInput").ap()
    d_cst = nc.dram_tensor("cst", [128, ncc], F32, kind="ExternalInput").ap()
    d_w = nc.dram_tensor("wts", [nslots, 128, SLOT_EL], F32, kind="ExternalInput").ap()
    d_y = nc.dram_tensor("yT", [128, KT, T], F32, kind="ExternalOutput").ap()
    with contextlib.ExitStack() as st:
        P = Prog(nc)
        g = Ctx()
        g.P, g.nc, g.voff, g.coff, g.soff = P, nc, voff, coff, soff
        g.dbg = dbg
        g.stop = stop
        g.sub = sub
        g.d_dbg = nc.dram_tensor("dbg", [8, 128, 512], F32, kind="ExternalOutput").ap() if dbg else None

        def sb(name, shape, dt):
            return st.enter_context(nc.sbuf_tensor(name, shape, dt))

        g.sb = sb
        xT = sb("xT_sb", [128, KT, T], F32)
        xtok = [[Tok() for _ in range(NB)] for _ in range(KT)]
        vec = sb("vec_sb", [128, nv], F32)
        t_vec = Tok()
        cst = sb("cst_sb", [128, ncc], F32)
        t_cst = Tok()
        identb = sb("identb", [128, 128], BF16)
        amaskb = sb("amaskb", [128, 128], BF16)
        onesb = sb("onesb", [128, 128], BF16)
        onesf = sb("onesf", [128, 128], F32)
        t_cb = Tok()
        hT = sb("hT", [128, KT, TB], BF16)
        t_h = Tok()
        bigb = sb("bigb", [128, 22, TB], BF16)
        tb = [Tok() for _ in range(22)]
        bigf = sb("bigf", [128, 8, TB], F32)
        tf = [Tok() for _ in range(8)]
        rstd = sb("rstd", [128, TB], F32)
        t_rstd = Tok()
        lnt = sb("lnt", [128, TB], F32)
        t_lnt = Tok()
        NSTG, NSL = 2, 3
        stage = [sb(f"stage{i}", [128, SLOT_EL], F32) for i in range(NSTG)]
        t_stage = [Tok() for _ in range(NSTG)]
        slots = [sb(f"slot{i}", [128, SLOT_EL], BF16) for i in range(NSL)]
        t_slot = [Tok() for _ in range(NSL)]
        ARENA32 = 11072
        arena = sb("arena", [128, ARENA32], F32)

        def mk_asb():
            off = [0]

            def asb(name, shape, dt):
                n = int(np.prod(shape[1:]))
                n32 = (n * (4 if dt in (F32, I32) else 2) + 3) // 4
                v = arena[:, off[0]:off[0] + n32]
                off[0] += n32
                assert off[0] <= ARENA32, (name, off[0])
                if dt != F32:
                    v = v.bitcast(dt)
                if len(shape) == 3:
                    v = v.rearrange("p (a b) -> p a b", b=shape[2])
                if shape[0] < 128:
                    v = v[0:shape[0]]
                return v
            return asb
        g.mk_asb = mk_asb
        g.arena_toks = []
        scr = sb("scr", [128, 2], F32)

        def arena_barrier():
            P.add("dve", lambda e: e.memset(scr[:, :], 0.0), writes=list(g.arena_toks))
        g.arena_barrier = arena_barrier
        ps = [st.enter_context(nc.psum_tensor(f"ps{i}", [128, 512], F32)) for i in range(8)]
        t_ps = [Tok() for _ in range(8)]
        g.xT, g.xtok, g.vec, g.t_vec, g.cst, g.t_cst = xT, xtok, vec, t_vec, cst, t_cst
        g.identb, g.amaskb, g.onesb, g.onesf, g.t_cb = identb, amaskb, onesb, onesf, t_cb
        g.hT, g.t_h, g.bigb, g.tb, g.bigf, g.tf = hT, t_h, bigb, tb, bigf, tf
        g.rstd, g.t_rstd, g.ps, g.t_ps = rstd, t_rstd, ps, t_ps
        g.d_pos = d_pos

        def V(name, k=0, n=1, rows=128):
            c = voff[name] + k
            return vec[0:rows, c:c + n]

        def C(name, rows=128, cols=128, c0=0):
            c = coff[name] + c0
            return cst[0:rows, c:c + cols]

        g.V, g.C = V, C

        class PsRot:
            def __init__(self, ids):
                self.ids = list(ids)
                self.i = 0

            def get(self):
                b = self.ids[self.i % len(self.ids)]
                self.i += 1
                return b
        g.PsRot = PsRot

        wctr = [0]

        def load_slot(idx, nel=SLOT_EL):
            k = wctr[0]
            wctr[0] += 1
            sg, sl = k % NSTG, k % NSL
            P.add("sp", lambda e: e.dma_start(out=stage[sg][:, 0:nel], in_=d_w[idx, :, 0:nel]),
                  writes=[t_stage[sg]], dma_key=("stg", sg))
            ce = cast_engs[k % len(cast_engs)]
            if ce == "act":
                fn = lambda e: e.copy(out=slots[sl][:, 0:nel], in_=stage[sg][:, 0:nel])
            else:
                fn = lambda e: e.tensor_copy(out=slots[sl][:, 0:nel], in_=stage[sg][:, 0:nel])
            P.add(ce, fn, reads=[t_stage[sg]], writes=[t_slot[sl]])
            return slots[sl], t_slot[sl]
        g.load_slot = load_slot

        P.add("sp", lambda e: e.dma_start(out=vec[:, :], in_=d_vec[:, :]), writes=[t_vec], dma_key="vec")
        P.add("sp", lambda e: e.dma_start(out=cst[:, :], in_=d_cst[:, :]), writes=[t_cst], dma_key="cst")
        for kt in range(KT):
            P.add("sp", lambda e, kt=kt: e.dma_start(out=xT[:, kt, :], in_=d_x[:, kt, :]),
                  writes=xtok[kt], dma_key=("x", kt))

        def setup_consts(e):
            e.tensor_copy(out=identb[:, :], in_=C("ident"))
            e.tensor_copy(out=amaskb[:, :], in_=C("amask"))
            e.memset(onesb[:, :], 1.0)
            return e.memset(onesf[:, :], 1.0)
        P.add("dve", setup_consts, reads=[t_cst], writes=[t_cb])

        def rstd_from_sq(sq_aps, sq_toks, n, dsum, eps, bank):
            def mm(e):
                ins = None
                for i, a in enumerate(sq_aps):
                    ins = e.matmul(ps[bank][:, 0:n], lhsT=onesb[:, :], rhs=a,
                                   start=(i == 0), stop=(i == len(sq_aps) - 1))
                return ins
            P.add("pe", mm, reads=list(sq_toks) + [t_cb], writes=[t_ps[bank]])
            P.add("act", lambda e: e.activation(out=lnt[:, 0:n], in_=ps[bank][:, 0:n], func=AF.Ln,
                                                scale=1.0 / dsum, bias=g.eps_ap),
                  reads=[t_ps[bank], t_cb2], writes=[t_lnt])
            P.add("act", lambda e: e.activation(out=rstd[:, 0:n], in_=lnt[:, 0:n], func=AF.Exp, scale=-0.5),
                  reads=[t_lnt], writes=[t_rstd])
        g.rstd_from_sq = rstd_from_sq
        epst = sb("epst", [128, 2], F32)
        t_cb2 = Tok()
        g.eps_ap = epst[:, 0:1]
        g.one_ap = epst[:, 1:2]
        g.t_cb2 = t_cb2

        def setup_eps(e):
            e.memset(epst[:, 0:1], EPS)
            return e.memset(epst[:, 1:2], 1.0)
        P.add("dve", setup_eps, writes=[t_cb2])

        def prenorm(b, wname, bank):
            blk = slice(b * TB, (b + 1) * TB)
            P.add("act", lambda e: e.activation(out=bigb[:, 0:8, :], in_=xT[:, :, blk], func=AF.Square),
                  reads=[xtok[kt][b] for kt in range(KT)], writes=tb[0:8])
            rstd_from_sq([bigb[:, kt, :] for kt in range(KT)], tb[0:8], TB, float(D), EPS, bank)

            def f(e):
                ins = None
                for kt in range(KT):
                    ins = e.scalar_tensor_tensor(out=hT[:, kt, :], in0=xT[:, kt, blk], scalar=V(wname, kt),
                                                 in1=rstd[:, :], op0=ALU.mult, op1=ALU.mult)
                return ins
            P.add("dve", f, reads=[xtok[kt][b] for kt in range(KT)] + [t_rstd, t_vec], writes=[t_h])
        g.prenorm = prenorm

        def post_evac(bank, m, wname):
            P.add("act", lambda e: e.activation(out=bigf[:, m, :], in_=ps[bank][:, :], func=AF.Identity,
                                                scale=V(wname, m)),
                  reads=[t_ps[bank], t_vec], writes=[tf[m]])
            P.add("act", lambda e: e.activation(out=hT[:, m, :], in_=ps[bank][:, :], func=AF.Square),
                  reads=[t_ps[bank]], writes=[t_h])
        g.post_evac = post_evac

        def post_finish(b, bank):
            blk = slice(b * TB, (b + 1) * TB)
            rstd_from_sq([hT[:, kt, :] for kt in range(KT)], [t_h], TB, float(D), EPS, bank)
            for kt in range(KT):
                P.add("dve", lambda e, kt=kt: e.tensor_tensor(out=bigf[:, kt, :], in0=bigf[:, kt, :], in1=rstd[:, :],
                                                              op=ALU.mult),
                      reads=[tf[kt], t_rstd], writes=[tf[kt]])
                P.add("dve", lambda e, kt=kt: e.tensor_tensor(out=xT[:, kt, blk], in0=xT[:, kt, blk],
                                                              in1=bigf[:, kt, :], op=ALU.add),
                      reads=[tf[kt], xtok[kt][b]], writes=[xtok[kt][b]])
        g.post_finish = post_finish

        sgt = [sb(f"sgt{i}", [128, TB], F32) for i in range(2)]
        t_sgt = [Tok(), Tok()]

        def ffn_block(l, b):
            rot = PsRot([0, 1, 2, 3, 4, 5])
            prenorm(b, f"nfp{l}", 7)
            for m in range(FT):
                sl, tsl = load_slot(soff[f"f_gu{l}"] + m)
                bg, bu = rot.get(), rot.get()

                def mm(e, sl=sl, bg=bg, bu=bu):
                    ins = None
                    for kt in range(KT):
                        ins = e.matmul(ps[bg][:, :], lhsT=sl[:, kt * 256:kt * 256 + 128], rhs=hT[:, kt, :],
                                       start=(kt == 0), stop=(kt == KT - 1))
                    for kt in range(KT):
                        ins = e.matmul(ps[bu][:, :], lhsT=sl[:, kt * 256 + 128:kt * 256 + 256], rhs=hT[:, kt, :],
                                       start=(kt == 0), stop=(kt == KT - 1))
                    return ins
                P.add("pe", mm, reads=[tsl, t_h], writes=[t_ps[bg], t_ps[bu]])
                s = m % 2
                P.add("act", lambda e, s=s, bg=bg: e.activation(out=sgt[s][:, :], in_=ps[bg][:, :], func=AF.Silu),
                      reads=[t_ps[bg]], writes=[t_sgt[s]])
                P.add("dve", lambda e, s=s, bu=bu, m=m: e.tensor_tensor(out=bigb[:, m, :], in0=sgt[s][:, :],
                                                                        in1=ps[bu][:, :], op=ALU.mult),
                      reads=[t_sgt[s], t_ps[bu]], writes=[tb[m]])
            for dm in range(8):
                s0, ts0 = load_slot(soff[f"f_down{l}"] + 2 * dm, 11 * 128)
                s1, ts1 = load_slot(soff[f"f_down{l}"] + 2 * dm + 1, 11 * 128)
                bk = rot.get()

                def mm(e, s0=s0, s1=s1, bk=bk):
                    ins = None
                    for k in range(FT):
                        s_, kk = (s0, k) if k < 11 else (s1, k - 11)
                        ins = e.matmul(ps[bk][:, :], lhsT=s_[:, kk * 128:(kk + 1) * 128], rhs=bigb[:, k, :],
                                       start=(k == 0), stop=(k == FT - 1))
                    return ins
                P.add("pe", mm, reads=[ts0, ts1] + tb, writes=[t_ps[bk]])
                post_evac(bk, dm, f"nfo{l}")
            post_finish(b, 7)
        g.ffn_block = ffn_block

        even_state = make_even(g) if do_mix else None
        odd_state = make_odd(g) if do_mix else None
        for l in layers:
            i = l // 2
            if do_mix:
                if l % 2 == 0:
                    even_state.begin_layer(l)
                else:
                    odd_state.begin_layer(l)
            for b in range(1 if dbg else NB):
                if do_mix:
                    if l % 2 == 0:
                        even_state.block(l, b)
                    else:
                        odd_state.block(l, b)
                if do_ffn:
                    ffn_block(l, b)

        outs = []
        for kt in range(KT):
            P.add("sp", lambda e, kt=kt: e.dma_start(out=d_y[:, kt, :], in_=xT[:, kt, :]),
                  reads=xtok[kt], dma_key=("y", kt))
            outs.extend(xtok[kt])
        P.add("sp", None, writes=outs)
        P.emit(st)
        g.stats = {e: len(P.ops[e]) for e in ENGS}
        g.stats["waits"] = P.nwaits
        g.stats["sems"] = P.n_sems
    return nc, g


def reg_tok(g):
    t = Tok()
    g.arena_toks.append(t)
    return t


def make_even(g):
    P, sb, ps, t_ps, V, C = g.P, g.sb, g.ps, g.t_ps, g.V, g.C
    bigb, tb, bigf, tf, hT, t_h = g.bigb, g.tb, g.bigf, g.tf, g.hT, g.t_h
    rstd, t_rstd, t_vec, t_cb = g.rstd, g.t_rstd, g.t_vec, g.t_cb
    soff = g.soff
    S = Ctx()
    sb = g.mk_asb()
    Tok = lambda: reg_tok(g)
    ckvn = sb("ckvn", [128, 2, T], BF16)
    t_ckv = [Tok() for _ in range(NB)]
    kpe = sb("kpe", [128, T], BF16)
    t_kpe = [Tok() for _ in range(NB)]
    Kh = sb("Kh", [128, T], BF16)
    t_Kh = [Tok() for _ in range(NB)]
    Vh = sb("Vh", [128, 16, 128], BF16)
    t_Vh = [Tok() for _ in range(NB)]
    qn = sb("qn", [128, TB], BF16)
    t_qn = Tok()
    qr = sb("qr", [128, TB], BF16)
    t_qr = Tok()
    PT = [sb(f"PT{i}", [128, TB], BF16) for i in range(2)]
    t_PT = [Tok(), Tok()]
    Ct = sb("Ct", [64, TB], F32)
    St = sb("St", [64, TB], F32)
    t_rope = Tok()
    posi = sb("posi", [64, TB], I32)
    t_posi = Tok()
    ki = sb("ki", [64, TB], I32)
    t_ki = Tok()
    xrw = sb("xrw", [128, TB + 4], F32)
    t_xrw = Tok()
    xcb = sb("xcb", [128, TB], BF16)
    t_xcb = Tok()
    tail = sb("rgtail", [128, 8, 3], F32)
    t_tail = [Tok() for _ in range(8)]
    hst = sb("hst", [128, 8], F32)
    t_hst = [Tok() for _ in range(8)]
    nsp8 = sb("nsp8", [128, 8], F32)
    t_nsp = Tok()
    lt8 = sb("lt8", [128, 8], F32)
    t_lt8 = Tok()
    PI_LO = 3.1415925
    c1 = 6.28125
    c2 = float(np.float32(TWO_PI - c1).view(np.uint32) & np.uint32(0xFFFFF000)) if False else None
    c2 = float((np.array([TWO_PI - c1], np.float32).view(np.uint32) & np.uint32(0xFFFFF000)).view(np.float32)[0])
    c3 = float(np.float32(TWO_PI - c1 - c2))
    SCALE = float(192 ** -0.5)

    def begin_layer(l):
        i = l // 2
        g.arena_barrier()
        P.add("act", lambda e: e.activation(out=lt8[:, :], in_=V(f"lam{i}", 0, 8), func=AF.Exp, scale=-1.0),
              reads=[t_vec], writes=[t_lt8])
        P.add("act", lambda e: e.activation(out=lt8[:, :], in_=lt8[:, :], func=AF.Ln, bias=g.one_ap),
              reads=[t_lt8, g.t_cb2], writes=[t_lt8])
        P.add("dve", lambda e: e.tensor_scalar(out=nsp8[:, :], in0=lt8[:, :], scalar1=-8.0, scalar2=None, op0=ALU.mult),
              reads=[t_lt8], writes=[t_nsp])
        P.add("dve", lambda e: e.memset(tail[:, :, :], 0.0), writes=t_tail)
        P.add("dve", lambda e: e.memset(kpe[64:128, :], 0.0), writes=t_kpe)
        P.add("dve", lambda e: e.memset(qr[64:128, :], 0.0), writes=[t_qr])
        P.add("dve", lambda e: e.memset(hst[:, :], 0.0), writes=t_hst)

    def rope_tables(b):
        blk = slice(b * TB, (b + 1) * TB)
        f0, f1, f2 = bigf[0:64, 0, :], bigf[0:64, 1, :], bigf[0:64, 2, :]
        P.add("sp", lambda e: e.dma_start(out=posi[:, :], in_=g.d_pos[:, blk]), writes=[t_posi], dma_key="pos")
        P.add("dve", lambda e: e.tensor_copy(out=f0, in_=posi[:, :]), reads=[t_posi], writes=[tf[0]])
        P.add("dve", lambda e: e.tensor_scalar(out=f1, in0=f0, scalar1=V("invf", 0, 1, 64), scalar2=None, op0=ALU.mult),
              reads=[tf[0], t_vec], writes=[tf[1]])
        P.add("dve", lambda e: e.tensor_scalar(out=ki[:, :], in0=f1, scalar1=float(1.0 / TWO_PI), scalar2=None, op0=ALU.mult),
              reads=[tf[1]], writes=[t_ki])
        P.add("dve", lambda e: e.tensor_copy(out=f2, in_=ki[:, :]), reads=[t_ki], writes=[tf[2]])
        for cc in (c1, c2, c3):
            P.add("dve", lambda e, cc=cc: e.scalar_tensor_tensor(out=f1, in0=f2, scalar=-cc, in1=f1, op0=ALU.mult, op1=ALU.add),
                  reads=[tf[1], tf[2]], writes=[tf[1]])

        def wrap(y, ty):
            P.add("dve", lambda e: e.tensor_scalar(out=f0, in0=y, scalar1=float(np.pi), scalar2=None, op0=ALU.is_gt),
                  reads=[ty], writes=[tf[0]])
            P.add("dve", lambda e: e.scalar_tensor_tensor(out=y, in0=f0, scalar=-TWO_PI, in1=y, op0=ALU.mult, op1=ALU.add),
                  reads=[tf[0], ty], writes=[ty])
            P.add("dve", lambda e: e.tensor_scalar(out=f0, in0=y, scalar1=float(-np.pi), scalar2=None, op0=ALU.is_lt),
                  reads=[ty], writes=[tf[0]])
            P.add("dve", lambda e: e.scalar_tensor_tensor(out=y, in0=f0, scalar=TWO_PI, in1=y, op0=ALU.mult, op1=ALU.add),
                  reads=[tf[0], ty], writes=[ty])
            P.add("dve", lambda e: e.tensor_scalar(out=y, in0=y, scalar1=PI_LO, scalar2=-PI_LO, op0=ALU.min, op1=ALU.max),
                  reads=[ty], writes=[ty])
        P.add("dve", lambda e: e.tensor_scalar(out=f2, in0=f1, scalar1=float(np.pi / 2), scalar2=None, op0=ALU.add),
              reads=[tf[1]], writes=[tf[2]])
        wrap(f1, tf[1])
        wrap(f2, tf[2])
        P.add("act", lambda e: e.activation(out=St[:, :], in_=f1, func=AF.Sin, scale=V("sgn", 0, 1, 64)),
              reads=[tf[1], t_vec], writes=[t_rope])
        P.add("act", lambda e: e.activation(out=Ct[:, :], in_=f2, func=AF.Sin),
              reads=[tf[2]], writes=[t_rope])

    def rope_apply(bA, bB, out_ap, out_toks):
        t1, t2 = bigf[0:64, 0, :], bigf[0:64, 1, :]
        P.add("dve", lambda e: e.tensor_tensor(out=t1, in0=ps[bA][0:64, :], in1=Ct[:, :], op=ALU.mult),
              reads=[t_ps[bA], t_rope], writes=[tf[0]])
        P.add("dve", lambda e: e.tensor_tensor(out=t2, in0=ps[bB][0:64, :], in1=St[:, :], op=ALU.mult),
              reads=[t_ps[bB], t_rope], writes=[tf[1]])
        P.add("dve", lambda e: e.tensor_tensor(out=out_ap, in0=t1, in1=t2, op=ALU.add),
              reads=[tf[0], tf[1]], writes=out_toks)

    def block(l, b):
        i = l // 2
        blk = slice(b * TB, (b + 1) * TB)
        rot = g.PsRot([0, 1, 2, 3])
        rope_tables(b)
        if g.stop < 1:
            return
        g.prenorm(b, f"nmp{l}", 7)
        s0 = soff[f"e_win{i}"]
        if g.stop < 2:
            return
        for half in range(2):
            sl, tsl = g.load_slot(s0 + 8 + half)
            for mm_ in range(2):
                mt = 2 * half + mm_
                bk = rot.get()

                def mm(e, sl=sl, bk=bk, mm_=mm_):
                    ins = None
                    for kt in range(KT):
                        ins = e.matmul(ps[bk][:, :], lhsT=sl[:, kt * 256 + mm_ * 128:kt * 256 + mm_ * 128 + 128],
                                       rhs=hT[:, kt, :], start=(kt == 0), stop=(kt == KT - 1))
                    return ins
                P.add("pe", mm, reads=[tsl, t_h], writes=[t_ps[bk]])
                P.add("act", lambda e, bk=bk, mt=mt: e.activation(out=bigf[:, 4 + mt, :], in_=ps[bk][:, :], func=AF.Copy),
                      reads=[t_ps[bk]], writes=[tf[4 + mt]])
                P.add("act", lambda e, bk=bk, mt=mt: e.activation(out=bigb[:, 8 + mt, :], in_=ps[bk][:, :], func=AF.Square),
                      reads=[t_ps[bk]], writes=[tb[8 + mt]])
        g.rstd_from_sq([bigb[:, 8 + mt, :] for mt in range(4)], tb[8:12], TB, 512.0, EPS, 7)
        for mt in range(4):
            P.add("dve", lambda e, mt=mt: e.scalar_tensor_tensor(out=bigb[:, 16 + mt, :], in0=bigf[:, 4 + mt, :],
                                                                 scalar=V(f"qn{i}", mt), in1=rstd[:, :],
                                                                 op0=ALU.mult, op1=ALU.mult),
                  reads=[tf[4 + mt], t_rstd, t_vec], writes=[tb[16 + mt]])
        sl, tsl = g.load_slot(s0 + 10)
        for mt in range(2):
            bk = rot.get()

            def mm(e, sl=sl, bk=bk, mt=mt):
                ins = None
                for kt in range(KT):
                    ins = e.matmul(ps[bk][:, :], lhsT=sl[:, kt * 256 + mt * 128:kt * 256 + mt * 128 + 128],
                                   rhs=hT[:, kt, :], start=(kt == 0), stop=(kt == KT - 1))
                return ins
            P.add("pe", mm, reads=[tsl, t_h], writes=[t_ps[bk]])
            P.add("act", lambda e, bk=bk, mt=mt: e.activation(out=bigf[:, 2 + mt, :], in_=ps[bk][:, :], func=AF.Copy),
                  reads=[t_ps[bk]], writes=[tf[2 + mt]])
            P.add("act", lambda e, bk=bk, mt=mt: e.activation(out=bigb[:, 20 + mt, :], in_=ps[bk][:, :], func=AF.Square),
                  reads=[t_ps[bk]], writes=[tb[20 + mt]])
        g.rstd_from_sq([bigb[:, 20 + mt, :] for mt in range(2)], tb[20:22], TB, 256.0, EPS, 7)
        for mt in range(2):
            P.add("dve", lambda e, mt=mt: e.scalar_tensor_tensor(out=ckvn[:, mt, blk], in0=bigf[:, 2 + mt, :],
                                                                 scalar=V(f"kvn{i}", mt), in1=rstd[:, :],
                                                                 op0=ALU.mult, op1=ALU.mult),
                  reads=[tf[2 + mt], t_rstd, t_vec], writes=[t_ckv[b]])
        sl, tsl = g.load_slot(s0 + 11, 8 * 128)
        bA, bB = rot.get(), rot.get()

        def mmk(e, sl=sl, bA=bA, bB=bB):
            ins = None
            for kt in range(KT):
                ins = e.matmul(ps[bA][0:64, :], lhsT=sl[:, kt * 128:kt * 128 + 64], rhs=hT[:, kt, :],
                               start=(kt == 0), stop=(kt == KT - 1))
            for kt in range(KT):
                ins = e.matmul(ps[bB][0:64, :], lhsT=sl[:, kt * 128 + 64:kt * 128 + 128], rhs=hT[:, kt, :],
                               start=(kt == 0), stop=(kt == KT - 1))
            return ins
        P.add("pe", mmk, reads=[tsl, t_h], writes=[t_ps[bA], t_ps[bB]])
        rope_apply(bA, bB, kpe[0:64, blk], [t_kpe[b]])
        if g.stop < 3:
            return
        gs, tgs = g.load_slot(soff[f"e_gate{i}"])
        P.add("pool", lambda e: e.tensor_copy(out=S.gatew[:, :], in_=gs[:, :]), reads=[tgs], writes=[S.t_gatew])
        for n in range(8):
            sl, tsl = g.load_slot(s0 + n)
            bX, bG = rot.get(), rot.get()

            def mm(e, sl=sl, bX=bX, bG=bG):
                ins = None
                for kt in range(KT):
                    ins = e.matmul(ps[bX][:, :], lhsT=sl[:, kt * 256:kt * 256 + 128], rhs=hT[:, kt, :],
                                   start=(kt == 0), stop=(kt == KT - 1))
                for kt in range(KT):
                    ins = e.matmul(ps[bG][:, :], lhsT=sl[:, kt * 256 + 128:kt * 256 + 256], rhs=hT[:, kt, :],
                                   start=(kt == 0), stop=(kt == KT - 1))
                return ins
            P.add("pe", mm, reads=[tsl, t_h], writes=[t_ps[bX], t_ps[bG]])
            P.add("dve", lambda e, n=n: e.tensor_copy(out=xrw[:, 0:3], in_=tail[:, n, :]), reads=[t_tail[n]], writes=[t_xrw])
            P.add("act", lambda e, bX=bX: e.activation(out=xrw[:, 3:TB + 3], in_=ps[bX][:, :], func=AF.Copy),
                  reads=[t_ps[bX]], writes=[t_xrw])
            xc = bigf[:, 0, :]
            P.add("dve", lambda e, n=n: e.tensor_scalar(out=xc, in0=xrw[:, 3:TB + 3], scalar1=V(f"cw{i}_3", n),
                                                        scalar2=V(f"cb{i}", n), op0=ALU.mult, op1=ALU.add),
                  reads=[t_xrw, t_vec], writes=[tf[0]])
            for j in (2, 1, 0):
                P.add("dve", lambda e, n=n, j=j: e.scalar_tensor_tensor(out=xc, in0=xrw[:, j:TB + j], scalar=V(f"cw{i}_{j}", n),
                                                                        in1=xc, op0=ALU.mult, op1=ALU.add),
                      reads=[t_xrw, t_vec, tf[0]], writes=[tf[0]])
            P.add("dve", lambda e, n=n: e.tensor_copy(out=tail[:, n, :], in_=xrw[:, TB:TB + 3]), reads=[t_xrw], writes=[t_tail[n]])
            P.add("act", lambda e: e.activation(out=xcb[:, :], in_=xc, func=AF.Copy), reads=[tf[0]], writes=[t_xcb])
            bR, bI = rot.get(), rot.get()

            def mmg(e, n=n, bR=bR, bI=bI):
                e.matmul(ps[bR][:, :], lhsT=S.gatew[:, n * 256:n * 256 + 128], rhs=xcb[:, :], start=True, stop=True)
                return e.matmul(ps[bI][:, :], lhsT=S.gatew[:, n * 256 + 128:n * 256 + 256], rhs=xcb[:, :], start=True, stop=True)
            P.add("pe", mmg, reads=[S.t_gatew, t_xcb], writes=[t_ps[bR], t_ps[bI]])
            P.add("act", lambda e, n=n, bR=bR: e.activation(out=bigf[:, 1, :], in_=ps[bR][:, :], func=AF.Sigmoid, bias=V(f"gab{i}", n)),
                  reads=[t_ps[bR], t_vec], writes=[tf[1]])
            P.add("act", lambda e, n=n, bI=bI: e.activation(out=bigf[:, 2, :], in_=ps[bI][:, :], func=AF.Sigmoid, bias=V(f"gxb{i}", n)),
                  reads=[t_ps[bI], t_vec], writes=[tf[2]])
            P.add("act", lambda e, n=n: e.activation(out=bigf[:, 3, :], in_=bigf[:, 1, :], func=AF.Exp, scale=nsp8[:, n:n + 1]),
                  reads=[tf[1], t_nsp], writes=[tf[3]])
            P.add("dve", lambda e: e.tensor_tensor(out=bigf[:, 4, :], in0=bigf[:, 3, :], in1=bigf[:, 3, :], op=ALU.mult),
                  reads=[tf[3]], writes=[tf[4]])
            P.add("act", lambda e: e.activation(out=bigf[:, 4, :], in_=bigf[:, 4, :], func=AF.Sqrt, scale=-1.0, bias=g.one_ap),
                  reads=[tf[4], g.t_cb2], writes=[tf[4]])
            P.add("dve", lambda e: e.tensor_tensor(out=bigf[:, 5, :], in0=bigf[:, 2, :], in1=xc, op=ALU.mult),
                  reads=[tf[2], tf[0]], writes=[tf[5]])
            P.add("dve", lambda e: e.tensor_tensor(out=bigf[:, 5, :], in0=bigf[:, 5, :], in1=bigf[:, 4, :], op=ALU.mult),
                  reads=[tf[5], tf[4]], writes=[tf[5]])
            P.add("dve", lambda e, n=n: e.tensor_tensor_scan(out=bigf[:, 6, :], data0=bigf[:, 3, :], data1=bigf[:, 5, :],
                                                             initial=hst[:, n:n + 1], op0=ALU.mult, op1=ALU.add),
                  reads=[tf[3], tf[5], t_hst[n]], writes=[tf[6]])
            P.add("dve", lambda e, n=n: e.tensor_copy(out=hst[:, n:n + 1], in_=bigf[:, 6, TB - 1:TB]), reads=[tf[6]], writes=[t_hst[n]])
            P.add("act", lambda e, bG=bG: e.activation(out=bigf[:, 7, :], in_=ps[bG][:, :], func=AF.Gelu_apprx_tanh),
                  reads=[t_ps[bG]], writes=[tf[7]])
            P.add("dve", lambda e, n=n: e.tensor_tensor(out=bigb[:, n, :], in0=bigf[:, 6, :], in1=bigf[:, 7, :], op=ALU.mult),
                  reads=[tf[6], tf[7]], writes=[tb[n]])
        if g.stop < 4:
            return
        nkc = b + 1
        srot = g.PsRot([4, 5])
        for h in range(8):
            if h % 2 == 0:
                uq, tuq = g.load_slot(soff[f"e_uq{i}"] + h // 2)
            if h % 4 == 0:
                ukv_s, tukv_s = g.load_slot(soff[f"e_ukv{i}"] + h // 4)
                P.add("pool", lambda e, ukv_s=ukv_s: e.tensor_copy(out=S.ukvw[:, :], in_=ukv_s[:, :]), reads=[tukv_s], writes=[S.t_ukvw])
            ukv, tukv = S.ukvw, S.t_ukvw
            qb = (h % 2) * 4 * 256
            kb_ = (h % 4) * 2 * 256
            bQ, bA, bB = rot.get(), rot.get(), rot.get()

            def mmq(e, uq=uq, qb=qb, bQ=bQ, bA=bA, bB=bB):
                ins = None
                for kt in range(4):
                    ins = e.matmul(ps[bQ][:, :], lhsT=uq[:, qb + kt * 256:qb + kt * 256 + 128], rhs=bigb[:, 16 + kt, :],
                                   start=(kt == 0), stop=(kt == 3))
                for kt in range(4):
                    ins = e.matmul(ps[bA][0:64, :], lhsT=uq[:, qb + kt * 256 + 128:qb + kt * 256 + 192], rhs=bigb[:, 16 + kt, :],
                                   start=(kt == 0), stop=(kt == 3))
                for kt in range(4):
                    ins = e.matmul(ps[bB][0:64, :], lhsT=uq[:, qb + kt * 256 + 192:qb + kt * 256 + 256], rhs=bigb[:, 16 + kt, :],
                                   start=(kt == 0), stop=(kt == 3))
                return ins
            P.add("pe", mmq, reads=[tuq] + tb[16:20], writes=[t_ps[bQ], t_ps[bA], t_ps[bB]])
            P.add("act", lambda e, bQ=bQ: e.activation(out=qn[:, :], in_=ps[bQ][:, :], func=AF.Copy), reads=[t_ps[bQ]], writes=[t_qn])
            rope_apply(bA, bB, qr[0:64, :], [t_qr])
            for c in range(nkc):
                bk = rot.get()

                def mmK(e, c=c, bk=bk, kb_=kb_):
                    ins = None
                    for kt in range(2):
                        ins = e.matmul(ps[bk][:, :], lhsT=ukv[:, kb_ + kt * 256:kb_ + kt * 256 + 128],
                                       rhs=ckvn[:, kt, c * TB:(c + 1) * TB], start=(kt == 0), stop=(kt == 1))
                    return ins
                P.add("pe", mmK, reads=[tukv, t_ckv[c]], writes=[t_ps[bk]])
                P.add("act", lambda e, c=c, bk=bk: e.activation(out=Kh[:, c * TB:(c + 1) * TB], in_=ps[bk][:, :], func=AF.Copy),
                      reads=[t_ps[bk]], writes=[t_Kh[c]])
                bv = rot.get()

                def mmV(e, c=c, bv=bv, kb_=kb_):
                    ins = None
                    for jj in range(4):
                        for kt in range(2):
                            ins = e.matmul(ps[bv][:, jj * 128:(jj + 1) * 128],
                                           lhsT=ckvn[:, kt, c * TB + jj * 128:c * TB + (jj + 1) * 128],
                                           rhs=ukv[:, kb_ + kt * 256 + 128:kb_ + kt * 256 + 256],
                                           start=(kt == 0), stop=(kt == 1))
                    return ins
                P.add("pe", mmV, reads=[tukv, t_ckv[c]], writes=[t_ps[bv]])
                P.add("dve", lambda e, c=c, bv=bv: e.tensor_copy(out=Vh[:, 4 * c:4 * c + 4, :], in_=ps[bv][:, :]),
                      reads=[t_ps[bv]], writes=[t_Vh[c]])
            nj = 4 * nkc
            for j in range(nj):
                jj = j - 4 * b
                c0 = max(0, jj) * 128
                sbk = srot.get()
                pt = j % 2

                def mmS(e, j=j, jj=jj, c0=c0, sbk=sbk):
                    e.matmul(ps[sbk][:, c0:TB], lhsT=Kh[:, j * 128:(j + 1) * 128], rhs=qn[:, c0:TB], start=True, stop=False)
                    ins = e.matmul(ps[sbk][:, c0:TB], lhsT=kpe[:, j * 128:(j + 1) * 128], rhs=qr[:, c0:TB],
                                   start=False, stop=(jj < 0))
                    if jj >= 0:
                        ins = e.matmul(ps[sbk][:, c0:c0 + 128], lhsT=g.identb[:, :], rhs=g.amaskb[:, :], start=False, stop=True)
                    return ins
                P.add("pe", mmS, reads=[t_Kh[j // 4], t_kpe[j // 4], t_qn, t_qr, t_cb], writes=[t_ps[sbk]])
                P.add("act", lambda e, c0=c0, sbk=sbk, pt=pt: e.activation(out=PT[pt][:, c0:TB], in_=ps[sbk][:, c0:TB],
                                                                           func=AF.Exp, scale=SCALE),
                      reads=[t_ps[sbk]], writes=[t_PT[pt]])

                def mmO(e, j=j, c0=c0, pt=pt, nj=nj):
                    e.matmul(ps[6][:, c0:TB], lhsT=Vh[:, j, :], rhs=PT[pt][:, c0:TB], start=(j == 0), stop=(j == nj - 1))
                    return e.matmul(ps[7][:, c0:TB], lhsT=g.onesb[:, :], rhs=PT[pt][:, c0:TB], start=(j == 0), stop=(j == nj - 1))
                P.add("pe", mmO, reads=[t_Vh[j // 4], t_PT[pt], t_cb], writes=[t_ps[6], t_ps[7]])
            P.add("dve", lambda e: e.reciprocal(out=bigf[:, 2, :], in_=ps[7][:, :]), reads=[t_ps[7]], writes=[tf[2]])
            P.add("dve", lambda e, h=h: e.tensor_tensor(out=bigb[:, 8 + h, :], in0=ps[6][:, :], in1=bigf[:, 2, :], op=ALU.mult),
                  reads=[t_ps[6], tf[2]], writes=[tb[8 + h]])
        if g.stop < 5:
            return
        for m in range(8):
            sl, tsl = g.load_slot(soff[f"e_wout{i}"] + m)
            bk = rot.get()

            def mmo(e, sl=sl, bk=bk):
                ins = None
                for kt in range(16):
                    ins = e.matmul(ps[bk][:, :], lhsT=sl[:, kt * 128:(kt + 1) * 128], rhs=bigb[:, kt, :],
                                   start=(kt == 0), stop=(kt == 15))
                return ins
            P.add("pe", mmo, reads=[tsl] + tb[0:16], writes=[t_ps[bk]])
            g.post_evac(bk, m, f"nmo{l}")
        g.post_finish(b, 7)

    S.gatew = sb("gatew", [128, SLOT_EL], BF16)
    S.t_gatew = Tok()
    S.ukvw = sb("ukvw", [128, SLOT_EL], BF16)
    S.t_ukvw = Tok()
    S.begin_layer = begin_layer
    S.block = block
    return S


def make_odd(g):
    P, ps, t_ps, V, C = g.P, g.ps, g.t_ps, g.V, g.C
    bigb, tb, bigf, tf, hT, t_h = g.bigb, g.tb, g.bigf, g.tf, g.hT, g.t_h
    rstd, t_rstd, t_vec, t_cb, t_cst = g.rstd, g.t_rstd, g.t_vec, g.t_cb, g.t_cst
    onesf = g.onesf
    soff = g.soff
    S = Ctx()
    sb = g.mk_asb()
    Tok = lambda: reg_tok(g)
    S_all = sb("S_all", [128, 16, 128], F32)
    t_S = [Tok() for _ in range(16)]
    gtail = sb("gtail", [128, 32, 3], F32)
    t_gt = [Tok() for _ in range(32)]
    xw = sb("gxw", [128, TB + 4], F32)
    t_xw = Tok()
    beta = sb("beta", [128, 4, 16], F32)
    nbeta = sb("nbeta", [128, 4, 16], F32)
    gg = sb("gg", [128, 4, 16], F32)
    eg = sb("eg", [128, 4, 16], F32)
    egr = sb("egr", [128, 4, 16], F32)
    dch = sb("dch", [128, 8, 16], F32)
    t_gate = Tok()
    nexpA = sb("nexpA", [128, 16], F32)
    t_nexp = Tok()
    KKs = sb("KKs", [128, 4, 128], F32)
    QKs = sb("QKs", [128, 4, 128], F32)
    Qt = sb("Qt", [128, 4, 128], F32)
    Ktt = sb("Ktt", [128, 4, 128], F32)
    t_pair = [Tok() for _ in range(4)]
    CH = []
    for p in range(4):
        c = Ctx()
        c.X = [sb(f"X{p}_{k}", [128, 128], F32) for k in range(2)]
        c.Y = [sb(f"Y{p}_{k}", [128, 128], F32) for k in range(2)]
        c.R = [sb(f"R{p}_{k}", [128, 128], F32) for k in range(2)]
        c.QKT = sb(f"QKT{p}", [128, 128], F32)
        c.kdec = sb(f"kdec{p}", [128, 128], F32)
        c.kdecB = sb(f"kdecB{p}", [128, 128], F32)
        c.Kbe = sb(f"Kbe{p}", [128, 128], F32)
        c.Vb = sb(f"Vb{p}", [128, 128], F32)
        c.tX = [Tok(), Tok()]
        c.tY = [Tok(), Tok()]
        c.tR = [Tok(), Tok()]
        c.tQKT, c.tkdec, c.tKbe, c.tVb = Tok(), Tok(), Tok(), Tok()
        CH.append(c)
    ident = C("ident")
    QSCALE = float(128 ** -0.5)

    def begin_layer(l):
        i = l // 2
        g.arena_barrier()
        P.add("act", lambda e: e.activation(out=nexpA[:, :], in_=V(f"alog{i}", 0, 16), func=AF.Exp),
              reads=[t_vec], writes=[t_nexp])
        P.add("dve", lambda e: e.tensor_scalar(out=nexpA[:, :], in0=nexpA[:, :], scalar1=-1.0, scalar2=None, op0=ALU.mult),
              reads=[t_nexp], writes=[t_nexp])
        P.add("dve", lambda e: e.memset(S_all[:, :, :], 0.0), writes=t_S)
        P.add("dve", lambda e: e.memset(gtail[:, :, :], 0.0), writes=t_gt)

    def conv_silu(i, bank, tile, out_ap, out_tok):
        P.add("dve", lambda e: e.tensor_copy(out=xw[:, 0:3], in_=gtail[:, tile, :]), reads=[t_gt[tile]], writes=[t_xw])
        P.add("act", lambda e: e.activation(out=xw[:, 3:TB + 3], in_=ps[bank][:, :], func=AF.Copy),
              reads=[t_ps[bank]], writes=[t_xw])
        P.add("dve", lambda e: e.tensor_scalar(out=out_ap, in0=xw[:, 3:TB + 3], scalar1=V(f"gcw{i}_3", tile), scalar2=None,
                                               op0=ALU.mult), reads=[t_xw, t_vec], writes=[out_tok])
        for j in (2, 1, 0):
            P.add("dve", lambda e, j=j: e.scalar_tensor_tensor(out=out_ap, in0=xw[:, j:TB + j], scalar=V(f"gcw{i}_{j}", tile),
                                                               in1=out_ap, op0=ALU.mult, op1=ALU.add),
                  reads=[t_xw, t_vec, out_tok], writes=[out_tok])
        P.add("dve", lambda e: e.tensor_copy(out=gtail[:, tile, :], in_=xw[:, TB:TB + 3]), reads=[t_xw], writes=[t_gt[tile]])
        P.add("act", lambda e: e.activation(out=out_ap, in_=out_ap, func=AF.Silu), reads=[out_tok], writes=[out_tok])

    def l2norm(x_ap, x_tok, scale):
        P.add("act", lambda e: e.activation(out=bigb[:, 17, :], in_=x_ap, func=AF.Square), reads=[x_tok], writes=[tb[17]])
        g.rstd_from_sq([bigb[:, 17, :]], [tb[17]], TB, 1.0, EPS, 7)
        P.add("dve", lambda e: e.scalar_tensor_tensor(out=x_ap, in0=x_ap, scalar=scale, in1=rstd[:, :], op0=ALU.mult, op1=ALU.mult),
              reads=[x_tok, t_rstd], writes=[x_tok])

    def proj2(sl, tsl, b0, b1):
        def mm(e):
            ins = None
            for kt in range(KT):
                ins = e.matmul(ps[b0][:, :], lhsT=sl[:, kt * 256:kt * 256 + 128], rhs=hT[:, kt, :], start=(kt == 0), stop=(kt == KT - 1))
            for kt in range(KT):
                ins = e.matmul(ps[b1][:, :], lhsT=sl[:, kt * 256 + 128:kt * 256 + 256], rhs=hT[:, kt, :], start=(kt == 0), stop=(kt == KT - 1))
            return ins
        P.add("pe", mm, reads=[tsl, t_h], writes=[t_ps[b0], t_ps[b1]])

    def block(l, b):
        i = l // 2
        rot = g.PsRot([0, 1, 2, 3, 4])
        g.prenorm(b, f"nmp{l}", 7)
        if g.stop < 1:
            return
        wba, twba = g.load_slot(soff[f"g_wba{i}"], 8 * 32)
        bk = rot.get()

        def mmba(e):
            ins = None
            for tt in range(4):
                for kt in range(KT):
                    ins = e.matmul(ps[bk][:, tt * 32:(tt + 1) * 32], lhsT=hT[:, kt, tt * 128:(tt + 1) * 128],
                                   rhs=wba[:, kt * 32:(kt + 1) * 32], start=(kt == 0), stop=(kt == KT - 1))
            return ins
        P.add("pe", mmba, reads=[twba, t_h], writes=[t_ps[bk]])
        for tt in range(4):
            P.add("act", lambda e, tt=tt: e.activation(out=beta[:, tt, :], in_=ps[bk][:, tt * 32:tt * 32 + 16], func=AF.Sigmoid),
                  reads=[t_ps[bk]], writes=[t_gate])
            P.add("dve", lambda e, tt=tt: e.tensor_tensor(out=gg[:, tt, :], in0=ps[bk][:, tt * 32 + 16:tt * 32 + 32],
                                                          in1=V(f"dtb{i}", 0, 16), op=ALU.add),
                  reads=[t_ps[bk], t_vec], writes=[t_gate])
        P.add("act", lambda e: e.activation(out=gg[:, :, :], in_=gg[:, :, :], func=AF.Exp), reads=[t_gate], writes=[t_gate])
        P.add("act", lambda e: e.activation(out=gg[:, :, :], in_=gg[:, :, :], func=AF.Ln, bias=g.one_ap),
              reads=[t_gate, g.t_cb2], writes=[t_gate])
        for tt in range(4):
            P.add("dve", lambda e, tt=tt: e.tensor_tensor(out=gg[:, tt, :], in0=gg[:, tt, :], in1=nexpA[:, :], op=ALU.mult),
                  reads=[t_gate, t_nexp], writes=[t_gate])
        P.add("dve", lambda e: e.tensor_scalar(out=nbeta[:, :, :], in0=beta[:, :, :], scalar1=-1.0, scalar2=None, op0=ALU.mult),
              reads=[t_gate], writes=[t_gate])
        for p in range(4):
            bc = rot.get()

            def mmc(e, p=p, bc=bc):
                e.matmul(ps[bc][:, 0:16], lhsT=C("U2"), rhs=gg[:, p, :], start=True, stop=True)
                e.matmul(ps[bc][:, 16:32], lhsT=C("SL2"), rhs=gg[:, p, :], start=True, stop=True)
                e.matmul(ps[bc][:, 32:48], lhsT=C("ONA"), rhs=gg[:, p, :], start=True, stop=True)
                return e.matmul(ps[bc][:, 48:64], lhsT=C("ONB"), rhs=gg[:, p, :], start=True, stop=True)
            P.add("pe", mmc, reads=[t_gate, t_cst, t_cb], writes=[t_ps[bc]])

            def ex(e, p=p, bc=bc):
                e.activation(out=eg[:, p, :], in_=ps[bc][:, 0:16], func=AF.Exp)
                e.activation(out=egr[:, p, :], in_=ps[bc][:, 16:32], func=AF.Exp)
                e.activation(out=dch[:, 2 * p, :], in_=ps[bc][:, 32:48], func=AF.Exp)
                return e.activation(out=dch[:, 2 * p + 1, :], in_=ps[bc][:, 48:64], func=AF.Exp)
            P.add("act", ex, reads=[t_ps[bc]], writes=[t_gate])
        if g.stop < 2:
            return
        qf, kf = bigf[:, 0, :], bigf[:, 1, :]
        vfs = [bigf[:, 2, :], bigf[:, 3, :]]
        zfs = [bigf[:, 4, :], bigf[:, 5, :]]
        for j in range(g.dbg[0] + 1 if g.dbg else 8):
            s0 = soff[f"g_win{i}"] + 3 * j
            sl, tsl = g.load_slot(s0)
            b0, b1 = rot.get(), rot.get()
            proj2(sl, tsl, b0, b1)
            conv_silu(i, b0, j, qf, tf[0])
            conv_silu(i, b1, 8 + j, kf, tf[1])
            l2norm(qf, tf[0], QSCALE)
            l2norm(kf, tf[1], 1.0)
            sl, tsl = g.load_slot(s0 + 1)
            b0, b1 = rot.get(), rot.get()
            proj2(sl, tsl, b0, b1)
            conv_silu(i, b0, 16 + 2 * j, vfs[0], tf[2])
            conv_silu(i, b1, 16 + 2 * j + 1, vfs[1], tf[3])
            sl, tsl = g.load_slot(s0 + 2)
            b0, b1 = rot.get(), rot.get()
            proj2(sl, tsl, b0, b1)
            P.add("act", lambda e, b0=b0: e.activation(out=zfs[0], in_=ps[b0][:, :], func=AF.Silu), reads=[t_ps[b0]], writes=[tf[4]])
            P.add("act", lambda e, b1=b1: e.activation(out=zfs[1], in_=ps[b1][:, :], func=AF.Silu), reads=[t_ps[b1]], writes=[tf[5]])
            for p in range(4):
                cols = slice(p * 128, (p + 1) * 128)
                bkk = rot.get()

                def mmr(e, cols=cols, bkk=bkk):
                    e.matmul(ps[bkk][:, 0:128], lhsT=kf[:, cols], rhs=kf[:, cols], start=True, stop=True)
                    e.matmul(ps[bkk][:, 128:256], lhsT=kf[:, cols], rhs=qf[:, cols], start=True, stop=True)
                    e.transpose(ps[bkk][:, 256:384], qf[:, cols], ident)
                    return e.transpose(ps[bkk][:, 384:512], kf[:, cols], ident)
                P.add("pe", mmr, reads=[tf[0], tf[1], t_cst], writes=[t_ps[bkk]])

                def ev1(e, p=p, bkk=bkk):
                    e.activation(out=KKs[:, p, :], in_=ps[bkk][:, 0:128], func=AF.Copy)
                    return e.activation(out=Qt[:, p, :], in_=ps[bkk][:, 256:384], func=AF.Copy)
                P.add("act", ev1, reads=[t_ps[bkk]], writes=[t_pair[p]])

                def ev2(e, p=p, bkk=bkk):
                    e.tensor_copy(out=QKs[:, p, :], in_=ps[bkk][:, 128:256])
                    return e.tensor_copy(out=Ktt[:, p, :], in_=ps[bkk][:, 384:512])
                P.add("dve", ev2, reads=[t_ps[bkk]], writes=[t_pair[p]])
            for half in range(2 if g.stop >= 3 else 0):
                combos = [(hh, 2 * half + q, CH[hh * 2 + q]) for hh in range(2) for q in range(2)]
                bvts = {}
                for hh in range(2):
                    vf, tvf = vfs[hh], tf[2 + hh]
                    bvt = rot.get()
                    bvts[hh] = bvt

                    def mmvt(e, vf=vf, bvt=bvt, half=half):
                        ins = None
                        for q in range(2):
                            p = 2 * half + q
                            ins = e.transpose(ps[bvt][:, q * 128:(q + 1) * 128], vf[:, p * 128:(p + 1) * 128], ident)
                        return ins
                    P.add("pe", mmvt, reads=[tvf, t_cst], writes=[t_ps[bvt]])
                for hh, p, c in combos:
                    h = 2 * j + hh
                    q = p - 2 * half
                    bvt = bvts[hh]
                    P.add("dve", lambda e, c=c, p=p, h=h: e.tensor_scalar(out=c.X[1][:, :], in0=C("U2"), scalar1=gg[:, p, h:h + 1],
                                                                          scalar2=None, op0=ALU.mult),
                          reads=[t_cst, t_gate], writes=[c.tX[1]])
                    P.add("dve", lambda e, c=c, p=p, h=h: e.tensor_scalar(out=c.Y[1][:, :], in0=C("SL2"), scalar1=gg[:, p, h:h + 1],
                                                                          scalar2=None, op0=ALU.mult),
                          reads=[t_cst, t_gate], writes=[c.tY[1]])
                    P.add("dve", lambda e, c=c, p=p, h=h: e.tensor_scalar(out=c.Kbe[:, :], in0=Ktt[:, p, :], scalar1=beta[:, p, h:h + 1],
                                                                          scalar2=eg[:, p, h:h + 1], op0=ALU.mult, op1=ALU.mult),
                          reads=[t_pair[p], t_gate], writes=[c.tKbe])
                    P.add("dve", lambda e, c=c, p=p, h=h: e.tensor_scalar(out=c.kdec[:, :], in0=Ktt[:, p, :], scalar1=egr[:, p, h:h + 1],
                                                                          scalar2=C("ONA", 128, 1), op0=ALU.mult, op1=ALU.mult),
                          reads=[t_pair[p], t_gate, t_cst], writes=[c.tkdec])
                    P.add("dve", lambda e, c=c, p=p, h=h: e.tensor_scalar(out=c.kdecB[:, :], in0=Ktt[:, p, :], scalar1=egr[:, p, h:h + 1],
                                                                          scalar2=C("ONB", 128, 1), op0=ALU.mult, op1=ALU.mult),
                          reads=[t_pair[p], t_gate, t_cst], writes=[c.tkdec])
                    P.add("dve", lambda e, c=c, p=p, h=h, bvt=bvt, q=q: e.tensor_scalar(out=c.Vb[:, :], in0=ps[bvt][:, q * 128:(q + 1) * 128],
                                                                                        scalar1=beta[:, p, h:h + 1], scalar2=None, op0=ALU.mult),
                          reads=[t_ps[bvt], t_gate], writes=[c.tVb])
                for hh, p, c in combos:
                    bD = rot.get()

                    def mmD(e, c=c, bD=bD):
                        e.matmul(ps[bD][:, 0:128], lhsT=c.X[1][:, :], rhs=C("SL2"), start=True, stop=False)
                        e.matmul(ps[bD][:, 0:128], lhsT=ident, rhs=C("NMS"), start=False, stop=True)
                        e.matmul(ps[bD][:, 128:256], lhsT=c.Y[1][:, :], rhs=C("U2"), start=True, stop=False)
                        return e.matmul(ps[bD][:, 128:256], lhsT=ident, rhs=C("NMC"), start=False, stop=True)
                    P.add("pe", mmD, reads=[c.tX[1], c.tY[1], t_cst], writes=[t_ps[bD]])
                    P.add("act", lambda e, c=c, bD=bD: e.activation(out=c.R[1][:, :], in_=ps[bD][:, 0:128], func=AF.Exp),
                          reads=[t_ps[bD]], writes=[c.tR[1]])
                    P.add("act", lambda e, c=c, bD=bD: e.activation(out=c.QKT[:, :], in_=ps[bD][:, 128:256], func=AF.Exp),
                          reads=[t_ps[bD]], writes=[c.tQKT])
                for hh, p, c in combos:
                    h = 2 * j + hh
                    P.add("dve", lambda e, c=c, p=p, h=h: e.scalar_tensor_tensor(out=c.X[0][:, :], in0=KKs[:, p, :], scalar=nbeta[:, p, h:h + 1],
                                                                                 in1=c.R[1][:, :], op0=ALU.mult, op1=ALU.mult),
                          reads=[t_pair[p], t_gate, c.tR[1]], writes=[c.tX[0]])
                    P.add("dve", lambda e, c=c, p=p: e.tensor_tensor(out=c.QKT[:, :], in0=QKs[:, p, :], in1=c.QKT[:, :], op=ALU.mult),
                          reads=[t_pair[p], c.tQKT], writes=[c.tQKT])
                bTs = []
                for hh, p, c in combos:
                    bT = rot.get()
                    bTs.append(bT)
                    P.add("pe", lambda e, c=c, bT=bT: e.transpose(ps[bT][:, 0:128], c.X[0][:, :], ident),
                          reads=[c.tX[0], t_cst], writes=[t_ps[bT]])
                for (hh, p, c), bT in zip(combos, bTs):
                    P.add("act", lambda e, c=c, bT=bT: e.activation(out=c.Y[0][:, :], in_=ps[bT][:, 0:128], func=AF.Copy),
                          reads=[t_ps[bT]], writes=[c.tY[0]])
                    P.add("dve", lambda e, c=c: e.tensor_tensor(out=c.R[0][:, :], in0=c.Y[0][:, :], in1=ident, op=ALU.add),
                          reads=[c.tY[0], t_cst], writes=[c.tR[0]])
                a, ra = 0, 0
                for k in range(1, 6):
                    b1s = []
                    for hh, p, c in combos:
                        b1_ = rot.get()
                        b1s.append(b1_)

                        def mmsq(e, c=c, b1_=b1_, a=a, k=k):
                            ins = e.matmul(ps[b1_][:, 0:128], lhsT=c.Y[a][:, :], rhs=c.X[a][:, :], start=True, stop=True)
                            if k < 5:
                                ins = e.matmul(ps[b1_][:, 128:256], lhsT=c.X[a][:, :], rhs=c.Y[a][:, :], start=True, stop=True)
                            return ins
                        P.add("pe", mmsq, reads=[c.tX[a], c.tY[a]], writes=[t_ps[b1_]])
                    for (hh, p, c), b1_ in zip(combos, b1s):
                        P.add("act", lambda e, c=c, b1_=b1_, a=a: e.activation(out=c.X[1 - a][:, :], in_=ps[b1_][:, 0:128], func=AF.Copy),
                              reads=[t_ps[b1_]], writes=[c.tX[1 - a]])
                        if k < 5:
                            P.add("act", lambda e, c=c, b1_=b1_, a=a: e.activation(out=c.Y[1 - a][:, :], in_=ps[b1_][:, 128:256], func=AF.Copy),
                                  reads=[t_ps[b1_]], writes=[c.tY[1 - a]])
                    b2s = []
                    for hh, p, c in combos:
                        b2_ = rot.get()
                        b2s.append(b2_)

                        def mmR(e, c=c, b2_=b2_, a=a, ra=ra):
                            e.matmul(ps[b2_][:, 0:128], lhsT=c.X[1 - a][:, :], rhs=c.R[ra][:, :], start=True, stop=False)
                            return e.matmul(ps[b2_][:, 0:128], lhsT=ident, rhs=c.R[ra][:, :], start=False, stop=True)
                        P.add("pe", mmR, reads=[c.tX[1 - a], c.tR[ra], t_cst], writes=[t_ps[b2_]])
                    for (hh, p, c), b2_ in zip(combos, b2s):
                        P.add("act", lambda e, c=c, b2_=b2_, ra=ra: e.activation(out=c.R[1 - ra][:, :], in_=ps[b2_][:, 0:128], func=AF.Copy),
                              reads=[t_ps[b2_]], writes=[c.tR[1 - ra]])
                    a, ra = 1 - a, 1 - ra
                bws = []
                for hh, p, c in combos:
                    h = 2 * j + hh
                    bw = rot.get()
                    bws.append(bw)

                    def mmw(e, c=c, bw=bw, ra=ra):
                        e.matmul(ps[bw][:, 0:128], lhsT=c.Kbe[:, :], rhs=c.R[ra][:, :], start=True, stop=True)
                        return e.matmul(ps[bw][:, 128:256], lhsT=c.R[ra][:, :], rhs=c.Vb[:, :], start=True, stop=True)
                    P.add("pe", mmw, reads=[c.tKbe, c.tVb, c.tR[ra]], writes=[t_ps[bw]])
                    P.add("dve", lambda e, c=c, p=p, h=h: e.tensor_scalar(out=c.Y[1][:, :], in0=ident, scalar1=eg[:, p, h:h + 1], scalar2=None,
                                                                          op0=ALU.mult),
                          reads=[t_cst, t_gate, c.tY[1]], writes=[c.tY[1]])
                for (hh, p, c), bw in zip(combos, bws):
                    P.add("act", lambda e, c=c, bw=bw: e.activation(out=c.X[0][:, :], in_=ps[bw][:, 0:128], func=AF.Identity, scale=-1.0),
                          reads=[t_ps[bw]], writes=[c.tX[0]])
                    P.add("act", lambda e, c=c, bw=bw: e.activation(out=c.X[1][:, :], in_=ps[bw][:, 128:256], func=AF.Copy),
                          reads=[t_ps[bw]], writes=[c.tX[1]])
                bqs = []
                for hh, p, c in combos:
                    bq = rot.get()
                    bqs.append(bq)
                    P.add("pe", lambda e, c=c, bq=bq, p=p: e.matmul(ps[bq][:, 0:128], lhsT=Qt[:, p, :], rhs=c.Y[1][:, :], start=True, stop=True),
                          reads=[t_pair[p], c.tY[1]], writes=[t_ps[bq]])
                for (hh, p, c), bq in zip(combos, bqs):
                    P.add("act", lambda e, c=c, bq=bq: e.activation(out=c.Y[0][:, :], in_=ps[bq][:, 0:128], func=AF.Copy),
                          reads=[t_ps[bq]], writes=[c.tY[0]])
                for lc in range(4):
                    cc = 4 * half + lc
                    p, s_ = cc // 2, cc % 2
                    for hh in range(2):
                        h = 2 * j + hh
                        c = CH[hh * 2 + (p - 2 * half)]
                        Sh = S_all[:, h, :]
                        ob = 6 if hh == 0 else 5
                        r0 = 64 * s_
                        rows = slice(r0, r0 + 64)
                        kd = c.kdec if s_ == 0 else c.kdecB
                        bw_ = rot.get()
                        dI = c.Kbe if s_ == 0 else c.Y[1]
                        tdI = c.tKbe if s_ == 0 else c.tY[1]
                        P.add("dve", lambda e, dI=dI, cc=cc, h=h: e.tensor_scalar(out=dI[:, :], in0=ident, scalar1=dch[:, cc, h:h + 1], scalar2=None,
                                                                               op0=ALU.mult),
                              reads=[t_cst, t_gate, tdI], writes=[tdI])

                        def mmv(e, c=c, bw_=bw_, Sh=Sh):
                            e.matmul(ps[bw_][:, 0:128], lhsT=ident, rhs=c.X[1][:, :], start=True, stop=False)
                            return e.matmul(ps[bw_][:, 0:128], lhsT=c.X[0][:, :], rhs=Sh, start=False, stop=True)
                        P.add("pe", mmv, reads=[c.tX[0], c.tX[1], t_S[h], t_cst], writes=[t_ps[bw_]])
                        P.add("act", lambda e, c=c, bw_=bw_, rows=rows: e.activation(out=c.X[1][rows, :], in_=ps[bw_][rows, 0:128], func=AF.Copy),
                              reads=[t_ps[bw_]], writes=[c.tX[1]])
                        bsu = rot.get()

                        def mmo(e, c=c, cc=cc, rows=rows, bsu=bsu, Sh=Sh, kd=kd, dI=dI, ob=ob):
                            e.matmul(ps[ob][:, cc * 64:(cc + 1) * 64], lhsT=Sh, rhs=c.Y[0][:, rows], start=True, stop=False)
                            e.matmul(ps[ob][:, cc * 64:(cc + 1) * 64], lhsT=c.X[1][:, :], rhs=c.QKT[:, rows], start=False, stop=True)
                            e.matmul(ps[bsu][:, 0:128], lhsT=kd[:, :], rhs=c.X[1][:, :], start=True, stop=False)
                            return e.matmul(ps[bsu][:, 0:128], lhsT=dI[:, :], rhs=Sh, start=False, stop=True)
                        P.add("pe", mmo, reads=[t_S[h], c.tY[0], c.tX[1], c.tQKT, c.tkdec, tdI], writes=[t_ps[ob], t_ps[bsu]])
                        P.add("act", lambda e, bsu=bsu, Sh=Sh: e.activation(out=Sh, in_=ps[bsu][:, 0:128], func=AF.Copy),
                              reads=[t_ps[bsu]], writes=[t_S[h]])
            for hh in range(2 if g.stop >= 3 else 0):
                h = 2 * j + hh
                ob = 6 if hh == 0 else 5
                P.add("act", lambda e, ob=ob: e.activation(out=bigb[:, 16, :], in_=ps[ob][:, :], func=AF.Square), reads=[t_ps[ob]], writes=[tb[16]])
                g.rstd_from_sq([bigb[:, 16, :]], [tb[16]], TB, 128.0, EPS, 7)
                P.add("act", lambda e, ob=ob: e.activation(out=bigf[:, 7, :], in_=ps[ob][:, :], func=AF.Identity, scale=V(f"gn{i}")),
                      reads=[t_ps[ob], t_vec], writes=[tf[7]])
                P.add("dve", lambda e: e.tensor_tensor(out=bigf[:, 6, :], in0=bigf[:, 7, :], in1=rstd[:, :], op=ALU.mult),
                      reads=[tf[7], t_rstd], writes=[tf[6]])
                P.add("dve", lambda e, h=h, hh=hh: e.tensor_tensor(out=bigb[:, h, :], in0=bigf[:, 6, :], in1=zfs[hh], op=ALU.mult),
                      reads=[tf[6], tf[4 + hh]], writes=[tb[h]])
        if g.dbg or g.stop < 6:
            return
        for m in range(8):
            sl, tsl = g.load_slot(soff[f"g_wout{i}"] + m)
            bk = rot.get()

            def mmo2(e, sl=sl, bk=bk):
                ins = None
                for kt in range(16):
                    ins = e.matmul(ps[bk][:, :], lhsT=sl[:, kt * 128:(kt + 1) * 128], rhs=bigb[:, kt, :],
                                   start=(kt == 0), stop=(kt == 15))
                return ins
            P.add("pe", mmo2, reads=[tsl] + tb[0:16], writes=[t_ps[bk]])
            g.post_evac(bk, m, f"nmo{l}")
        g.post_finish(b, 7)

    S.begin_layer = begin_layer
    S.block = block
    return S


_CACHE = {}


def kernel(**inputs):
    inp = {k: np.asarray(v) for k, v in inputs.items()}
    W = pack_weights(inp)
    Vv = pack_vecs(inp)
    Cc = make_consts()
    if "nc" not in _CACHE:
        _CACHE["nc"] = build_program()[0]
    nc = _CACHE["nc"]
    in_maps = []
    for b in range(8):
        x = np.asarray(inp["x"][b], np.float32)
        xT = np.ascontiguousarray(x.T.reshape(8, 128, T).transpose(1, 0, 2))
        pos = np.ascontiguousarray(np.broadcast_to(np.asarray(inp["positions"][b], np.int32)[None, :], (64, T)))
        in_maps.append({"xT": xT, "pos": pos, "vecs": Vv, "cst": Cc, "wts": W})
    res = run_bass_kernel_spmd(nc, in_maps, core_ids=list(range(8)))
    out = np.stack([np.asarray(r["yT"]).transpose(1, 0, 2).reshape(D, T).T for r in res.results])
    return np.ascontiguousarray(out.astype(np.float32))
```

```python
import contextlib
import numpy as np
import concourse.bass as bass
import concourse.mybir as mybir
from concourse.bass_utils import run_bass_kernel_spmd

F32 = mybir.dt.float32
BF16 = mybir.dt.bfloat16
I32 = mybir.dt.int32
AF = mybir.ActivationFunctionType
ALU = mybir.AluOpType
AX = mybir.AxisListType

ENGS = ("pe", "dve", "act", "pool", "sp")
EPOCH = 24000


class Tok:
    __slots__ = ("lw", "rd", "name")

    def __init__(self, name=""):
        self.lw = None
        self.rd = {}
        self.name = name


class Op:
    __slots__ = ("eng", "fn", "waits", "inc", "known", "dma")

    def __init__(self, eng, fn):
        self.eng = eng
        self.fn = fn
        self.waits = []
        self.inc = False
        self.known = None
        self.dma = None


class Prog:
    def __init__(self, nc, same_engine_sync=True):
        self.nc = nc
        self.ops = {e: [] for e in ENGS}
        self.known = {e: {} for e in ENGS}
        self.dma_cnt = {}
        self.dma_ops = {}
        self.same_engine_sync = same_engine_sync
        self.nwaits = 0

    def _lookup(self, s, p):
        if isinstance(s, tuple):
            return self.dma_ops[s][p - 1]
        return self.ops[s][p]

    def add(self, eng, fn, reads=(), writes=(), dma_key=None):
        ops = self.ops[eng]
        idx = len(ops)
        op = Op(eng, fn)
        deps = {}
        for t in reads:
            if t.lw is not None:
                s, p = t.lw
                if deps.get(s, -1) < p:
                    deps[s] = p
        for t in writes:
            if t.lw is not None:
                s, p = t.lw
                if deps.get(s, -1) < p:
                    deps[s] = p
            for s, p in t.rd.items():
                if deps.get(s, -1) < p:
                    deps[s] = p
        known = self.known[eng]
        for s, p in deps.items():
            if s == eng and dma_key is None and (eng == "pe" or not self.same_engine_sync):
                continue
            if known.get(s, -1) >= p:
                continue
            op.waits.append((s, p))
            known[s] = p
            src = self._lookup(s, p)
            for s2, p2 in src.known.items():
                if known.get(s2, -1) < p2:
                    known[s2] = p2
            src.inc = True
        self.nwaits += len(op.waits)
        if dma_key is None:
            mypos = (eng, idx)
        else:
            k = ("dma", dma_key)
            n = self.dma_cnt.get(k, 0) + 1
            self.dma_cnt[k] = n
            self.dma_ops.setdefault(k, []).append(op)
            mypos = (k, n)
            op.dma = k
        op.known = dict(known)
        for t in reads:
            if t.rd.get(mypos[0], -1) < mypos[1]:
                t.rd[mypos[0]] = mypos[1]
        for t in writes:
            t.lw = mypos
            t.rd = {}
        ops.append(op)
        return op

    def emit(self, stack):
        nc = self.nc
        pref = {}
        nsem = {}
        for e in ENGS:
            c = 0
            arr = []
            for op in self.ops[e]:
                if op.inc and op.dma is None:
                    c += 1
                arr.append(c)
            pref[e] = arr
            nsem[e] = (c + EPOCH - 1) // EPOCH
        sems = {e: [stack.enter_context(nc.semaphore(f"s_{e}_{i}")) for i in range(nsem[e])]
                for e in ENGS}
        dsems = {k: stack.enter_context(nc.semaphore("d_%d" % i))
                 for i, k in enumerate(self.dma_cnt)}
        self.n_sems = sum(nsem.values()) + len(dsems)
        block = stack.enter_context(nc.Block())
        hw = {"pe": block.tensor, "dve": block.vector, "act": block.scalar,
              "pool": block.gpsimd, "sp": block.sync}

        def run(e):
            def body(engine):
                for op in self.ops[e]:
                    for s, p in op.waits:
                        if isinstance(s, tuple):
                            engine.wait_ge(dsems[s], 16 * p)
                        else:
                            c = pref[s][p]
                            engine.wait_ge(sems[s][(c - 1) // EPOCH], (c - 1) % EPOCH + 1)
                    if op.fn is None:
                        continue
                    ins = op.fn(engine)
                    if op.dma is not None:
                        ins.then_inc(dsems[op.dma], 16)
                    elif op.inc:
                        c = pref[e][self.ops[e].index(op)] if False else None
                        ins.then_inc(sems[e][(op_c[id(op)] - 1) // EPOCH], 1)
            return body

        op_c = {}
        for e in ENGS:
            for i, op in enumerate(self.ops[e]):
                if op.inc and op.dma is None:
                    op_c[id(op)] = pref[e][i]
        for e in ENGS:
            hw[e](run(e))


T = 2048
D = 1024
KT = 8
TB = 512
NB = T // TB
DEPTH = 4
FF = 2816
FT = 22
SLOT_EL = 2048
EPS = 1e-6
NEG = -30000.0
TWO_PI = 6.283185307179586


def vec_layout():
    off = {}
    c = 0

    def put(name, n):
        nonlocal c
        off[name] = c
        c += n
    for l in range(4):
        for nm in ("nmp", "nmo", "nfp", "nfo"):
            put(f"{nm}{l}", 8)
    for i in range(2):
        for j in range(4):
            put(f"cw{i}_{j}", 8)
        for nm in ("cb", "gab", "gxb", "lam"):
            put(f"{nm}{i}", 8)
        put(f"qn{i}", 4)
        put(f"kvn{i}", 2)
    for i in range(2):
        for j in range(4):
            put(f"gcw{i}_{j}", 32)
        put(f"gn{i}", 1)
        put(f"alog{i}", 16)
        put(f"dtb{i}", 16)
    put("invf", 1)
    put("sgn", 1)
    return off, c


def cst_layout():
    off = {}
    c = 0
    for nm in ("ident", "amask", "U2", "SL2", "NMS", "NMC", "BD", "ONA", "ONB"):
        off[nm] = c
        c += 128
    return off, c


def slot_layout():
    off = {}
    c = 0

    def put(name, n):
        nonlocal c
        off[name] = c
        c += n
    for i in range(2):
        put(f"e_win{i}", 12)
        put(f"e_gate{i}", 1)
        put(f"e_uq{i}", 4)
        put(f"e_ukv{i}", 2)
        put(f"e_wout{i}", 8)
    for i in range(2):
        put(f"g_win{i}", 24)
        put(f"g_wba{i}", 1)
        put(f"g_wout{i}", 8)
    for l in range(4):
        put(f"f_gu{l}", 22)
        put(f"f_down{l}", 16)
    return off, c


def _tile_w(W, k0, nk, cols):
    sub = W[k0:k0 + nk * 128][:, cols]
    return sub.reshape(nk, 128, -1).transpose(1, 0, 2).reshape(128, -1)


def _pad(a):
    out = np.zeros((128, SLOT_EL), np.float32)
    out[:, :a.shape[1]] = a
    return out


def pack_weights(inp):
    soff, ns = slot_layout()
    W = np.zeros((ns, 128, SLOT_EL), np.float32)
    ar = np.arange
    for i in range(2):
        win = inp["hy_w_in"][i]
        s0 = soff[f"e_win{i}"]
        for n in range(8):
            cols = np.concatenate([ar(n * 128, n * 128 + 128), ar(1024 + n * 128, 1024 + n * 128 + 128)])
            W[s0 + n] = _pad(_tile_w(win, 0, 8, cols))
        W[s0 + 8] = _pad(_tile_w(win, 0, 8, ar(2048, 2304)))
        W[s0 + 9] = _pad(_tile_w(win, 0, 8, ar(2304, 2560)))
        W[s0 + 10] = _pad(_tile_w(win, 0, 8, ar(2560, 2816)))
        cols = np.concatenate([ar(2816, 2880), ar(2848, 2880), ar(2816, 2848)])
        W[s0 + 11] = _pad(_tile_w(win, 0, 8, cols))
        ga, gx = inp["rg_gate_a_w"][i], inp["rg_gate_x_w"][i]
        g = np.concatenate([ga, gx], axis=2)
        W[soff[f"e_gate{i}"]] = _pad(g.transpose(1, 0, 2).reshape(128, -1))
        uq = inp["mla_w_uq"][i]
        for s in range(4):
            parts = []
            for hh in range(2):
                h = 2 * s + hh
                b = h * 192
                cols = np.concatenate([ar(b, b + 192), ar(b + 160, b + 192), ar(b + 128, b + 160)])
                parts.append(_tile_w(uq, 0, 4, cols))
            W[soff[f"e_uq{i}"] + s] = _pad(np.concatenate(parts, axis=1))
        ukv = inp["mla_w_ukv"][i]
        for s in range(2):
            parts = [_tile_w(ukv, 0, 2, ar((4 * s + hh) * 256, (4 * s + hh) * 256 + 256)) for hh in range(4)]
            W[soff[f"e_ukv{i}"] + s] = _pad(np.concatenate(parts, axis=1))
        wo = inp["hy_w_out"][i]
        for m in range(8):
            W[soff[f"e_wout{i}"] + m] = _pad(_tile_w(wo, 0, 16, ar(m * 128, m * 128 + 128)))
    for i in range(2):
        win = inp["gdn_w_in"][i]
        s0 = soff[f"g_win{i}"]
        for j in range(8):
            q = ar(j * 128, j * 128 + 128)
            k = ar(1024 + j * 128, 1024 + j * 128 + 128)
            v = ar(2048 + 2 * j * 128, 2048 + 2 * j * 128 + 256)
            z = ar(4096 + 2 * j * 128, 4096 + 2 * j * 128 + 256)
            W[s0 + 3 * j] = _pad(_tile_w(win, 0, 8, np.concatenate([q, k])))
            W[s0 + 3 * j + 1] = _pad(_tile_w(win, 0, 8, v))
            W[s0 + 3 * j + 2] = _pad(_tile_w(win, 0, 8, z))
        W[soff[f"g_wba{i}"]] = _pad(_tile_w(win, 0, 8, ar(6144, 6176)))
        wo = inp["gdn_w_out"][i]
        for m in range(8):
            W[soff[f"g_wout{i}"] + m] = _pad(_tile_w(wo, 0, 16, ar(m * 128, m * 128 + 128)))
    for l in range(4):
        wg, wu, wd = inp["ffn_w_gate"][l], inp["ffn_w_up"][l], inp["ffn_w_down"][l]
        for m in range(FT):
            c = ar(m * 128, m * 128 + 128)
            W[soff[f"f_gu{l}"] + m] = _pad(np.concatenate([_tile_w(wg, 0, 8, c), _tile_w(wu, 0, 8, c)], axis=1)
                                           .reshape(128, 2, 8, 128).transpose(0, 2, 1, 3).reshape(128, -1))
        for m in range(8):
            for hf in range(2):
                W[soff[f"f_down{l}"] + 2 * m + hf] = _pad(_tile_w(wd, hf * 11 * 128, 11, ar(m * 128, m * 128 + 128)))
    return W


def pack_vecs(inp):
    voff, nv = vec_layout()
    V = np.zeros((128, nv), np.float32)

    def v8(name, v):
        a = np.asarray(v, np.float32).reshape(-1, 128).T
        V[:, voff[name]:voff[name] + a.shape[1]] = a
    for l in range(4):
        v8(f"nmp{l}", inp["norm_mix_pre"][l])
        v8(f"nmo{l}", inp["norm_mix_post"][l])
        v8(f"nfp{l}", inp["norm_ffn_pre"][l])
        v8(f"nfo{l}", inp["norm_ffn_post"][l])
    for i in range(2):
        for j in range(4):
            v8(f"cw{i}_{j}", inp["rg_conv_w"][i][j])
        v8(f"cb{i}", inp["rg_conv_b"][i])
        v8(f"gab{i}", inp["rg_gate_a_b"][i])
        v8(f"gxb{i}", inp["rg_gate_x_b"][i])
        v8(f"lam{i}", inp["rg_lambda"][i])
        v8(f"qn{i}", inp["mla_q_norm"][i])
        v8(f"kvn{i}", inp["mla_kv_norm"][i])
    for i in range(2):
        for j in range(4):
            v8(f"gcw{i}_{j}", inp["gdn_conv_w"][i][j])
        v8(f"gn{i}", inp["gdn_norm"][i])
        V[:, voff[f"alog{i}"]:voff[f"alog{i}"] + 16] = np.asarray(inp["gdn_a_log"][i], np.float32)[None, :]
        V[:, voff[f"dtb{i}"]:voff[f"dtb{i}"] + 16] = np.asarray(inp["gdn_dt_bias"][i], np.float32)[None, :]
    inv_freq = (1.0 / (np.float32(10000.0) ** (np.arange(0, 64, 2, dtype=np.float32) / np.float32(64)))).astype(np.float32)
    V[:64, voff["invf"]] = np.concatenate([inv_freq, inv_freq])
    V[:32, voff["sgn"]] = -1.0
    V[32:64, voff["sgn"]] = 1.0
    return V


def make_consts():
    coff, ncc = cst_layout()
    C = np.zeros((128, ncc), np.float32)
    idx = np.arange(128)
    C[:, coff["ident"]:coff["ident"] + 128] = np.eye(128, dtype=np.float32)
    C[:, coff["amask"]:coff["amask"] + 128] = np.where(idx[:, None] <= idx[None, :], 0.0, NEG)
    same = (idx[:, None] // 64) == (idx[None, :] // 64)
    C[:, coff["U2"]:coff["U2"] + 128] = (same & (idx[:, None] <= idx[None, :])).astype(np.float32)
    C[:, coff["SL2"]:coff["SL2"] + 128] = (same & (idx[:, None] > idx[None, :])).astype(np.float32)
    C[:, coff["NMS"]:coff["NMS"] + 128] = np.where(same & (idx[:, None] > idx[None, :]), 0.0, NEG)
    C[:, coff["NMC"]:coff["NMC"] + 128] = np.where(same & (idx[:, None] <= idx[None, :]), 0.0, NEG)
    C[:, coff["BD"]:coff["BD"] + 128] = same.astype(np.float32)
    C[0:64, coff["ONA"]:coff["ONA"] + 128] = 1.0
    C[64:128, coff["ONB"]:coff["ONB"] + 128] = 1.0
    return C


class Ctx:
    pass


def build_program(layers=(0, 1, 2, 3), do_ffn=True, do_mix=True, cast_engs=("dve", "act", "dve"), dbg=False, stop=99, sub=99):
    nc = bass.Bass("TRN2", target_bir_lowering=False, dynamic_dma_scratch_size=1024)
    voff, nv = vec_layout()
    coff, ncc = cst_layout()
    soff, nslots = slot_layout()
    d_x = nc.dram_tensor("xT", [128, KT, T], F32, kind="ExternalInput").ap()
    d_pos = nc.dram_tensor("pos", [64, T], I32, kind="ExternalInput").ap()
    d_vec = nc.dram_tensor("vecs", [128, nv], F32, kind="ExternalInput").ap()
    d_cst = nc.dram_tensor("cst", [128, ncc], F32, kind="ExternalInput").ap()
    d_w = nc.dram_tensor("wts", [nslots, 128, SLOT_EL], F32, kind="ExternalInput").ap()
    d_y = nc.dram_tensor("yT", [128, KT, T], F32, kind="ExternalOutput").ap()
    with contextlib.ExitStack() as st:
        P = Prog(nc)
        g = Ctx()
        g.P, g.nc, g.voff, g.coff, g.soff = P, nc, voff, coff, soff
        g.dbg = dbg
        g.stop = stop
        g.sub = sub
        g.d_dbg = nc.dram_tensor("dbg", [8, 128, 512], F32, kind="ExternalOutput").ap() if dbg else None

        def sb(name, shape, dt):
            return st.enter_context(nc.sbuf_tensor(name, shape, dt))

        g.sb = sb
        xT = sb("xT_sb", [128, KT, T], F32)
        xtok = [[Tok() for _ in range(NB)] for _ in range(KT)]
        vec = sb("vec_sb", [128, nv], F32)
        t_vec = Tok()
        cst = sb("cst_sb", [128, ncc], F32)
        t_cst = Tok()
        identb = sb("identb", [128, 128], BF16)
        amaskb = sb("amaskb", [128, 128], BF16)
        onesb = sb("onesb", [128, 128], BF16)
        onesf = sb("onesf", [128, 128], F32)
        t_cb = Tok()
        hT = sb("hT", [128, KT, TB], BF16)
        t_h = Tok()
        bigb = sb("bigb", [128, 22, TB], BF16)
        tb = [Tok() for _ in range(22)]
        bigf = sb("bigf", [128, 8, TB], F32)
        tf = [Tok() for _ in range(8)]
        rstd = sb("rstd", [128, TB], F32)
        t_rstd = Tok()
        lnt = sb("lnt", [128, TB], F32)
        t_lnt = Tok()
        NSTG, NSL = 2, 3
        stage = [sb(f"stage{i}", [128, SLOT_EL], F32) for i in range(NSTG)]
        t_stage = [Tok() for _ in range(NSTG)]
        slots = [sb(f"slot{i}", [128, SLOT_EL], BF16) for i in range(NSL)]
        t_slot = [Tok() for _ in range(NSL)]
        ARENA32 = 11072
        arena = sb("arena", [128, ARENA32], F32)

        def mk_asb():
            off = [0]

            def asb(name, shape, dt):
                n = int(np.prod(shape[1:]))
                n32 = (n * (4 if dt in (F32, I32) else 2) + 3) // 4
                v = arena[:, off[0]:off[0] + n32]
                off[0] += n32
                assert off[0] <= ARENA32, (name, off[0])
                if dt != F32:
                    v = v.bitcast(dt)
                if len(shape) == 3:
                    v = v.rearrange("p (a b) -> p a b", b=shape[2])
                if shape[0] < 128:
                    v = v[0:shape[0]]
                return v
            return asb
        g.mk_asb = mk_asb
        g.arena_toks = []
        scr = sb("scr", [128, 2], F32)

        def arena_barrier():
            P.add("dve", lambda e: e.memset(scr[:, :], 0.0), writes=list(g.arena_toks))
        g.arena_barrier = arena_barrier
        ps = [st.enter_context(nc.psum_tensor(f"ps{i}", [128, 512], F32)) for i in range(8)]
        t_ps = [Tok() for _ in range(8)]
        g.xT, g.xtok, g.vec, g.t_vec, g.cst, g.t_cst = xT, xtok, vec, t_vec, cst, t_cst
        g.identb, g.amaskb, g.onesb, g.onesf, g.t_cb = identb, amaskb, onesb, onesf, t_cb
        g.hT, g.t_h, g.bigb, g.tb, g.bigf, g.tf = hT, t_h, bigb, tb, bigf, tf
        g.rstd, g.t_rstd, g.ps, g.t_ps = rstd, t_rstd, ps, t_ps
        g.d_pos = d_pos

        def V(name, k=0, n=1, rows=128):
            c = voff[name] + k
            return vec[0:rows, c:c + n]

        def C(name, rows=128, cols=128, c0=0):
            c = coff[name] + c0
            return cst[0:rows, c:c + cols]

        g.V, g.C = V, C

        class PsRot:
            def __init__(self, ids):
                self.ids = list(ids)
                self.i = 0

            def get(self):
                b = self.ids[self.i % len(self.ids)]
                self.i += 1
                return b
        g.PsRot = PsRot

        wctr = [0]

        def load_slot(idx, nel=SLOT_EL):
            k = wctr[0]
            wctr[0] += 1
            sg, sl = k % NSTG, k % NSL
            P.add("sp", lambda e: e.dma_start(out=stage[sg][:, 0:nel], in_=d_w[idx, :, 0:nel]),
                  writes=[t_stage[sg]], dma_key=("stg", sg))
            ce = cast_engs[k % len(cast_engs)]
            if ce == "act":
                fn = lambda e: e.copy(out=slots[sl][:, 0:nel], in_=stage[sg][:, 0:nel])
            else:
                fn = lambda e: e.tensor_copy(out=slots[sl][:, 0:nel], in_=stage[sg][:, 0:nel])
            P.add(ce, fn, reads=[t_stage[sg]], writes=[t_slot[sl]])
            return slots[sl], t_slot[sl]
        g.load_slot = load_slot

        P.add("sp", lambda e: e.dma_start(out=vec[:, :], in_=d_vec[:, :]), writes=[t_vec], dma_key="vec")
        P.add("sp", lambda e: e.dma_start(out=cst[:, :], in_=d_cst[:, :]), writes=[t_cst], dma_key="cst")
        for kt in range(KT):
            P.add("sp", lambda e, kt=kt: e.dma_start(out=xT[:, kt, :], in_=d_x[:, kt, :]),
                  writes=xtok[kt], dma_key=("x", kt))

        def setup_consts(e):
            e.tensor_copy(out=identb[:, :], in_=C("ident"))
            e.tensor_copy(out=amaskb[:, :], in_=C("amask"))
            e.memset(onesb[:, :], 1.0)
            return e.memset(onesf[:, :], 1.0)
        P.add("dve", setup_consts, reads=[t_cst], writes=[t_cb])

        def rstd_from_sq(sq_aps, sq_toks, n, dsum, eps, bank):
            def mm(e):
                ins = None
                for i, a in enumerate(sq_aps):
                    ins = e.matmul(ps[bank][:, 0:n], lhsT=onesb[:, :], rhs=a,
                                   start=(i == 0), stop=(i == len(sq_aps) - 1))
                return ins
            P.add("pe", mm, reads=list(sq_toks) + [t_cb], writes=[t_ps[bank]])
            P.add("act", lambda e: e.activation(out=lnt[:, 0:n], in_=ps[bank][:, 0:n], func=AF.Ln,
                                                scale=1.0 / dsum, bias=g.eps_ap),
                  reads=[t_ps[bank], t_cb2], writes=[t_lnt])
            P.add("act", lambda e: e.activation(out=rstd[:, 0:n], in_=lnt[:, 0:n], func=AF.Exp, scale=-0.5),
                  reads=[t_lnt], writes=[t_rstd])
        g.rstd_from_sq = rstd_from_sq
        epst = sb("epst", [128, 2], F32)
        t_cb2 = Tok()
        g.eps_ap = epst[:, 0:1]
        g.one_ap = epst[:, 1:2]
        g.t_cb2 = t_cb2

        def setup_eps(e):
            e.memset(epst[:, 0:1], EPS)
            return e.memset(epst[:, 1:2], 1.0)
        P.add("dve", setup_eps, writes=[t_cb2])

        def prenorm(b, wname, bank):
            blk = slice(b * TB, (b + 1) * TB)
            P.add("act", lambda e: e.activation(out=bigb[:, 0:8, :], in_=xT[:, :, blk], func=AF.Square),
                  reads=[xtok[kt][b] for kt in range(KT)], writes=tb[0:8])
            rstd_from_sq([bigb[:, kt, :] for kt in range(KT)], tb[0:8], TB, float(D), EPS, bank)

            def f(e):
                ins = None
                for kt in range(KT):
                    ins = e.scalar_tensor_tensor(out=hT[:, kt, :], in0=xT[:, kt, blk], scalar=V(wname, kt),
                                                 in1=rstd[:, :], op0=ALU.mult, op1=ALU.mult)
                return ins
            P.add("dve", f, reads=[xtok[kt][b] for kt in range(KT)] + [t_rstd, t_vec], writes=[t_h])
        g.prenorm = prenorm

        def post_evac(bank, m, wname):
            P.add("act", lambda e: e.activation(out=bigf[:, m, :], in_=ps[bank][:, :], func=AF.Identity,
                                                scale=V(wname, m)),
                  reads=[t_ps[bank], t_vec], writes=[tf[m]])
            P.add("act", lambda e: e.activation(out=hT[:, m, :], in_=ps[bank][:, :], func=AF.Square),
                  reads=[t_ps[bank]], writes=[t_h])
        g.post_evac = post_evac

        def post_finish(b, bank):
            blk = slice(b * TB, (b + 1) * TB)
            rstd_from_sq([hT[:, kt, :] for kt in range(KT)], [t_h], TB, float(D), EPS, bank)
            for kt in range(KT):
                P.add("dve", lambda e, kt=kt: e.tensor_tensor(out=bigf[:, kt, :], in0=bigf[:, kt, :], in1=rstd[:, :],
                                                              op=ALU.mult),
                      reads=[tf[kt], t_rstd], writes=[tf[kt]])
                P.add("dve", lambda e, kt=kt: e.tensor_tensor(out=xT[:, kt, blk], in0=xT[:, kt, blk],
                                                              in1=bigf[:, kt, :], op=ALU.add),
                      reads=[tf[kt], xtok[kt][b]], writes=[xtok[kt][b]])
        g.post_finish = post_finish

        sgt = [sb(f"sgt{i}", [128, TB], F32) for i in range(2)]
        t_sgt = [Tok(), Tok()]

        def ffn_block(l, b):
            rot = PsRot([0, 1, 2, 3, 4, 5])
            prenorm(b, f"nfp{l}", 7)
            for m in range(FT):
                sl, tsl = load_slot(soff[f"f_gu{l}"] + m)
                bg, bu = rot.get(), rot.get()

                def mm(e, sl=sl, bg=bg, bu=bu):
                    ins = None
                    for kt in range(KT):
                        ins = e.matmul(ps[bg][:, :], lhsT=sl[:, kt * 256:kt * 256 + 128], rhs=hT[:, kt, :],
                                       start=(kt == 0), stop=(kt == KT - 1))
                    for kt in range(KT):
                        ins = e.matmul(ps[bu][:, :], lhsT=sl[:, kt * 256 + 128:kt * 256 + 256], rhs=hT[:, kt, :],
                                       start=(kt == 0), stop=(kt == KT - 1))
                    return ins
                P.add("pe", mm, reads=[tsl, t_h], writes=[t_ps[bg], t_ps[bu]])
                s = m % 2
                P.add("act", lambda e, s=s, bg=bg: e.activation(out=sgt[s][:, :], in_=ps[bg][:, :], func=AF.Silu),
                      reads=[t_ps[bg]], writes=[t_sgt[s]])
                P.add("dve", lambda e, s=s, bu=bu, m=m: e.tensor_tensor(out=bigb[:, m, :], in0=sgt[s][:, :],
                                                                        in1=ps[bu][:, :], op=ALU.mult),
                      reads=[t_sgt[s], t_ps[bu]], writes=[tb[m]])
            for dm in range(8):
                s0, ts0 = load_slot(soff[f"f_down{l}"] + 2 * dm, 11 * 128)
                s1, ts1 = load_slot(soff[f"f_down{l}"] + 2 * dm + 1, 11 * 128)
                bk = rot.get()

                def mm(e, s0=s0, s1=s1, bk=bk):
                    ins = None
                    for k in range(FT):
                        s_, kk = (s0, k) if k < 11 else (s1, k - 11)
                        ins = e.matmul(ps[bk][:, :], lhsT=s_[:, kk * 128:(kk + 1) * 128], rhs=bigb[:, k, :],
                                       start=(k == 0), stop=(k == FT - 1))
                    return ins
                P.add("pe", mm, reads=[ts0, ts1] + tb, writes=[t_ps[bk]])
                post_evac(bk, dm, f"nfo{l}")
            post_finish(b, 7)
        g.ffn_block = ffn_block

        even_state = make_even(g) if do_mix else None
        odd_state = make_odd(g) if do_mix else None
        for l in layers:
            i = l // 2
            if do_mix:
                if l % 2 == 0:
                    even_state.begin_layer(l)
                else:
                    odd_state.begin_layer(l)
            for b in range(1 if dbg else NB):
                if do_mix:
                    if l % 2 == 0:
                        even_state.block(l, b)
                    else:
                        odd_state.block(l, b)
                if do_ffn:
                    ffn_block(l, b)

        outs = []
        for kt in range(KT):
            P.add("sp", lambda e, kt=kt: e.dma_start(out=d_y[:, kt, :], in_=xT[:, kt, :]),
                  reads=xtok[kt], dma_key=("y", kt))
            outs.extend(xtok[kt])
        P.add("sp", None, writes=outs)
        P.emit(st)
        g.stats = {e: len(P.ops[e]) for e in ENGS}
        g.stats["waits"] = P.nwaits
        g.stats["sems"] = P.n_sems
    return nc, g


def reg_tok(g):
    t = Tok()
    g.arena_toks.append(t)
    return t


def make_even(g):
    P, sb, ps, t_ps, V, C = g.P, g.sb, g.ps, g.t_ps, g.V, g.C
    bigb, tb, bigf, tf, hT, t_h = g.bigb, g.tb, g.bigf, g.tf, g.hT, g.t_h
    rstd, t_rstd, t_vec, t_cb = g.rstd, g.t_rstd, g.t_vec, g.t_cb
    soff = g.soff
    S = Ctx()
    sb = g.mk_asb()
    Tok = lambda: reg_tok(g)
    ckvn = sb("ckvn", [128, 2, T], BF16)
    t_ckv = [Tok() for _ in range(NB)]
    kpe = sb("kpe", [128, T], BF16)
    t_kpe = [Tok() for _ in range(NB)]
    Kh = sb("Kh", [128, T], BF16)
    t_Kh = [Tok() for _ in range(NB)]
    Vh = sb("Vh", [128, 16, 128], BF16)
    t_Vh = [Tok() for _ in range(NB)]
    qn = sb("qn", [128, TB], BF16)
    t_qn = Tok()
    qr = sb("qr", [128, TB], BF16)
    t_qr = Tok()
    PT = [sb(f"PT{i}", [128, TB], BF16) for i in range(2)]
    t_PT = [Tok(), Tok()]
    Ct = sb("Ct", [64, TB], F32)
    St = sb("St", [64, TB], F32)
    t_rope = Tok()
    posi = sb("posi", [64, TB], I32)
    t_posi = Tok()
    ki = sb("ki", [64, TB], I32)
    t_ki = Tok()
    xrw = sb("xrw", [128, TB + 4], F32)
    t_xrw = Tok()
    xcb = sb("xcb", [128, TB], BF16)
    t_xcb = Tok()
    tail = sb("rgtail", [128, 8, 3], F32)
    t_tail = [Tok() for _ in range(8)]
    hst = sb("hst", [128, 8], F32)
    t_hst = [Tok() for _ in range(8)]
    nsp8 = sb("nsp8", [128, 8], F32)
    t_nsp = Tok()
    lt8 = sb("lt8", [128, 8], F32)
    t_lt8 = Tok()
    PI_LO = 3.1415925
    c1 = 6.28125
    c2 = float(np.float32(TWO_PI - c1).view(np.uint32) & np.uint32(0xFFFFF000)) if False else None
    c2 = float((np.array([TWO_PI - c1], np.float32).view(np.uint32) & np.uint32(0xFFFFF000)).view(np.float32)[0])
    c3 = float(np.float32(TWO_PI - c1 - c2))
    SCALE = float(192 ** -0.5)

    def begin_layer(l):
        i = l // 2
        g.arena_barrier()
        P.add("act", lambda e: e.activation(out=lt8[:, :], in_=V(f"lam{i}", 0, 8), func=AF.Exp, scale=-1.0),
              reads=[t_vec], writes=[t_lt8])
        P.add("act", lambda e: e.activation(out=lt8[:, :], in_=lt8[:, :], func=AF.Ln, bias=g.one_ap),
              reads=[t_lt8, g.t_cb2], writes=[t_lt8])
        P.add("dve", lambda e: e.tensor_scalar(out=nsp8[:, :], in0=lt8[:, :], scalar1=-8.0, scalar2=None, op0=ALU.mult),
              reads=[t_lt8], writes=[t_nsp])
        P.add("dve", lambda e: e.memset(tail[:, :, :], 0.0), writes=t_tail)
        P.add("dve", lambda e: e.memset(kpe[64:128, :], 0.0), writes=t_kpe)
        P.add("dve", lambda e: e.memset(qr[64:128, :], 0.0), writes=[t_qr])
        P.add("dve", lambda e: e.memset(hst[:, :], 0.0), writes=t_hst)

    def rope_tables(b):
        blk = slice(b * TB, (b + 1) * TB)
        f0, f1, f2 = bigf[0:64, 0, :], bigf[0:64, 1, :], bigf[0:64, 2, :]
        P.add("sp", lambda e: e.dma_start(out=posi[:, :], in_=g.d_pos[:, blk]), writes=[t_posi], dma_key="pos")
        P.add("dve", lambda e: e.tensor_copy(out=f0, in_=posi[:, :]), reads=[t_posi], writes=[tf[0]])
        P.add("dve", lambda e: e.tensor_scalar(out=f1, in0=f0, scalar1=V("invf", 0, 1, 64), scalar2=None, op0=ALU.mult),
              reads=[tf[0], t_vec], writes=[tf[1]])
        P.add("dve", lambda e: e.tensor_scalar(out=ki[:, :], in0=f1, scalar1=float(1.0 / TWO_PI), scalar2=None, op0=ALU.mult),
              reads=[tf[1]], writes=[t_ki])
        P.add("dve", lambda e: e.tensor_copy(out=f2, in_=ki[:, :]), reads=[t_ki], writes=[tf[2]])
        for cc in (c1, c2, c3):
            P.add("dve", lambda e, cc=cc: e.scalar_tensor_tensor(out=f1, in0=f2, scalar=-cc, in1=f1, op0=ALU.mult, op1=ALU.add),
                  reads=[tf[1], tf[2]], writes=[tf[1]])

        def wrap(y, ty):
            P.add("dve", lambda e: e.tensor_scalar(out=f0, in0=y, scalar1=float(np.pi), scalar2=None, op0=ALU.is_gt),
                  reads=[ty], writes=[tf[0]])
            P.add("dve", lambda e: e.scalar_tensor_tensor(out=y, in0=f0, scalar=-TWO_PI, in1=y, op0=ALU.mult, op1=ALU.add),
                  reads=[tf[0], ty], writes=[ty])
            P.add("dve", lambda e: e.tensor_scalar(out=f0, in0=y, scalar1=float(-np.pi), scalar2=None, op0=ALU.is_lt),
                  reads=[ty], writes=[tf[0]])
            P.add("dve", lambda e: e.scalar_tensor_tensor(out=y, in0=f0, scalar=TWO_PI, in1=y, op0=ALU.mult, op1=ALU.add),
                  reads=[tf[0], ty], writes=[ty])
            P.add("dve", lambda e: e.tensor_scalar(out=y, in0=y, scalar1=PI_LO, scalar2=-PI_LO, op0=ALU.min, op1=ALU.max),
                  reads=[ty], writes=[ty])
        P.add("dve", lambda e: e.tensor_scalar(out=f2, in0=f1, scalar1=float(np.pi / 2), scalar2=None, op0=ALU.add),
              reads=[tf[1]], writes=[tf[2]])
        wrap(f1, tf[1])
        wrap(f2, tf[2])
        P.add("act", lambda e: e.activation(out=St[:, :], in_=f1, func=AF.Sin, scale=V("sgn", 0, 1, 64)),
              reads=[tf[1], t_vec], writes=[t_rope])
        P.add("act", lambda e: e.activation(out=Ct[:, :], in_=f2, func=AF.Sin),
              reads=[tf[2]], writes=[t_rope])

    def rope_apply(bA, bB, out_ap, out_toks):
        t1, t2 = bigf[0:64, 0, :], bigf[0:64, 1, :]
        P.add("dve", lambda e: e.tensor_tensor(out=t1, in0=ps[bA][0:64, :], in1=Ct[:, :], op=ALU.mult),
              reads=[t_ps[bA], t_rope], writes=[tf[0]])
        P.add("dve", lambda e: e.tensor_tensor(out=t2, in0=ps[bB][0:64, :], in1=St[:, :], op=ALU.mult),
              reads=[t_ps[bB], t_rope], writes=[tf[1]])
        P.add("dve", lambda e: e.tensor_tensor(out=out_ap, in0=t1, in1=t2, op=ALU.add),
              reads=[tf[0], tf[1]], writes=out_toks)

    def block(l, b):
        i = l // 2
        blk = slice(b * TB, (b + 1) * TB)
        rot = g.PsRot([0, 1, 2, 3])
        rope_tables(b)
        if g.stop < 1:
            return
        g.prenorm(b, f"nmp{l}", 7)
        s0 = soff[f"e_win{i}"]
        if g.stop < 2:
            return
        for half in range(2):
            sl, tsl = g.load_slot(s0 + 8 + half)
            for mm_ in range(2):
                mt = 2 * half + mm_
                bk = rot.get()

                def mm(e, sl=sl, bk=bk, mm_=mm_):
                    ins = None
                    for kt in range(KT):
                        ins = e.matmul(ps[bk][:, :], lhsT=sl[:, kt * 256 + mm_ * 128:kt * 256 + mm_ * 128 + 128],
                                       rhs=hT[:, kt, :], start=(kt == 0), stop=(kt == KT - 1))
                    return ins
                P.add("pe", mm, reads=[tsl, t_h], writes=[t_ps[bk]])
                P.add("act", lambda e, bk=bk, mt=mt: e.activation(out=bigf[:, 4 + mt, :], in_=ps[bk][:, :], func=AF.Copy),
                      reads=[t_ps[bk]], writes=[tf[4 + mt]])
                P.add("act", lambda e, bk=bk, mt=mt: e.activation(out=bigb[:, 8 + mt, :], in_=ps[bk][:, :], func=AF.Square),
                      reads=[t_ps[bk]], writes=[tb[8 + mt]])
        g.rstd_from_sq([bigb[:, 8 + mt, :] for mt in range(4)], tb[8:12], TB, 512.0, EPS, 7)
        for mt in range(4):
            P.add("dve", lambda e, mt=mt: e.scalar_tensor_tensor(out=bigb[:, 16 + mt, :], in0=bigf[:, 4 + mt, :],
                                                                 scalar=V(f"qn{i}", mt), in1=rstd[:, :],
                                                                 op0=ALU.mult, op1=ALU.mult),
                  reads=[tf[4 + mt], t_rstd, t_vec], writes=[tb[16 + mt]])
        sl, tsl = g.load_slot(s0 + 10)
        for mt in range(2):
            bk = rot.get()

            def mm(e, sl=sl, bk=bk, mt=mt):
                ins = None
                for kt in range(KT):
                    ins = e.matmul(ps[bk][:, :], lhsT=sl[:, kt * 256 + mt * 128:kt * 256 + mt * 128 + 128],
                                   rhs=hT[:, kt, :], start=(kt == 0), stop=(kt == KT - 1))
                return ins
            P.add("pe", mm, reads=[tsl, t_h], writes=[t_ps[bk]])
            P.add("act", lambda e, bk=bk, mt=mt: e.activation(out=bigf[:, 2 + mt, :], in_=ps[bk][:, :], func=AF.Copy),
                  reads=[t_ps[bk]], writes=[tf[2 + mt]])
            P.add("act", lambda e, bk=bk, mt=mt: e.activation(out=bigb[:, 20 + mt, :], in_=ps[bk][:, :], func=AF.Square),
                  reads=[t_ps[bk]], writes=[tb[20 + mt]])
        g.rstd_from_sq([bigb[:, 20 + mt, :] for mt in range(2)], tb[20:22], TB, 256.0, EPS, 7)
        for mt in range(2):
            P.add("dve", lambda e, mt=mt: e.scalar_tensor_tensor(out=ckvn[:, mt, blk], in0=bigf[:, 2 + mt, :],
                                                                 scalar=V(f"kvn{i}", mt), in1=rstd[:, :],
                                                                 op0=ALU.mult, op1=ALU.mult),
                  reads=[tf[2 + mt], t_rstd, t_vec], writes=[t_ckv[b]])
        sl, tsl = g.load_slot(s0 + 11, 8 * 128)
        bA, bB = rot.get(), rot.get()

        def mmk(e, sl=sl, bA=bA, bB=bB):
            ins = None
            for kt in range(KT):
                ins = e.matmul(ps[bA][0:64, :], lhsT=sl[:, kt * 128:kt * 128 + 64], rhs=hT[:, kt, :],
                               start=(kt == 0), stop=(kt == KT - 1))
            for kt in range(KT):
                ins = e.matmul(ps[bB][0:64, :], lhsT=sl[:, kt * 128 + 64:kt * 128 + 128], rhs=hT[:, kt, :],
                               start=(kt == 0), stop=(kt == KT - 1))
            return ins
        P.add("pe", mmk, reads=[tsl, t_h], writes=[t_ps[bA], t_ps[bB]])
        rope_apply(bA, bB, kpe[0:64, blk], [t_kpe[b]])
        if g.stop < 3:
            return
        gs, tgs = g.load_slot(soff[f"e_gate{i}"])
        P.add("pool", lambda e: e.tensor_copy(out=S.gatew[:, :], in_=gs[:, :]), reads=[tgs], writes=[S.t_gatew])
        for n in range(8):
            sl, tsl = g.load_slot(s0 + n)
            bX, bG = rot.get(), rot.get()

            def mm(e, sl=sl, bX=bX, bG=bG):
                ins = None
                for kt in range(KT):
                    ins = e.matmul(ps[bX][:, :], lhsT=sl[:, kt * 256:kt * 256 + 128], rhs=hT[:, kt, :],
                                   start=(kt == 0), stop=(kt == KT - 1))
                for kt in range(KT):
                    ins = e.matmul(ps[bG][:, :], lhsT=sl[:, kt * 256 + 128:kt * 256 + 256], rhs=hT[:, kt, :],
                                   start=(kt == 0), stop=(kt == KT - 1))
                return ins
            P.add("pe", mm, reads=[tsl, t_h], writes=[t_ps[bX], t_ps[bG]])
            P.add("dve", lambda e, n=n: e.tensor_copy(out=xrw[:, 0:3], in_=tail[:, n, :]), reads=[t_tail[n]], writes=[t_xrw])
            P.add("act", lambda e, bX=bX: e.activation(out=xrw[:, 3:TB + 3], in_=ps[bX][:, :], func=AF.Copy),
                  reads=[t_ps[bX]], writes=[t_xrw])
            xc = bigf[:, 0, :]
            P.add("dve", lambda e, n=n: e.tensor_scalar(out=xc, in0=xrw[:, 3:TB + 3], scalar1=V(f"cw{i}_3", n),
                                                        scalar2=V(f"cb{i}", n), op0=ALU.mult, op1=ALU.add),
                  reads=[t_xrw, t_vec], writes=[tf[0]])
            for j in (2, 1, 0):
                P.add("dve", lambda e, n=n, j=j: e.scalar_tensor_tensor(out=xc, in0=xrw[:, j:TB + j], scalar=V(f"cw{i}_{j}", n),
                                                                        in1=xc, op0=ALU.mult, op1=ALU.add),
                      reads=[t_xrw, t_vec, tf[0]], writes=[tf[0]])
            P.add("dve", lambda e, n=n: e.tensor_copy(out=tail[:, n, :], in_=xrw[:, TB:TB + 3]), reads=[t_xrw], writes=[t_tail[n]])
            P.add("act", lambda e: e.activation(out=xcb[:, :], in_=xc, func=AF.Copy), reads=[tf[0]], writes=[t_xcb])
            bR, bI = rot.get(), rot.get()

            def mmg(e, n=n, bR=bR, bI=bI):
                e.matmul(ps[bR][:, :], lhsT=S.gatew[:, n * 256:n * 256 + 128], rhs=xcb[:, :], start=True, stop=True)
                return e.matmul(ps[bI][:, :], lhsT=S.gatew[:, n * 256 + 128:n * 256 + 256], rhs=xcb[:, :], start=True, stop=True)
            P.add("pe", mmg, reads=[S.t_gatew, t_xcb], writes=[t_ps[bR], t_ps[bI]])
            P.add("act", lambda e, n=n, bR=bR: e.activation(out=bigf[:, 1, :], in_=ps[bR][:, :], func=AF.Sigmoid, bias=V(f"gab{i}", n)),
                  reads=[t_ps[bR], t_vec], writes=[tf[1]])
            P.add("act", lambda e, n=n, bI=bI: e.activation(out=bigf[:, 2, :], in_=ps[bI][:, :], func=AF.Sigmoid, bias=V(f"gxb{i}", n)),
                  reads=[t_ps[bI], t_vec], writes=[tf[2]])
            P.add("act", lambda e, n=n: e.activation(out=bigf[:, 3, :], in_=bigf[:, 1, :], func=AF.Exp, scale=nsp8[:, n:n + 1]),
                  reads=[tf[1], t_nsp], writes=[tf[3]])
            P.add("dve", lambda e: e.tensor_tensor(out=bigf[:, 4, :], in0=bigf[:, 3, :], in1=bigf[:, 3, :], op=ALU.mult),
                  reads=[tf[3]], writes=[tf[4]])
            P.add("act", lambda e: e.activation(out=bigf[:, 4, :], in_=bigf[:, 4, :], func=AF.Sqrt, scale=-1.0, bias=g.one_ap),
                  reads=[tf[4], g.t_cb2], writes=[tf[4]])
            P.add("dve", lambda e: e.tensor_tensor(out=bigf[:, 5, :], in0=bigf[:, 2, :], in1=xc, op=ALU.mult),
                  reads=[tf[2], tf[0]], writes=[tf[5]])
            P.add("dve", lambda e: e.tensor_tensor(out=bigf[:, 5, :], in0=bigf[:, 5, :], in1=bigf[:, 4, :], op=ALU.mult),
                  reads=[tf[5], tf[4]], writes=[tf[5]])
            P.add("dve", lambda e, n=n: e.tensor_tensor_scan(out=bigf[:, 6, :], data0=bigf[:, 3, :], data1=bigf[:, 5, :],
                                                             initial=hst[:, n:n + 1], op0=ALU.mult, op1=ALU.add),
                  reads=[tf[3], tf[5], t_hst[n]], writes=[tf[6]])
            P.add("dve", lambda e, n=n: e.tensor_copy(out=hst[:, n:n + 1], in_=bigf[:, 6, TB - 1:TB]), reads=[tf[6]], writes=[t_hst[n]])
            P.add("act", lambda e, bG=bG: e.activation(out=bigf[:, 7, :], in_=ps[bG][:, :], func=AF.Gelu_apprx_tanh),
                  reads=[t_ps[bG]], writes=[tf[7]])
            P.add("dve", lambda e, n=n: e.tensor_tensor(out=bigb[:, n, :], in0=bigf[:, 6, :], in1=bigf[:, 7, :], op=ALU.mult),
                  reads=[tf[6], tf[7]], writes=[tb[n]])
        if g.stop < 4:
            return
        nkc = b + 1
        srot = g.PsRot([4, 5])
        for h in range(8):
            if h % 2 == 0:
                uq, tuq = g.load_slot(soff[f"e_uq{i}"] + h // 2)
            if h % 4 == 0:
                ukv_s, tukv_s = g.load_slot(soff[f"e_ukv{i}"] + h // 4)
                P.add("pool", lambda e, ukv_s=ukv_s: e.tensor_copy(out=S.ukvw[:, :], in_=ukv_s[:, :]), reads=[tukv_s], writes=[S.t_ukvw])
            ukv, tukv = S.ukvw, S.t_ukvw
            qb = (h % 2) * 4 * 256
            kb_ = (h % 4) * 2 * 256
            bQ, bA, bB = rot.get(), rot.get(), rot.get()

            def mmq(e, uq=uq, qb=qb, bQ=bQ, bA=bA, bB=bB):
                ins = None
                for kt in range(4):
                    ins = e.matmul(ps[bQ][:, :], lhsT=uq[:, qb + kt * 256:qb + kt * 256 + 128], rhs=bigb[:, 16 + kt, :],
                                   start=(kt == 0), stop=(kt == 3))
                for kt in range(4):
                    ins = e.matmul(ps[bA][0:64, :], lhsT=uq[:, qb + kt * 256 + 128:qb + kt * 256 + 192], rhs=bigb[:, 16 + kt, :],
                                   start=(kt == 0), stop=(kt == 3))
                for kt in range(4):
                    ins = e.matmul(ps[bB][0:64, :], lhsT=uq[:, qb + kt * 256 + 192:qb + kt * 256 + 256], rhs=bigb[:, 16 + kt, :],
                                   start=(kt == 0), stop=(kt == 3))
                return ins
            P.add("pe", mmq, reads=[tuq] + tb[16:20], writes=[t_ps[bQ], t_ps[bA], t_ps[bB]])
            P.add("act", lambda e, bQ=bQ: e.activation(out=qn[:, :], in_=ps[bQ][:, :], func=AF.Copy), reads=[t_ps[bQ]], writes=[t_qn])
            rope_apply(bA, bB, qr[0:64, :], [t_qr])
            for c in range(nkc):
                bk = rot.get()

                def mmK(e, c=c, bk=bk, kb_=kb_):
                    ins = None
                    for kt in range(2):
                        ins = e.matmul(ps[bk][:, :], lhsT=ukv[:, kb_ + kt * 256:kb_ + kt * 256 + 128],
                                       rhs=ckvn[:, kt, c * TB:(c + 1) * TB], start=(kt == 0), stop=(kt == 1))
                    return ins
                P.add("pe", mmK, reads=[tukv, t_ckv[c]], writes=[t_ps[bk]])
                P.add("act", lambda e, c=c, bk=bk: e.activation(out=Kh[:, c * TB:(c + 1) * TB], in_=ps[bk][:, :], func=AF.Copy),
                      reads=[t_ps[bk]], writes=[t_Kh[c]])
                bv = rot.get()

                def mmV(e, c=c, bv=bv, kb_=kb_):
                    ins = None
                    for jj in range(4):
                        for kt in range(2):
                            ins = e.matmul(ps[bv][:, jj * 128:(jj + 1) * 128],
                                           lhsT=ckvn[:, kt, c * TB + jj * 128:c * TB + (jj + 1) * 128],
                                           rhs=ukv[:, kb_ + kt * 256 + 128:kb_ + kt * 256 + 256],
                                           start=(kt == 0), stop=(kt == 1))
                    return ins
                P.add("pe", mmV, reads=[tukv, t_ckv[c]], writes=[t_ps[bv]])
                P.add("dve", lambda e, c=c, bv=bv: e.tensor_copy(out=Vh[:, 4 * c:4 * c + 4, :], in_=ps[bv][:, :]),
                      reads=[t_ps[bv]], writes=[t_Vh[c]])
            nj = 4 * nkc
            for j in range(nj):
                jj = j - 4 * b
                c0 = max(0, jj) * 128
                sbk = srot.get()
                pt = j % 2

                def mmS(e, j=j, jj=jj, c0=c0, sbk=sbk):
                    e.matmul(ps[sbk][:, c0:TB], lhsT=Kh[:, j * 128:(j + 1) * 128], rhs=qn[:, c0:TB], start=True, stop=False)
                    ins = e.matmul(ps[sbk][:, c0:TB], lhsT=kpe[:, j * 128:(j + 1) * 128], rhs=qr[:, c0:TB],
                                   start=False, stop=(jj < 0))
                    if jj >= 0:
                        ins = e.matmul(ps[sbk][:, c0:c0 + 128], lhsT=g.identb[:, :], rhs=g.amaskb[:, :], start=False, stop=True)
                    return ins
                P.add("pe", mmS, reads=[t_Kh[j // 4], t_kpe[j // 4], t_qn, t_qr, t_cb], writes=[t_ps[sbk]])
                P.add("act", lambda e, c0=c0, sbk=sbk, pt=pt: e.activation(out=PT[pt][:, c0:TB], in_=ps[sbk][:, c0:TB],
                                                                           func=AF.Exp, scale=SCALE),
                      reads=[t_ps[sbk]], writes=[t_PT[pt]])

                def mmO(e, j=j, c0=c0, pt=pt, nj=nj):
                    e.matmul(ps[6][:, c0:TB], lhsT=Vh[:, j, :], rhs=PT[pt][:, c0:TB], start=(j == 0), stop=(j == nj - 1))
                    return e.matmul(ps[7][:, c0:TB], lhsT=g.onesb[:, :], rhs=PT[pt][:, c0:TB], start=(j == 0), stop=(j == nj - 1))
                P.add("pe", mmO, reads=[t_Vh[j // 4], t_PT[pt], t_cb], writes=[t_ps[6], t_ps[7]])
            P.add("dve", lambda e: e.reciprocal(out=bigf[:, 2, :], in_=ps[7][:, :]), reads=[t_ps[7]], writes=[tf[2]])
            P.add("dve", lambda e, h=h: e.tensor_tensor(out=bigb[:, 8 + h, :], in0=ps[6][:, :], in1=bigf[:, 2, :], op=ALU.mult),
                  reads=[t_ps[6], tf[2]], writes=[tb[8 + h]])
        if g.stop < 5:
            return
        for m in range(8):
            sl, tsl = g.load_slot(soff[f"e_wout{i}"] + m)
            bk = rot.get()

            def mmo(e, sl=sl, bk=bk):
                ins = None
                for kt in range(16):
                    ins = e.matmul(ps[bk][:, :], lhsT=sl[:, kt * 128:(kt + 1) * 128], rhs=bigb[:, kt, :],
                                   start=(kt == 0), stop=(kt == 15))
                return ins
            P.add("pe", mmo, reads=[tsl] + tb[0:16], writes=[t_ps[bk]])
            g.post_evac(bk, m, f"nmo{l}")
        g.post_finish(b, 7)

    S.gatew = sb("gatew", [128, SLOT_EL], BF16)
    S.t_gatew = Tok()
    S.ukvw = sb("ukvw", [128, SLOT_EL], BF16)
    S.t_ukvw = Tok()
    S.begin_layer = begin_layer
    S.block = block
    return S


def make_odd(g):
    P, ps, t_ps, V, C = g.P, g.ps, g.t_ps, g.V, g.C
    bigb, tb, bigf, tf, hT, t_h = g.bigb, g.tb, g.bigf, g.tf, g.hT, g.t_h
    rstd, t_rstd, t_vec, t_cb, t_cst = g.rstd, g.t_rstd, g.t_vec, g.t_cb, g.t_cst
    onesf = g.onesf
    soff = g.soff
    S = Ctx()
    sb = g.mk_asb()
    Tok = lambda: reg_tok(g)
    S_all = sb("S_all", [128, 16, 128], F32)
    t_S = [Tok() for _ in range(16)]
    gtail = sb("gtail", [128, 32, 3], F32)
    t_gt = [Tok() for _ in range(32)]
    xw = sb("gxw", [128, TB + 4], F32)
    t_xw = Tok()
    beta = sb("beta", [128, 4, 16], F32)
    nbeta = sb("nbeta", [128, 4, 16], F32)
    gg = sb("gg", [128, 4, 16], F32)
    eg = sb("eg", [128, 4, 16], F32)
    egr = sb("egr", [128, 4, 16], F32)
    dch = sb("dch", [128, 8, 16], F32)
    t_gate = Tok()
    nexpA = sb("nexpA", [128, 16], F32)
    t_nexp = Tok()
    KKs = sb("KKs", [128, 4, 128], F32)
    QKs = sb("QKs", [128, 4, 128], F32)
    Qt = sb("Qt", [128, 4, 128], F32)
    Ktt = sb("Ktt", [128, 4, 128], F32)
    t_pair = [Tok() for _ in range(4)]
    CH = []
    for p in range(4):
        c = Ctx()
        c.XY = [sb(f"XY{p}_{k}", [128, 256], F32) for k in range(2)]
        c.X = [c.XY[k][:, 0:128] for k in range(2)]
        c.Y = [c.XY[k][:, 128:256] for k in range(2)]
        c.R = [sb(f"R{p}_{k}", [128, 128], F32) for k in range(2)]
        c.QKT = sb(f"QKT{p}", [128, 128], F32)
        c.kdec = sb(f"kdec{p}", [128, 128], F32)
        c.kdecB = sb(f"kdecB{p}", [128, 128], F32)
        c.Kbe = sb(f"Kbe{p}", [128, 128], F32)
        c.Vb = sb(f"Vb{p}", [128, 128], F32)
        c.tX = [Tok(), Tok()]
        c.tY = [Tok(), Tok()]
        c.tR = [Tok(), Tok()]
        c.tQKT, c.tkdec, c.tKbe, c.tVb = Tok(), Tok(), Tok(), Tok()
        CH.append(c)
    ident = C("ident")
    QSCALE = float(128 ** -0.5)

    def begin_layer(l):
        i = l // 2
        g.arena_barrier()
        P.add("act", lambda e: e.activation(out=nexpA[:, :], in_=V(f"alog{i}", 0, 16), func=AF.Exp),
              reads=[t_vec], writes=[t_nexp])
        P.add("dve", lambda e: e.tensor_scalar(out=nexpA[:, :], in0=nexpA[:, :], scalar1=-1.0, scalar2=None, op0=ALU.mult),
              reads=[t_nexp], writes=[t_nexp])
        P.add("dve", lambda e: e.memset(S_all[:, :, :], 0.0), writes=t_S)
        P.add("dve", lambda e: e.memset(gtail[:, :, :], 0.0), writes=t_gt)

    def conv_silu(i, bank, tile, out_ap, out_tok):
        P.add("dve", lambda e: e.tensor_copy(out=xw[:, 0:3], in_=gtail[:, tile, :]), reads=[t_gt[tile]], writes=[t_xw])
        P.add("act", lambda e: e.activation(out=xw[:, 3:TB + 3], in_=ps[bank][:, :], func=AF.Copy),
              reads=[t_ps[bank]], writes=[t_xw])
        P.add("dve", lambda e: e.tensor_scalar(out=out_ap, in0=xw[:, 3:TB + 3], scalar1=V(f"gcw{i}_3", tile), scalar2=None,
                                               op0=ALU.mult), reads=[t_xw, t_vec], writes=[out_tok])
        for j in (2, 1, 0):
            P.add("dve", lambda e, j=j: e.scalar_tensor_tensor(out=out_ap, in0=xw[:, j:TB + j], scalar=V(f"gcw{i}_{j}", tile),
                                                               in1=out_ap, op0=ALU.mult, op1=ALU.add),
                  reads=[t_xw, t_vec, out_tok], writes=[out_tok])
        P.add("dve", lambda e: e.tensor_copy(out=gtail[:, tile, :], in_=xw[:, TB:TB + 3]), reads=[t_xw], writes=[t_gt[tile]])
        P.add("act", lambda e: e.activation(out=out_ap, in_=out_ap, func=AF.Silu), reads=[out_tok], writes=[out_tok])

    def l2norm(x_ap, x_tok, scale):
        P.add("act", lambda e: e.activation(out=bigb[:, 17, :], in_=x_ap, func=AF.Square), reads=[x_tok], writes=[tb[17]])
        g.rstd_from_sq([bigb[:, 17, :]], [tb[17]], TB, 1.0, EPS, 7)
        P.add("dve", lambda e: e.scalar_tensor_tensor(out=x_ap, in0=x_ap, scalar=scale, in1=rstd[:, :], op0=ALU.mult, op1=ALU.mult),
              reads=[x_tok, t_rstd], writes=[x_tok])

    def proj2(sl, tsl, b0, b1):
        def mm(e):
            ins = None
            for kt in range(KT):
                ins = e.matmul(ps[b0][:, :], lhsT=sl[:, kt * 256:kt * 256 + 128], rhs=hT[:, kt, :], start=(kt == 0), stop=(kt == KT - 1))
            for kt in range(KT):
                ins = e.matmul(ps[b1][:, :], lhsT=sl[:, kt * 256 + 128:kt * 256 + 256], rhs=hT[:, kt, :], start=(kt == 0), stop=(kt == KT - 1))
            return ins
        P.add("pe", mm, reads=[tsl, t_h], writes=[t_ps[b0], t_ps[b1]])

    def block(l, b):
        i = l // 2
        rot = g.PsRot([0, 1, 2, 3, 4])
        g.prenorm(b, f"nmp{l}", 7)
        if g.stop < 1:
            return
        wba, twba = g.load_slot(soff[f"g_wba{i}"], 8 * 32)
        bk = rot.get()

        def mmba(e):
            ins = None
            for tt in range(4):
                for kt in range(KT):
                    ins = e.matmul(ps[bk][:, tt * 32:(tt + 1) * 32], lhsT=hT[:, kt, tt * 128:(tt + 1) * 128],
                                   rhs=wba[:, kt * 32:(kt + 1) * 32], start=(kt == 0), stop=(kt == KT - 1))
            return ins
        P.add("pe", mmba, reads=[twba, t_h], writes=[t_ps[bk]])
        for tt in range(4):
            P.add("act", lambda e, tt=tt: e.activation(out=beta[:, tt, :], in_=ps[bk][:, tt * 32:tt * 32 + 16], func=AF.Sigmoid),
                  reads=[t_ps[bk]], writes=[t_gate])
            P.add("dve", lambda e, tt=tt: e.tensor_tensor(out=gg[:, tt, :], in0=ps[bk][:, tt * 32 + 16:tt * 32 + 32],
                                                          in1=V(f"dtb{i}", 0, 16), op=ALU.add),
                  reads=[t_ps[bk], t_vec], writes=[t_gate])
        P.add("act", lambda e: e.activation(out=gg[:, :, :], in_=gg[:, :, :], func=AF.Exp), reads=[t_gate], writes=[t_gate])
        P.add("act", lambda e: e.activation(out=gg[:, :, :], in_=gg[:, :, :], func=AF.Ln, bias=g.one_ap),
              reads=[t_gate, g.t_cb2], writes=[t_gate])
        for tt in range(4):
            P.add("dve", lambda e, tt=tt: e.tensor_tensor(out=gg[:, tt, :], in0=gg[:, tt, :], in1=nexpA[:, :], op=ALU.mult),
                  reads=[t_gate, t_nexp], writes=[t_gate])
        P.add("dve", lambda e: e.tensor_scalar(out=nbeta[:, :, :], in0=beta[:, :, :], scalar1=-1.0, scalar2=None, op0=ALU.mult),
              reads=[t_gate], writes=[t_gate])
        for p in range(4):
            bc = rot.get()

            def mmc(e, p=p, bc=bc):
                e.matmul(ps[bc][:, 0:16], lhsT=C("U2"), rhs=gg[:, p, :], start=True, stop=True)
                e.matmul(ps[bc][:, 16:32], lhsT=C("SL2"), rhs=gg[:, p, :], start=True, stop=True)
                e.matmul(ps[bc][:, 32:48], lhsT=C("ONA"), rhs=gg[:, p, :], start=True, stop=True)
                return e.matmul(ps[bc][:, 48:64], lhsT=C("ONB"), rhs=gg[:, p, :], start=True, stop=True)
            P.add("pe", mmc, reads=[t_gate, t_cst, t_cb], writes=[t_ps[bc]])

            def ex(e, p=p, bc=bc):
                e.activation(out=eg[:, p, :], in_=ps[bc][:, 0:16], func=AF.Exp)
                e.activation(out=egr[:, p, :], in_=ps[bc][:, 16:32], func=AF.Exp)
                e.activation(out=dch[:, 2 * p, :], in_=ps[bc][:, 32:48], func=AF.Exp)
                return e.activation(out=dch[:, 2 * p + 1, :], in_=ps[bc][:, 48:64], func=AF.Exp)
            P.add("act", ex, reads=[t_ps[bc]], writes=[t_gate])
        if g.stop < 2:
            return
        qf, kf = bigf[:, 0, :], bigf[:, 1, :]
        vfs = [bigf[:, 2, :], bigf[:, 3, :]]
        zfs = [bigf[:, 4, :], bigf[:, 5, :]]
        for j in range(g.dbg[0] + 1 if g.dbg else 8):
            s0 = soff[f"g_win{i}"] + 3 * j
            sl, tsl = g.load_slot(s0)
            b0, b1 = rot.get(), rot.get()
            proj2(sl, tsl, b0, b1)
            conv_silu(i, b0, j, qf, tf[0])
            conv_silu(i, b1, 8 + j, kf, tf[1])
            l2norm(qf, tf[0], QSCALE)
            l2norm(kf, tf[1], 1.0)
            sl, tsl = g.load_slot(s0 + 1)
            b0, b1 = rot.get(), rot.get()
            proj2(sl, tsl, b0, b1)
            conv_silu(i, b0, 16 + 2 * j, vfs[0], tf[2])
            conv_silu(i, b1, 16 + 2 * j + 1, vfs[1], tf[3])
            sl, tsl = g.load_slot(s0 + 2)
            b0, b1 = rot.get(), rot.get()
            proj2(sl, tsl, b0, b1)
            P.add("act", lambda e, b0=b0: e.activation(out=zfs[0], in_=ps[b0][:, :], func=AF.Silu), reads=[t_ps[b0]], writes=[tf[4]])
            P.add("act", lambda e, b1=b1: e.activation(out=zfs[1], in_=ps[b1][:, :], func=AF.Silu), reads=[t_ps[b1]], writes=[tf[5]])
            for p in range(4):
                cols = slice(p * 128, (p + 1) * 128)
                bkk = rot.get()

                def mmr(e, cols=cols, bkk=bkk):
                    e.matmul(ps[bkk][:, 0:128], lhsT=kf[:, cols], rhs=kf[:, cols], start=True, stop=True)
                    e.matmul(ps[bkk][:, 128:256], lhsT=kf[:, cols], rhs=qf[:, cols], start=True, stop=True)
                    e.transpose(ps[bkk][:, 256:384], qf[:, cols], ident)
                    return e.transpose(ps[bkk][:, 384:512], kf[:, cols], ident)
                P.add("pe", mmr, reads=[tf[0], tf[1], t_cst], writes=[t_ps[bkk]])

                def ev1(e, p=p, bkk=bkk):
                    e.activation(out=KKs[:, p, :], in_=ps[bkk][:, 0:128], func=AF.Copy)
                    return e.activation(out=Qt[:, p, :], in_=ps[bkk][:, 256:384], func=AF.Copy)
                P.add("act", ev1, reads=[t_ps[bkk]], writes=[t_pair[p]])

                def ev2(e, p=p, bkk=bkk):
                    e.tensor_copy(out=QKs[:, p, :], in_=ps[bkk][:, 128:256])
                    return e.tensor_copy(out=Ktt[:, p, :], in_=ps[bkk][:, 384:512])
                P.add("dve", ev2, reads=[t_ps[bkk]], writes=[t_pair[p]])
            for half in range(2 if g.stop >= 3 else 0):
                combos = [(hh, 2 * half + q, CH[hh * 2 + q]) for hh in range(2) for q in range(2)]
                bvts = {}
                for hh in range(2):
                    vf, tvf = vfs[hh], tf[2 + hh]
                    bvt = rot.get()
                    bvts[hh] = bvt

                    def mmvt(e, vf=vf, bvt=bvt, half=half):
                        ins = None
                        for q in range(2):
                            p = 2 * half + q
                            ins = e.transpose(ps[bvt][:, q * 128:(q + 1) * 128], vf[:, p * 128:(p + 1) * 128], ident)
                        return ins
                    P.add("pe", mmvt, reads=[tvf, t_cst], writes=[t_ps[bvt]])
                for hh, p, c in combos:
                    h = 2 * j + hh
                    q = p - 2 * half
                    bvt = bvts[hh]
                    P.add("dve", lambda e, c=c, p=p, h=h: e.tensor_scalar(out=c.X[1][:, :], in0=C("U2"), scalar1=gg[:, p, h:h + 1],
                                                                          scalar2=None, op0=ALU.mult),
                          reads=[t_cst, t_gate], writes=[c.tX[1]])
                    P.add("dve", lambda e, c=c, p=p, h=h: e.tensor_scalar(out=c.Y[1][:, :], in0=C("SL2"), scalar1=gg[:, p, h:h + 1],
                                                                          scalar2=None, op0=ALU.mult),
                          reads=[t_cst, t_gate], writes=[c.tY[1]])
                    P.add("dve", lambda e, c=c, p=p, h=h: e.tensor_scalar(out=c.Kbe[:, :], in0=Ktt[:, p, :], scalar1=beta[:, p, h:h + 1],
                                                                          scalar2=eg[:, p, h:h + 1], op0=ALU.mult, op1=ALU.mult),
                          reads=[t_pair[p], t_gate], writes=[c.tKbe])
                    P.add("dve", lambda e, c=c, p=p, h=h: e.tensor_scalar(out=c.kdec[:, :], in0=Ktt[:, p, :], scalar1=egr[:, p, h:h + 1],
                                                                          scalar2=C("ONA", 128, 1), op0=ALU.mult, op1=ALU.mult),
                          reads=[t_pair[p], t_gate, t_cst], writes=[c.tkdec])
                    P.add("dve", lambda e, c=c, p=p, h=h: e.tensor_scalar(out=c.kdecB[:, :], in0=Ktt[:, p, :], scalar1=egr[:, p, h:h + 1],
                                                                          scalar2=C("ONB", 128, 1), op0=ALU.mult, op1=ALU.mult),
                          reads=[t_pair[p], t_gate, t_cst], writes=[c.tkdec])
                    P.add("dve", lambda e, c=c, p=p, h=h, bvt=bvt, q=q: e.tensor_scalar(out=c.Vb[:, :], in0=ps[bvt][:, q * 128:(q + 1) * 128],
                                                                                        scalar1=beta[:, p, h:h + 1], scalar2=None, op0=ALU.mult),
                          reads=[t_ps[bvt], t_gate], writes=[c.tVb])
                for hh, p, c in combos:
                    bD = rot.get()

                    def mmD(e, c=c, bD=bD):
                        e.matmul(ps[bD][:, 0:128], lhsT=c.X[1][:, :], rhs=C("SL2"), start=True, stop=False)
                        e.matmul(ps[bD][:, 0:128], lhsT=ident, rhs=C("NMS"), start=False, stop=True)
                        e.matmul(ps[bD][:, 128:256], lhsT=c.Y[1][:, :], rhs=C("U2"), start=True, stop=False)
                        return e.matmul(ps[bD][:, 128:256], lhsT=ident, rhs=C("NMC"), start=False, stop=True)
                    P.add("pe", mmD, reads=[c.tX[1], c.tY[1], t_cst], writes=[t_ps[bD]])
                    P.add("act", lambda e, c=c, bD=bD: e.activation(out=c.R[1][:, :], in_=ps[bD][:, 0:128], func=AF.Exp),
                          reads=[t_ps[bD]], writes=[c.tR[1]])
                    P.add("act", lambda e, c=c, bD=bD: e.activation(out=c.QKT[:, :], in_=ps[bD][:, 128:256], func=AF.Exp),
                          reads=[t_ps[bD]], writes=[c.tQKT])
                for hh, p, c in combos:
                    h = 2 * j + hh
                    P.add("dve", lambda e, c=c, p=p, h=h: e.scalar_tensor_tensor(out=c.X[0][:, :], in0=KKs[:, p, :], scalar=nbeta[:, p, h:h + 1],
                                                                                 in1=c.R[1][:, :], op0=ALU.mult, op1=ALU.mult),
                          reads=[t_pair[p], t_gate, c.tR[1]], writes=[c.tX[0]])
                    P.add("dve", lambda e, c=c, p=p: e.tensor_tensor(out=c.QKT[:, :], in0=QKs[:, p, :], in1=c.QKT[:, :], op=ALU.mult),
                          reads=[t_pair[p], c.tQKT], writes=[c.tQKT])
                bTs = []
                for hh, p, c in combos:
                    bT = rot.get()
                    bTs.append(bT)
                    P.add("pe", lambda e, c=c, bT=bT: e.transpose(ps[bT][:, 0:128], c.X[0][:, :], ident),
                          reads=[c.tX[0], t_cst], writes=[t_ps[bT]])
                for (hh, p, c), bT in zip(combos, bTs):
                    P.add("act", lambda e, c=c, bT=bT: e.activation(out=c.Y[0][:, :], in_=ps[bT][:, 0:128], func=AF.Copy),
                          reads=[t_ps[bT]], writes=[c.tY[0]])
                    P.add("dve", lambda e, c=c: e.tensor_tensor(out=c.R[0][:, :], in0=c.Y[0][:, :], in1=ident, op=ALU.add),
                          reads=[c.tY[0], t_cst], writes=[c.tR[0]])
                a, ra = 0, 0
                for k in range(1, 6):
                    b1s = []
                    for hh, p, c in combos:
                        b1_ = rot.get()
                        b1s.append(b1_)

                        def mmsq(e, c=c, b1_=b1_, a=a, k=k):
                            ins = e.matmul(ps[b1_][:, 0:128], lhsT=c.Y[a][:, :], rhs=c.X[a][:, :], start=True, stop=True)
                            if k < 5:
                                ins = e.matmul(ps[b1_][:, 128:256], lhsT=c.X[a][:, :], rhs=c.Y[a][:, :], start=True, stop=True)
                            return ins
                        P.add("pe", mmsq, reads=[c.tX[a], c.tY[a]], writes=[t_ps[b1_]])
                    for (hh, p, c), b1_ in zip(combos, b1s):
                        if k < 5:
                            P.add("act", lambda e, c=c, b1_=b1_, a=a: e.activation(out=c.XY[1 - a][:, 0:256], in_=ps[b1_][:, 0:256], func=AF.Copy),
                                  reads=[t_ps[b1_]], writes=[c.tX[1 - a], c.tY[1 - a]])
                        else:
                            P.add("act", lambda e, c=c, b1_=b1_, a=a: e.activation(out=c.X[1 - a][:, :], in_=ps[b1_][:, 0:128], func=AF.Copy),
                                  reads=[t_ps[b1_]], writes=[c.tX[1 - a]])
                    b2s = []
                    for hh, p, c in combos:
                        b2_ = rot.get()
                        b2s.append(b2_)

                        def mmR(e, c=c, b2_=b2_, a=a, ra=ra):
                            return e.matmul(ps[b2_][:, 0:128], lhsT=c.X[1 - a][:, :], rhs=c.R[ra][:, :], start=True, stop=True)
                        P.add("pe", mmR, reads=[c.tX[1 - a], c.tR[ra], t_cst], writes=[t_ps[b2_]])
                    for (hh, p, c), b2_ in zip(combos, b2s):
                        P.add("dve", lambda e, c=c, b2_=b2_, ra=ra: e.tensor_tensor(out=c.R[1 - ra][:, :], in0=c.R[ra][:, :], in1=ps[b2_][:, 0:128],
                                                                                   op=ALU.add),
                              reads=[c.tR[ra], t_ps[b2_]], writes=[c.tR[1 - ra]])
                    a, ra = 1 - a, 1 - ra
                bws = []
                for hh, p, c in combos:
                    h = 2 * j + hh
                    bw = rot.get()
                    bws.append(bw)

                    def mmw(e, c=c, bw=bw, ra=ra):
                        e.matmul(ps[bw][:, 0:128], lhsT=c.Kbe[:, :], rhs=c.R[ra][:, :], start=True, stop=True)
                        return e.matmul(ps[bw][:, 128:256], lhsT=c.R[ra][:, :], rhs=c.Vb[:, :], start=True, stop=True)
                    P.add("pe", mmw, reads=[c.tKbe, c.tVb, c.tR[ra]], writes=[t_ps[bw]])
                    P.add("dve", lambda e, c=c, p=p, h=h: e.tensor_scalar(out=c.Y[1][:, :], in0=ident, scalar1=eg[:, p, h:h + 1], scalar2=None,
                                                                          op0=ALU.mult),
                          reads=[t_cst, t_gate, c.tY[1]], writes=[c.tY[1]])
                for (hh, p, c), bw in zip(combos, bws):
                    P.add("act", lambda e, c=c, bw=bw: e.activation(out=c.X[0][:, :], in_=ps[bw][:, 0:128], func=AF.Identity, scale=-1.0),
                          reads=[t_ps[bw]], writes=[c.tX[0]])
                    P.add("act", lambda e, c=c, bw=bw: e.activation(out=c.X[1][:, :], in_=ps[bw][:, 128:256], func=AF.Copy),
                          reads=[t_ps[bw]], writes=[c.tX[1]])
                bqs = []
                for hh, p, c in combos:
                    bq = rot.get()
                    bqs.append(bq)
                    P.add("pe", lambda e, c=c, bq=bq, p=p: e.matmul(ps[bq][:, 0:128], lhsT=Qt[:, p, :], rhs=c.Y[1][:, :], start=True, stop=True),
                          reads=[t_pair[p], c.tY[1]], writes=[t_ps[bq]])
                for (hh, p, c), bq in zip(combos, bqs):
                    P.add("act", lambda e, c=c, bq=bq: e.activation(out=c.Y[0][:, :], in_=ps[bq][:, 0:128], func=AF.Copy),
                          reads=[t_ps[bq]], writes=[c.tY[0]])
                for lc in range(4):
                    cc = 4 * half + lc
                    p, s_ = cc // 2, cc % 2
                    for hh in range(2):
                        h = 2 * j + hh
                        c = CH[hh * 2 + (p - 2 * half)]
                        Sh = S_all[:, h, :]
                        ob = 6 if hh == 0 else 5
                        r0 = 64 * s_
                        rows = slice(r0, r0 + 64)
                        kd = c.kdec if s_ == 0 else c.kdecB
                        bw_ = rot.get()
                        def mmv(e, c=c, bw_=bw_, Sh=Sh):
                            return e.matmul(ps[bw_][:, 0:128], lhsT=c.X[0][:, :], rhs=Sh, start=True, stop=True)
                        P.add("pe", mmv, reads=[c.tX[0], t_S[h]], writes=[t_ps[bw_]])
                        P.add("dve", lambda e, c=c, bw_=bw_, rows=rows: e.tensor_tensor(out=c.X[1][rows, :], in0=c.X[1][rows, :],
                                                                                       in1=ps[bw_][rows, 0:128], op=ALU.add),
                              reads=[c.tX[1], t_ps[bw_]], writes=[c.tX[1]])
                        bsu = rot.get()

                        def mmo(e, c=c, cc=cc, rows=rows, bsu=bsu, Sh=Sh, kd=kd, ob=ob):
                            e.matmul(ps[ob][:, cc * 64:(cc + 1) * 64], lhsT=Sh, rhs=c.Y[0][:, rows], start=True, stop=False)
                            e.matmul(ps[ob][:, cc * 64:(cc + 1) * 64], lhsT=c.X[1][:, :], rhs=c.QKT[:, rows], start=False, stop=True)
                            return e.matmul(ps[bsu][:, 0:128], lhsT=kd[:, :], rhs=c.X[1][:, :], start=True, stop=True)
                        P.add("pe", mmo, reads=[t_S[h], c.tY[0], c.tX[1], c.tQKT, c.tkdec], writes=[t_ps[ob], t_ps[bsu]])
                        P.add("dve", lambda e, cc=cc, h=h, bsu=bsu, Sh=Sh: e.scalar_tensor_tensor(out=Sh, in0=Sh, scalar=dch[:, cc, h:h + 1],
                                                                                          in1=ps[bsu][:, 0:128], op0=ALU.mult, op1=ALU.add),
                              reads=[t_S[h], t_gate, t_ps[bsu]], writes=[t_S[h]])
            for hh in range(2 if g.stop >= 3 else 0):
                h = 2 * j + hh
                ob = 6 if hh == 0 else 5
                P.add("act", lambda e, ob=ob: e.activation(out=bigb[:, 16, :], in_=ps[ob][:, :], func=AF.Square), reads=[t_ps[ob]], writes=[tb[16]])
                g.rstd_from_sq([bigb[:, 16, :]], [tb[16]], TB, 128.0, EPS, 7)
                P.add("act", lambda e, ob=ob: e.activation(out=bigf[:, 7, :], in_=ps[ob][:, :], func=AF.Identity, scale=V(f"gn{i}")),
                      reads=[t_ps[ob], t_vec], writes=[tf[7]])
                P.add("dve", lambda e: e.tensor_tensor(out=bigf[:, 6, :], in0=bigf[:, 7, :], in1=rstd[:, :], op=ALU.mult),
                      reads=[tf[7], t_rstd], writes=[tf[6]])
                P.add("dve", lambda e, h=h, hh=hh: e.tensor_tensor(out=bigb[:, h, :], in0=bigf[:, 6, :], in1=zfs[hh], op=ALU.mult),
                      reads=[tf[6], tf[4 + hh]], writes=[tb[h]])
        if g.dbg or g.stop < 6:
            return
        for m in range(8):
            sl, tsl = g.load_slot(soff[f"g_wout{i}"] + m)
            bk = rot.get()

            def mmo2(e, sl=sl, bk=bk):
                ins = None
                for kt in range(16):
                    ins = e.matmul(ps[bk][:, :], lhsT=sl[:, kt * 128:(kt + 1) * 128], rhs=bigb[:, kt, :],
                                   start=(kt == 0), stop=(kt == 15))
                return ins
            P.add("pe", mmo2, reads=[tsl] + tb[0:16], writes=[t_ps[bk]])
            g.post_evac(bk, m, f"nmo{l}")
        g.post_finish(b, 7)

    S.begin_layer = begin_layer
    S.block = block
    return S


_CACHE = {}


def kernel(**inputs):
    inp = {k: np.asarray(v) for k, v in inputs.items()}
    W = pack_weights(inp)
    Vv = pack_vecs(inp)
    Cc = make_consts()
    if "nc" not in _CACHE:
        _CACHE["nc"] = build_program()[0]
    nc = _CACHE["nc"]
    in_maps = []
    for b in range(8):
        x = np.asarray(inp["x"][b], np.float32)
        xT = np.ascontiguousarray(x.T.reshape(8, 128, T).transpose(1, 0, 2))
        pos = np.ascontiguousarray(np.broadcast_to(np.asarray(inp["positions"][b], np.int32)[None, :], (64, T)))
        in_maps.append({"xT": xT, "pos": pos, "vecs": Vv, "cst": Cc, "wts": W})
    res = run_bass_kernel_spmd(nc, in_maps, core_ids=list(range(8)))
    out = np.stack([np.asarray(r["yT"]).transpose(1, 0, 2).reshape(D, T).T for r in res.results])
    return np.ascontiguousarray(out.astype(np.float32))
```

```python
import contextlib
import numpy as np
import concourse.bass as bass
import concourse.mybir as mybir
from concourse.bass_utils import run_bass_kernel_spmd

F32 = mybir.dt.float32
BF16 = mybir.dt.bfloat16
I32 = mybir.dt.int32
AF = mybir.ActivationFunctionType
ALU = mybir.AluOpType
AX = mybir.AxisListType

ENGS = ("pe", "dve", "act", "pool", "sp")
EPOCH = 24000


class Tok:
    __slots__ = ("lw", "rd", "name")

    def __init__(self, name=""):
        self.lw = None
        self.rd = {}
        self.name = name


class Op:
    __slots__ = ("eng", "fn", "waits", "inc", "known", "dma")

    def __init__(self, eng, fn):
        self.eng = eng
        self.fn = fn
        self.waits = []
        self.inc = False
        self.known = None
        self.dma = None


class Prog:
    def __init__(self, nc, same_engine_sync=True):
        self.nc = nc
        self.ops = {e: [] for e in ENGS}
        self.known = {e: {} for e in ENGS}
        self.dma_cnt = {}
        self.dma_ops = {}
        self.same_engine_sync = same_engine_sync
        self.nwaits = 0

    def _lookup(self, s, p):
        if isinstance(s, tuple):
            return self.dma_ops[s][p - 1]
        return self.ops[s][p]

    def add(self, eng, fn, reads=(), writes=(), dma_key=None):
        ops = self.ops[eng]
        idx = len(ops)
        op = Op(eng, fn)
        deps = {}
        for t in reads:
            if t.lw is not None:
                s, p = t.lw
                if deps.get(s, -1) < p:
                    deps[s] = p
        for t in writes:
            if t.lw is not None:
                s, p = t.lw
                if deps.get(s, -1) < p:
                    deps[s] = p
            for s, p in t.rd.items():
                if deps.get(s, -1) < p:
                    deps[s] = p
        known = self.known[eng]
        for s, p in deps.items():
            if s == eng and dma_key is None and (eng == "pe" or not self.same_engine_sync):
                continue
            if known.get(s, -1) >= p:
                continue
            op.waits.append((s, p))
            known[s] = p
            src = self._lookup(s, p)
            for s2, p2 in src.known.items():
                if known.get(s2, -1) < p2:
                    known[s2] = p2
            src.inc = True
        self.nwaits += len(op.waits)
        if dma_key is None:
            mypos = (eng, idx)
        else:
            k = ("dma", dma_key)
            n = self.dma_cnt.get(k, 0) + 1
            self.dma_cnt[k] = n
            self.dma_ops.setdefault(k, []).append(op)
            mypos = (k, n)
            op.dma = k
        op.known = dict(known)
        for t in reads:
            if t.rd.get(mypos[0], -1) < mypos[1]:
                t.rd[mypos[0]] = mypos[1]
        for t in writes:
            t.lw = mypos
            t.rd = {}
        ops.append(op)
        return op

    def emit(self, stack):
        nc = self.nc
        pref = {}
        nsem = {}
        for e in ENGS:
            c = 0
            arr = []
            for op in self.ops[e]:
                if op.inc and op.dma is None:
                    c += 1
                arr.append(c)
            pref[e] = arr
            nsem[e] = (c + EPOCH - 1) // EPOCH
        sems = {e: [stack.enter_context(nc.semaphore(f"s_{e}_{i}")) for i in range(nsem[e])]
                for e in ENGS}
        dsems = {k: stack.enter_context(nc.semaphore("d_%d" % i))
                 for i, k in enumerate(self.dma_cnt)}
        self.n_sems = sum(nsem.values()) + len(dsems)
        block = stack.enter_context(nc.Block())
        hw = {"pe": block.tensor, "dve": block.vector, "act": block.scalar,
              "pool": block.gpsimd, "sp": block.sync}

        def run(e):
            def body(engine):
                for op in self.ops[e]:
                    for s, p in op.waits:
                        if isinstance(s, tuple):
                            engine.wait_ge(dsems[s], 16 * p)
                        else:
                            c = pref[s][p]
                            engine.wait_ge(sems[s][(c - 1) // EPOCH], (c - 1) % EPOCH + 1)
                    if op.fn is None:
                        continue
                    ins = op.fn(engine)
                    if op.dma is not None:
                        ins.then_inc(dsems[op.dma], 16)
                    elif op.inc:
                        c = pref[e][self.ops[e].index(op)] if False else None
                        ins.then_inc(sems[e][(op_c[id(op)] - 1) // EPOCH], 1)
            return body

        op_c = {}
        for e in ENGS:
            for i, op in enumerate(self.ops[e]):
                if op.inc and op.dma is None:
                    op_c[id(op)] = pref[e][i]
        for e in ENGS:
            hw[e](run(e))


T = 2048
D = 1024
KT = 8
TB = 512
NB = T // TB
DEPTH = 4
FF = 2816
FT = 22
SLOT_EL = 2048
EPS = 1e-6
NEG = -30000.0
TWO_PI = 6.283185307179586


def vec_layout():
    off = {}
    c = 0

    def put(name, n):
        nonlocal c
        off[name] = c
        c += n
    for l in range(4):
        for nm in ("nmp", "nmo", "nfp", "nfo"):
            put(f"{nm}{l}", 8)
    for i in range(2):
        for j in range(4):
            put(f"cw{i}_{j}", 8)
        for nm in ("cb", "gab", "gxb", "lam"):
            put(f"{nm}{i}", 8)
        put(f"qn{i}", 4)
        put(f"kvn{i}", 2)
    for i in range(2):
        for j in range(4):
            put(f"gcw{i}_{j}", 32)
        put(f"gn{i}", 1)
        put(f"alog{i}", 16)
        put(f"dtb{i}", 16)
    put("invf", 1)
    put("sgn", 1)
    return off, c


def cst_layout():
    off = {}
    c = 0
    for nm in ("ident", "amask", "U2", "SL2", "NMS", "NMC", "BD", "ONA", "ONB"):
        off[nm] = c
        c += 128
    return off, c


def slot_layout():
    off = {}
    c = 0

    def put(name, n):
        nonlocal c
        off[name] = c
        c += n
    for i in range(2):
        put(f"e_win{i}", 12)
        put(f"e_gate{i}", 1)
        put(f"e_uq{i}", 4)
        put(f"e_ukv{i}", 2)
        put(f"e_wout{i}", 8)
    for i in range(2):
        put(f"g_win{i}", 24)
        put(f"g_wba{i}", 1)
        put(f"g_wout{i}", 8)
    for l in range(4):
        put(f"f_gu{l}", 22)
        put(f"f_down{l}", 16)
    return off, c


def _tile_w(W, k0, nk, cols):
    sub = W[k0:k0 + nk * 128][:, cols]
    return sub.reshape(nk, 128, -1).transpose(1, 0, 2).reshape(128, -1)


def _pad(a):
    out = np.zeros((128, SLOT_EL), np.float32)
    out[:, :a.shape[1]] = a
    return out


def pack_weights(inp):
    soff, ns = slot_layout()
    W = np.zeros((ns, 128, SLOT_EL), np.float32)
    ar = np.arange
    for i in range(2):
        win = inp["hy_w_in"][i]
        s0 = soff[f"e_win{i}"]
        for n in range(8):
            cols = np.concatenate([ar(n * 128, n * 128 + 128), ar(1024 + n * 128, 1024 + n * 128 + 128)])
            W[s0 + n] = _pad(_tile_w(win, 0, 8, cols))
        W[s0 + 8] = _pad(_tile_w(win, 0, 8, ar(2048, 2304)))
        W[s0 + 9] = _pad(_tile_w(win, 0, 8, ar(2304, 2560)))
        W[s0 + 10] = _pad(_tile_w(win, 0, 8, ar(2560, 2816)))
        cols = np.concatenate([ar(2816, 2880), ar(2848, 2880), ar(2816, 2848)])
        W[s0 + 11] = _pad(_tile_w(win, 0, 8, cols))
        ga, gx = inp["rg_gate_a_w"][i], inp["rg_gate_x_w"][i]
        g = np.concatenate([ga, gx], axis=2)
        W[soff[f"e_gate{i}"]] = _pad(g.transpose(1, 0, 2).reshape(128, -1))
        uq = inp["mla_w_uq"][i]
        for s in range(4):
            parts = []
            for hh in range(2):
                h = 2 * s + hh
                b = h * 192
                cols = np.concatenate([ar(b, b + 192), ar(b + 160, b + 192), ar(b + 128, b + 160)])
                parts.append(_tile_w(uq, 0, 4, cols))
            W[soff[f"e_uq{i}"] + s] = _pad(np.concatenate(parts, axis=1))
        ukv = inp["mla_w_ukv"][i]
        for s in range(2):
            parts = [_tile_w(ukv, 0, 2, ar((4 * s + hh) * 256, (4 * s + hh) * 256 + 256)) for hh in range(4)]
            W[soff[f"e_ukv{i}"] + s] = _pad(np.concatenate(parts, axis=1))
        wo = inp["hy_w_out"][i]
        for m in range(8):
            W[soff[f"e_wout{i}"] + m] = _pad(_tile_w(wo, 0, 16, ar(m * 128, m * 128 + 128)))
    for i in range(2):
        win = inp["gdn_w_in"][i]
        s0 = soff[f"g_win{i}"]
        for j in range(8):
            q = ar(j * 128, j * 128 + 128)
            k = ar(1024 + j * 128, 1024 + j * 128 + 128)
            v = ar(2048 + 2 * j * 128, 2048 + 2 * j * 128 + 256)
            z = ar(4096 + 2 * j * 128, 4096 + 2 * j * 128 + 256)
            W[s0 + 3 * j] = _pad(_tile_w(win, 0, 8, np.concatenate([q, k])))
            W[s0 + 3 * j + 1] = _pad(_tile_w(win, 0, 8, v))
            W[s0 + 3 * j + 2] = _pad(_tile_w(win, 0, 8, z))
        W[soff[f"g_wba{i}"]] = _pad(_tile_w(win, 0, 8, ar(6144, 6176)))
        wo = inp["gdn_w_out"][i]
        for m in range(8):
            W[soff[f"g_wout{i}"] + m] = _pad(_tile_w(wo, 0, 16, ar(m * 128, m * 128 + 128)))
    for l in range(4):
        wg, wu, wd = inp["ffn_w_gate"][l], inp["ffn_w_up"][l], inp["ffn_w_down"][l]
        for m in range(FT):
            c = ar(m * 128, m * 128 + 128)
            W[soff[f"f_gu{l}"] + m] = _pad(np.concatenate([_tile_w(wg, 0, 8, c), _tile_w(wu, 0, 8, c)], axis=1)
                                           .reshape(128, 2, 8, 128).transpose(0, 2, 1, 3).reshape(128, -1))
        for m in range(8):
            for hf in range(2):
                W[soff[f"f_down{l}"] + 2 * m + hf] = _pad(_tile_w(wd, hf * 11 * 128, 11, ar(m * 128, m * 128 + 128)))
    return W


def pack_vecs(inp):
    voff, nv = vec_layout()
    V = np.zeros((128, nv), np.float32)

    def v8(name, v):
        a = np.asarray(v, np.float32).reshape(-1, 128).T
        V[:, voff[name]:voff[name] + a.shape[1]] = a
    for l in range(4):
        v8(f"nmp{l}", inp["norm_mix_pre"][l])
        v8(f"nmo{l}", inp["norm_mix_post"][l])
        v8(f"nfp{l}", inp["norm_ffn_pre"][l])
        v8(f"nfo{l}", inp["norm_ffn_post"][l])
    for i in range(2):
        for j in range(4):
            v8(f"cw{i}_{j}", inp["rg_conv_w"][i][j])
        v8(f"cb{i}", inp["rg_conv_b"][i])
        v8(f"gab{i}", inp["rg_gate_a_b"][i])
        v8(f"gxb{i}", inp["rg_gate_x_b"][i])
        v8(f"lam{i}", inp["rg_lambda"][i])
        v8(f"qn{i}", inp["mla_q_norm"][i])
        v8(f"kvn{i}", inp["mla_kv_norm"][i])
    for i in range(2):
        for j in range(4):
            v8(f"gcw{i}_{j}", inp["gdn_conv_w"][i][j])
        v8(f"gn{i}", inp["gdn_norm"][i])
        V[:, voff[f"alog{i}"]:voff[f"alog{i}"] + 16] = np.asarray(inp["gdn_a_log"][i], np.float32)[None, :]
        V[:, voff[f"dtb{i}"]:voff[f"dtb{i}"] + 16] = np.asarray(inp["gdn_dt_bias"][i], np.float32)[None, :]
    inv_freq = (1.0 / (np.float32(10000.0) ** (np.arange(0, 64, 2, dtype=np.float32) / np.float32(64)))).astype(np.float32)
    V[:64, voff["invf"]] = np.concatenate([inv_freq, inv_freq])
    V[:32, voff["sgn"]] = -1.0
    V[32:64, voff["sgn"]] = 1.0
    return V


def make_consts():
    coff, ncc = cst_layout()
    C = np.zeros((128, ncc), np.float32)
    idx = np.arange(128)
    C[:, coff["ident"]:coff["ident"] + 128] = np.eye(128, dtype=np.float32)
    C[:, coff["amask"]:coff["amask"] + 128] = np.where(idx[:, None] <= idx[None, :], 0.0, NEG)
    same = (idx[:, None] // 64) == (idx[None, :] // 64)
    C[:, coff["U2"]:coff["U2"] + 128] = (same & (idx[:, None] <= idx[None, :])).astype(np.float32)
    C[:, coff["SL2"]:coff["SL2"] + 128] = (same & (idx[:, None] > idx[None, :])).astype(np.float32)
    C[:, coff["NMS"]:coff["NMS"] + 128] = np.where(same & (idx[:, None] > idx[None, :]), 0.0, NEG)
    C[:, coff["NMC"]:coff["NMC"] + 128] = np.where(same & (idx[:, None] <= idx[None, :]), 0.0, NEG)
    C[:, coff["BD"]:coff["BD"] + 128] = same.astype(np.float32)
    C[0:64, coff["ONA"]:coff["ONA"] + 128] = 1.0
    C[64:128, coff["ONB"]:coff["ONB"] + 128] = 1.0
    return C


class Ctx:
    pass


def build_program(layers=(0, 1, 2, 3), do_ffn=True, do_mix=True, cast_engs=("dve", "act", "dve"), dbg=False, stop=99, sub=99):
    nc = bass.Bass("TRN2", target_bir_lowering=False, dynamic_dma_scratch_size=1024)
    voff, nv = vec_layout()
    coff, ncc = cst_layout()
    soff, nslots = slot_layout()
    d_x = nc.dram_tensor("xT", [128, KT, T], F32, kind="ExternalInput").ap()
    d_pos = nc.dram_tensor("pos", [64, T], I32, kind="ExternalInput").ap()
    d_vec = nc.dram_tensor("vecs", [128, nv], F32, kind="ExternalInput").ap()
    d_cst = nc.dram_tensor("cst", [128, ncc], F32, kind="ExternalInput").ap()
    d_w = nc.dram_tensor("wts", [nslots, 128, SLOT_EL], F32, kind="ExternalInput").ap()
    d_y = nc.dram_tensor("yT", [128, KT, T], F32, kind="ExternalOutput").ap()
    with contextlib.ExitStack() as st:
        P = Prog(nc)
        g = Ctx()
        g.P, g.nc, g.voff, g.coff, g.soff = P, nc, voff, coff, soff
        g.dbg = dbg
        g.stop = stop
        g.sub = sub
        g.d_dbg = nc.dram_tensor("dbg", [8, 128, 512], F32, kind="ExternalOutput").ap() if dbg else None

        def sb(name, shape, dt):
            return st.enter_context(nc.sbuf_tensor(name, shape, dt))

        g.sb = sb
        xT = sb("xT_sb", [128, KT, T], F32)
        xtok = [[Tok() for _ in range(NB)] for _ in range(KT)]
        vec = sb("vec_sb", [128, nv], F32)
        t_vec = Tok()
        cst = sb("cst_sb", [128, ncc], F32)
        t_cst = Tok()
        identb = sb("identb", [128, 128], BF16)
        amaskb = sb("amaskb", [128, 128], BF16)
        onesb = sb("onesb", [128, 128], BF16)
        onesf = sb("onesf", [128, 128], F32)
        t_cb = Tok()
        hT = sb("hT", [128, KT, TB], BF16)
        t_h = Tok()
        bigb = sb("bigb", [128, 22, TB], BF16)
        tb = [Tok() for _ in range(22)]
        bigf = sb("bigf", [128, 8, TB], F32)
        tf = [Tok() for _ in range(8)]
        rstd = sb("rstd", [128, TB], F32)
        t_rstd = Tok()
        lnt = sb("lnt", [128, TB], F32)
        t_lnt = Tok()
        NSTG, NSL = 2, 3
        stage = [sb(f"stage{i}", [128, SLOT_EL], F32) for i in range(NSTG)]
        t_stage = [Tok() for _ in range(NSTG)]
        slots = [sb(f"slot{i}", [128, SLOT_EL], BF16) for i in range(NSL)]
        t_slot = [Tok() for _ in range(NSL)]
        ARENA32 = 11072
        arena = sb("arena", [128, ARENA32], F32)

        def mk_asb():
            off = [0]

            def asb(name, shape, dt):
                n = int(np.prod(shape[1:]))
                n32 = (n * (4 if dt in (F32, I32) else 2) + 3) // 4
                v = arena[:, off[0]:off[0] + n32]
                off[0] += n32
                assert off[0] <= ARENA32, (name, off[0])
                if dt != F32:
                    v = v.bitcast(dt)
                if len(shape) == 3:
                    v = v.rearrange("p (a b) -> p a b", b=shape[2])
                if shape[0] < 128:
                    v = v[0:shape[0]]
                return v
            return asb
        g.mk_asb = mk_asb
        g.arena_toks = []
        scr = sb("scr", [128, 2], F32)

        def arena_barrier():
            P.add("dve", lambda e: e.memset(scr[:, :], 0.0), writes=list(g.arena_toks))
        g.arena_barrier = arena_barrier
        ps = [st.enter_context(nc.psum_tensor(f"ps{i}", [128, 512], F32)) for i in range(8)]
        t_ps = [Tok() for _ in range(8)]
        g.xT, g.xtok, g.vec, g.t_vec, g.cst, g.t_cst = xT, xtok, vec, t_vec, cst, t_cst
        g.identb, g.amaskb, g.onesb, g.onesf, g.t_cb = identb, amaskb, onesb, onesf, t_cb
        g.hT, g.t_h, g.bigb, g.tb, g.bigf, g.tf = hT, t_h, bigb, tb, bigf, tf
        g.rstd, g.t_rstd, g.ps, g.t_ps = rstd, t_rstd, ps, t_ps
        g.d_pos = d_pos

        def V(name, k=0, n=1, rows=128):
            c = voff[name] + k
            return vec[0:rows, c:c + n]

        def C(name, rows=128, cols=128, c0=0):
            c = coff[name] + c0
            return cst[0:rows, c:c + cols]

        g.V, g.C = V, C

        class PsRot:
            def __init__(self, ids):
                self.ids = list(ids)
                self.i = 0

            def get(self):
                b = self.ids[self.i % len(self.ids)]
                self.i += 1
                return b
        g.PsRot = PsRot

        wctr = [0]

        def load_slot(idx, nel=SLOT_EL):
            k = wctr[0]
            wctr[0] += 1
            sg, sl = k % NSTG, k % NSL
            P.add("sp", lambda e: e.dma_start(out=stage[sg][:, 0:nel], in_=d_w[idx, :, 0:nel]),
                  writes=[t_stage[sg]], dma_key=("stg", sg))
            ce = cast_engs[k % len(cast_engs)]
            if ce == "act":
                fn = lambda e: e.copy(out=slots[sl][:, 0:nel], in_=stage[sg][:, 0:nel])
            else:
                fn = lambda e: e.tensor_copy(out=slots[sl][:, 0:nel], in_=stage[sg][:, 0:nel])
            P.add(ce, fn, reads=[t_stage[sg]], writes=[t_slot[sl]])
            return slots[sl], t_slot[sl]
        g.load_slot = load_slot

        P.add("sp", lambda e: e.dma_start(out=vec[:, :], in_=d_vec[:, :]), writes=[t_vec], dma_key="vec")
        P.add("sp", lambda e: e.dma_start(out=cst[:, :], in_=d_cst[:, :]), writes=[t_cst], dma_key="cst")
        for kt in range(KT):
            P.add("sp", lambda e, kt=kt: e.dma_start(out=xT[:, kt, :], in_=d_x[:, kt, :]),
                  writes=xtok[kt], dma_key=("x", kt))

        def setup_consts(e):
            e.tensor_copy(out=identb[:, :], in_=C("ident"))
            e.tensor_copy(out=amaskb[:, :], in_=C("amask"))
            e.memset(onesb[:, :], 1.0)
            return e.memset(onesf[:, :], 1.0)
        P.add("dve", setup_consts, reads=[t_cst], writes=[t_cb])

        def rstd_from_sq(sq_aps, sq_toks, n, dsum, eps, bank):
            def mm(e):
                ins = None
                for i, a in enumerate(sq_aps):
                    ins = e.matmul(ps[bank][:, 0:n], lhsT=onesb[:, :], rhs=a,
                                   start=(i == 0), stop=(i == len(sq_aps) - 1))
                return ins
            P.add("pe", mm, reads=list(sq_toks) + [t_cb], writes=[t_ps[bank]])
            P.add("act", lambda e: e.activation(out=lnt[:, 0:n], in_=ps[bank][:, 0:n], func=AF.Ln,
                                                scale=1.0 / dsum, bias=g.eps_ap),
                  reads=[t_ps[bank], t_cb2], writes=[t_lnt])
            P.add("act", lambda e: e.activation(out=rstd[:, 0:n], in_=lnt[:, 0:n], func=AF.Exp, scale=-0.5),
                  reads=[t_lnt], writes=[t_rstd])
        g.rstd_from_sq = rstd_from_sq
        epst = sb("epst", [128, 2], F32)
        t_cb2 = Tok()
        g.eps_ap = epst[:, 0:1]
        g.one_ap = epst[:, 1:2]
        g.t_cb2 = t_cb2

        def setup_eps(e):
            e.memset(epst[:, 0:1], EPS)
            return e.memset(epst[:, 1:2], 1.0)
        P.add("dve", setup_eps, writes=[t_cb2])

        def prenorm(b, wname, bank):
            blk = slice(b * TB, (b + 1) * TB)
            P.add("act", lambda e: e.activation(out=bigb[:, 0:8, :], in_=xT[:, :, blk], func=AF.Square),
                  reads=[xtok[kt][b] for kt in range(KT)], writes=tb[0:8])
            rstd_from_sq([bigb[:, kt, :] for kt in range(KT)], tb[0:8], TB, float(D), EPS, bank)

            def f(e):
                ins = None
                for kt in range(KT):
                    ins = e.scalar_tensor_tensor(out=hT[:, kt, :], in0=xT[:, kt, blk], scalar=V(wname, kt),
                                                 in1=rstd[:, :], op0=ALU.mult, op1=ALU.mult)
                return ins
            P.add("dve", f, reads=[xtok[kt][b] for kt in range(KT)] + [t_rstd, t_vec], writes=[t_h])
        g.prenorm = prenorm

        def post_evac(bank, m, wname):
            P.add("act", lambda e: e.activation(out=bigf[:, m, :], in_=ps[bank][:, :], func=AF.Identity,
                                                scale=V(wname, m)),
                  reads=[t_ps[bank], t_vec], writes=[tf[m]])
            P.add("act", lambda e: e.activation(out=hT[:, m, :], in_=ps[bank][:, :], func=AF.Square),
                  reads=[t_ps[bank]], writes=[t_h])
        g.post_evac = post_evac

        def post_finish(b, bank):
            blk = slice(b * TB, (b + 1) * TB)
            rstd_from_sq([hT[:, kt, :] for kt in range(KT)], [t_h], TB, float(D), EPS, bank)
            for kt in range(KT):
                P.add("dve", lambda e, kt=kt: e.tensor_tensor(out=bigf[:, kt, :], in0=bigf[:, kt, :], in1=rstd[:, :],
                                                              op=ALU.mult),
                      reads=[tf[kt], t_rstd], writes=[tf[kt]])
                P.add("dve", lambda e, kt=kt: e.tensor_tensor(out=xT[:, kt, blk], in0=xT[:, kt, blk],
                                                              in1=bigf[:, kt, :], op=ALU.add),
                      reads=[tf[kt], xtok[kt][b]], writes=[xtok[kt][b]])
        g.post_finish = post_finish

        sgt = [sb(f"sgt{i}", [128, TB], F32) for i in range(2)]
        t_sgt = [Tok(), Tok()]

        def ffn_block(l, b):
            rot = PsRot([0, 1, 2, 3, 4, 5])
            prenorm(b, f"nfp{l}", 7)
            for m in range(FT):
                sl, tsl = load_slot(soff[f"f_gu{l}"] + m)
                bg, bu = rot.get(), rot.get()

                def mm(e, sl=sl, bg=bg, bu=bu):
                    ins = None
                    for kt in range(KT):
                        ins = e.matmul(ps[bg][:, :], lhsT=sl[:, kt * 256:kt * 256 + 128], rhs=hT[:, kt, :],
                                       start=(kt == 0), stop=(kt == KT - 1))
                    for kt in range(KT):
                        ins = e.matmul(ps[bu][:, :], lhsT=sl[:, kt * 256 + 128:kt * 256 + 256], rhs=hT[:, kt, :],
                                       start=(kt == 0), stop=(kt == KT - 1))
                    return ins
                P.add("pe", mm, reads=[tsl, t_h], writes=[t_ps[bg], t_ps[bu]])
                s = m % 2
                P.add("act", lambda e, s=s, bg=bg: e.activation(out=sgt[s][:, :], in_=ps[bg][:, :], func=AF.Silu),
                      reads=[t_ps[bg]], writes=[t_sgt[s]])
                P.add("dve", lambda e, s=s, bu=bu, m=m: e.tensor_tensor(out=bigb[:, m, :], in0=sgt[s][:, :],
                                                                        in1=ps[bu][:, :], op=ALU.mult),
                      reads=[t_sgt[s], t_ps[bu]], writes=[tb[m]])
            for dm in range(8):
                s0, ts0 = load_slot(soff[f"f_down{l}"] + 2 * dm, 11 * 128)
                s1, ts1 = load_slot(soff[f"f_down{l}"] + 2 * dm + 1, 11 * 128)
                bk = rot.get()

                def mm(e, s0=s0, s1=s1, bk=bk):
                    ins = None
                    for k in range(FT):
                        s_, kk = (s0, k) if k < 11 else (s1, k - 11)
                        ins = e.matmul(ps[bk][:, :], lhsT=s_[:, kk * 128:(kk + 1) * 128], rhs=bigb[:, k, :],
                                       start=(k == 0), stop=(k == FT - 1))
                    return ins
                P.add("pe", mm, reads=[ts0, ts1] + tb, writes=[t_ps[bk]])
                post_evac(bk, dm, f"nfo{l}")
            post_finish(b, 7)
        g.ffn_block = ffn_block

        even_state = make_even(g) if do_mix else None
        odd_state = make_odd(g) if do_mix else None
        for l in layers:
            i = l // 2
            if do_mix:
                if l % 2 == 0:
                    even_state.begin_layer(l)
                else:
                    odd_state.begin_layer(l)
            for b in range(1 if dbg else NB):
                if do_mix:
                    if l % 2 == 0:
                        even_state.block(l, b)
                    else:
                        odd_state.block(l, b)
                if do_ffn:
                    ffn_block(l, b)

        outs = []
        for kt in range(KT):
            P.add("sp", lambda e, kt=kt: e.dma_start(out=d_y[:, kt, :], in_=xT[:, kt, :]),
                  reads=xtok[kt], dma_key=("y", kt))
            outs.extend(xtok[kt])
        P.add("sp", None, writes=outs)
        P.emit(st)
        g.stats = {e: len(P.ops[e]) for e in ENGS}
        g.stats["waits"] = P.nwaits
        g.stats["sems"] = P.n_sems
    return nc, g


def reg_tok(g):
    t = Tok()
    g.arena_toks.append(t)
    return t


def make_even(g):
    P, sb, ps, t_ps, V, C = g.P, g.sb, g.ps, g.t_ps, g.V, g.C
    bigb, tb, bigf, tf, hT, t_h = g.bigb, g.tb, g.bigf, g.tf, g.hT, g.t_h
    rstd, t_rstd, t_vec, t_cb = g.rstd, g.t_rstd, g.t_vec, g.t_cb
    soff = g.soff
    S = Ctx()
    sb = g.mk_asb()
    Tok = lambda: reg_tok(g)
    ckvn = sb("ckvn", [128, 2, T], BF16)
    t_ckv = [Tok() for _ in range(NB)]
    kpe = sb("kpe", [128, T], BF16)
    t_kpe = [Tok() for _ in range(NB)]
    Kh = sb("Kh", [128, T], BF16)
    t_Kh = [Tok() for _ in range(NB)]
    Vh = sb("Vh", [128, 16, 128], BF16)
    t_Vh = [Tok() for _ in range(NB)]
    qn = sb("qn", [128, TB], BF16)
    t_qn = Tok()
    qr = sb("qr", [128, TB], BF16)
    t_qr = Tok()
    PT = [sb(f"PT{i}", [128, TB], BF16) for i in range(2)]
    t_PT = [Tok(), Tok()]
    Ct = sb("Ct", [64, TB], F32)
    St = sb("St", [64, TB], F32)
    t_rope = Tok()
    posi = sb("posi", [64, TB], I32)
    t_posi = Tok()
    ki = sb("ki", [64, TB], I32)
    t_ki = Tok()
    xrw = sb("xrw", [128, TB + 4], F32)
    t_xrw = Tok()
    xcb = sb("xcb", [128, TB], BF16)
    t_xcb = Tok()
    tail = sb("rgtail", [128, 8, 3], F32)
    t_tail = [Tok() for _ in range(8)]
    hst = sb("hst", [128, 8], F32)
    t_hst = [Tok() for _ in range(8)]
    nsp8 = sb("nsp8", [128, 8], F32)
    t_nsp = Tok()
    lt8 = sb("lt8", [128, 8], F32)
    t_lt8 = Tok()
    PI_LO = 3.1415925
    c1 = 6.28125
    c2 = float(np.float32(TWO_PI - c1).view(np.uint32) & np.uint32(0xFFFFF000)) if False else None
    c2 = float((np.array([TWO_PI - c1], np.float32).view(np.uint32) & np.uint32(0xFFFFF000)).view(np.float32)[0])
    c3 = float(np.float32(TWO_PI - c1 - c2))
    SCALE = float(192 ** -0.5)

    def begin_layer(l):
        i = l // 2
        g.arena_barrier()
        P.add("act", lambda e: e.activation(out=lt8[:, :], in_=V(f"lam{i}", 0, 8), func=AF.Exp, scale=-1.0),
              reads=[t_vec], writes=[t_lt8])
        P.add("act", lambda e: e.activation(out=lt8[:, :], in_=lt8[:, :], func=AF.Ln, bias=g.one_ap),
              reads=[t_lt8, g.t_cb2], writes=[t_lt8])
        P.add("dve", lambda e: e.tensor_scalar(out=nsp8[:, :], in0=lt8[:, :], scalar1=-8.0, scalar2=None, op0=ALU.mult),
              reads=[t_lt8], writes=[t_nsp])
        P.add("dve", lambda e: e.memset(tail[:, :, :], 0.0), writes=t_tail)
        P.add("dve", lambda e: e.memset(kpe[64:128, :], 0.0), writes=t_kpe)
        P.add("dve", lambda e: e.memset(qr[64:128, :], 0.0), writes=[t_qr])
        P.add("dve", lambda e: e.memset(hst[:, :], 0.0), writes=t_hst)

    def rope_tables(b):
        blk = slice(b * TB, (b + 1) * TB)
        f0, f1, f2 = bigf[0:64, 0, :], bigf[0:64, 1, :], bigf[0:64, 2, :]
        P.add("sp", lambda e: e.dma_start(out=posi[:, :], in_=g.d_pos[:, blk]), writes=[t_posi], dma_key="pos")
        P.add("dve", lambda e: e.tensor_copy(out=f0, in_=posi[:, :]), reads=[t_posi], writes=[tf[0]])
        P.add("dve", lambda e: e.tensor_scalar(out=f1, in0=f0, scalar1=V("invf", 0, 1, 64), scalar2=None, op0=ALU.mult),
              reads=[tf[0], t_vec], writes=[tf[1]])
        P.add("dve", lambda e: e.tensor_scalar(out=ki[:, :], in0=f1, scalar1=float(1.0 / TWO_PI), scalar2=None, op0=ALU.mult),
              reads=[tf[1]], writes=[t_ki])
        P.add("dve", lambda e: e.tensor_copy(out=f2, in_=ki[:, :]), reads=[t_ki], writes=[tf[2]])
        for cc in (c1, c2, c3):
            P.add("dve", lambda e, cc=cc: e.scalar_tensor_tensor(out=f1, in0=f2, scalar=-cc, in1=f1, op0=ALU.mult, op1=ALU.add),
                  reads=[tf[1], tf[2]], writes=[tf[1]])

        def wrap(y, ty):
            P.add("dve", lambda e: e.tensor_scalar(out=f0, in0=y, scalar1=float(np.pi), scalar2=None, op0=ALU.is_gt),
                  reads=[ty], writes=[tf[0]])
            P.add("dve", lambda e: e.scalar_tensor_tensor(out=y, in0=f0, scalar=-TWO_PI, in1=y, op0=ALU.mult, op1=ALU.add),
                  reads=[tf[0], ty], writes=[ty])
            P.add("dve", lambda e: e.tensor_scalar(out=f0, in0=y, scalar1=float(-np.pi), scalar2=None, op0=ALU.is_lt),
                  reads=[ty], writes=[tf[0]])
            P.add("dve", lambda e: e.scalar_tensor_tensor(out=y, in0=f0, scalar=TWO_PI, in1=y, op0=ALU.mult, op1=ALU.add),
                  reads=[tf[0], ty], writes=[ty])
            P.add("dve", lambda e: e.tensor_scalar(out=y, in0=y, scalar1=PI_LO, scalar2=-PI_LO, op0=ALU.min, op1=ALU.max),
                  reads=[ty], writes=[ty])
        P.add("dve", lambda e: e.tensor_scalar(out=f2, in0=f1, scalar1=float(np.pi / 2), scalar2=None, op0=ALU.add),
              reads=[tf[1]], writes=[tf[2]])
        wrap(f1, tf[1])
        wrap(f2, tf[2])
        P.add("act", lambda e: e.activation(out=St[:, :], in_=f1, func=AF.Sin, scale=V("sgn", 0, 1, 64)),
              reads=[tf[1], t_vec], writes=[t_rope])
        P.add("act", lambda e: e.activation(out=Ct[:, :], in_=f2, func=AF.Sin),
              reads=[tf[2]], writes=[t_rope])

    def rope_apply(bA, bB, out_ap, out_toks):
        t1, t2 = bigf[0:64, 0, :], bigf[0:64, 1, :]
        P.add("dve", lambda e: e.tensor_tensor(out=t1, in0=ps[bA][0:64, :], in1=Ct[:, :], op=ALU.mult),
              reads=[t_ps[bA], t_rope], writes=[tf[0]])
        P.add("dve", lambda e: e.tensor_tensor(out=t2, in0=ps[bB][0:64, :], in1=St[:, :], op=ALU.mult),
              reads=[t_ps[bB], t_rope], writes=[tf[1]])
        P.add("dve", lambda e: e.tensor_tensor(out=out_ap, in0=t1, in1=t2, op=ALU.add),
              reads=[tf[0], tf[1]], writes=out_toks)

    def block(l, b):
        i = l // 2
        blk = slice(b * TB, (b + 1) * TB)
        rot = g.PsRot([0, 1, 2, 3])
        rope_tables(b)
        if g.stop < 1:
            return
        g.prenorm(b, f"nmp{l}", 7)
        s0 = soff[f"e_win{i}"]
        if g.stop < 2:
            return
        for half in range(2):
            sl, tsl = g.load_slot(s0 + 8 + half)
            for mm_ in range(2):
                mt = 2 * half + mm_
                bk = rot.get()

                def mm(e, sl=sl, bk=bk, mm_=mm_):
                    ins = None
                    for kt in range(KT):
                        ins = e.matmul(ps[bk][:, :], lhsT=sl[:, kt * 256 + mm_ * 128:kt * 256 + mm_ * 128 + 128],
                                       rhs=hT[:, kt, :], start=(kt == 0), stop=(kt == KT - 1))
                    return ins
                P.add("pe", mm, reads=[tsl, t_h], writes=[t_ps[bk]])
                P.add("act", lambda e, bk=bk, mt=mt: e.activation(out=bigf[:, 4 + mt, :], in_=ps[bk][:, :], func=AF.Copy),
                      reads=[t_ps[bk]], writes=[tf[4 + mt]])
                P.add("act", lambda e, bk=bk, mt=mt: e.activation(out=bigb[:, 8 + mt, :], in_=ps[bk][:, :], func=AF.Square),
                      reads=[t_ps[bk]], writes=[tb[8 + mt]])
        g.rstd_from_sq([bigb[:, 8 + mt, :] for mt in range(4)], tb[8:12], TB, 512.0, EPS, 7)
        for mt in range(4):
            P.add("dve", lambda e, mt=mt: e.scalar_tensor_tensor(out=bigb[:, 16 + mt, :], in0=bigf[:, 4 + mt, :],
                                                                 scalar=V(f"qn{i}", mt), in1=rstd[:, :],
                                                                 op0=ALU.mult, op1=ALU.mult),
                  reads=[tf[4 + mt], t_rstd, t_vec], writes=[tb[16 + mt]])
        sl, tsl = g.load_slot(s0 + 10)
        for mt in range(2):
            bk = rot.get()

            def mm(e, sl=sl, bk=bk, mt=mt):
                ins = None
                for kt in range(KT):
                    ins = e.matmul(ps[bk][:, :], lhsT=sl[:, kt * 256 + mt * 128:kt * 256 + mt * 128 + 128],
                                   rhs=hT[:, kt, :], start=(kt == 0), stop=(kt == KT - 1))
                return ins
            P.add("pe", mm, reads=[tsl, t_h], writes=[t_ps[bk]])
            P.add("act", lambda e, bk=bk, mt=mt: e.activation(out=bigf[:, 2 + mt, :], in_=ps[bk][:, :], func=AF.Copy),
                  reads=[t_ps[bk]], writes=[tf[2 + mt]])
            P.add("act", lambda e, bk=bk, mt=mt: e.activation(out=bigb[:, 20 + mt, :], in_=ps[bk][:, :], func=AF.Square),
                  reads=[t_ps[bk]], writes=[tb[20 + mt]])
        g.rstd_from_sq([bigb[:, 20 + mt, :] for mt in range(2)], tb[20:22], TB, 256.0, EPS, 7)
        for mt in range(2):
            P.add("dve", lambda e, mt=mt: e.scalar_tensor_tensor(out=ckvn[:, mt, blk], in0=bigf[:, 2 + mt, :],
                                                                 scalar=V(f"kvn{i}", mt), in1=rstd[:, :],
                                                                 op0=ALU.mult, op1=ALU.mult),
                  reads=[tf[2 + mt], t_rstd, t_vec], writes=[t_ckv[b]])
        sl, tsl = g.load_slot(s0 + 11, 8 * 128)
        bA, bB = rot.get(), rot.get()

        def mmk(e, sl=sl, bA=bA, bB=bB):
            ins = None
            for kt in range(KT):
                ins = e.matmul(ps[bA][0:64, :], lhsT=sl[:, kt * 128:kt * 128 + 64], rhs=hT[:, kt, :],
                               start=(kt == 0), stop=(kt == KT - 1))
            for kt in range(KT):
                ins = e.matmul(ps[bB][0:64, :], lhsT=sl[:, kt * 128 + 64:kt * 128 + 128], rhs=hT[:, kt, :],
                               start=(kt == 0), stop=(kt == KT - 1))
            return ins
        P.add("pe", mmk, reads=[tsl, t_h], writes=[t_ps[bA], t_ps[bB]])
        rope_apply(bA, bB, kpe[0:64, blk], [t_kpe[b]])
        if g.stop < 3:
            return
        gs, tgs = g.load_slot(soff[f"e_gate{i}"])
        P.add("pool", lambda e: e.tensor_copy(out=S.gatew[:, :], in_=gs[:, :]), reads=[tgs], writes=[S.t_gatew])
        for n in range(8):
            sl, tsl = g.load_slot(s0 + n)
            bX, bG = rot.get(), rot.get()

            def mm(e, sl=sl, bX=bX, bG=bG):
                ins = None
                for kt in range(KT):
                    ins = e.matmul(ps[bX][:, :], lhsT=sl[:, kt * 256:kt * 256 + 128], rhs=hT[:, kt, :],
                                   start=(kt == 0), stop=(kt == KT - 1))
                for kt in range(KT):
                    ins = e.matmul(ps[bG][:, :], lhsT=sl[:, kt * 256 + 128:kt * 256 + 256], rhs=hT[:, kt, :],
                                   start=(kt == 0), stop=(kt == KT - 1))
                return ins
            P.add("pe", mm, reads=[tsl, t_h], writes=[t_ps[bX], t_ps[bG]])
            P.add("dve", lambda e, n=n: e.tensor_copy(out=xrw[:, 0:3], in_=tail[:, n, :]), reads=[t_tail[n]], writes=[t_xrw])
            P.add("act", lambda e, bX=bX: e.activation(out=xrw[:, 3:TB + 3], in_=ps[bX][:, :], func=AF.Copy),
                  reads=[t_ps[bX]], writes=[t_xrw])
            xc = bigf[:, 0, :]
            P.add("dve", lambda e, n=n: e.tensor_scalar(out=xc, in0=xrw[:, 3:TB + 3], scalar1=V(f"cw{i}_3", n),
                                                        scalar2=V(f"cb{i}", n), op0=ALU.mult, op1=ALU.add),
                  reads=[t_xrw, t_vec], writes=[tf[0]])
            for j in (2, 1, 0):
                P.add("dve", lambda e, n=n, j=j: e.scalar_tensor_tensor(out=xc, in0=xrw[:, j:TB + j], scalar=V(f"cw{i}_{j}", n),
                                                                        in1=xc, op0=ALU.mult, op1=ALU.add),
                      reads=[t_xrw, t_vec, tf[0]], writes=[tf[0]])
            P.add("dve", lambda e, n=n: e.tensor_copy(out=tail[:, n, :], in_=xrw[:, TB:TB + 3]), reads=[t_xrw], writes=[t_tail[n]])
            P.add("act", lambda e: e.activation(out=xcb[:, :], in_=xc, func=AF.Copy), reads=[tf[0]], writes=[t_xcb])
            bR, bI = rot.get(), rot.get()

            def mmg(e, n=n, bR=bR, bI=bI):
                e.matmul(ps[bR][:, :], lhsT=S.gatew[:, n * 256:n * 256 + 128], rhs=xcb[:, :], start=True, stop=True)
                return e.matmul(ps[bI][:, :], lhsT=S.gatew[:, n * 256 + 128:n * 256 + 256], rhs=xcb[:, :], start=True, stop=True)
            P.add("pe", mmg, reads=[S.t_gatew, t_xcb], writes=[t_ps[bR], t_ps[bI]])
            P.add("act", lambda e, n=n, bR=bR: e.activation(out=bigf[:, 1, :], in_=ps[bR][:, :], func=AF.Sigmoid, bias=V(f"gab{i}", n)),
                  reads=[t_ps[bR], t_vec], writes=[tf[1]])
            P.add("act", lambda e, n=n, bI=bI: e.activation(out=bigf[:, 2, :], in_=ps[bI][:, :], func=AF.Sigmoid, bias=V(f"gxb{i}", n)),
                  reads=[t_ps[bI], t_vec], writes=[tf[2]])
            P.add("act", lambda e, n=n: e.activation(out=bigf[:, 3, :], in_=bigf[:, 1, :], func=AF.Exp, scale=nsp8[:, n:n + 1]),
                  reads=[tf[1], t_nsp], writes=[tf[3]])
            P.add("dve", lambda e: e.tensor_tensor(out=bigf[:, 4, :], in0=bigf[:, 3, :], in1=bigf[:, 3, :], op=ALU.mult),
                  reads=[tf[3]], writes=[tf[4]])
            P.add("act", lambda e: e.activation(out=bigf[:, 4, :], in_=bigf[:, 4, :], func=AF.Sqrt, scale=-1.0, bias=g.one_ap),
                  reads=[tf[4], g.t_cb2], writes=[tf[4]])
            P.add("dve", lambda e: e.tensor_tensor(out=bigf[:, 5, :], in0=bigf[:, 2, :], in1=xc, op=ALU.mult),
                  reads=[tf[2], tf[0]], writes=[tf[5]])
            P.add("dve", lambda e: e.tensor_tensor(out=bigf[:, 5, :], in0=bigf[:, 5, :], in1=bigf[:, 4, :], op=ALU.mult),
                  reads=[tf[5], tf[4]], writes=[tf[5]])
            P.add("dve", lambda e, n=n: e.tensor_tensor_scan(out=bigf[:, 6, :], data0=bigf[:, 3, :], data1=bigf[:, 5, :],
                                                             initial=hst[:, n:n + 1], op0=ALU.mult, op1=ALU.add),
                  reads=[tf[3], tf[5], t_hst[n]], writes=[tf[6]])
            P.add("dve", lambda e, n=n: e.tensor_copy(out=hst[:, n:n + 1], in_=bigf[:, 6, TB - 1:TB]), reads=[tf[6]], writes=[t_hst[n]])
            P.add("act", lambda e, bG=bG: e.activation(out=bigf[:, 7, :], in_=ps[bG][:, :], func=AF.Gelu_apprx_tanh),
                  reads=[t_ps[bG]], writes=[tf[7]])
            P.add("dve", lambda e, n=n: e.tensor_tensor(out=bigb[:, n, :], in0=bigf[:, 6, :], in1=bigf[:, 7, :], op=ALU.mult),
                  reads=[tf[6], tf[7]], writes=[tb[n]])
        if g.stop < 4:
            return
        nkc = b + 1
        srot = g.PsRot([4, 5])
        for h in range(8):
            if h % 2 == 0:
                uq, tuq = g.load_slot(soff[f"e_uq{i}"] + h // 2)
            if h % 4 == 0:
                ukv_s, tukv_s = g.load_slot(soff[f"e_ukv{i}"] + h // 4)
                P.add("pool", lambda e, ukv_s=ukv_s: e.tensor_copy(out=S.ukvw[:, :], in_=ukv_s[:, :]), reads=[tukv_s], writes=[S.t_ukvw])
            ukv, tukv = S.ukvw, S.t_ukvw
            qb = (h % 2) * 4 * 256
            kb_ = (h % 4) * 2 * 256
            bQ, bA, bB = rot.get(), rot.get(), rot.get()

            def mmq(e, uq=uq, qb=qb, bQ=bQ, bA=bA, bB=bB):
                ins = None
                for kt in range(4):
                    ins = e.matmul(ps[bQ][:, :], lhsT=uq[:, qb + kt * 256:qb + kt * 256 + 128], rhs=bigb[:, 16 + kt, :],
                                   start=(kt == 0), stop=(kt == 3))
                for kt in range(4):
                    ins = e.matmul(ps[bA][0:64, :], lhsT=uq[:, qb + kt * 256 + 128:qb + kt * 256 + 192], rhs=bigb[:, 16 + kt, :],
                                   start=(kt == 0), stop=(kt == 3))
                for kt in range(4):
                    ins = e.matmul(ps[bB][0:64, :], lhsT=uq[:, qb + kt * 256 + 192:qb + kt * 256 + 256], rhs=bigb[:, 16 + kt, :],
                                   start=(kt == 0), stop=(kt == 3))
                return ins
            P.add("pe", mmq, reads=[tuq] + tb[16:20], writes=[t_ps[bQ], t_ps[bA], t_ps[bB]])
            P.add("act", lambda e, bQ=bQ: e.activation(out=qn[:, :], in_=ps[bQ][:, :], func=AF.Copy), reads=[t_ps[bQ]], writes=[t_qn])
            rope_apply(bA, bB, qr[0:64, :], [t_qr])
            for c in range(nkc):
                bk = rot.get()

                def mmK(e, c=c, bk=bk, kb_=kb_):
                    ins = None
                    for kt in range(2):
                        ins = e.matmul(ps[bk][:, :], lhsT=ukv[:, kb_ + kt * 256:kb_ + kt * 256 + 128],
                                       rhs=ckvn[:, kt, c * TB:(c + 1) * TB], start=(kt == 0), stop=(kt == 1))
                    return ins
                P.add("pe", mmK, reads=[tukv, t_ckv[c]], writes=[t_ps[bk]])
                P.add("act", lambda e, c=c, bk=bk: e.activation(out=Kh[:, c * TB:(c + 1) * TB], in_=ps[bk][:, :], func=AF.Copy),
                      reads=[t_ps[bk]], writes=[t_Kh[c]])
                bv = rot.get()

                def mmV(e, c=c, bv=bv, kb_=kb_):
                    ins = None
                    for jj in range(4):
                        for kt in range(2):
                            ins = e.matmul(ps[bv][:, jj * 128:(jj + 1) * 128],
                                           lhsT=ckvn[:, kt, c * TB + jj * 128:c * TB + (jj + 1) * 128],
                                           rhs=ukv[:, kb_ + kt * 256 + 128:kb_ + kt * 256 + 256],
                                           start=(kt == 0), stop=(kt == 1))
                    return ins
                P.add("pe", mmV, reads=[tukv, t_ckv[c]], writes=[t_ps[bv]])
                P.add("dve", lambda e, c=c, bv=bv: e.tensor_copy(out=Vh[:, 4 * c:4 * c + 4, :], in_=ps[bv][:, :]),
                      reads=[t_ps[bv]], writes=[t_Vh[c]])
            nj = 4 * nkc
            sbks = {}

            def emit_S(j):
                jj = j - 4 * b
                c0 = max(0, jj) * 128
                sbk = srot.get()
                sbks[j] = sbk

                def mmS(e, j=j, jj=jj, c0=c0, sbk=sbk):
                    e.matmul(ps[sbk][:, c0:TB], lhsT=Kh[:, j * 128:(j + 1) * 128], rhs=qn[:, c0:TB], start=True, stop=False)
                    ins = e.matmul(ps[sbk][:, c0:TB], lhsT=kpe[:, j * 128:(j + 1) * 128], rhs=qr[:, c0:TB],
                                   start=False, stop=(jj < 0))
                    if jj >= 0:
                        ins = e.matmul(ps[sbk][:, c0:c0 + 128], lhsT=g.identb[:, :], rhs=g.amaskb[:, :], start=False, stop=True)
                    return ins
                P.add("pe", mmS, reads=[t_Kh[j // 4], t_kpe[j // 4], t_qn, t_qr, t_cb], writes=[t_ps[sbk]])

            def emit_rest(j):
                jj = j - 4 * b
                c0 = max(0, jj) * 128
                sbk = sbks[j]
                pt = j % 2
                P.add("act", lambda e, c0=c0, sbk=sbk, pt=pt: e.activation(out=PT[pt][:, c0:TB], in_=ps[sbk][:, c0:TB],
                                                                           func=AF.Exp, scale=SCALE),
                      reads=[t_ps[sbk]], writes=[t_PT[pt]])

                def mmO(e, j=j, c0=c0, pt=pt, nj=nj):
                    e.matmul(ps[6][:, c0:TB], lhsT=Vh[:, j, :], rhs=PT[pt][:, c0:TB], start=(j == 0), stop=(j == nj - 1))
                    return e.matmul(ps[7][:, c0:TB], lhsT=g.onesb[:, :], rhs=PT[pt][:, c0:TB], start=(j == 0), stop=(j == nj - 1))
                P.add("pe", mmO, reads=[t_Vh[j // 4], t_PT[pt], t_cb], writes=[t_ps[6], t_ps[7]])
            emit_S(0)
            for j in range(nj):
                if j + 1 < nj:
                    emit_S(j + 1)
                emit_rest(j)
            P.add("dve", lambda e: e.reciprocal(out=bigf[:, 2, :], in_=ps[7][:, :]), reads=[t_ps[7]], writes=[tf[2]])
            P.add("dve", lambda e, h=h: e.tensor_tensor(out=bigb[:, 8 + h, :], in0=ps[6][:, :], in1=bigf[:, 2, :], op=ALU.mult),
                  reads=[t_ps[6], tf[2]], writes=[tb[8 + h]])
        if g.stop < 5:
            return
        for m in range(8):
            sl, tsl = g.load_slot(soff[f"e_wout{i}"] + m)
            bk = rot.get()

            def mmo(e, sl=sl, bk=bk):
                ins = None
                for kt in range(16):
                    ins = e.matmul(ps[bk][:, :], lhsT=sl[:, kt * 128:(kt + 1) * 128], rhs=bigb[:, kt, :],
                                   start=(kt == 0), stop=(kt == 15))
                return ins
            P.add("pe", mmo, reads=[tsl] + tb[0:16], writes=[t_ps[bk]])
            g.post_evac(bk, m, f"nmo{l}")
        g.post_finish(b, 7)

    S.gatew = sb("gatew", [128, SLOT_EL], BF16)
    S.t_gatew = Tok()
    S.ukvw = sb("ukvw", [128, SLOT_EL], BF16)
    S.t_ukvw = Tok()
    S.begin_layer = begin_layer
    S.block = block
    return S


def make_odd(g):
    P, ps, t_ps, V, C = g.P, g.ps, g.t_ps, g.V, g.C
    bigb, tb, bigf, tf, hT, t_h = g.bigb, g.tb, g.bigf, g.tf, g.hT, g.t_h
    rstd, t_rstd, t_vec, t_cb, t_cst = g.rstd, g.t_rstd, g.t_vec, g.t_cb, g.t_cst
    onesf = g.onesf
    soff = g.soff
    S = Ctx()
    sb = g.mk_asb()
    Tok = lambda: reg_tok(g)
    S_all = sb("S_all", [128, 16, 128], F32)
    t_S = [Tok() for _ in range(16)]
    gtail = sb("gtail", [128, 32, 3], F32)
    t_gt = [Tok() for _ in range(32)]
    xw = sb("gxw", [128, TB + 4], F32)
    t_xw = Tok()
    beta = sb("beta", [128, 4, 16], F32)
    nbeta = sb("nbeta", [128, 4, 16], F32)
    gg = sb("gg", [128, 4, 16], F32)
    eg = sb("eg", [128, 4, 16], F32)
    egr = sb("egr", [128, 4, 16], F32)
    dch = sb("dch", [128, 8, 16], F32)
    t_gate = Tok()
    nexpA = sb("nexpA", [128, 16], F32)
    t_nexp = Tok()
    KKs = sb("KKs", [128, 4, 128], F32)
    QKs = sb("QKs", [128, 4, 128], F32)
    Qt = sb("Qt", [128, 4, 128], F32)
    Ktt = sb("Ktt", [128, 4, 128], F32)
    t_pair = [Tok() for _ in range(4)]
    CH = []
    for p in range(4):
        c = Ctx()
        c.XY = [sb(f"XY{p}_{k}", [128, 256], F32) for k in range(2)]
        c.X = [c.XY[k][:, 0:128] for k in range(2)]
        c.Y = [c.XY[k][:, 128:256] for k in range(2)]
        c.R = [sb(f"R{p}_{k}", [128, 128], F32) for k in range(2)]
        c.QKT = sb(f"QKT{p}", [128, 128], F32)
        c.kdec = sb(f"kdec{p}", [128, 128], F32)
        c.kdecB = sb(f"kdecB{p}", [128, 128], F32)
        c.Kbe = sb(f"Kbe{p}", [128, 128], F32)
        c.Vb = sb(f"Vb{p}", [128, 128], F32)
        c.tX = [Tok(), Tok()]
        c.tY = [Tok(), Tok()]
        c.tR = [Tok(), Tok()]
        c.tQKT, c.tkdec, c.tKbe, c.tVb = Tok(), Tok(), Tok(), Tok()
        CH.append(c)
    ident = C("ident")
    QSCALE = float(128 ** -0.5)

    def begin_layer(l):
        i = l // 2
        g.arena_barrier()
        P.add("act", lambda e: e.activation(out=nexpA[:, :], in_=V(f"alog{i}", 0, 16), func=AF.Exp),
              reads=[t_vec], writes=[t_nexp])
        P.add("dve", lambda e: e.tensor_scalar(out=nexpA[:, :], in0=nexpA[:, :], scalar1=-1.0, scalar2=None, op0=ALU.mult),
              reads=[t_nexp], writes=[t_nexp])
        P.add("dve", lambda e: e.memset(S_all[:, :, :], 0.0), writes=t_S)
        P.add("dve", lambda e: e.memset(gtail[:, :, :], 0.0), writes=t_gt)

    def conv_silu(i, bank, tile, out_ap, out_tok):
        P.add("dve", lambda e: e.tensor_copy(out=xw[:, 0:3], in_=gtail[:, tile, :]), reads=[t_gt[tile]], writes=[t_xw])
        P.add("act", lambda e: e.activation(out=xw[:, 3:TB + 3], in_=ps[bank][:, :], func=AF.Copy),
              reads=[t_ps[bank]], writes=[t_xw])
        P.add("dve", lambda e: e.tensor_scalar(out=out_ap, in0=xw[:, 3:TB + 3], scalar1=V(f"gcw{i}_3", tile), scalar2=None,
                                               op0=ALU.mult), reads=[t_xw, t_vec], writes=[out_tok])
        for j in (2, 1, 0):
            P.add("dve", lambda e, j=j: e.scalar_tensor_tensor(out=out_ap, in0=xw[:, j:TB + j], scalar=V(f"gcw{i}_{j}", tile),
                                                               in1=out_ap, op0=ALU.mult, op1=ALU.add),
                  reads=[t_xw, t_vec, out_tok], writes=[out_tok])
        P.add("dve", lambda e: e.tensor_copy(out=gtail[:, tile, :], in_=xw[:, TB:TB + 3]), reads=[t_xw], writes=[t_gt[tile]])
        P.add("act", lambda e: e.activation(out=out_ap, in_=out_ap, func=AF.Silu), reads=[out_tok], writes=[out_tok])

    def l2norm(x_ap, x_tok, scale):
        P.add("act", lambda e: e.activation(out=bigb[:, 17, :], in_=x_ap, func=AF.Square), reads=[x_tok], writes=[tb[17]])
        g.rstd_from_sq([bigb[:, 17, :]], [tb[17]], TB, 1.0, EPS, 7)
        P.add("dve", lambda e: e.scalar_tensor_tensor(out=x_ap, in0=x_ap, scalar=scale, in1=rstd[:, :], op0=ALU.mult, op1=ALU.mult),
              reads=[x_tok, t_rstd], writes=[x_tok])

    def proj2(sl, tsl, b0, b1):
        def mm(e):
            ins = None
            for kt in range(KT):
                ins = e.matmul(ps[b0][:, :], lhsT=sl[:, kt * 256:kt * 256 + 128], rhs=hT[:, kt, :], start=(kt == 0), stop=(kt == KT - 1))
            for kt in range(KT):
                ins = e.matmul(ps[b1][:, :], lhsT=sl[:, kt * 256 + 128:kt * 256 + 256], rhs=hT[:, kt, :], start=(kt == 0), stop=(kt == KT - 1))
            return ins
        P.add("pe", mm, reads=[tsl, t_h], writes=[t_ps[b0], t_ps[b1]])

    def block(l, b):
        i = l // 2
        rot = g.PsRot([0, 1, 2, 3, 4])
        g.prenorm(b, f"nmp{l}", 7)
        if g.stop < 1:
            return
        wba, twba = g.load_slot(soff[f"g_wba{i}"], 8 * 32)
        bk = rot.get()

        def mmba(e):
            ins = None
            for tt in range(4):
                for kt in range(KT):
                    ins = e.matmul(ps[bk][:, tt * 32:(tt + 1) * 32], lhsT=hT[:, kt, tt * 128:(tt + 1) * 128],
                                   rhs=wba[:, kt * 32:(kt + 1) * 32], start=(kt == 0), stop=(kt == KT - 1))
            return ins
        P.add("pe", mmba, reads=[twba, t_h], writes=[t_ps[bk]])
        for tt in range(4):
            P.add("act", lambda e, tt=tt: e.activation(out=beta[:, tt, :], in_=ps[bk][:, tt * 32:tt * 32 + 16], func=AF.Sigmoid),
                  reads=[t_ps[bk]], writes=[t_gate])
            P.add("dve", lambda e, tt=tt: e.tensor_tensor(out=gg[:, tt, :], in0=ps[bk][:, tt * 32 + 16:tt * 32 + 32],
                                                          in1=V(f"dtb{i}", 0, 16), op=ALU.add),
                  reads=[t_ps[bk], t_vec], writes=[t_gate])
        P.add("act", lambda e: e.activation(out=gg[:, :, :], in_=gg[:, :, :], func=AF.Exp), reads=[t_gate], writes=[t_gate])
        P.add("act", lambda e: e.activation(out=gg[:, :, :], in_=gg[:, :, :], func=AF.Ln, bias=g.one_ap),
              reads=[t_gate, g.t_cb2], writes=[t_gate])
        for tt in range(4):
            P.add("dve", lambda e, tt=tt: e.tensor_tensor(out=gg[:, tt, :], in0=gg[:, tt, :], in1=nexpA[:, :], op=ALU.mult),
                  reads=[t_gate, t_nexp], writes=[t_gate])
        P.add("dve", lambda e: e.tensor_scalar(out=nbeta[:, :, :], in0=beta[:, :, :], scalar1=-1.0, scalar2=None, op0=ALU.mult),
              reads=[t_gate], writes=[t_gate])
        for p in range(4):
            bc = rot.get()

            def mmc(e, p=p, bc=bc):
                e.matmul(ps[bc][:, 0:16], lhsT=C("U2"), rhs=gg[:, p, :], start=True, stop=True)
                e.matmul(ps[bc][:, 16:32], lhsT=C("SL2"), rhs=gg[:, p, :], start=True, stop=True)
                e.matmul(ps[bc][:, 32:48], lhsT=C("ONA"), rhs=gg[:, p, :], start=True, stop=True)
                return e.matmul(ps[bc][:, 48:64], lhsT=C("ONB"), rhs=gg[:, p, :], start=True, stop=True)
            P.add("pe", mmc, reads=[t_gate, t_cst, t_cb], writes=[t_ps[bc]])

            def ex(e, p=p, bc=bc):
                e.activation(out=eg[:, p, :], in_=ps[bc][:, 0:16], func=AF.Exp)
                e.activation(out=egr[:, p, :], in_=ps[bc][:, 16:32], func=AF.Exp)
                e.activation(out=dch[:, 2 * p, :], in_=ps[bc][:, 32:48], func=AF.Exp)
                return e.activation(out=dch[:, 2 * p + 1, :], in_=ps[bc][:, 48:64], func=AF.Exp)
            P.add("act", ex, reads=[t_ps[bc]], writes=[t_gate])
        if g.stop < 2:
            return
        qf, kf = bigf[:, 0, :], bigf[:, 1, :]
        vfs = [bigf[:, 2, :], bigf[:, 3, :]]
        zfs = [bigf[:, 4, :], bigf[:, 5, :]]
        for j in range(g.dbg[0] + 1 if g.dbg else 8):
            s0 = soff[f"g_win{i}"] + 3 * j
            sl, tsl = g.load_slot(s0)
            b0, b1 = rot.get(), rot.get()
            proj2(sl, tsl, b0, b1)
            conv_silu(i, b0, j, qf, tf[0])
            conv_silu(i, b1, 8 + j, kf, tf[1])
            l2norm(qf, tf[0], QSCALE)
            l2norm(kf, tf[1], 1.0)
            sl, tsl = g.load_slot(s0 + 1)
            b0, b1 = rot.get(), rot.get()
            proj2(sl, tsl, b0, b1)
            conv_silu(i, b0, 16 + 2 * j, vfs[0], tf[2])
            conv_silu(i, b1, 16 + 2 * j + 1, vfs[1], tf[3])
            sl, tsl = g.load_slot(s0 + 2)
            b0, b1 = rot.get(), rot.get()
            proj2(sl, tsl, b0, b1)
            P.add("act", lambda e, b0=b0: e.activation(out=zfs[0], in_=ps[b0][:, :], func=AF.Silu), reads=[t_ps[b0]], writes=[tf[4]])
            P.add("act", lambda e, b1=b1: e.activation(out=zfs[1], in_=ps[b1][:, :], func=AF.Silu), reads=[t_ps[b1]], writes=[tf[5]])
            for p in range(4):
                cols = slice(p * 128, (p + 1) * 128)
                bkk = rot.get()

                def mmr(e, cols=cols, bkk=bkk):
                    e.matmul(ps[bkk][:, 0:128], lhsT=kf[:, cols], rhs=kf[:, cols], start=True, stop=True)
                    e.matmul(ps[bkk][:, 128:256], lhsT=kf[:, cols], rhs=qf[:, cols], start=True, stop=True)
                    e.transpose(ps[bkk][:, 256:384], qf[:, cols], ident)
                    return e.transpose(ps[bkk][:, 384:512], kf[:, cols], ident)
                P.add("pe", mmr, reads=[tf[0], tf[1], t_cst], writes=[t_ps[bkk]])

                def ev1(e, p=p, bkk=bkk):
                    e.activation(out=KKs[:, p, :], in_=ps[bkk][:, 0:128], func=AF.Copy)
                    return e.activation(out=Qt[:, p, :], in_=ps[bkk][:, 256:384], func=AF.Copy)
                P.add("act", ev1, reads=[t_ps[bkk]], writes=[t_pair[p]])

                def ev2(e, p=p, bkk=bkk):
                    e.tensor_copy(out=QKs[:, p, :], in_=ps[bkk][:, 128:256])
                    return e.tensor_copy(out=Ktt[:, p, :], in_=ps[bkk][:, 384:512])
                P.add("dve", ev2, reads=[t_ps[bkk]], writes=[t_pair[p]])
            for half in range(2 if g.stop >= 3 else 0):
                combos = [(hh, 2 * half + q, CH[hh * 2 + q]) for hh in range(2) for q in range(2)]
                bvts = {}
                for hh in range(2):
                    vf, tvf = vfs[hh], tf[2 + hh]
                    bvt = rot.get()
                    bvts[hh] = bvt

                    def mmvt(e, vf=vf, bvt=bvt, half=half):
                        ins = None
                        for q in range(2):
                            p = 2 * half + q
                            ins = e.transpose(ps[bvt][:, q * 128:(q + 1) * 128], vf[:, p * 128:(p + 1) * 128], ident)
                        return ins
                    P.add("pe", mmvt, reads=[tvf, t_cst], writes=[t_ps[bvt]])
                for hh, p, c in combos:
                    h = 2 * j + hh
                    q = p - 2 * half
                    bvt = bvts[hh]
                    P.add("dve", lambda e, c=c, p=p, h=h: e.tensor_scalar(out=c.X[1][:, :], in0=C("U2"), scalar1=gg[:, p, h:h + 1],
                                                                          scalar2=None, op0=ALU.mult),
                          reads=[t_cst, t_gate], writes=[c.tX[1]])
                    P.add("dve", lambda e, c=c, p=p, h=h: e.tensor_scalar(out=c.Y[1][:, :], in0=C("SL2"), scalar1=gg[:, p, h:h + 1],
                                                                          scalar2=None, op0=ALU.mult),
                          reads=[t_cst, t_gate], writes=[c.tY[1]])
                    P.add("dve", lambda e, c=c, p=p, h=h: e.tensor_scalar(out=c.Kbe[:, :], in0=Ktt[:, p, :], scalar1=beta[:, p, h:h + 1],
                                                                          scalar2=eg[:, p, h:h + 1], op0=ALU.mult, op1=ALU.mult),
                          reads=[t_pair[p], t_gate], writes=[c.tKbe])
                    P.add("dve", lambda e, c=c, p=p, h=h: e.tensor_scalar(out=c.kdec[:, :], in0=Ktt[:, p, :], scalar1=egr[:, p, h:h + 1],
                                                                          scalar2=C("ONA", 128, 1), op0=ALU.mult, op1=ALU.mult),
                          reads=[t_pair[p], t_gate, t_cst], writes=[c.tkdec])
                    P.add("dve", lambda e, c=c, p=p, h=h: e.tensor_scalar(out=c.kdecB[:, :], in0=Ktt[:, p, :], scalar1=egr[:, p, h:h + 1],
                                                                          scalar2=C("ONB", 128, 1), op0=ALU.mult, op1=ALU.mult),
                          reads=[t_pair[p], t_gate, t_cst], writes=[c.tkdec])
                    P.add("dve", lambda e, c=c, p=p, h=h, bvt=bvt, q=q: e.tensor_scalar(out=c.Vb[:, :], in0=ps[bvt][:, q * 128:(q + 1) * 128],
                                                                                        scalar1=beta[:, p, h:h + 1], scalar2=None, op0=ALU.mult),
                          reads=[t_ps[bvt], t_gate], writes=[c.tVb])
                for hh, p, c in combos:
                    bD = rot.get()

                    def mmD(e, c=c, bD=bD):
                        e.matmul(ps[bD][:, 0:128], lhsT=c.X[1][:, :], rhs=C("SL2"), start=True, stop=False)
                        e.matmul(ps[bD][:, 0:128], lhsT=ident, rhs=C("NMS"), start=False, stop=True)
                        e.matmul(ps[bD][:, 128:256], lhsT=c.Y[1][:, :], rhs=C("U2"), start=True, stop=False)
                        return e.matmul(ps[bD][:, 128:256], lhsT=ident, rhs=C("NMC"), start=False, stop=True)
                    P.add("pe", mmD, reads=[c.tX[1], c.tY[1], t_cst], writes=[t_ps[bD]])
                    P.add("act", lambda e, c=c, bD=bD: e.activation(out=c.R[1][:, :], in_=ps[bD][:, 0:128], func=AF.Exp),
                          reads=[t_ps[bD]], writes=[c.tR[1]])
                    P.add("act", lambda e, c=c, bD=bD: e.activation(out=c.QKT[:, :], in_=ps[bD][:, 128:256], func=AF.Exp),
                          reads=[t_ps[bD]], writes=[c.tQKT])
                for hh, p, c in combos:
                    h = 2 * j + hh
                    P.add("dve", lambda e, c=c, p=p, h=h: e.scalar_tensor_tensor(out=c.X[0][:, :], in0=KKs[:, p, :], scalar=nbeta[:, p, h:h + 1],
                                                                                 in1=c.R[1][:, :], op0=ALU.mult, op1=ALU.mult),
                          reads=[t_pair[p], t_gate, c.tR[1]], writes=[c.tX[0]])
                    P.add("dve", lambda e, c=c, p=p: e.tensor_tensor(out=c.QKT[:, :], in0=QKs[:, p, :], in1=c.QKT[:, :], op=ALU.mult),
                          reads=[t_pair[p], c.tQKT], writes=[c.tQKT])
                bTs = []
                for hh, p, c in combos:
                    bT = rot.get()
                    bTs.append(bT)
                    P.add("pe", lambda e, c=c, bT=bT: e.transpose(ps[bT][:, 0:128], c.X[0][:, :], ident),
                          reads=[c.tX[0], t_cst], writes=[t_ps[bT]])
                for (hh, p, c), bT in zip(combos, bTs):
                    P.add("act", lambda e, c=c, bT=bT: e.activation(out=c.Y[0][:, :], in_=ps[bT][:, 0:128], func=AF.Copy),
                          reads=[t_ps[bT]], writes=[c.tY[0]])
                    P.add("dve", lambda e, c=c: e.tensor_tensor(out=c.R[0][:, :], in0=c.Y[0][:, :], in1=ident, op=ALU.add),
                          reads=[c.tY[0], t_cst], writes=[c.tR[0]])
                a, ra = 0, 0
                for k in range(1, 6):
                    b1s = []
                    for hh, p, c in combos:
                        b1_ = rot.get()
                        b1s.append(b1_)

                        def mmsq(e, c=c, b1_=b1_, a=a, k=k):
                            ins = e.matmul(ps[b1_][:, 0:128], lhsT=c.Y[a][:, :], rhs=c.X[a][:, :], start=True, stop=True)
                            if k < 5:
                                ins = e.matmul(ps[b1_][:, 128:256], lhsT=c.X[a][:, :], rhs=c.Y[a][:, :], start=True, stop=True)
                            return ins
                        P.add("pe", mmsq, reads=[c.tX[a], c.tY[a]], writes=[t_ps[b1_]])
                    for (hh, p, c), b1_ in zip(combos, b1s):
                        if k < 5:
                            P.add("act", lambda e, c=c, b1_=b1_, a=a: e.activation(out=c.XY[1 - a][:, 0:256], in_=ps[b1_][:, 0:256], func=AF.Copy),
                                  reads=[t_ps[b1_]], writes=[c.tX[1 - a], c.tY[1 - a]])
                        else:
                            P.add("act", lambda e, c=c, b1_=b1_, a=a: e.activation(out=c.X[1 - a][:, :], in_=ps[b1_][:, 0:128], func=AF.Copy),
                                  reads=[t_ps[b1_]], writes=[c.tX[1 - a]])
                    b2s = []
                    for hh, p, c in combos:
                        b2_ = rot.get()
                        b2s.append(b2_)

                        def mmR(e, c=c, b2_=b2_, a=a, ra=ra):
                            return e.matmul(ps[b2_][:, 0:128], lhsT=c.X[1 - a][:, :], rhs=c.R[ra][:, :], start=True, stop=True)
                        P.add("pe", mmR, reads=[c.tX[1 - a], c.tR[ra], t_cst], writes=[t_ps[b2_]])
                    for (hh, p, c), b2_ in zip(combos, b2s):
                        P.add("dve", lambda e, c=c, b2_=b2_, ra=ra: e.tensor_tensor(out=c.R[1 - ra][:, :], in0=c.R[ra][:, :], in1=ps[b2_][:, 0:128],
                                                                                   op=ALU.add),
                              reads=[c.tR[ra], t_ps[b2_]], writes=[c.tR[1 - ra]])
                    a, ra = 1 - a, 1 - ra
                bws = []
                for hh, p, c in combos:
                    h = 2 * j + hh
                    bw = rot.get()
                    bws.append(bw)

                    def mmw(e, c=c, bw=bw, ra=ra):
                        e.matmul(ps[bw][:, 0:128], lhsT=c.Kbe[:, :], rhs=c.R[ra][:, :], start=True, stop=True)
                        return e.matmul(ps[bw][:, 128:256], lhsT=c.R[ra][:, :], rhs=c.Vb[:, :], start=True, stop=True)
                    P.add("pe", mmw, reads=[c.tKbe, c.tVb, c.tR[ra]], writes=[t_ps[bw]])
                    P.add("dve", lambda e, c=c, p=p, h=h: e.tensor_scalar(out=c.Y[1][:, :], in0=ident, scalar1=eg[:, p, h:h + 1], scalar2=None,
                                                                          op0=ALU.mult),
                          reads=[t_cst, t_gate, c.tY[1]], writes=[c.tY[1]])
                for (hh, p, c), bw in zip(combos, bws):
                    P.add("act", lambda e, c=c, bw=bw: e.activation(out=c.X[0][:, :], in_=ps[bw][:, 0:128], func=AF.Identity, scale=-1.0),
                          reads=[t_ps[bw]], writes=[c.tX[0]])
                    P.add("act", lambda e, c=c, bw=bw: e.activation(out=c.X[1][:, :], in_=ps[bw][:, 128:256], func=AF.Copy),
                          reads=[t_ps[bw]], writes=[c.tX[1]])
                bqs = []
                for hh, p, c in combos:
                    bq = rot.get()
                    bqs.append(bq)
                    P.add("pe", lambda e, c=c, bq=bq, p=p: e.matmul(ps[bq][:, 0:128], lhsT=Qt[:, p, :], rhs=c.Y[1][:, :], start=True, stop=True),
                          reads=[t_pair[p], c.tY[1]], writes=[t_ps[bq]])
                for (hh, p, c), bq in zip(combos, bqs):
                    P.add("act", lambda e, c=c, bq=bq: e.activation(out=c.Y[0][:, :], in_=ps[bq][:, 0:128], func=AF.Copy),
                          reads=[t_ps[bq]], writes=[c.tY[0]])
                for lc in range(4):
                    cc = 4 * half + lc
                    p, s_ = cc // 2, cc % 2
                    for hh in range(2):
                        h = 2 * j + hh
                        c = CH[hh * 2 + (p - 2 * half)]
                        Sh = S_all[:, h, :]
                        ob = 6 if hh == 0 else 5
                        r0 = 64 * s_
                        rows = slice(r0, r0 + 64)
                        kd = c.kdec if s_ == 0 else c.kdecB
                        bw_ = rot.get()
                        def mmv(e, c=c, bw_=bw_, Sh=Sh):
                            return e.matmul(ps[bw_][:, 0:128], lhsT=c.X[0][:, :], rhs=Sh, start=True, stop=True)
                        P.add("pe", mmv, reads=[c.tX[0], t_S[h]], writes=[t_ps[bw_]])
                        P.add("dve", lambda e, c=c, bw_=bw_, rows=rows: e.tensor_tensor(out=c.X[1][rows, :], in0=c.X[1][rows, :],
                                                                                       in1=ps[bw_][rows, 0:128], op=ALU.add),
                              reads=[c.tX[1], t_ps[bw_]], writes=[c.tX[1]])
                        bsu = rot.get()

                        def mmo(e, c=c, cc=cc, rows=rows, bsu=bsu, Sh=Sh, kd=kd, ob=ob):
                            e.matmul(ps[ob][:, cc * 64:(cc + 1) * 64], lhsT=Sh, rhs=c.Y[0][:, rows], start=True, stop=False)
                            e.matmul(ps[ob][:, cc * 64:(cc + 1) * 64], lhsT=c.X[1][:, :], rhs=c.QKT[:, rows], start=False, stop=True)
                            return e.matmul(ps[bsu][:, 0:128], lhsT=kd[:, :], rhs=c.X[1][:, :], start=True, stop=True)
                        P.add("pe", mmo, reads=[t_S[h], c.tY[0], c.tX[1], c.tQKT, c.tkdec], writes=[t_ps[ob], t_ps[bsu]])
                        P.add("dve", lambda e, cc=cc, h=h, bsu=bsu, Sh=Sh: e.scalar_tensor_tensor(out=Sh, in0=Sh, scalar=dch[:, cc, h:h + 1],
                                                                                          in1=ps[bsu][:, 0:128], op0=ALU.mult, op1=ALU.add),
                              reads=[t_S[h], t_gate, t_ps[bsu]], writes=[t_S[h]])
            for hh in range(2 if g.stop >= 3 else 0):
                h = 2 * j + hh
                ob = 6 if hh == 0 else 5
                P.add("act", lambda e, ob=ob: e.activation(out=bigb[:, 16, :], in_=ps[ob][:, :], func=AF.Square), reads=[t_ps[ob]], writes=[tb[16]])
                g.rstd_from_sq([bigb[:, 16, :]], [tb[16]], TB, 128.0, EPS, 7)
                P.add("act", lambda e, ob=ob: e.activation(out=bigf[:, 7, :], in_=ps[ob][:, :], func=AF.Identity, scale=V(f"gn{i}")),
                      reads=[t_ps[ob], t_vec], writes=[tf[7]])
                P.add("dve", lambda e: e.tensor_tensor(out=bigf[:, 6, :], in0=bigf[:, 7, :], in1=rstd[:, :], op=ALU.mult),
                      reads=[tf[7], t_rstd], writes=[tf[6]])
                P.add("dve", lambda e, h=h, hh=hh: e.tensor_tensor(out=bigb[:, h, :], in0=bigf[:, 6, :], in1=zfs[hh], op=ALU.mult),
                      reads=[tf[6], tf[4 + hh]], writes=[tb[h]])
        if g.dbg or g.stop < 6:
            return
        for m in range(8):
            sl, tsl = g.load_slot(soff[f"g_wout{i}"] + m)
            bk = rot.get()

            def mmo2(e, sl=sl, bk=bk):
                ins = None
                for kt in range(16):
                    ins = e.matmul(ps[bk][:, :], lhsT=sl[:, kt * 128:(kt + 1) * 128], rhs=bigb[:, kt, :],
                                   start=(kt == 0), stop=(kt == 15))
                return ins
            P.add("pe", mmo2, reads=[tsl] + tb[0:16], writes=[t_ps[bk]])
            g.post_evac(bk, m, f"nmo{l}")
        g.post_finish(b, 7)

    S.begin_layer = begin_layer
    S.block = block
    return S


_CACHE = {}


def kernel(**inputs):
    inp = {k: np.asarray(v) for k, v in inputs.items()}
    W = pack_weights(inp)
    Vv = pack_vecs(inp)
    Cc = make_consts()
    if "nc" not in _CACHE:
        _CACHE["nc"] = build_program()[0]
    nc = _CACHE["nc"]
    in_maps = []
    for b in range(8):
        x = np.asarray(inp["x"][b], np.float32)
        xT = np.ascontiguousarray(x.T.reshape(8, 128, T).transpose(1, 0, 2))
        pos = np.ascontiguousarray(np.broadcast_to(np.asarray(inp["positions"][b], np.int32)[None, :], (64, T)))
        in_maps.append({"xT": xT, "pos": pos, "vecs": Vv, "cst": Cc, "wts": W})
    res = run_bass_kernel_spmd(nc, in_maps, core_ids=list(range(8)))
    out = np.stack([np.asarray(r["yT"]).transpose(1, 0, 2).reshape(D, T).T for r in res.results])
    return np.ascontiguousarray(out.astype(np.float32))
```

```python
import contextlib
import numpy as np
import concourse.bass as bass
import concourse.mybir as mybir
from concourse.bass_utils import run_bass_kernel_spmd

F32 = mybir.dt.float32
BF16 = mybir.dt.bfloat16
I32 = mybir.dt.int32
AF = mybir.ActivationFunctionType
ALU = mybir.AluOpType
AX = mybir.AxisListType

ENGS = ("pe", "dve", "act", "pool", "sp")
EPOCH = 24000


class Tok:
    __slots__ = ("lw", "rd", "name")

    def __init__(self, name=""):
        self.lw = None
        self.rd = {}
        self.name = name


class Op:
    __slots__ = ("eng", "fn", "waits", "inc", "known", "dma")

    def __init__(self, eng, fn):
        self.eng = eng
        self.fn = fn
        self.waits = []
        self.inc = False
        self.known = None
        self.dma = None


class Prog:
    def __init__(self, nc, same_engine_sync=True):
        self.nc = nc
        self.ops = {e: [] for e in ENGS}
        self.known = {e: {} for e in ENGS}
        self.dma_cnt = {}
        self.dma_ops = {}
        self.same_engine_sync = same_engine_sync
        self.nwaits = 0

    def _lookup(self, s, p):
        if isinstance(s, tuple):
            return self.dma_ops[s][p - 1]
        return self.ops[s][p]

    def add(self, eng, fn, reads=(), writes=(), dma_key=None):
        ops = self.ops[eng]
        idx = len(ops)
        op = Op(eng, fn)
        deps = {}
        for t in reads:
            if t.lw is not None:
                s, p = t.lw
                if deps.get(s, -1) < p:
                    deps[s] = p
        for t in writes:
            if t.lw is not None:
                s, p = t.lw
                if deps.get(s, -1) < p:
                    deps[s] = p
            for s, p in t.rd.items():
                if deps.get(s, -1) < p:
                    deps[s] = p
        known = self.known[eng]
        for s, p in deps.items():
            if s == eng and dma_key is None and (eng == "pe" or not self.same_engine_sync):
                continue
            if known.get(s, -1) >= p:
                continue
            op.waits.append((s, p))
            known[s] = p
            src = self._lookup(s, p)
            for s2, p2 in src.known.items():
                if known.get(s2, -1) < p2:
                    known[s2] = p2
            src.inc = True
        self.nwaits += len(op.waits)
        if dma_key is None:
            mypos = (eng, idx)
        else:
            k = ("dma", dma_key)
            n = self.dma_cnt.get(k, 0) + 1
            self.dma_cnt[k] = n
            self.dma_ops.setdefault(k, []).append(op)
            mypos = (k, n)
            op.dma = k
        op.known = dict(known)
        for t in reads:
            if t.rd.get(mypos[0], -1) < mypos[1]:
                t.rd[mypos[0]] = mypos[1]
        for t in writes:
            t.lw = mypos
            t.rd = {}
        ops.append(op)
        return op

    def emit(self, stack):
        nc = self.nc
        pref = {}
        nsem = {}
        for e in ENGS:
            c = 0
            arr = []
            for op in self.ops[e]:
                if op.inc and op.dma is None:
                    c += 1
                arr.append(c)
            pref[e] = arr
            nsem[e] = (c + EPOCH - 1) // EPOCH
        sems = {e: [stack.enter_context(nc.semaphore(f"s_{e}_{i}")) for i in range(nsem[e])]
                for e in ENGS}
        dsems = {k: stack.enter_context(nc.semaphore("d_%d" % i))
                 for i, k in enumerate(self.dma_cnt)}
        self.n_sems = sum(nsem.values()) + len(dsems)
        block = stack.enter_context(nc.Block())
        hw = {"pe": block.tensor, "dve": block.vector, "act": block.scalar,
              "pool": block.gpsimd, "sp": block.sync}

        def run(e):
            def body(engine):
                for op in self.ops[e]:
                    for s, p in op.waits:
                        if isinstance(s, tuple):
                            engine.wait_ge(dsems[s], 16 * p)
                        else:
                            c = pref[s][p]
                            engine.wait_ge(sems[s][(c - 1) // EPOCH], (c - 1) % EPOCH + 1)
                    if op.fn is None:
                        continue
                    ins = op.fn(engine)
                    if op.dma is not None:
                        ins.then_inc(dsems[op.dma], 16)
                    elif op.inc:
                        c = pref[e][self.ops[e].index(op)] if False else None
                        ins.then_inc(sems[e][(op_c[id(op)] - 1) // EPOCH], 1)
            return body

        op_c = {}
        for e in ENGS:
            for i, op in enumerate(self.ops[e]):
                if op.inc and op.dma is None:
                    op_c[id(op)] = pref[e][i]
        for e in ENGS:
            hw[e](run(e))


T = 2048
D = 1024
KT = 8
TB = 512
NB = T // TB
DEPTH = 4
FF = 2816
FT = 22
SLOT_EL = 2048
EPS = 1e-6
NEG = -30000.0
TWO_PI = 6.283185307179586


def vec_layout():
    off = {}
    c = 0

    def put(name, n):
        nonlocal c
        off[name] = c
        c += n
    for l in range(4):
        for nm in ("nmp", "nmo", "nfp", "nfo"):
            put(f"{nm}{l}", 8)
    for i in range(2):
        for j in range(4):
            put(f"cw{i}_{j}", 8)
        for nm in ("cb", "gab", "gxb", "lam"):
            put(f"{nm}{i}", 8)
        put(f"qn{i}", 4)
        put(f"kvn{i}", 2)
    for i in range(2):
        for j in range(4):
            put(f"gcw{i}_{j}", 32)
        put(f"gn{i}", 1)
        put(f"alog{i}", 16)
        put(f"dtb{i}", 16)
    put("invf", 1)
    put("sgn", 1)
    return off, c


def cst_layout():
    off = {}
    c = 0
    for nm in ("ident", "amask", "U2", "SL2", "NMS", "NMC", "BD", "ONA", "ONB"):
        off[nm] = c
        c += 128
    return off, c


def slot_layout():
    off = {}
    c = 0

    def put(name, n):
        nonlocal c
        off[name] = c
        c += n
    for i in range(2):
        put(f"e_win{i}", 12)
        put(f"e_gate{i}", 1)
        put(f"e_uq{i}", 4)
        put(f"e_ukv{i}", 2)
        put(f"e_wout{i}", 8)
    for i in range(2):
        put(f"g_win{i}", 24)
        put(f"g_wba{i}", 1)
        put(f"g_wout{i}", 8)
    for l in range(4):
        put(f"f_gu{l}", 22)
        put(f"f_down{l}", 16)
    return off, c


def _tile_w(W, k0, nk, cols):
    sub = W[k0:k0 + nk * 128][:, cols]
    return sub.reshape(nk, 128, -1).transpose(1, 0, 2).reshape(128, -1)


def _pad(a):
    out = np.zeros((128, SLOT_EL), np.float32)
    out[:, :a.shape[1]] = a
    return out


def pack_weights(inp):
    soff, ns = slot_layout()
    W = np.zeros((ns, 128, SLOT_EL), np.float32)
    ar = np.arange
    for i in range(2):
        win = inp["hy_w_in"][i]
        s0 = soff[f"e_win{i}"]
        for n in range(8):
            cols = np.concatenate([ar(n * 128, n * 128 + 128), ar(1024 + n * 128, 1024 + n * 128 + 128)])
            W[s0 + n] = _pad(_tile_w(win, 0, 8, cols))
        W[s0 + 8] = _pad(_tile_w(win, 0, 8, ar(2048, 2304)))
        W[s0 + 9] = _pad(_tile_w(win, 0, 8, ar(2304, 2560)))
        W[s0 + 10] = _pad(_tile_w(win, 0, 8, ar(2560, 2816)))
        cols = np.concatenate([ar(2816, 2880), ar(2848, 2880), ar(2816, 2848)])
        W[s0 + 11] = _pad(_tile_w(win, 0, 8, cols))
        ga, gx = inp["rg_gate_a_w"][i], inp["rg_gate_x_w"][i]
        g = np.concatenate([ga, gx], axis=2)
        W[soff[f"e_gate{i}"]] = _pad(g.transpose(1, 0, 2).reshape(128, -1))
        uq = inp["mla_w_uq"][i]
        for s in range(4):
            parts = []
            for hh in range(2):
                h = 2 * s + hh
                b = h * 192
                cols = np.concatenate([ar(b, b + 192), ar(b + 160, b + 192), ar(b + 128, b + 160)])
                parts.append(_tile_w(uq, 0, 4, cols))
            W[soff[f"e_uq{i}"] + s] = _pad(np.concatenate(parts, axis=1))
        ukv = inp["mla_w_ukv"][i]
        for s in range(2):
            parts = [_tile_w(ukv, 0, 2, ar((4 * s + hh) * 256, (4 * s + hh) * 256 + 256)) for hh in range(4)]
            W[soff[f"e_ukv{i}"] + s] = _pad(np.concatenate(parts, axis=1))
        wo = inp["hy_w_out"][i]
        for m in range(8):
            W[soff[f"e_wout{i}"] + m] = _pad(_tile_w(wo, 0, 16, ar(m * 128, m * 128 + 128)))
    for i in range(2):
        win = inp["gdn_w_in"][i]
        s0 = soff[f"g_win{i}"]
        for j in range(8):
            q = ar(j * 128, j * 128 + 128)
            k = ar(1024 + j * 128, 1024 + j * 128 + 128)
            v = ar(2048 + 2 * j * 128, 2048 + 2 * j * 128 + 256)
            z = ar(4096 + 2 * j * 128, 4096 + 2 * j * 128 + 256)
            W[s0 + 3 * j] = _pad(_tile_w(win, 0, 8, np.concatenate([q, k])))
            W[s0 + 3 * j + 1] = _pad(_tile_w(win, 0, 8, v))
            W[s0 + 3 * j + 2] = _pad(_tile_w(win, 0, 8, z))
        W[soff[f"g_wba{i}"]] = _pad(_tile_w(win, 0, 8, ar(6144, 6176)))
        wo = inp["gdn_w_out"][i]
        for m in range(8):
            W[soff[f"g_wout{i}"] + m] = _pad(_tile_w(wo, 0, 16, ar(m * 128, m * 128 + 128)))
    for l in range(4):
        wg, wu, wd = inp["ffn_w_gate"][l], inp["ffn_w_up"][l], inp["ffn_w_down"][l]
        for m in range(FT):
            c = ar(m * 128, m * 128 + 128)
            W[soff[f"f_gu{l}"] + m] = _pad(np.concatenate([_tile_w(wg, 0, 8, c), _tile_w(wu, 0, 8, c)], axis=1)
                                           .reshape(128, 2, 8, 128).transpose(0, 2, 1, 3).reshape(128, -1))
        for m in range(8):
            for hf in range(2):
                W[soff[f"f_down{l}"] + 2 * m + hf] = _pad(_tile_w(wd, hf * 11 * 128, 11, ar(m * 128, m * 128 + 128)))
    return W


def pack_vecs(inp):
    voff, nv = vec_layout()
    V = np.zeros((128, nv), np.float32)

    def v8(name, v):
        a = np.asarray(v, np.float32).reshape(-1, 128).T
        V[:, voff[name]:voff[name] + a.shape[1]] = a
    for l in range(4):
        v8(f"nmp{l}", inp["norm_mix_pre"][l])
        v8(f"nmo{l}", inp["norm_mix_post"][l])
        v8(f"nfp{l}", inp["norm_ffn_pre"][l])
        v8(f"nfo{l}", inp["norm_ffn_post"][l])
    for i in range(2):
        for j in range(4):
            v8(f"cw{i}_{j}", inp["rg_conv_w"][i][j])
        v8(f"cb{i}", inp["rg_conv_b"][i])
        v8(f"gab{i}", inp["rg_gate_a_b"][i])
        v8(f"gxb{i}", inp["rg_gate_x_b"][i])
        v8(f"lam{i}", inp["rg_lambda"][i])
        v8(f"qn{i}", inp["mla_q_norm"][i])
        v8(f"kvn{i}", inp["mla_kv_norm"][i])
    for i in range(2):
        for j in range(4):
            v8(f"gcw{i}_{j}", inp["gdn_conv_w"][i][j])
        v8(f"gn{i}", inp["gdn_norm"][i])
        V[:, voff[f"alog{i}"]:voff[f"alog{i}"] + 16] = np.asarray(inp["gdn_a_log"][i], np.float32)[None, :]
        V[:, voff[f"dtb{i}"]:voff[f"dtb{i}"] + 16] = np.asarray(inp["gdn_dt_bias"][i], np.float32)[None, :]
    inv_freq = (1.0 / (np.float32(10000.0) ** (np.arange(0, 64, 2, dtype=np.float32) / np.float32(64)))).astype(np.float32)
    V[:64, voff["invf"]] = np.concatenate([inv_freq, inv_freq])
    V[:32, voff["sgn"]] = -1.0
    V[32:64, voff["sgn"]] = 1.0
    return V


def make_consts():
    coff, ncc = cst_layout()
    C = np.zeros((128, ncc), np.float32)
    idx = np.arange(128)
    C[:, coff["ident"]:coff["ident"] + 128] = np.eye(128, dtype=np.float32)
    C[:, coff["amask"]:coff["amask"] + 128] = np.where(idx[:, None] <= idx[None, :], 0.0, NEG)
    same = (idx[:, None] // 64) == (idx[None, :] // 64)
    C[:, coff["U2"]:coff["U2"] + 128] = (same & (idx[:, None] <= idx[None, :])).astype(np.float32)
    C[:, coff["SL2"]:coff["SL2"] + 128] = (same & (idx[:, None] > idx[None, :])).astype(np.float32)
    C[:, coff["NMS"]:coff["NMS"] + 128] = np.where(same & (idx[:, None] > idx[None, :]), 0.0, NEG)
    C[:, coff["NMC"]:coff["NMC"] + 128] = np.where(same & (idx[:, None] <= idx[None, :]), 0.0, NEG)
    C[:, coff["BD"]:coff["BD"] + 128] = same.astype(np.float32)
    C[0:64, coff["ONA"]:coff["ONA"] + 128] = 1.0
    C[64:128, coff["ONB"]:coff["ONB"] + 128] = 1.0
    return C


class Ctx:
    pass


def build_program(layers=(0, 1, 2, 3), do_ffn=True, do_mix=True, cast_engs=("dve", "act", "dve"), dbg=False, stop=99, sub=99):
    nc = bass.Bass("TRN2", target_bir_lowering=False, dynamic_dma_scratch_size=1024)
    voff, nv = vec_layout()
    coff, ncc = cst_layout()
    soff, nslots = slot_layout()
    d_x = nc.dram_tensor("xT", [128, KT, T], F32, kind="ExternalInput").ap()
    d_pos = nc.dram_tensor("pos", [64, T], I32, kind="ExternalInput").ap()
    d_vec = nc.dram_tensor("vecs", [128, nv], F32, kind="ExternalInput").ap()
    d_cst = nc.dram_tensor("cst", [128, ncc], F32, kind="ExternalInput").ap()
    d_w = nc.dram_tensor("wts", [nslots, 128, SLOT_EL], F32, kind="ExternalInput").ap()
    d_y = nc.dram_tensor("yT", [128, KT, T], F32, kind="ExternalOutput").ap()
    with contextlib.ExitStack() as st:
        P = Prog(nc)
        g = Ctx()
        g.P, g.nc, g.voff, g.coff, g.soff = P, nc, voff, coff, soff
        g.dbg = dbg
        g.stop = stop
        g.sub = sub
        g.d_dbg = nc.dram_tensor("dbg", [8, 128, 512], F32, kind="ExternalOutput").ap() if dbg else None

        def sb(name, shape, dt):
            return st.enter_context(nc.sbuf_tensor(name, shape, dt))

        g.sb = sb
        xT = sb("xT_sb", [128, KT, T], F32)
        xtok = [[Tok() for _ in range(NB)] for _ in range(KT)]
        vec = sb("vec_sb", [128, nv], F32)
        t_vec = Tok()
        cst = sb("cst_sb", [128, ncc], F32)
        t_cst = Tok()
        identb = sb("identb", [128, 128], BF16)
        amaskb = sb("amaskb", [128, 128], BF16)
        onesb = sb("onesb", [128, 128], BF16)
        onesf = sb("onesf", [128, 128], F32)
        t_cb = Tok()
        hT = sb("hT", [128, KT, TB], BF16)
        t_h = Tok()
        bigb = sb("bigb", [128, 22, TB], BF16)
        tb = [Tok() for _ in range(22)]
        bigf = sb("bigf", [128, 8, TB], F32)
        tf = [Tok() for _ in range(8)]
        rstd = sb("rstd", [128, TB], F32)
        t_rstd = Tok()
        lnt = sb("lnt", [128, TB], F32)
        t_lnt = Tok()
        NSTG, NSL = 2, 3
        stage = [sb(f"stage{i}", [128, SLOT_EL], F32) for i in range(NSTG)]
        t_stage = [Tok() for _ in range(NSTG)]
        slots = [sb(f"slot{i}", [128, SLOT_EL], BF16) for i in range(NSL)]
        t_slot = [Tok() for _ in range(NSL)]
        ARENA32 = 11072
        arena = sb("arena", [128, ARENA32], F32)

        def mk_asb():
            off = [0]

            def asb(name, shape, dt):
                n = int(np.prod(shape[1:]))
                n32 = (n * (4 if dt in (F32, I32) else 2) + 3) // 4
                v = arena[:, off[0]:off[0] + n32]
                off[0] += n32
                assert off[0] <= ARENA32, (name, off[0])
                if dt != F32:
                    v = v.bitcast(dt)
                if len(shape) == 3:
                    v = v.rearrange("p (a b) -> p a b", b=shape[2])
                if shape[0] < 128:
                    v = v[0:shape[0]]
                return v
            return asb
        g.mk_asb = mk_asb
        g.arena_toks = []
        scr = sb("scr", [128, 2], F32)

        def arena_barrier():
            P.add("dve", lambda e: e.memset(scr[:, :], 0.0), writes=list(g.arena_toks))
        g.arena_barrier = arena_barrier
        ps = [st.enter_context(nc.psum_tensor(f"ps{i}", [128, 512], F32)) for i in range(8)]
        t_ps = [Tok() for _ in range(8)]
        g.xT, g.xtok, g.vec, g.t_vec, g.cst, g.t_cst = xT, xtok, vec, t_vec, cst, t_cst
        g.identb, g.amaskb, g.onesb, g.onesf, g.t_cb = identb, amaskb, onesb, onesf, t_cb
        g.hT, g.t_h, g.bigb, g.tb, g.bigf, g.tf = hT, t_h, bigb, tb, bigf, tf
        g.rstd, g.t_rstd, g.ps, g.t_ps = rstd, t_rstd, ps, t_ps
        g.d_pos = d_pos

        def V(name, k=0, n=1, rows=128):
            c = voff[name] + k
            return vec[0:rows, c:c + n]

        def C(name, rows=128, cols=128, c0=0):
            c = coff[name] + c0
            return cst[0:rows, c:c + cols]

        g.V, g.C = V, C

        class PsRot:
            def __init__(self, ids):
                self.ids = list(ids)
                self.i = 0

            def get(self):
                b = self.ids[self.i % len(self.ids)]
                self.i += 1
                return b
        g.PsRot = PsRot

        wctr = [0]

        def load_slot(idx, nel=SLOT_EL):
            k = wctr[0]
            wctr[0] += 1
            sg, sl = k % NSTG, k % NSL
            P.add("sp", lambda e: e.dma_start(out=stage[sg][:, 0:nel], in_=d_w[idx, :, 0:nel]),
                  writes=[t_stage[sg]], dma_key=("stg", sg))
            ce = cast_engs[k % len(cast_engs)]
            if ce == "act":
                fn = lambda e: e.copy(out=slots[sl][:, 0:nel], in_=stage[sg][:, 0:nel])
            else:
                fn = lambda e: e.tensor_copy(out=slots[sl][:, 0:nel], in_=stage[sg][:, 0:nel])
            P.add(ce, fn, reads=[t_stage[sg]], writes=[t_slot[sl]])
            return slots[sl], t_slot[sl]
        g.load_slot = load_slot

        P.add("sp", lambda e: e.dma_start(out=vec[:, :], in_=d_vec[:, :]), writes=[t_vec], dma_key="vec")
        P.add("sp", lambda e: e.dma_start(out=cst[:, :], in_=d_cst[:, :]), writes=[t_cst], dma_key="cst")
        for kt in range(KT):
            P.add("sp", lambda e, kt=kt: e.dma_start(out=xT[:, kt, :], in_=d_x[:, kt, :]),
                  writes=xtok[kt], dma_key=("x", kt))

        def setup_consts(e):
            e.tensor_copy(out=identb[:, :], in_=C("ident"))
            e.tensor_copy(out=amaskb[:, :], in_=C("amask"))
            e.memset(onesb[:, :], 1.0)
            return e.memset(onesf[:, :], 1.0)
        P.add("dve", setup_consts, reads=[t_cst], writes=[t_cb])

        def rstd_from_sq(sq_aps, sq_toks, n, dsum, eps, bank):
            def mm(e):
                ins = None
                for i, a in enumerate(sq_aps):
                    ins = e.matmul(ps[bank][:, 0:n], lhsT=onesb[:, :], rhs=a,
                                   start=(i == 0), stop=(i == len(sq_aps) - 1))
                return ins
            P.add("pe", mm, reads=list(sq_toks) + [t_cb], writes=[t_ps[bank]])
            P.add("act", lambda e: e.activation(out=lnt[:, 0:n], in_=ps[bank][:, 0:n], func=AF.Ln,
                                                scale=1.0 / dsum, bias=g.eps_ap),
                  reads=[t_ps[bank], t_cb2], writes=[t_lnt])
            P.add("act", lambda e: e.activation(out=rstd[:, 0:n], in_=lnt[:, 0:n], func=AF.Exp, scale=-0.5),
                  reads=[t_lnt], writes=[t_rstd])
        g.rstd_from_sq = rstd_from_sq
        epst = sb("epst", [128, 2], F32)
        t_cb2 = Tok()
        g.eps_ap = epst[:, 0:1]
        g.one_ap = epst[:, 1:2]
        g.t_cb2 = t_cb2

        def setup_eps(e):
            e.memset(epst[:, 0:1], EPS)
            return e.memset(epst[:, 1:2], 1.0)
        P.add("dve", setup_eps, writes=[t_cb2])

        def prenorm(b, wname, bank):
            blk = slice(b * TB, (b + 1) * TB)
            P.add("act", lambda e: e.activation(out=bigb[:, 0:8, :], in_=xT[:, :, blk], func=AF.Square),
                  reads=[xtok[kt][b] for kt in range(KT)], writes=tb[0:8])
            rstd_from_sq([bigb[:, kt, :] for kt in range(KT)], tb[0:8], TB, float(D), EPS, bank)

            def f(e):
                ins = None
                for kt in range(KT):
                    ins = e.scalar_tensor_tensor(out=hT[:, kt, :], in0=xT[:, kt, blk], scalar=V(wname, kt),
                                                 in1=rstd[:, :], op0=ALU.mult, op1=ALU.mult)
                return ins
            P.add("dve", f, reads=[xtok[kt][b] for kt in range(KT)] + [t_rstd, t_vec], writes=[t_h])
        g.prenorm = prenorm

        def post_evac(bank, m, wname):
            P.add("act", lambda e: e.activation(out=bigf[:, m, :], in_=ps[bank][:, :], func=AF.Identity,
                                                scale=V(wname, m)),
                  reads=[t_ps[bank], t_vec], writes=[tf[m]])
            P.add("act", lambda e: e.activation(out=hT[:, m, :], in_=ps[bank][:, :], func=AF.Square),
                  reads=[t_ps[bank]], writes=[t_h])
        g.post_evac = post_evac

        def post_finish(b, bank):
            blk = slice(b * TB, (b + 1) * TB)
            rstd_from_sq([hT[:, kt, :] for kt in range(KT)], [t_h], TB, float(D), EPS, bank)
            for kt in range(KT):
                P.add("dve", lambda e, kt=kt: e.tensor_tensor(out=bigf[:, kt, :], in0=bigf[:, kt, :], in1=rstd[:, :],
                                                              op=ALU.mult),
                      reads=[tf[kt], t_rstd], writes=[tf[kt]])
                P.add("dve", lambda e, kt=kt: e.tensor_tensor(out=xT[:, kt, blk], in0=xT[:, kt, blk],
                                                              in1=bigf[:, kt, :], op=ALU.add),
                      reads=[tf[kt], xtok[kt][b]], writes=[xtok[kt][b]])
        g.post_finish = post_finish

        sgt = [sb(f"sgt{i}", [128, TB], F32) for i in range(2)]
        t_sgt = [Tok(), Tok()]

        def ffn_block(l, b):
            rot = PsRot([0, 1, 2, 3, 4, 5])
            prenorm(b, f"nfp{l}", 7)
            for m in range(FT):
                sl, tsl = load_slot(soff[f"f_gu{l}"] + m)
                bg, bu = rot.get(), rot.get()

                def mm(e, sl=sl, bg=bg, bu=bu):
                    ins = None
                    for kt in range(KT):
                        ins = e.matmul(ps[bg][:, :], lhsT=sl[:, kt * 256:kt * 256 + 128], rhs=hT[:, kt, :],
                                       start=(kt == 0), stop=(kt == KT - 1))
                    for kt in range(KT):
                        ins = e.matmul(ps[bu][:, :], lhsT=sl[:, kt * 256 + 128:kt * 256 + 256], rhs=hT[:, kt, :],
                                       start=(kt == 0), stop=(kt == KT - 1))
                    return ins
                P.add("pe", mm, reads=[tsl, t_h], writes=[t_ps[bg], t_ps[bu]])
                s = m % 2
                P.add("act", lambda e, s=s, bg=bg: e.activation(out=sgt[s][:, :], in_=ps[bg][:, :], func=AF.Silu),
                      reads=[t_ps[bg]], writes=[t_sgt[s]])
                P.add("dve", lambda e, s=s, bu=bu, m=m: e.tensor_tensor(out=bigb[:, m, :], in0=sgt[s][:, :],
                                                                        in1=ps[bu][:, :], op=ALU.mult),
                      reads=[t_sgt[s], t_ps[bu]], writes=[tb[m]])
            for dm in range(8):
                s0, ts0 = load_slot(soff[f"f_down{l}"] + 2 * dm, 11 * 128)
                s1, ts1 = load_slot(soff[f"f_down{l}"] + 2 * dm + 1, 11 * 128)
                bk = rot.get()

                def mm(e, s0=s0, s1=s1, bk=bk):
                    ins = None
                    for k in range(FT):
                        s_, kk = (s0, k) if k < 11 else (s1, k - 11)
                        ins = e.matmul(ps[bk][:, :], lhsT=s_[:, kk * 128:(kk + 1) * 128], rhs=bigb[:, k, :],
                                       start=(k == 0), stop=(k == FT - 1))
                    return ins
                P.add("pe", mm, reads=[ts0, ts1] + tb, writes=[t_ps[bk]])
                post_evac(bk, dm, f"nfo{l}")
            post_finish(b, 7)
        g.ffn_block = ffn_block

        even_state = make_even(g) if do_mix else None
        odd_state = make_odd(g) if do_mix else None
        for l in layers:
            i = l // 2
            if do_mix:
                if l % 2 == 0:
                    even_state.begin_layer(l)
                else:
                    odd_state.begin_layer(l)
            for b in range(1 if dbg else NB):
                if do_mix:
                    if l % 2 == 0:
                        even_state.block(l, b)
                    else:
                        odd_state.block(l, b)
                if do_ffn:
                    ffn_block(l, b)

        outs = []
        for kt in range(KT):
            P.add("sp", lambda e, kt=kt: e.dma_start(out=d_y[:, kt, :], in_=xT[:, kt, :]),
                  reads=xtok[kt], dma_key=("y", kt))
            outs.extend(xtok[kt])
        P.add("sp", None, writes=outs)
        P.emit(st)
        g.stats = {e: len(P.ops[e]) for e in ENGS}
        g.stats["waits"] = P.nwaits
        g.stats["sems"] = P.n_sems
    return nc, g


def reg_tok(g):
    t = Tok()
    g.arena_toks.append(t)
    return t


def make_even(g):
    P, sb, ps, t_ps, V, C = g.P, g.sb, g.ps, g.t_ps, g.V, g.C
    bigb, tb, bigf, tf, hT, t_h = g.bigb, g.tb, g.bigf, g.tf, g.hT, g.t_h
    rstd, t_rstd, t_vec, t_cb = g.rstd, g.t_rstd, g.t_vec, g.t_cb
    soff = g.soff
    S = Ctx()
    sb = g.mk_asb()
    Tok = lambda: reg_tok(g)
    ckvn = sb("ckvn", [128, 2, T], BF16)
    t_ckv = [Tok() for _ in range(NB)]
    kpe = sb("kpe", [128, T], BF16)
    t_kpe = [Tok() for _ in range(NB)]
    Kh = sb("Kh", [128, T], BF16)
    t_Kh = [Tok() for _ in range(NB)]
    Vh = sb("Vh", [128, 16, 128], BF16)
    t_Vh = [Tok() for _ in range(NB)]
    qn = sb("qn", [128, TB], BF16)
    t_qn = Tok()
    qr = sb("qr", [128, TB], BF16)
    t_qr = Tok()
    PT = [sb(f"PT{i}", [128, TB], BF16) for i in range(2)]
    t_PT = [Tok(), Tok()]
    Ct = sb("Ct", [64, TB], F32)
    St = sb("St", [64, TB], F32)
    t_rope = Tok()
    posi = sb("posi", [64, TB], I32)
    t_posi = Tok()
    ki = sb("ki", [64, TB], I32)
    t_ki = Tok()
    xrw = sb("xrw", [128, TB + 4], F32)
    t_xrw = Tok()
    xcb = sb("xcb", [128, TB], BF16)
    t_xcb = Tok()
    tail = sb("rgtail", [128, 8, 3], F32)
    t_tail = [Tok() for _ in range(8)]
    hst = sb("hst", [128, 8], F32)
    t_hst = [Tok() for _ in range(8)]
    nsp8 = sb("nsp8", [128, 8], F32)
    t_nsp = Tok()
    lt8 = sb("lt8", [128, 8], F32)
    t_lt8 = Tok()
    PI_LO = 3.1415925
    c1 = 6.28125
    c2 = float(np.float32(TWO_PI - c1).view(np.uint32) & np.uint32(0xFFFFF000)) if False else None
    c2 = float((np.array([TWO_PI - c1], np.float32).view(np.uint32) & np.uint32(0xFFFFF000)).view(np.float32)[0])
    c3 = float(np.float32(TWO_PI - c1 - c2))
    SCALE = float(192 ** -0.5)

    def begin_layer(l):
        i = l // 2
        g.arena_barrier()
        P.add("act", lambda e: e.activation(out=lt8[:, :], in_=V(f"lam{i}", 0, 8), func=AF.Exp, scale=-1.0),
              reads=[t_vec], writes=[t_lt8])
        P.add("act", lambda e: e.activation(out=lt8[:, :], in_=lt8[:, :], func=AF.Ln, bias=g.one_ap),
              reads=[t_lt8, g.t_cb2], writes=[t_lt8])
        P.add("dve", lambda e: e.tensor_scalar(out=nsp8[:, :], in0=lt8[:, :], scalar1=-8.0, scalar2=None, op0=ALU.mult),
              reads=[t_lt8], writes=[t_nsp])
        P.add("dve", lambda e: e.memset(tail[:, :, :], 0.0), writes=t_tail)
        P.add("dve", lambda e: e.memset(kpe[64:128, :], 0.0), writes=t_kpe)
        P.add("dve", lambda e: e.memset(qr[64:128, :], 0.0), writes=[t_qr])
        P.add("dve", lambda e: e.memset(hst[:, :], 0.0), writes=t_hst)

    def rope_tables(b):
        blk = slice(b * TB, (b + 1) * TB)
        f0, f1, f2 = bigf[0:64, 0, :], bigf[0:64, 1, :], bigf[0:64, 2, :]
        P.add("sp", lambda e: e.dma_start(out=posi[:, :], in_=g.d_pos[:, blk]), writes=[t_posi], dma_key="pos")
        P.add("dve", lambda e: e.tensor_copy(out=f0, in_=posi[:, :]), reads=[t_posi], writes=[tf[0]])
        P.add("dve", lambda e: e.tensor_scalar(out=f1, in0=f0, scalar1=V("invf", 0, 1, 64), scalar2=None, op0=ALU.mult),
              reads=[tf[0], t_vec], writes=[tf[1]])
        P.add("dve", lambda e: e.tensor_scalar(out=ki[:, :], in0=f1, scalar1=float(1.0 / TWO_PI), scalar2=None, op0=ALU.mult),
              reads=[tf[1]], writes=[t_ki])
        P.add("dve", lambda e: e.tensor_copy(out=f2, in_=ki[:, :]), reads=[t_ki], writes=[tf[2]])
        for cc in (c1, c2, c3):
            P.add("dve", lambda e, cc=cc: e.scalar_tensor_tensor(out=f1, in0=f2, scalar=-cc, in1=f1, op0=ALU.mult, op1=ALU.add),
                  reads=[tf[1], tf[2]], writes=[tf[1]])

        def wrap(y, ty):
            P.add("dve", lambda e: e.tensor_scalar(out=f0, in0=y, scalar1=float(np.pi), scalar2=None, op0=ALU.is_gt),
                  reads=[ty], writes=[tf[0]])
            P.add("dve", lambda e: e.scalar_tensor_tensor(out=y, in0=f0, scalar=-TWO_PI, in1=y, op0=ALU.mult, op1=ALU.add),
                  reads=[tf[0], ty], writes=[ty])
            P.add("dve", lambda e: e.tensor_scalar(out=f0, in0=y, scalar1=float(-np.pi), scalar2=None, op0=ALU.is_lt),
                  reads=[ty], writes=[tf[0]])
            P.add("dve", lambda e: e.scalar_tensor_tensor(out=y, in0=f0, scalar=TWO_PI, in1=y, op0=ALU.mult, op1=ALU.add),
                  reads=[tf[0], ty], writes=[ty])
            P.add("dve", lambda e: e.tensor_scalar(out=y, in0=y, scalar1=PI_LO, scalar2=-PI_LO, op0=ALU.min, op1=ALU.max),
                  reads=[ty], writes=[ty])
        P.add("dve", lambda e: e.tensor_scalar(out=f2, in0=f1, scalar1=float(np.pi / 2), scalar2=None, op0=ALU.add),
              reads=[tf[1]], writes=[tf[2]])
        wrap(f1, tf[1])
        wrap(f2, tf[2])
        P.add("act", lambda e: e.activation(out=St[:, :], in_=f1, func=AF.Sin, scale=V("sgn", 0, 1, 64)),
              reads=[tf[1], t_vec], writes=[t_rope])
        P.add("act", lambda e: e.activation(out=Ct[:, :], in_=f2, func=AF.Sin),
              reads=[tf[2]], writes=[t_rope])

    def rope_apply(bA, bB, out_ap, out_toks):
        t1, t2 = bigf[0:64, 0, :], bigf[0:64, 1, :]
        P.add("dve", lambda e: e.tensor_tensor(out=t1, in0=ps[bA][0:64, :], in1=Ct[:, :], op=ALU.mult),
              reads=[t_ps[bA], t_rope], writes=[tf[0]])
        P.add("dve", lambda e: e.tensor_tensor(out=t2, in0=ps[bB][0:64, :], in1=St[:, :], op=ALU.mult),
              reads=[t_ps[bB], t_rope], writes=[tf[1]])
        P.add("dve", lambda e: e.tensor_tensor(out=out_ap, in0=t1, in1=t2, op=ALU.add),
              reads=[tf[0], tf[1]], writes=out_toks)

    def block(l, b):
        i = l // 2
        blk = slice(b * TB, (b + 1) * TB)
        rot = g.PsRot([0, 1, 2, 3])
        rope_tables(b)
        if g.stop < 1:
            return
        g.prenorm(b, f"nmp{l}", 7)
        s0 = soff[f"e_win{i}"]
        if g.stop < 2:
            return
        for half in range(2):
            sl, tsl = g.load_slot(s0 + 8 + half)
            for mm_ in range(2):
                mt = 2 * half + mm_
                bk = rot.get()

                def mm(e, sl=sl, bk=bk, mm_=mm_):
                    ins = None
                    for kt in range(KT):
                        ins = e.matmul(ps[bk][:, :], lhsT=sl[:, kt * 256 + mm_ * 128:kt * 256 + mm_ * 128 + 128],
                                       rhs=hT[:, kt, :], start=(kt == 0), stop=(kt == KT - 1))
                    return ins
                P.add("pe", mm, reads=[tsl, t_h], writes=[t_ps[bk]])
                P.add("act", lambda e, bk=bk, mt=mt: e.activation(out=bigf[:, 4 + mt, :], in_=ps[bk][:, :], func=AF.Copy),
                      reads=[t_ps[bk]], writes=[tf[4 + mt]])
                P.add("act", lambda e, bk=bk, mt=mt: e.activation(out=bigb[:, 8 + mt, :], in_=ps[bk][:, :], func=AF.Square),
                      reads=[t_ps[bk]], writes=[tb[8 + mt]])
        g.rstd_from_sq([bigb[:, 8 + mt, :] for mt in range(4)], tb[8:12], TB, 512.0, EPS, 7)
        for mt in range(4):
            P.add("dve", lambda e, mt=mt: e.scalar_tensor_tensor(out=bigb[:, 16 + mt, :], in0=bigf[:, 4 + mt, :],
                                                                 scalar=V(f"qn{i}", mt), in1=rstd[:, :],
                                                                 op0=ALU.mult, op1=ALU.mult),
                  reads=[tf[4 + mt], t_rstd, t_vec], writes=[tb[16 + mt]])
        sl, tsl = g.load_slot(s0 + 10)
        for mt in range(2):
            bk = rot.get()

            def mm(e, sl=sl, bk=bk, mt=mt):
                ins = None
                for kt in range(KT):
                    ins = e.matmul(ps[bk][:, :], lhsT=sl[:, kt * 256 + mt * 128:kt * 256 + mt * 128 + 128],
                                   rhs=hT[:, kt, :], start=(kt == 0), stop=(kt == KT - 1))
                return ins
            P.add("pe", mm, reads=[tsl, t_h], writes=[t_ps[bk]])
            P.add("act", lambda e, bk=bk, mt=mt: e.activation(out=bigf[:, 2 + mt, :], in_=ps[bk][:, :], func=AF.Copy),
                  reads=[t_ps[bk]], writes=[tf[2 + mt]])
            P.add("act", lambda e, bk=bk, mt=mt: e.activation(out=bigb[:, 20 + mt, :], in_=ps[bk][:, :], func=AF.Square),
                  reads=[t_ps[bk]], writes=[tb[20 + mt]])
        g.rstd_from_sq([bigb[:, 20 + mt, :] for mt in range(2)], tb[20:22], TB, 256.0, EPS, 7)
        for mt in range(2):
            P.add("dve", lambda e, mt=mt: e.scalar_tensor_tensor(out=ckvn[:, mt, blk], in0=bigf[:, 2 + mt, :],
                                                                 scalar=V(f"kvn{i}", mt), in1=rstd[:, :],
                                                                 op0=ALU.mult, op1=ALU.mult),
                  reads=[tf[2 + mt], t_rstd, t_vec], writes=[t_ckv[b]])
        sl, tsl = g.load_slot(s0 + 11, 8 * 128)
        bA, bB = rot.get(), rot.get()

        def mmk(e, sl=sl, bA=bA, bB=bB):
            ins = None
            for kt in range(KT):
                ins = e.matmul(ps[bA][0:64, :], lhsT=sl[:, kt * 128:kt * 128 + 64], rhs=hT[:, kt, :],
                               start=(kt == 0), stop=(kt == KT - 1))
            for kt in range(KT):
                ins = e.matmul(ps[bB][0:64, :], lhsT=sl[:, kt * 128 + 64:kt * 128 + 128], rhs=hT[:, kt, :],
                               start=(kt == 0), stop=(kt == KT - 1))
            return ins
        P.add("pe", mmk, reads=[tsl, t_h], writes=[t_ps[bA], t_ps[bB]])
        rope_apply(bA, bB, kpe[0:64, blk], [t_kpe[b]])
        if g.stop < 3:
            return
        gs, tgs = g.load_slot(soff[f"e_gate{i}"])
        P.add("pool", lambda e: e.tensor_copy(out=S.gatew[:, :], in_=gs[:, :]), reads=[tgs], writes=[S.t_gatew])
        for n in range(8):
            sl, tsl = g.load_slot(s0 + n)
            bX, bG = rot.get(), rot.get()

            def mm(e, sl=sl, bX=bX, bG=bG):
                ins = None
                for kt in range(KT):
                    ins = e.matmul(ps[bX][:, :], lhsT=sl[:, kt * 256:kt * 256 + 128], rhs=hT[:, kt, :],
                                   start=(kt == 0), stop=(kt == KT - 1))
                for kt in range(KT):
                    ins = e.matmul(ps[bG][:, :], lhsT=sl[:, kt * 256 + 128:kt * 256 + 256], rhs=hT[:, kt, :],
                                   start=(kt == 0), stop=(kt == KT - 1))
                return ins
            P.add("pe", mm, reads=[tsl, t_h], writes=[t_ps[bX], t_ps[bG]])
            P.add("dve", lambda e, n=n: e.tensor_copy(out=xrw[:, 0:3], in_=tail[:, n, :]), reads=[t_tail[n]], writes=[t_xrw])
            P.add("act", lambda e, bX=bX: e.activation(out=xrw[:, 3:TB + 3], in_=ps[bX][:, :], func=AF.Copy),
                  reads=[t_ps[bX]], writes=[t_xrw])
            xc = bigf[:, 0, :]
            P.add("dve", lambda e, n=n: e.tensor_scalar(out=xc, in0=xrw[:, 3:TB + 3], scalar1=V(f"cw{i}_3", n),
                                                        scalar2=V(f"cb{i}", n), op0=ALU.mult, op1=ALU.add),
                  reads=[t_xrw, t_vec], writes=[tf[0]])
            for j in (2, 1, 0):
                P.add("dve", lambda e, n=n, j=j: e.scalar_tensor_tensor(out=xc, in0=xrw[:, j:TB + j], scalar=V(f"cw{i}_{j}", n),
                                                                        in1=xc, op0=ALU.mult, op1=ALU.add),
                      reads=[t_xrw, t_vec, tf[0]], writes=[tf[0]])
            P.add("dve", lambda e, n=n: e.tensor_copy(out=tail[:, n, :], in_=xrw[:, TB:TB + 3]), reads=[t_xrw], writes=[t_tail[n]])
            P.add("act", lambda e: e.activation(out=xcb[:, :], in_=xc, func=AF.Copy), reads=[tf[0]], writes=[t_xcb])
            bR, bI = rot.get(), rot.get()

            def mmg(e, n=n, bR=bR, bI=bI):
                e.matmul(ps[bR][:, :], lhsT=S.gatew[:, n * 256:n * 256 + 128], rhs=xcb[:, :], start=True, stop=True)
                return e.matmul(ps[bI][:, :], lhsT=S.gatew[:, n * 256 + 128:n * 256 + 256], rhs=xcb[:, :], start=True, stop=True)
            P.add("pe", mmg, reads=[S.t_gatew, t_xcb], writes=[t_ps[bR], t_ps[bI]])
            P.add("act", lambda e, n=n, bR=bR: e.activation(out=bigf[:, 1, :], in_=ps[bR][:, :], func=AF.Sigmoid, bias=V(f"gab{i}", n)),
                  reads=[t_ps[bR], t_vec], writes=[tf[1]])
            P.add("act", lambda e, n=n, bI=bI: e.activation(out=bigf[:, 2, :], in_=ps[bI][:, :], func=AF.Sigmoid, bias=V(f"gxb{i}", n)),
                  reads=[t_ps[bI], t_vec], writes=[tf[2]])
            P.add("act", lambda e, n=n: e.activation(out=bigf[:, 3, :], in_=bigf[:, 1, :], func=AF.Exp, scale=nsp8[:, n:n + 1]),
                  reads=[tf[1], t_nsp], writes=[tf[3]])
            P.add("dve", lambda e: e.tensor_tensor(out=bigf[:, 4, :], in0=bigf[:, 3, :], in1=bigf[:, 3, :], op=ALU.mult),
                  reads=[tf[3]], writes=[tf[4]])
            P.add("act", lambda e: e.activation(out=bigf[:, 4, :], in_=bigf[:, 4, :], func=AF.Sqrt, scale=-1.0, bias=g.one_ap),
                  reads=[tf[4], g.t_cb2], writes=[tf[4]])
            P.add("dve", lambda e: e.tensor_tensor(out=bigf[:, 5, :], in0=bigf[:, 2, :], in1=xc, op=ALU.mult),
                  reads=[tf[2], tf[0]], writes=[tf[5]])
            P.add("dve", lambda e: e.tensor_tensor(out=bigf[:, 5, :], in0=bigf[:, 5, :], in1=bigf[:, 4, :], op=ALU.mult),
                  reads=[tf[5], tf[4]], writes=[tf[5]])
            P.add("dve", lambda e, n=n: e.tensor_tensor_scan(out=bigf[:, 6, :], data0=bigf[:, 3, :], data1=bigf[:, 5, :],
                                                             initial=hst[:, n:n + 1], op0=ALU.mult, op1=ALU.add),
                  reads=[tf[3], tf[5], t_hst[n]], writes=[tf[6]])
            P.add("dve", lambda e, n=n: e.tensor_copy(out=hst[:, n:n + 1], in_=bigf[:, 6, TB - 1:TB]), reads=[tf[6]], writes=[t_hst[n]])
            P.add("act", lambda e, bG=bG: e.activation(out=bigf[:, 7, :], in_=ps[bG][:, :], func=AF.Gelu_apprx_tanh),
                  reads=[t_ps[bG]], writes=[tf[7]])
            P.add("dve", lambda e, n=n: e.tensor_tensor(out=bigb[:, n, :], in0=bigf[:, 6, :], in1=bigf[:, 7, :], op=ALU.mult),
                  reads=[tf[6], tf[7]], writes=[tb[n]])
        if g.stop < 4:
            return
        nkc = b + 1
        srot = g.PsRot([4, 5])
        for h in range(8):
            if h % 2 == 0:
                uq, tuq = g.load_slot(soff[f"e_uq{i}"] + h // 2)
            if h % 4 == 0:
                ukv_s, tukv_s = g.load_slot(soff[f"e_ukv{i}"] + h // 4)
                P.add("pool", lambda e, ukv_s=ukv_s: e.tensor_copy(out=S.ukvw[:, :], in_=ukv_s[:, :]), reads=[tukv_s], writes=[S.t_ukvw])
            ukv, tukv = S.ukvw, S.t_ukvw
            qb = (h % 2) * 4 * 256
            kb_ = (h % 4) * 2 * 256
            bQ, bA, bB = rot.get(), rot.get(), rot.get()

            def mmq(e, uq=uq, qb=qb, bQ=bQ, bA=bA, bB=bB):
                ins = None
                for kt in range(4):
                    ins = e.matmul(ps[bQ][:, :], lhsT=uq[:, qb + kt * 256:qb + kt * 256 + 128], rhs=bigb[:, 16 + kt, :],
                                   start=(kt == 0), stop=(kt == 3))
                for kt in range(4):
                    ins = e.matmul(ps[bA][0:64, :], lhsT=uq[:, qb + kt * 256 + 128:qb + kt * 256 + 192], rhs=bigb[:, 16 + kt, :],
                                   start=(kt == 0), stop=(kt == 3))
                for kt in range(4):
                    ins = e.matmul(ps[bB][0:64, :], lhsT=uq[:, qb + kt * 256 + 192:qb + kt * 256 + 256], rhs=bigb[:, 16 + kt, :],
                                   start=(kt == 0), stop=(kt == 3))
                return ins
            P.add("pe", mmq, reads=[tuq] + tb[16:20], writes=[t_ps[bQ], t_ps[bA], t_ps[bB]])
            P.add("act", lambda e, bQ=bQ: e.activation(out=qn[:, :], in_=ps[bQ][:, :], func=AF.Copy), reads=[t_ps[bQ]], writes=[t_qn])
            rope_apply(bA, bB, qr[0:64, :], [t_qr])
            for c in range(nkc):
                bk = rot.get()

                def mmK(e, c=c, bk=bk, kb_=kb_):
                    ins = None
                    for kt in range(2):
                        ins = e.matmul(ps[bk][:, :], lhsT=ukv[:, kb_ + kt * 256:kb_ + kt * 256 + 128],
                                       rhs=ckvn[:, kt, c * TB:(c + 1) * TB], start=(kt == 0), stop=(kt == 1))
                    return ins
                P.add("pe", mmK, reads=[tukv, t_ckv[c]], writes=[t_ps[bk]])
                P.add("act", lambda e, c=c, bk=bk: e.activation(out=Kh[:, c * TB:(c + 1) * TB], in_=ps[bk][:, :], func=AF.Copy),
                      reads=[t_ps[bk]], writes=[t_Kh[c]])
                bv = rot.get()

                def mmV(e, c=c, bv=bv, kb_=kb_):
                    ins = None
                    for jj in range(4):
                        for kt in range(2):
                            ins = e.matmul(ps[bv][:, jj * 128:(jj + 1) * 128],
                                           lhsT=ckvn[:, kt, c * TB + jj * 128:c * TB + (jj + 1) * 128],
                                           rhs=ukv[:, kb_ + kt * 256 + 128:kb_ + kt * 256 + 256],
                                           start=(kt == 0), stop=(kt == 1))
                    return ins
                P.add("pe", mmV, reads=[tukv, t_ckv[c]], writes=[t_ps[bv]])
                P.add("dve", lambda e, c=c, bv=bv: e.tensor_copy(out=Vh[:, 4 * c:4 * c + 4, :], in_=ps[bv][:, :]),
                      reads=[t_ps[bv]], writes=[t_Vh[c]])
            nj = 4 * nkc
            sbks = {}

            def emit_S(j):
                jj = j - 4 * b
                c0 = max(0, jj) * 128
                sbk = srot.get()
                sbks[j] = sbk

                def mmS(e, j=j, jj=jj, c0=c0, sbk=sbk):
                    e.matmul(ps[sbk][:, c0:TB], lhsT=Kh[:, j * 128:(j + 1) * 128], rhs=qn[:, c0:TB], start=True, stop=False)
                    ins = e.matmul(ps[sbk][:, c0:TB], lhsT=kpe[:, j * 128:(j + 1) * 128], rhs=qr[:, c0:TB],
                                   start=False, stop=(jj < 0))
                    if jj >= 0:
                        ins = e.matmul(ps[sbk][:, c0:c0 + 128], lhsT=g.identb[:, :], rhs=g.amaskb[:, :], start=False, stop=True)
                    return ins
                P.add("pe", mmS, reads=[t_Kh[j // 4], t_kpe[j // 4], t_qn, t_qr, t_cb], writes=[t_ps[sbk]])

            def emit_rest(j):
                jj = j - 4 * b
                c0 = max(0, jj) * 128
                sbk = sbks[j]
                pt = j % 2
                P.add("act", lambda e, c0=c0, sbk=sbk, pt=pt: e.activation(out=PT[pt][:, c0:TB], in_=ps[sbk][:, c0:TB],
                                                                           func=AF.Exp, scale=SCALE),
                      reads=[t_ps[sbk]], writes=[t_PT[pt]])

                def mmO(e, j=j, c0=c0, pt=pt, nj=nj):
                    e.matmul(ps[6][:, c0:TB], lhsT=Vh[:, j, :], rhs=PT[pt][:, c0:TB], start=(j == 0), stop=(j == nj - 1))
                    return e.matmul(ps[7][:, c0:TB], lhsT=g.onesb[:, :], rhs=PT[pt][:, c0:TB], start=(j == 0), stop=(j == nj - 1))
                P.add("pe", mmO, reads=[t_Vh[j // 4], t_PT[pt], t_cb], writes=[t_ps[6], t_ps[7]])
            emit_S(0)
            for j in range(nj):
                if j + 1 < nj:
                    emit_S(j + 1)
                emit_rest(j)
            P.add("dve", lambda e: e.reciprocal(out=bigf[:, 2, :], in_=ps[7][:, :]), reads=[t_ps[7]], writes=[tf[2]])
            P.add("dve", lambda e, h=h: e.tensor_tensor(out=bigb[:, 8 + h, :], in0=ps[6][:, :], in1=bigf[:, 2, :], op=ALU.mult),
                  reads=[t_ps[6], tf[2]], writes=[tb[8 + h]])
        if g.stop < 5:
            return
        for m in range(8):
            sl, tsl = g.load_slot(soff[f"e_wout{i}"] + m)
            bk = rot.get()

            def mmo(e, sl=sl, bk=bk):
                ins = None
                for kt in range(16):
                    ins = e.matmul(ps[bk][:, :], lhsT=sl[:, kt * 128:(kt + 1) * 128], rhs=bigb[:, kt, :],
                                   start=(kt == 0), stop=(kt == 15))
                return ins
            P.add("pe", mmo, reads=[tsl] + tb[0:16], writes=[t_ps[bk]])
            g.post_evac(bk, m, f"nmo{l}")
        g.post_finish(b, 7)

    S.gatew = sb("gatew", [128, SLOT_EL], BF16)
    S.t_gatew = Tok()
    S.ukvw = sb("ukvw", [128, SLOT_EL], BF16)
    S.t_ukvw = Tok()
    S.begin_layer = begin_layer
    S.block = block
    return S


def make_odd(g):
    P, ps, t_ps, V, C = g.P, g.ps, g.t_ps, g.V, g.C
    bigb, tb, bigf, tf, hT, t_h = g.bigb, g.tb, g.bigf, g.tf, g.hT, g.t_h
    rstd, t_rstd, t_vec, t_cb, t_cst = g.rstd, g.t_rstd, g.t_vec, g.t_cb, g.t_cst
    onesf = g.onesf
    soff = g.soff
    S = Ctx()
    sb = g.mk_asb()
    Tok = lambda: reg_tok(g)
    S_all = sb("S_all", [128, 16, 128], F32)
    t_S = [Tok() for _ in range(16)]
    gtail = sb("gtail", [128, 32, 3], F32)
    t_gt = [Tok() for _ in range(32)]
    xw = sb("gxw", [128, TB + 4], F32)
    t_xw = Tok()
    beta = sb("beta", [128, 4, 16], F32)
    nbeta = sb("nbeta", [128, 4, 16], F32)
    gg = sb("gg", [128, 4, 16], F32)
    eg = sb("eg", [128, 4, 16], F32)
    egr = sb("egr", [128, 4, 16], F32)
    dch = sb("dch", [128, 8, 16], F32)
    t_gate = Tok()
    nexpA = sb("nexpA", [128, 16], F32)
    t_nexp = Tok()
    KKs = sb("KKs", [128, 4, 128], F32)
    QKs = sb("QKs", [128, 4, 128], F32)
    Qt = sb("Qt", [128, 4, 128], F32)
    Ktt = sb("Ktt", [128, 4, 128], F32)
    t_pair = [Tok() for _ in range(4)]
    CH = []
    for p in range(4):
        c = Ctx()
        c.XY = [sb(f"XY{p}_{k}", [128, 256], F32) for k in range(2)]
        c.X = [c.XY[k][:, 0:128] for k in range(2)]
        c.Y = [c.XY[k][:, 128:256] for k in range(2)]
        c.R = [sb(f"R{p}_{k}", [128, 128], F32) for k in range(2)]
        c.QKT = sb(f"QKT{p}", [128, 128], F32)
        c.kdec = sb(f"kdec{p}", [128, 128], F32)
        c.kdecB = sb(f"kdecB{p}", [128, 128], F32)
        c.Kbe = sb(f"Kbe{p}", [128, 128], F32)
        c.Vb = sb(f"Vb{p}", [128, 128], F32)
        c.tX = [Tok(), Tok()]
        c.tY = [Tok(), Tok()]
        c.tR = [Tok(), Tok()]
        c.tQKT, c.tkdec, c.tKbe, c.tVb = Tok(), Tok(), Tok(), Tok()
        CH.append(c)
    ident = C("ident")
    QSCALE = float(128 ** -0.5)

    def begin_layer(l):
        i = l // 2
        g.arena_barrier()
        P.add("act", lambda e: e.activation(out=nexpA[:, :], in_=V(f"alog{i}", 0, 16), func=AF.Exp),
              reads=[t_vec], writes=[t_nexp])
        P.add("dve", lambda e: e.tensor_scalar(out=nexpA[:, :], in0=nexpA[:, :], scalar1=-1.0, scalar2=None, op0=ALU.mult),
              reads=[t_nexp], writes=[t_nexp])
        P.add("dve", lambda e: e.memset(S_all[:, :, :], 0.0), writes=t_S)
        P.add("dve", lambda e: e.memset(gtail[:, :, :], 0.0), writes=t_gt)

    def conv_silu(i, bank, tile, out_ap, out_tok):
        P.add("dve", lambda e: e.tensor_copy(out=xw[:, 0:3], in_=gtail[:, tile, :]), reads=[t_gt[tile]], writes=[t_xw])
        P.add("act", lambda e: e.activation(out=xw[:, 3:TB + 3], in_=ps[bank][:, :], func=AF.Copy),
              reads=[t_ps[bank]], writes=[t_xw])
        P.add("dve", lambda e: e.tensor_scalar(out=out_ap, in0=xw[:, 3:TB + 3], scalar1=V(f"gcw{i}_3", tile), scalar2=None,
                                               op0=ALU.mult), reads=[t_xw, t_vec], writes=[out_tok])
        for j in (2, 1, 0):
            P.add("dve", lambda e, j=j: e.scalar_tensor_tensor(out=out_ap, in0=xw[:, j:TB + j], scalar=V(f"gcw{i}_{j}", tile),
                                                               in1=out_ap, op0=ALU.mult, op1=ALU.add),
                  reads=[t_xw, t_vec, out_tok], writes=[out_tok])
        P.add("dve", lambda e: e.tensor_copy(out=gtail[:, tile, :], in_=xw[:, TB:TB + 3]), reads=[t_xw], writes=[t_gt[tile]])
        P.add("act", lambda e: e.activation(out=out_ap, in_=out_ap, func=AF.Silu), reads=[out_tok], writes=[out_tok])

    def l2norm(x_ap, x_tok, scale):
        P.add("act", lambda e: e.activation(out=bigb[:, 17, :], in_=x_ap, func=AF.Square), reads=[x_tok], writes=[tb[17]])
        g.rstd_from_sq([bigb[:, 17, :]], [tb[17]], TB, 1.0, EPS, 7)
        P.add("dve", lambda e: e.scalar_tensor_tensor(out=x_ap, in0=x_ap, scalar=scale, in1=rstd[:, :], op0=ALU.mult, op1=ALU.mult),
              reads=[x_tok, t_rstd], writes=[x_tok])

    def proj2(sl, tsl, b0, b1):
        def mm(e):
            ins = None
            for kt in range(KT):
                ins = e.matmul(ps[b0][:, :], lhsT=sl[:, kt * 256:kt * 256 + 128], rhs=hT[:, kt, :], start=(kt == 0), stop=(kt == KT - 1))
            for kt in range(KT):
                ins = e.matmul(ps[b1][:, :], lhsT=sl[:, kt * 256 + 128:kt * 256 + 256], rhs=hT[:, kt, :], start=(kt == 0), stop=(kt == KT - 1))
            return ins
        P.add("pe", mm, reads=[tsl, t_h], writes=[t_ps[b0], t_ps[b1]])

    def block(l, b):
        i = l // 2
        rot = g.PsRot([0, 1, 2, 3, 4])
        g.prenorm(b, f"nmp{l}", 7)
        if g.stop < 1:
            return
        wba, twba = g.load_slot(soff[f"g_wba{i}"], 8 * 32)
        bk = rot.get()

        def mmba(e):
            ins = None
            for tt in range(4):
                for kt in range(KT):
                    ins = e.matmul(ps[bk][:, tt * 32:(tt + 1) * 32], lhsT=hT[:, kt, tt * 128:(tt + 1) * 128],
                                   rhs=wba[:, kt * 32:(kt + 1) * 32], start=(kt == 0), stop=(kt == KT - 1))
            return ins
        P.add("pe", mmba, reads=[twba, t_h], writes=[t_ps[bk]])
        for tt in range(4):
            P.add("act", lambda e, tt=tt: e.activation(out=beta[:, tt, :], in_=ps[bk][:, tt * 32:tt * 32 + 16], func=AF.Sigmoid),
                  reads=[t_ps[bk]], writes=[t_gate])
            P.add("dve", lambda e, tt=tt: e.tensor_tensor(out=gg[:, tt, :], in0=ps[bk][:, tt * 32 + 16:tt * 32 + 32],
                                                          in1=V(f"dtb{i}", 0, 16), op=ALU.add),
                  reads=[t_ps[bk], t_vec], writes=[t_gate])
        P.add("act", lambda e: e.activation(out=gg[:, :, :], in_=gg[:, :, :], func=AF.Exp), reads=[t_gate], writes=[t_gate])
        P.add("act", lambda e: e.activation(out=gg[:, :, :], in_=gg[:, :, :], func=AF.Ln, bias=g.one_ap),
              reads=[t_gate, g.t_cb2], writes=[t_gate])
        for tt in range(4):
            P.add("dve", lambda e, tt=tt: e.tensor_tensor(out=gg[:, tt, :], in0=gg[:, tt, :], in1=nexpA[:, :], op=ALU.mult),
                  reads=[t_gate, t_nexp], writes=[t_gate])
        P.add("dve", lambda e: e.tensor_scalar(out=nbeta[:, :, :], in0=beta[:, :, :], scalar1=-1.0, scalar2=None, op0=ALU.mult),
              reads=[t_gate], writes=[t_gate])
        for p in range(4):
            bc = rot.get()

            def mmc(e, p=p, bc=bc):
                e.matmul(ps[bc][:, 0:16], lhsT=C("U2"), rhs=gg[:, p, :], start=True, stop=True)
                e.matmul(ps[bc][:, 16:32], lhsT=C("SL2"), rhs=gg[:, p, :], start=True, stop=True)
                e.matmul(ps[bc][:, 32:48], lhsT=C("ONA"), rhs=gg[:, p, :], start=True, stop=True)
                return e.matmul(ps[bc][:, 48:64], lhsT=C("ONB"), rhs=gg[:, p, :], start=True, stop=True)
            P.add("pe", mmc, reads=[t_gate, t_cst, t_cb], writes=[t_ps[bc]])

            def ex(e, p=p, bc=bc):
                e.activation(out=eg[:, p, :], in_=ps[bc][:, 0:16], func=AF.Exp)
                e.activation(out=egr[:, p, :], in_=ps[bc][:, 16:32], func=AF.Exp)
                e.activation(out=dch[:, 2 * p, :], in_=ps[bc][:, 32:48], func=AF.Exp)
                return e.activation(out=dch[:, 2 * p + 1, :], in_=ps[bc][:, 48:64], func=AF.Exp)
            P.add("act", ex, reads=[t_ps[bc]], writes=[t_gate])
        if g.stop < 2:
            return
        qf, kf = bigf[:, 0, :], bigf[:, 1, :]
        vfs = [bigf[:, 2, :], bigf[:, 3, :]]
        zfs = [bigf[:, 4, :], bigf[:, 5, :]]
        for j in range(g.dbg[0] + 1 if g.dbg else 8):
            s0 = soff[f"g_win{i}"] + 3 * j
            sl, tsl = g.load_slot(s0)
            b0, b1 = rot.get(), rot.get()
            proj2(sl, tsl, b0, b1)
            conv_silu(i, b0, j, qf, tf[0])
            conv_silu(i, b1, 8 + j, kf, tf[1])
            l2norm(qf, tf[0], QSCALE)
            l2norm(kf, tf[1], 1.0)
            sl, tsl = g.load_slot(s0 + 1)
            b0, b1 = rot.get(), rot.get()
            proj2(sl, tsl, b0, b1)
            conv_silu(i, b0, 16 + 2 * j, vfs[0], tf[2])
            conv_silu(i, b1, 16 + 2 * j + 1, vfs[1], tf[3])
            sl, tsl = g.load_slot(s0 + 2)
            b0, b1 = rot.get(), rot.get()
            proj2(sl, tsl, b0, b1)
            P.add("act", lambda e, b0=b0: e.activation(out=zfs[0], in_=ps[b0][:, :], func=AF.Silu), reads=[t_ps[b0]], writes=[tf[4]])
            P.add("act", lambda e, b1=b1: e.activation(out=zfs[1], in_=ps[b1][:, :], func=AF.Silu), reads=[t_ps[b1]], writes=[tf[5]])
            for p in range(4):
                cols = slice(p * 128, (p + 1) * 128)
                bkk = rot.get()

                def mmr(e, cols=cols, bkk=bkk):
                    e.matmul(ps[bkk][:, 0:128], lhsT=kf[:, cols], rhs=kf[:, cols], start=True, stop=True)
                    e.matmul(ps[bkk][:, 128:256], lhsT=kf[:, cols], rhs=qf[:, cols], start=True, stop=True)
                    e.transpose(ps[bkk][:, 256:384], qf[:, cols], ident)
                    return e.transpose(ps[bkk][:, 384:512], kf[:, cols], ident)
                P.add("pe", mmr, reads=[tf[0], tf[1], t_cst], writes=[t_ps[bkk]])

                def ev1(e, p=p, bkk=bkk):
                    e.activation(out=KKs[:, p, :], in_=ps[bkk][:, 0:128], func=AF.Copy)
                    return e.activation(out=Qt[:, p, :], in_=ps[bkk][:, 256:384], func=AF.Copy)
                P.add("act", ev1, reads=[t_ps[bkk]], writes=[t_pair[p]])

                def ev2(e, p=p, bkk=bkk):
                    e.tensor_copy(out=QKs[:, p, :], in_=ps[bkk][:, 128:256])
                    return e.tensor_copy(out=Ktt[:, p, :], in_=ps[bkk][:, 384:512])
                P.add("dve", ev2, reads=[t_ps[bkk]], writes=[t_pair[p]])
            for half in range(2 if g.stop >= 3 else 0):
                combos = [(hh, 2 * half + q, CH[hh * 2 + q]) for hh in range(2) for q in range(2)]
                bvts = {}
                for hh in range(2):
                    vf, tvf = vfs[hh], tf[2 + hh]
                    bvt = rot.get()
                    bvts[hh] = bvt

                    def mmvt(e, vf=vf, bvt=bvt, half=half):
                        ins = None
                        for q in range(2):
                            p = 2 * half + q
                            ins = e.transpose(ps[bvt][:, q * 128:(q + 1) * 128], vf[:, p * 128:(p + 1) * 128], ident)
                        return ins
                    P.add("pe", mmvt, reads=[tvf, t_cst], writes=[t_ps[bvt]])
                for hh, p, c in combos:
                    h = 2 * j + hh
                    q = p - 2 * half
                    bvt = bvts[hh]
                    P.add("dve", lambda e, c=c, p=p, h=h: e.tensor_scalar(out=c.X[1][:, :], in0=C("U2"), scalar1=gg[:, p, h:h + 1],
                                                                          scalar2=None, op0=ALU.mult),
                          reads=[t_cst, t_gate], writes=[c.tX[1]])
                    P.add("dve", lambda e, c=c, p=p, h=h: e.tensor_scalar(out=c.Y[1][:, :], in0=C("SL2"), scalar1=gg[:, p, h:h + 1],
                                                                          scalar2=None, op0=ALU.mult),
                          reads=[t_cst, t_gate], writes=[c.tY[1]])
                    P.add("dve", lambda e, c=c, p=p, h=h: e.tensor_scalar(out=c.Kbe[:, :], in0=Ktt[:, p, :], scalar1=beta[:, p, h:h + 1],
                                                                          scalar2=eg[:, p, h:h + 1], op0=ALU.mult, op1=ALU.mult),
                          reads=[t_pair[p], t_gate], writes=[c.tKbe])
                    P.add("dve", lambda e, c=c, p=p, h=h: e.tensor_scalar(out=c.kdec[:, :], in0=Ktt[:, p, :], scalar1=egr[:, p, h:h + 1],
                                                                          scalar2=C("ONA", 128, 1), op0=ALU.mult, op1=ALU.mult),
                          reads=[t_pair[p], t_gate, t_cst], writes=[c.tkdec])
                    P.add("dve", lambda e, c=c, p=p, h=h: e.tensor_scalar(out=c.kdecB[:, :], in0=Ktt[:, p, :], scalar1=egr[:, p, h:h + 1],
                                                                          scalar2=C("ONB", 128, 1), op0=ALU.mult, op1=ALU.mult),
                          reads=[t_pair[p], t_gate, t_cst], writes=[c.tkdec])
                    P.add("dve", lambda e, c=c, p=p, h=h, bvt=bvt, q=q: e.tensor_scalar(out=c.Vb[:, :], in0=ps[bvt][:, q * 128:(q + 1) * 128],
                                                                                        scalar1=beta[:, p, h:h + 1], scalar2=None, op0=ALU.mult),
                          reads=[t_ps[bvt], t_gate], writes=[c.tVb])
                for hh, p, c in combos:
                    bD = rot.get()

                    def mmD(e, c=c, bD=bD):
                        e.matmul(ps[bD][:, 0:128], lhsT=c.X[1][:, :], rhs=C("SL2"), start=True, stop=False)
                        e.matmul(ps[bD][:, 0:128], lhsT=ident, rhs=C("NMS"), start=False, stop=True)
                        e.matmul(ps[bD][:, 128:256], lhsT=c.Y[1][:, :], rhs=C("U2"), start=True, stop=False)
                        return e.matmul(ps[bD][:, 128:256], lhsT=ident, rhs=C("NMC"), start=False, stop=True)
                    P.add("pe", mmD, reads=[c.tX[1], c.tY[1], t_cst], writes=[t_ps[bD]])
                    P.add("act", lambda e, c=c, bD=bD: e.activation(out=c.R[1][:, :], in_=ps[bD][:, 0:128], func=AF.Exp),
                          reads=[t_ps[bD]], writes=[c.tR[1]])
                    P.add("act", lambda e, c=c, bD=bD: e.activation(out=c.QKT[:, :], in_=ps[bD][:, 128:256], func=AF.Exp),
                          reads=[t_ps[bD]], writes=[c.tQKT])
                for hh, p, c in combos:
                    h = 2 * j + hh
                    P.add("dve", lambda e, c=c, p=p, h=h: e.scalar_tensor_tensor(out=c.X[0][:, :], in0=KKs[:, p, :], scalar=nbeta[:, p, h:h + 1],
                                                                                 in1=c.R[1][:, :], op0=ALU.mult, op1=ALU.mult),
                          reads=[t_pair[p], t_gate, c.tR[1]], writes=[c.tX[0]])
                    P.add("dve", lambda e, c=c, p=p: e.tensor_tensor(out=c.QKT[:, :], in0=QKs[:, p, :], in1=c.QKT[:, :], op=ALU.mult),
                          reads=[t_pair[p], c.tQKT], writes=[c.tQKT])
                bTs = []
                for hh, p, c in combos:
                    bT = rot.get()
                    bTs.append(bT)
                    P.add("pe", lambda e, c=c, bT=bT: e.transpose(ps[bT][:, 0:128], c.X[0][:, :], ident),
                          reads=[c.tX[0], t_cst], writes=[t_ps[bT]])
                for (hh, p, c), bT in zip(combos, bTs):
                    P.add("act", lambda e, c=c, bT=bT: e.activation(out=c.Y[0][:, :], in_=ps[bT][:, 0:128], func=AF.Copy),
                          reads=[t_ps[bT]], writes=[c.tY[0]])
                    P.add("dve", lambda e, c=c: e.tensor_tensor(out=c.R[0][:, :], in0=c.Y[0][:, :], in1=ident, op=ALU.add),
                          reads=[c.tY[0], t_cst], writes=[c.tR[0]])
                a, ra = 0, 0
                for k in range(1, 6):
                    b1s = []
                    for hh, p, c in combos:
                        b1_ = rot.get()
                        b1s.append(b1_)

                        def mmsq(e, c=c, b1_=b1_, a=a, k=k):
                            ins = e.matmul(ps[b1_][:, 0:128], lhsT=c.Y[a][:, :], rhs=c.X[a][:, :], start=True, stop=True)
                            if k < 5:
                                ins = e.matmul(ps[b1_][:, 128:256], lhsT=c.X[a][:, :], rhs=c.Y[a][:, :], start=True, stop=True)
                            return ins
                        P.add("pe", mmsq, reads=[c.tX[a], c.tY[a]], writes=[t_ps[b1_]])
                    for (hh, p, c), b1_ in zip(combos, b1s):
                        if k < 5:
                            P.add("act", lambda e, c=c, b1_=b1_, a=a: e.activation(out=c.XY[1 - a][:, 0:256], in_=ps[b1_][:, 0:256], func=AF.Copy),
                                  reads=[t_ps[b1_]], writes=[c.tX[1 - a], c.tY[1 - a]])
                        else:
                            P.add("act", lambda e, c=c, b1_=b1_, a=a: e.activation(out=c.X[1 - a][:, :], in_=ps[b1_][:, 0:128], func=AF.Copy),
                                  reads=[t_ps[b1_]], writes=[c.tX[1 - a]])
                    b2s = []
                    for hh, p, c in combos:
                        b2_ = rot.get()
                        b2s.append(b2_)

                        def mmR(e, c=c, b2_=b2_, a=a, ra=ra):
                            return e.matmul(ps[b2_][:, 0:128], lhsT=c.X[1 - a][:, :], rhs=c.R[ra][:, :], start=True, stop=True)
                        P.add("pe", mmR, reads=[c.tX[1 - a], c.tR[ra], t_cst], writes=[t_ps[b2_]])
                    for (hh, p, c), b2_ in zip(combos, b2s):
                        P.add("dve", lambda e, c=c, b2_=b2_, ra=ra: e.tensor_tensor(out=c.R[1 - ra][:, :], in0=c.R[ra][:, :], in1=ps[b2_][:, 0:128],
                                                                                   op=ALU.add),
                              reads=[c.tR[ra], t_ps[b2_]], writes=[c.tR[1 - ra]])
                    a, ra = 1 - a, 1 - ra
                bws = []
                for hh, p, c in combos:
                    h = 2 * j + hh
                    bw = rot.get()
                    bws.append(bw)

                    def mmw(e, c=c, bw=bw, ra=ra):
                        e.matmul(ps[bw][:, 0:128], lhsT=c.Kbe[:, :], rhs=c.R[ra][:, :], start=True, stop=True)
                        return e.matmul(ps[bw][:, 128:256], lhsT=c.R[ra][:, :], rhs=c.Vb[:, :], start=True, stop=True)
                    P.add("pe", mmw, reads=[c.tKbe, c.tVb, c.tR[ra]], writes=[t_ps[bw]])
                    P.add("dve", lambda e, c=c, p=p, h=h: e.tensor_scalar(out=c.Y[1][:, :], in0=ident, scalar1=eg[:, p, h:h + 1], scalar2=None,
                                                                          op0=ALU.mult),
                          reads=[t_cst, t_gate, c.tY[1]], writes=[c.tY[1]])
                for (hh, p, c), bw in zip(combos, bws):
                    P.add("act", lambda e, c=c, bw=bw: e.activation(out=c.X[0][:, :], in_=ps[bw][:, 0:128], func=AF.Identity, scale=-1.0),
                          reads=[t_ps[bw]], writes=[c.tX[0]])
                    P.add("act", lambda e, c=c, bw=bw: e.activation(out=c.X[1][:, :], in_=ps[bw][:, 128:256], func=AF.Copy),
                          reads=[t_ps[bw]], writes=[c.tX[1]])
                bqs = []
                for hh, p, c in combos:
                    bq = rot.get()
                    bqs.append(bq)
                    P.add("pe", lambda e, c=c, bq=bq, p=p: e.matmul(ps[bq][:, 0:128], lhsT=Qt[:, p, :], rhs=c.Y[1][:, :], start=True, stop=True),
                          reads=[t_pair[p], c.tY[1]], writes=[t_ps[bq]])
                for (hh, p, c), bq in zip(combos, bqs):
                    P.add("act", lambda e, c=c, bq=bq: e.activation(out=c.Y[0][:, :], in_=ps[bq][:, 0:128], func=AF.Copy),
                          reads=[t_ps[bq]], writes=[c.tY[0]])
                for lc in range(4):
                    cc = 4 * half + lc
                    p, s_ = cc // 2, cc % 2
                    r0 = 64 * s_
                    rows = slice(r0, r0 + 64)
                    ctxs = []
                    for hh in range(2):
                        h = 2 * j + hh
                        c = CH[hh * 2 + (p - 2 * half)]
                        ctxs.append((hh, h, c, S_all[:, h, :], 6 if hh == 0 else 5, c.kdec if s_ == 0 else c.kdecB, rot.get(), rot.get()))
                    for hh, h, c, Sh, ob, kd, bw_, bsu in ctxs:
                        P.add("pe", lambda e, c=c, bw_=bw_, Sh=Sh: e.matmul(ps[bw_][:, 0:128], lhsT=c.X[0][:, :], rhs=Sh, start=True, stop=True),
                              reads=[c.tX[0], t_S[h]], writes=[t_ps[bw_]])
                    for hh, h, c, Sh, ob, kd, bw_, bsu in ctxs:
                        P.add("dve", lambda e, c=c, bw_=bw_, rows=rows: e.tensor_tensor(out=c.X[1][rows, :], in0=c.X[1][rows, :],
                                                                                       in1=ps[bw_][rows, 0:128], op=ALU.add),
                              reads=[c.tX[1], t_ps[bw_]], writes=[c.tX[1]])
                    for hh, h, c, Sh, ob, kd, bw_, bsu in ctxs:
                        def mmo(e, c=c, cc=cc, rows=rows, bsu=bsu, Sh=Sh, kd=kd, ob=ob):
                            e.matmul(ps[ob][:, cc * 64:(cc + 1) * 64], lhsT=Sh, rhs=c.Y[0][:, rows], start=True, stop=False)
                            e.matmul(ps[ob][:, cc * 64:(cc + 1) * 64], lhsT=c.X[1][:, :], rhs=c.QKT[:, rows], start=False, stop=True)
                            return e.matmul(ps[bsu][:, 0:128], lhsT=kd[:, :], rhs=c.X[1][:, :], start=True, stop=True)
                        P.add("pe", mmo, reads=[t_S[h], c.tY[0], c.tX[1], c.tQKT, c.tkdec], writes=[t_ps[ob], t_ps[bsu]])
                    for hh, h, c, Sh, ob, kd, bw_, bsu in ctxs:
                        P.add("dve", lambda e, cc=cc, h=h, bsu=bsu, Sh=Sh: e.scalar_tensor_tensor(out=Sh, in0=Sh, scalar=dch[:, cc, h:h + 1],
                                                                                          in1=ps[bsu][:, 0:128], op0=ALU.mult, op1=ALU.add),
                              reads=[t_S[h], t_gate, t_ps[bsu]], writes=[t_S[h]])
            for hh in range(2 if g.stop >= 3 else 0):
                h = 2 * j + hh
                ob = 6 if hh == 0 else 5
                P.add("act", lambda e, ob=ob: e.activation(out=bigb[:, 16, :], in_=ps[ob][:, :], func=AF.Square), reads=[t_ps[ob]], writes=[tb[16]])
                g.rstd_from_sq([bigb[:, 16, :]], [tb[16]], TB, 128.0, EPS, 7)
                P.add("act", lambda e, ob=ob: e.activation(out=bigf[:, 7, :], in_=ps[ob][:, :], func=AF.Identity, scale=V(f"gn{i}")),
                      reads=[t_ps[ob], t_vec], writes=[tf[7]])
                P.add("dve", lambda e: e.tensor_tensor(out=bigf[:, 6, :], in0=bigf[:, 7, :], in1=rstd[:, :], op=ALU.mult),
                      reads=[tf[7], t_rstd], writes=[tf[6]])
                P.add("dve", lambda e, h=h, hh=hh: e.tensor_tensor(out=bigb[:, h, :], in0=bigf[:, 6, :], in1=zfs[hh], op=ALU.mult),
                      reads=[tf[6], tf[4 + hh]], writes=[tb[h]])
        if g.dbg or g.stop < 6:
            return
        for m in range(8):
            sl, tsl = g.load_slot(soff[f"g_wout{i}"] + m)
            bk = rot.get()

            def mmo2(e, sl=sl, bk=bk):
                ins = None
                for kt in range(16):
                    ins = e.matmul(ps[bk][:, :], lhsT=sl[:, kt * 128:(kt + 1) * 128], rhs=bigb[:, kt, :],
                                   start=(kt == 0), stop=(kt == 15))
                return ins
            P.add("pe", mmo2, reads=[tsl] + tb[0:16], writes=[t_ps[bk]])
            g.post_evac(bk, m, f"nmo{l}")
        g.post_finish(b, 7)

    S.begin_layer = begin_layer
    S.block = block
    return S


_CACHE = {}


def kernel(**inputs):
    inp = {k: np.asarray(v) for k, v in inputs.items()}
    W = pack_weights(inp)
    Vv = pack_vecs(inp)
    Cc = make_consts()
    if "nc" not in _CACHE:
        _CACHE["nc"] = build_program()[0]
    nc = _CACHE["nc"]
    in_maps = []
    for b in range(8):
        x = np.asarray(inp["x"][b], np.float32)
        xT = np.ascontiguousarray(x.T.reshape(8, 128, T).transpose(1, 0, 2))
        pos = np.ascontiguousarray(np.broadcast_to(np.asarray(inp["positions"][b], np.int32)[None, :], (64, T)))
        in_maps.append({"xT": xT, "pos": pos, "vecs": Vv, "cst": Cc, "wts": W})
    res = run_bass_kernel_spmd(nc, in_maps, core_ids=list(range(8)))
    out = np.stack([np.asarray(r["yT"]).transpose(1, 0, 2).reshape(D, T).T for r in res.results])
    return np.ascontiguousarray(out.astype(np.float32))
```
